# Optimizing a Trainium2 kernel written in Bass

```python
import math
import jax, jax.numpy as jnp
from jax import lax
import numpy as np

D_MODEL = 1024
BATCH = 32
SEQ = 256
DEPTH = 4
DEC_BATCH = 4
DEC_SEQ = 4096
PAST_LEN = 512

GRID_W = 64
N_MIXERS = 3
N_SSD = (DEPTH + 2) // 3
N_HY = (DEPTH + 1) // 3
N_LRU = DEPTH // 3
ALPHA = (2.0 * DEPTH) ** 0.25
BETA = (8.0 * DEPTH) ** -0.25
LN_EPS = 1e-5
RMS_EPS = 1e-5

SSD_INNER = 2 * D_MODEL
SSD_HEADDIM = 64
SSD_HEADS = SSD_INNER // SSD_HEADDIM
SSD_GROUPS = 8
SSD_STATE = 128
SSD_CONV = 4
SSD_CHUNK = 128
SSD_XBC = SSD_INNER + 2 * SSD_GROUPS * SSD_STATE
SSD_PROJ = SSD_INNER + SSD_XBC + 2 * SSD_HEADS

HY_WIDTH = D_MODEL
HY_ORDER = 2
HY_SHORT = 3
HY_BANDS = 16
HY_EMB = 2 * HY_BANDS + 1
HY_FFN = 64
HY_FAST_DECAY = 0.3
HY_SLOW_DECAY = 1.5
HY_TARGET = 1e-2

LRU_WIDTH = D_MODEL
LRU_BLOCKS = 4
LRU_BLOCK = LRU_WIDTH // LRU_BLOCKS
LRU_CONV = 4
LRU_C = 8.0

kernel_name = 'hybrid_ssd_hyena_rglru_diffusion_step'

f32 = jnp.float32


def layer_norm(x, g, b):
    xf = x.astype(f32)
    mu = jnp.mean(xf, -1, keepdims=True)
    var = jnp.mean(jnp.square(xf - mu), -1, keepdims=True)
    return ((xf - mu) * lax.rsqrt(var + LN_EPS) * g + b).astype(x.dtype)


def rms_norm(x, g):
    xf = x.astype(f32)
    return (xf * lax.rsqrt(jnp.mean(xf * xf, -1, keepdims=True) + RMS_EPS) * g).astype(x.dtype)


def dw_conv(x, w, b):
    K = w.shape[0]
    left = (K - 1) // 2
    right = K - 1 - left
    L = x.shape[1]
    xp = jnp.pad(x, ((0, 0), (left, right), (0, 0)))
    out = xp[:, 0:L] * w[0]
    for k in range(1, K):
        out = out + xp[:, k:k + L] * w[k]
    return out + b


def modulation(cond, w, b):
    m = jax.nn.silu(cond) @ w + b
    shift, scale, gate = jnp.split(m, 3, axis=-1)
    return shift[:, None], scale[:, None], gate[:, None]


def to_col_major(x):
    b, L, C = x.shape
    rows = L // GRID_W
    return x.reshape(b, rows, GRID_W, C).transpose(0, 2, 1, 3).reshape(b, L, C)


def from_col_major(x):
    b, L, C = x.shape
    rows = L // GRID_W
    return x.reshape(b, GRID_W, rows, C).transpose(0, 2, 1, 3).reshape(b, L, C)


def flip(t):
    return jnp.flip(t, axis=1)


def ssd_scan(x, dt, A, B, C, h0):
    b, L, H, P = x.shape
    G, N = B.shape[2], B.shape[3]
    Hg = H // G
    Q = SSD_CHUNK
    nc = L // Q
    xc = x.reshape(b, nc, Q, G, Hg, P)
    dtc = dt.reshape(b, nc, Q, G, Hg)
    Bc = B.reshape(b, nc, Q, G, N)
    Cc = C.reshape(b, nc, Q, G, N)
    acum = jnp.cumsum(dtc * A.reshape(G, Hg), axis=2)
    xdt = xc * dtc[..., None]
    seg = acum[:, :, :, None] - acum[:, :, None]
    mask = jnp.tril(jnp.ones((Q, Q), dtype=bool))[:, :, None, None]
    decay = jnp.exp(jnp.where(mask, seg, -jnp.inf))
    scores = jnp.einsum('bcign,bcjgn->bcijg', Cc, Bc)
    y_diag = jnp.einsum('bcijg,bcijgh,bcjghp->bcighp', scores, decay, xdt)
    decay_end = jnp.exp(acum[:, :, -1:] - acum)
    states = jnp.einsum('bcqgn,bcqgh,bcqghp->bcghpn', Bc, decay_end, xdt).astype(f32)
    chunk_decay = jnp.exp(acum[:, :, -1])

    def step(h, inp):
        s, d = inp
        return h * d[..., None, None] + s, h

    h0g = h0.astype(f32).reshape(b, G, Hg, P, N)
    h_last, h_prev = lax.scan(step, h0g, (jnp.swapaxes(states, 0, 1), jnp.swapaxes(chunk_decay, 0, 1)))
    h_prev = jnp.swapaxes(h_prev, 0, 1)
    y_off = jnp.einsum('bcqgn,bcghpn,bcqgh->bcqghp', Cc, h_prev, jnp.exp(acum))
    y = (y_diag + y_off).reshape(b, L, H, P)
    return y, h_last.reshape(b, H, P, N)


def ssd_mixer(h, in_w, conv_w, conv_b, dt_bias, a_log, d_skip, norm_g, out_w, h0_f, h0_b):
    b, L, _ = h.shape
    proj = h @ in_w
    z, xbc, dt_raw = jnp.split(proj, [SSD_INNER, SSD_INNER + SSD_XBC], axis=-1)
    xbc = jax.nn.silu(dw_conv(xbc, conv_w, conv_b))
    xs, Bm, Cm = jnp.split(xbc, [SSD_INNER, SSD_INNER + SSD_GROUPS * SSD_STATE], axis=-1)
    xs = xs.reshape(b, L, SSD_HEADS, SSD_HEADDIM)
    Bm = Bm.reshape(b, L, SSD_GROUPS, SSD_STATE)
    Cm = Cm.reshape(b, L, SSD_GROUPS, SSD_STATE)
    dt = jax.nn.softplus(dt_raw.astype(f32).reshape(b, L, 2, SSD_HEADS) + dt_bias.astype(f32))
    A = -jnp.exp(a_log.astype(f32))
    y_f, hf = ssd_scan(xs, dt[:, :, 0], A[0], Bm, Cm, h0_f)
    y_b, hb = ssd_scan(flip(xs), flip(dt[:, :, 1]), A[1], flip(Bm), flip(Cm), h0_b)
    y = y_f + flip(y_b) + xs * d_skip[:, None]
    y = y.reshape(b, L, SSD_INNER) * jax.nn.silu(z)
    y = rms_norm(y, norm_g)
    return (y @ out_w).astype(h.dtype), hf, hb


def hyena_filters(L, f_w1, f_b1, f_w2, f_b2, f_w3, f_freq):
    t = jnp.arange(L, dtype=f32) / L
    w = 2.0 * math.pi * jnp.arange(L, dtype=f32) / L
    fr = jnp.linspace(1e-4, HY_BANDS - 1, HY_BANDS, dtype=f32)
    ang = w[:, None] * fr
    z = jnp.concatenate([t[:, None], jnp.cos(ang), jnp.sin(ang)], axis=-1)
    hdn = jnp.sin(f_freq[0] * (z @ f_w1 + f_b1))
    hdn = jnp.sin(f_freq[1] * (hdn @ f_w2 + f_b2))
    k = (hdn @ f_w3).reshape(L, HY_ORDER, 2, HY_WIDTH)
    max_decay = math.log(HY_TARGET) / HY_FAST_DECAY
    min_decay = math.log(HY_TARGET) / HY_SLOW_DECAY
    deltas = jnp.linspace(min_decay, max_decay, HY_WIDTH, dtype=f32)
    window = jnp.exp(-t[:, None] * jnp.abs(deltas))
    k = k * window[:, None, None, :]
    return k / jnp.sum(jnp.abs(k), axis=(0, 2), keepdims=True)


def long_conv_bidir(u, k_fwd, k_bwd, bias):
    L = u.shape[1]
    n = 2 * L
    k_full = jnp.concatenate([k_fwd, jnp.zeros_like(k_fwd[:1]), jnp.flip(k_bwd[1:], axis=0)], axis=0)
    uf = jnp.fft.rfft(u.astype(f32), n=n, axis=1)
    kf = jnp.fft.rfft(k_full.astype(f32), n=n, axis=0)
    y = jnp.fft.irfft(uf * kf[None], n=n, axis=1)[:, :L]
    return (y + u * bias).astype(u.dtype)


def hyena_mixer(h, in_w, conv_w, conv_b, f_w1, f_b1, f_w2, f_b2, f_w3, f_freq, f_bias, out_w):
    L = h.shape[1]
    proj = h @ in_w
    vx, z = jnp.split(proj, [3 * HY_WIDTH], axis=-1)
    vx = dw_conv(vx, conv_w, conv_b)
    v, x1, x2 = jnp.split(vx, 3, axis=-1)
    k = hyena_filters(L, f_w1, f_b1, f_w2, f_b2, f_w3, f_freq)
    u = v
    for o, g in enumerate((x1, x2)):
        u = g * long_conv_bidir(u, k[:, o, 0], k[:, o, 1], f_bias[o])
    y = u * jax.nn.silu(z)
    return (y @ out_w).astype(h.dtype)


def linear_scan(a, bx, h0):
    def combine(l, r):
        a_l, b_l = l
        a_r, b_r = r
        return a_l * a_r, a_r * b_l + b_r
    a_cum, b_cum = lax.associative_scan(combine, (a, bx), axis=1)
    hs = a_cum * h0.astype(f32)[:, None] + b_cum
    return hs, hs[:, -1]


def rglru_mixer(h, in_w, conv_w, conv_b, gate_w, gate_b, a_param, out_w, h0_f, h0_b):
    b, L, _ = h.shape
    xr, z = jnp.split(h @ in_w, 2, axis=-1)
    xr = dw_conv(xr, conv_w, conv_b)
    xblk = xr.reshape(b, L, LRU_BLOCKS, LRU_BLOCK)
    gates = jnp.einsum('blnk,dgnkj->dgblnj', xblk, gate_w).reshape(2, 2, b, L, LRU_WIDTH)
    gates = jax.nn.sigmoid((gates + gate_b[:, :, None, None, :]).astype(f32))
    r, i = gates[:, 0], gates[:, 1]
    log_a = -LRU_C * r * jax.nn.softplus(-a_param.astype(f32))[:, None, None]
    a = jnp.exp(log_a)
    mult = jnp.sqrt(jnp.maximum(-jnp.expm1(2.0 * log_a), 0.0))
    bx = mult * i * xr.astype(f32)[None]
    y_f, hf = linear_scan(a[0], bx[0], h0_f)
    y_b, hb = linear_scan(flip(a[1]), flip(bx[1]), h0_b)
    y = (y_f + flip(y_b)) * jax.nn.silu(z)
    return (y @ out_w).astype(h.dtype), hf, hb


def setup_inputs(seed: int = 0) -> dict:
    key = jax.random.key(seed)
    ks = iter(jax.random.split(key, 48))

    def nrm(shape, scale):
        return jax.random.normal(next(ks), shape, f32) * scale

    def gain(shape):
        return 1.0 + nrm(shape, 0.02)

    D = D_MODEL
    dt0 = jnp.exp(jax.random.uniform(next(ks), (N_SSD, 2, SSD_HEADS), f32, math.log(1e-3), math.log(1e-1)))
    a0 = jax.random.uniform(next(ks), (N_LRU, 2, LRU_WIDTH), f32, 0.9, 0.999)
    return {
        'x_prompt': nrm((BATCH, SEQ, D), 1.0),
        'x_sample': nrm((DEC_BATCH, DEC_SEQ, D), 1.0),
        'state_ssd': nrm((DEC_BATCH, N_SSD, 2, SSD_HEADS, SSD_HEADDIM, SSD_STATE), 0.5),
        'state_lru': nrm((DEC_BATCH, N_LRU, 2, LRU_WIDTH), 0.5),
        'c': nrm((DEC_BATCH, D), 1.0),
        'c_ctx': nrm((D,), 1.0),
        'mod_w': nrm((DEPTH, D, 3 * D), 0.3 * D ** -0.5),
        'mod_b': nrm((DEPTH, 3 * D), 0.02),
        'ln_g': gain((DEPTH, D)),
        'ln_b': nrm((DEPTH, D), 0.02),
        'ssd_in_w': nrm((N_SSD, D, SSD_PROJ), D ** -0.5),
        'ssd_conv_w': nrm((N_SSD, SSD_CONV, SSD_XBC), SSD_CONV ** -0.5),
        'ssd_conv_b': nrm((N_SSD, SSD_XBC), 0.02),
        'ssd_dt_bias': dt0 + jnp.log(-jnp.expm1(-dt0)),
        'ssd_a_log': jnp.log(jax.random.uniform(next(ks), (N_SSD, 2, SSD_HEADS), f32, 1.0, 16.0)),
        'ssd_d': gain((N_SSD, SSD_HEADS)),
        'ssd_norm_g': gain((N_SSD, SSD_INNER)),
        'ssd_out_w': nrm((N_SSD, SSD_INNER, D), BETA * SSD_INNER ** -0.5),
        'hy_in_w': nrm((N_HY, D, 4 * HY_WIDTH), D ** -0.5),
        'hy_conv_w': nrm((N_HY, HY_SHORT, 3 * HY_WIDTH), HY_SHORT ** -0.5),
        'hy_conv_b': nrm((N_HY, 3 * HY_WIDTH), 0.02),
        'hy_f_w1': nrm((N_HY, HY_EMB, HY_FFN), HY_EMB ** -0.5),
        'hy_f_b1': nrm((N_HY, HY_FFN), 0.02),
        'hy_f_w2': nrm((N_HY, HY_FFN, HY_FFN), HY_FFN ** -0.5),
        'hy_f_b2': nrm((N_HY, HY_FFN), 0.02),
        'hy_f_w3': nrm((N_HY, HY_FFN, HY_ORDER * 2 * HY_WIDTH), HY_FFN ** -0.5),
        'hy_f_freq': gain((N_HY, 2, HY_FFN)),
        'hy_f_bias': nrm((N_HY, HY_ORDER, HY_WIDTH), 0.5),
        'hy_out_w': nrm((N_HY, HY_WIDTH, D), BETA * HY_WIDTH ** -0.5),
        'lru_in_w': nrm((N_LRU, D, 2 * LRU_WIDTH), D ** -0.5),
        'lru_conv_w': nrm((N_LRU, LRU_CONV, LRU_WIDTH), LRU_CONV ** -0.5),
        'lru_conv_b': nrm((N_LRU, LRU_WIDTH), 0.02),
        'lru_gate_w': nrm((N_LRU, 2, 2, LRU_BLOCKS, LRU_BLOCK, LRU_BLOCK), LRU_BLOCK ** -0.5),
        'lru_gate_b': nrm((N_LRU, 2, 2, LRU_WIDTH), 0.02),
        'lru_a_param': jnp.log(a0) - jnp.log1p(-a0),
        'lru_out_w': nrm((N_LRU, LRU_WIDTH, D), BETA * LRU_WIDTH ** -0.5),
    }


def reference(x_prompt, x_sample, state_ssd, state_lru, c, c_ctx, mod_w, mod_b, ln_g, ln_b,
              ssd_in_w, ssd_conv_w, ssd_conv_b, ssd_dt_bias, ssd_a_log, ssd_d, ssd_norm_g, ssd_out_w,
              hy_in_w, hy_conv_w, hy_conv_b, hy_f_w1, hy_f_b1, hy_f_w2, hy_f_b2, hy_f_w3, hy_f_freq,
              hy_f_bias, hy_out_w,
              lru_in_w, lru_conv_w, lru_conv_b, lru_gate_w, lru_gate_b, lru_a_param, lru_out_w):

    def run_trunk(x, cond, h0_ssd, h0_lru, latent):
        b = x.shape[0]
        ssd_fin, lru_fin = [], []
        for i in range(DEPTH):
            kind, slot = i % N_MIXERS, i // N_MIXERS
            col = latent and (slot % 2 == 1)
            shift, scale, gate = modulation(cond, mod_w[i], mod_b[i])
            h = x * (1.0 + scale) + shift
            if col:
                h = to_col_major(h)
            if kind == 0:
                if h0_ssd is None:
                    z0 = jnp.zeros((b, SSD_HEADS, SSD_HEADDIM, SSD_STATE), f32)
                    hf0, hb0 = z0, z0
                else:
                    hf0, hb0 = h0_ssd[:, slot, 0], h0_ssd[:, slot, 1]
                out, hf, hb = ssd_mixer(h, ssd_in_w[slot], ssd_conv_w[slot], ssd_conv_b[slot],
                                        ssd_dt_bias[slot], ssd_a_log[slot], ssd_d[slot],
                                        ssd_norm_g[slot], ssd_out_w[slot], hf0, hb0)
                ssd_fin.append(jnp.stack([hf, hb], axis=1))
            elif kind == 1:
                out = hyena_mixer(h, hy_in_w[slot], hy_conv_w[slot], hy_conv_b[slot],
                                  hy_f_w1[slot], hy_f_b1[slot], hy_f_w2[slot], hy_f_b2[slot],
                                  hy_f_w3[slot], hy_f_freq[slot], hy_f_bias[slot], hy_out_w[slot])
            else:
                if h0_lru is None:
                    z0 = jnp.zeros((b, LRU_WIDTH), f32)
                    hf0, hb0 = z0, z0
                else:
                    hf0, hb0 = h0_lru[:, slot, 0], h0_lru[:, slot, 1]
                out, hf, hb = rglru_mixer(h, lru_in_w[slot], lru_conv_w[slot], lru_conv_b[slot],
                                          lru_gate_w[slot], lru_gate_b[slot], lru_a_param[slot],
                                          lru_out_w[slot], hf0, hb0)
                lru_fin.append(jnp.stack([hf, hb], axis=1))
            if col:
                out = from_col_major(out)
            x = layer_norm(ALPHA * x + gate * out, ln_g[i], ln_b[i])
        return x, ssd_fin, lru_fin

    y_prompt, ssd_fin, lru_fin = run_trunk(x_prompt, c_ctx[None], None, None, False)
    new_state_ssd = jnp.stack(ssd_fin, axis=1).astype(x_prompt.dtype)
    new_state_lru = jnp.stack(lru_fin, axis=1).astype(x_prompt.dtype)

    y_sample, _, _ = run_trunk(x_sample, c, state_ssd, state_lru, True)

    return (y_prompt, y_sample, new_state_ssd, new_state_lru)
```

```python
import contextlib
import math
import numpy as np
import ml_dtypes
import concourse.bass as bass
import concourse.mybir as mybir
from concourse.bass_utils import run_bass_kernel_spmd

F32 = mybir.dt.float32
F32R = mybir.dt.float32r
BF16 = mybir.dt.bfloat16
AF = mybir.ActivationFunctionType
ALU = mybir.AluOpType

EPOCH = 8000
NDMA = 24
import os as _os
MAXOPS = int(_os.environ.get("MAXOPS", "1000000000"))

D = 1024
NPS = 4
LP = 256
LS = 4096
DEPTH = 4
ALPHA = (2.0 * DEPTH) ** 0.25
LN_EPS = 1e-5
RMS_EPS = 1e-5
SSD_PROJ = 6208
HY_T = 1e-2


class Sched:
    ENG = ("pe", "act", "dve", "pool")

    def __init__(self, nc, stack):
        self.nc = nc
        self.stack = stack
        self.eng = {"pe": nc.tensor, "act": nc.scalar, "dve": nc.vector,
                    "pool": nc.gpsimd, "sp": nc.sync}
        self.ops = {e: [] for e in self.eng}
        self.cnt = {e: 0 for e in self.ENG}
        self.esems = {e: [] for e in self.ENG}
        self.dsems = [stack.enter_context(nc.semaphore(f"dma{i}")) for i in range(NDMA)]
        self.dval = [0] * NDMA
        self.dnext = 0
        self.waited = {e: {} for e in self.eng}
        self.lastw = {}
        self.readers = {}
        self.n_inst = 0

    def _esem(self, e, count):
        k = (count - 1) // EPOCH
        while len(self.esems[e]) <= k:
            self.esems[e].append(self.stack.enter_context(
                self.nc.semaphore(f"s_{e}_{len(self.esems[e])}")))
        return self.esems[e][k], (count - 1) % EPOCH + 1, k

    def _emit_wait(self, e, ev):
        if ev[0] == "e":
            _, src, count = ev
            if src == e and e == "pe":
                return
            sem, val, k = self._esem(src, count)
            key = ("e", src, k)
        else:
            _, idx, val = ev
            sem = self.dsems[idx]
            key = ("d", idx)
        if self.waited[e].get(key, 0) >= val:
            return
        self.waited[e][key] = val
        engobj = self.eng[e]
        self.ops[e].append(lambda engobj=engobj, sem=sem, val=val: engobj.wait_ge(sem, val))

    def _deps(self, e, reads, writes):
        evs = []
        for r in reads:
            if r in self.lastw:
                evs.append(self.lastw[r])
        for w in writes:
            if w in self.lastw:
                evs.append(self.lastw[w])
            evs.extend(self.readers.get(w, ()))
        for ev in evs:
            self._emit_wait(e, ev)

    def _commit(self, ev, reads, writes):
        for r in reads:
            self.readers.setdefault(r, []).append(ev)
        for w in writes:
            self.lastw[w] = ev
            self.readers[w] = []

    def op(self, e, fn, reads=(), writes=()):
        if self.n_inst >= MAXOPS:
            return
        self._deps(e, reads, writes)
        self.cnt[e] += 1
        count = self.cnt[e]
        sem, val, k = self._esem(e, count)
        self.ops[e].append(lambda fn=fn, sem=sem: fn().then_inc(sem, 1))
        self._commit(("e", e, count), reads, writes)
        self.n_inst += 1

    def dma(self, q, fn, reads=(), writes=()):
        if self.n_inst >= MAXOPS:
            return
        idx = self.dnext
        self.dnext = (self.dnext + 1) % NDMA
        if self.dval[idx] > 0:
            self._emit_wait(q, ("d", idx, self.dval[idx]))
        self._deps(q, reads, writes)
        self.dval[idx] += 16
        val = self.dval[idx]
        sem = self.dsems[idx]
        self.ops[q].append(lambda fn=fn, sem=sem: fn().then_inc(sem, 16))
        self._commit(("d", idx, val), reads, writes)
        self.n_inst += 1

    def barrier(self):
        for e in self.eng:
            for en in self.ENG:
                if self.cnt[en] and not (en == e):
                    self._emit_wait(e, ("e", en, self.cnt[en]))
                elif self.cnt[en] and e != "pe":
                    self._emit_wait(e, ("e", en, self.cnt[en]))
            for i in range(NDMA):
                if self.dval[i]:
                    self._emit_wait(e, ("d", i, self.dval[i]))
        self.lastw = {}
        self.readers = {}

    def finish(self):
        self.barrier()
        nc = self.nc
        with nc.Block() as block:
            @block.tensor
            def _(t):
                for f in self.ops["pe"]:
                    f()

            @block.scalar
            def _(t):
                for f in self.ops["act"]:
                    f()

            @block.vector
            def _(t):
                for f in self.ops["dve"]:
                    f()

            @block.gpsimd
            def _(t):
                for f in self.ops["pool"]:
                    f()

            @block.sync
            def _(t):
                for f in self.ops["sp"]:
                    f()


class Rot:
    def __init__(self, tiles, name):
        self.tiles = tiles
        self.name = name
        self.i = -1

    def next(self):
        self.i += 1
        k = self.i % len(self.tiles)
        return self.tiles[k], f"{self.name}{k}"


class Grp:
    pass


class KB:
    def __init__(self, layers=(0, 1, 2, 3), do_prompt=True, do_sample=True):
        self.layers = layers
        self.do_prompt = do_prompt
        self.do_sample = do_sample
        self.nc = bass.Bass("TRN2", target_bir_lowering=False)
        self.I = {}
        self.O = {}
        self.uid = 0
        self.nphase = 0
        self.max_phase = 10 ** 9

    def skip(self):
        self.nphase += 1
        return self.nphase > self.max_phase

    def inp(self, name, shape, dt=F32):
        self.I[name] = self.nc.dram_tensor(name, list(shape), dt, kind="ExternalInput").ap()
        return self.I[name]

    def outp(self, name, shape, dt=F32):
        self.O[name] = self.nc.dram_tensor(name, list(shape), dt, kind="ExternalOutput").ap()
        return self.O[name]

    def scr(self, name, shape, dt):
        return self.nc.dram_tensor(name, list(shape), dt, kind="Internal").ap()

    def nm(self, p):
        self.uid += 1
        return f"{p}_{self.uid}"

    def sb(self, shape, dt, name="t"):
        return self.ph.enter_context(self.nc.sbuf_tensor(self.nm(name), list(shape), dt))

    def ps(self, shape, dt, name="p"):
        return self.ph.enter_context(self.nc.psum_tensor(self.nm(name), list(shape), dt))

    @contextlib.contextmanager
    def phase(self):
        with contextlib.ExitStack() as ph:
            old = getattr(self, "ph", None)
            self.ph = ph
            yield
            self.S.barrier()
            self.ph = old

    def dma(self, out, in_, r=(), w=(), q="sp", **kw):
        eng = self.nc.sync if q == "sp" else self.nc.gpsimd
        self.S.dma(q, lambda: eng.dma_start(out=out, in_=in_, **kw), reads=r, writes=w)

    def mm(self, out, lhsT, rhs, start=True, stop=True, r=(), w=()):
        self.S.op("pe", lambda: self.nc.tensor.matmul(out, lhsT=lhsT, rhs=rhs, start=start, stop=stop),
                  reads=r, writes=w)

    def tr(self, out, in_, ident, r=(), w=()):
        self.S.op("pe", lambda: self.nc.tensor.transpose(out=out, in_=in_, identity=ident), reads=r, writes=w)

    def act(self, out, in_, func, r=(), w=(), **kw):
        self.S.op("act", lambda: self.nc.scalar.activation(out=out, in_=in_, func=func, **kw), reads=r, writes=w)

    def E(self, e):
        return self.nc.vector if e == "dve" else self.nc.gpsimd

    def tt(self, e, out, in0, in1, op, r=(), w=()):
        self.S.op(e, lambda: self.E(e).tensor_tensor(out=out, in0=in0, in1=in1, op=op), reads=r, writes=w)

    def ts(self, e, out, in0, s1, s2, op0, op1=None, r=(), w=()):
        if op1 is None:
            self.S.op(e, lambda: self.E(e).tensor_scalar(out=out, in0=in0, scalar1=s1, scalar2=None, op0=op0),
                      reads=r, writes=w)
        else:
            self.S.op(e, lambda: self.E(e).tensor_scalar(out=out, in0=in0, scalar1=s1, scalar2=s2, op0=op0, op1=op1),
                      reads=r, writes=w)

    def stt(self, out, in0, scalar, in1, op0, op1, r=(), w=()):
        self.S.op("dve", lambda: self.nc.vector.scalar_tensor_tensor(out=out, in0=in0, scalar=scalar, in1=in1,
                                                                     op0=op0, op1=op1), reads=r, writes=w)

    def cp(self, e, out, in_, r=(), w=()):
        if e == "act":
            self.S.op("act", lambda: self.nc.scalar.copy(out=out, in_=in_), reads=r, writes=w)
        else:
            self.S.op(e, lambda: self.E(e).tensor_copy(out=out, in_=in_), reads=r, writes=w)

    def memset(self, e, ap, val, w=()):
        self.S.op(e, lambda: self.E(e).memset(ap, val), writes=w)

    def declare(self):
        inp = self.inp
        inp("xp", [NPS * LP, D]); inp("xs", [LS, D])
        inp("st_ssd", [2, 2, 2048, 128]); inp("st_lru", [2, D]); inp("cond", [2, D])
        inp("mod_w", [4, D, 3 * D]); inp("mod_b", [4, 3 * D]); inp("ln_g", [4, D]); inp("ln_b", [4, D])
        inp("ssd_in_w", [2, D, SSD_PROJ]); inp("ssd_conv_w", [2, 4, 4096]); inp("ssd_conv_b", [2, 4096])
        inp("ssd_dt_bias", [2, 64]); inp("ssd_a_log", [2, 64]); inp("ssd_d", [2, 32])
        inp("ssd_norm_g", [2, 2048]); inp("ssd_out_w", [2, 2048, D])
        inp("hy_in_w", [D, 4096]); inp("hy_conv_w", [3, 3072]); inp("hy_conv_b", [3072])
        inp("hy_f_w1", [33, 64]); inp("hy_f_b1", [64]); inp("hy_f_w2", [64, 64]); inp("hy_f_b2", [64])
        inp("hy_f_w3", [64, 4096]); inp("hy_f_freq", [2, 64]); inp("hy_f_bias", [2, D]); inp("hy_out_w", [D, D])
        inp("lru_in_w", [D, 2048]); inp("lru_conv_w", [4, D]); inp("lru_conv_b", [D])
        inp("lru_gate_w", [2, 2, 4, 256, 256]); inp("lru_gate_b", [4, D]); inp("lru_a_param", [2, D])
        inp("lru_out_w", [D, D])
        inp("masks", [4, 128, 128])
        if 1 in self.layers:
            inp("dft_p", [4, LP // 128, 128, LP // 128, 128], BF16)
            inp("dft_s", [4, LS // 128, 128, LS // 128, 128], BF16)
            inp("dfti_p", [2, 1, 128, LP // 128, 256], BF16)
            inp("dfti_s", [2, LS // 512, 128, LS // 128, 512], BF16)
            inp("hz_p", [33, LP]); inp("hz_s", [33, LS])
            inp("win_p", [LP, D]); inp("win_s", [LS, D])
        self.outp("yp", [NPS * LP, D]); self.outp("ys", [LS, D])
        self.outp("nss", [NPS, 2, 2, 2048, 128]); self.outp("nsl", [NPS, 2, D])

    def build(self):
        nc = self.nc
        self.declare()
        with contextlib.ExitStack() as st:
            self.S = Sched(nc, st)
            self.ph = st
            self.ident_f = self.sb([128, 128], F32, "identf")
            self.ident_b = self.sb([128, 128], BF16, "identb")
            self.masks = self.sb([128, 4, 128], F32, "masks")
            self.ones_f = self.sb([128, 128], F32, "ones")
            self.zero_b = self.sb([128, 8, 4], BF16, "zerob")
            self.dma(self.masks[:], self.I["masks"].rearrange("m p i -> p m i"), w=["masks"])
            self.memset("pool", self.ident_f[:], 1.0, w=["identf"])
            self.S.op("pool", lambda: nc.gpsimd.affine_select(
                out=self.ident_f[:], in_=self.ident_f[:], pattern=[[-1, 128]], compare_op=ALU.is_equal,
                fill=0.0, base=0, channel_multiplier=1), reads=["identf"], writes=["identf"])
            self.cp("dve", self.ident_b[:], self.ident_f[:], r=["identf"], w=["identb"])
            self.memset("dve", self.ones_f[:], 1.0, w=["ones"])
            self.masks_r = self.sb([128, 4, 128], F32R, "masksr")
            self.ones_r = self.sb([128, 128], F32R, "onesr")
            self.cp("dve", self.masks_r[:], self.masks[:], r=["masks"], w=["masksr"])
            self.cp("dve", self.ones_r[:], self.ones_f[:], r=["ones"], w=["onesr"])
            self.memset("dve", self.zero_b[:], 0.0, w=["zerob"])
            self.lru_h0 = self.sb([128, 2, 8], F32, "lruh0")
            self.S.barrier()

            self.mod_scr = self.scr("mod_scr", [4, 2, 3 * D], F32)
            groups = []
            if self.do_prompt:
                g = Grp(); g.name = "p"; g.nseq = NPS; g.L = LP; g.cond = 0; g.latent = False
                g.x_in = self.I["xp"]; g.x_out = self.O["yp"]
                groups.append(g)
            if self.do_sample:
                g = Grp(); g.name = "s"; g.nseq = 1; g.L = LS; g.cond = 1; g.latent = True
                g.x_in = self.I["xs"]; g.x_out = self.O["ys"]
                groups.append(g)
            for g in groups:
                g.T = g.nseq * g.L
                g.W = g.nseq * (g.L + 3)
                g.xa = self.scr(f"xa_{g.name}", [g.T, D], F32)
                g.xb = self.scr(f"xb_{g.name}", [g.T, D], F32)
                g.hT = self.scr(f"hT_{g.name}", [8, 128, g.W], BF16)
                g.yT = self.scr(f"yT_{g.name}", [16, 128, g.T], BF16)
                g.xs_tm = self.scr(f"xstm_{g.name}", [g.T, 2048], BF16)
                g.b_tm = self.scr(f"btm_{g.name}", [g.T, 1024], BF16)
                g.bcT = self.scr(f"bcT_{g.name}", [16, 128, g.T], BF16)
                g.dta = self.scr(f"dta_{g.name}", [g.T, 128], F32)
                g.sloc = self.scr(f"sloc_{g.name}", [g.T // 128, 2, 128, 2048], F32)
                g.cdec = self.scr(f"cdec_{g.name}", [g.T // 128, 128, 64], F32)
                g.hprev = self.scr(f"hprev_{g.name}", [g.T // 128, 2, 128, 2048], BF16)
                g.fm = self.scr(f"fm_{g.name}", [4, 8, 128, g.T], F32)
                g.utm = self.scr(f"utm_{g.name}", [g.T, D], BF16)
                g.kw = self.scr(f"kw_{g.name}", [g.L, 4096], F32)
                g.khat = self.scr(f"khat_{g.name}", [2, 2, g.L, D], F32)
                g.yspec = self.scr(f"ysp_{g.name}", [2, 8, 128, g.L // 128, 128], BF16)
                g.khat2 = self.scr(f"khat2_{g.name}", [2, 2, g.L, D], F32)
            self.groups = groups

            self.modulation()
            for li in self.layers:
                last = (li == self.layers[-1])
                for g in groups:
                    X_in = g.x_in if li == self.layers[0] else (g.xa if (li % 2 == 1) else g.xb)
                    X_out = g.x_out if last else (g.xa if (li % 2 == 0) else g.xb)
                    col = g.latent and li == 3
                    self.pass_A(g, li, X_in, col)
                    kind = li % 3
                    if kind == 0:
                        self.ssd_layer(g, li // 3, li)
                        KC = 16
                    elif kind == 1:
                        self.hyena_layer(g)
                        KC = 8
                    else:
                        self.lru_layer(g)
                        KC = 8
                    wname = {0: "ssd_out_w", 1: "hy_out_w", 2: "lru_out_w"}[kind]
                    wout = self.I[wname][li // 3] if kind == 0 else self.I[wname]
                    self.pass_E(g, li, X_in, X_out, col, wout, KC)
            self.S.finish()
        return nc

    def xrows(self, g, X, s, c, col):
        if not col:
            r0 = s * g.L + c * 128
            return [(X[r0:r0 + 128, :], 0, 128)]
        Xv = X.rearrange("(r w) f -> w r f", w=64)
        return [(Xv[2 * c + wo], wo * 64, 64) for wo in range(2)]

    def load_T(self, dst, src2d, R, name):
        stg = self.sb([128, 128], F32, "ldT")
        if getattr(self, "_ldT_ph", None) is not self.ph:
            self._ldT_ph = self.ph
            self._ldT_ps = self.ps([128, 128], F32, "ldTp")
        pt = self._ldT_ps
        k = self.nm("ldT")
        self.dma(stg[0:R, :], src2d, w=[k])
        self.tr(pt[:, 0:R], stg[0:R, :], self.ident_f[0:R, 0:R], r=[k, "identf"], w=["ldTp"])
        self.cp("dve", dst, pt[:, 0:R], r=["ldTp"], w=[name])

    def modulation(self):
        if self.skip():
            return
        nc = self.nc
        with self.phase():
            cond = self.sb([2, D], F32, "cond")
            cs = self.sb([2, D], F32, "cs")
            condT = self.sb([128, 8, 2], BF16, "condT")
            pT = self.ps([128, 8, 2], F32, "pT")
            self.dma(cond[:], self.I["cond"], w=["cond"])
            self.act(cs[:], cond[:], AF.Silu, r=["cond"], w=["cs"])
            for k in range(8):
                self.tr(pT[:, k, :], cs[0:2, k * 128:(k + 1) * 128], self.ident_f[0:2, 0:2],
                        r=["cs", "identf"], w=["pT"])
            self.cp("dve", condT[:], pT[:], r=["pT"], w=["condT"])
            mw = self.sb([128, 8, 3 * D], BF16, "mw")
            mb = self.sb([2, 3 * D], F32, "mb")
            msb = self.sb([2, 3 * D], F32, "msb")
            pm = [self.ps([2, 512], F32, "pm") for _ in range(2)]
            for li in self.layers:
                for k in range(8):
                    self.dma(mw[:, k, :], self.I["mod_w"][li, k * 128:(k + 1) * 128, :], w=[f"mw{k}"], q="pool")
                for c in range(2):
                    self.dma(mb[c:c + 1, :], self.I["mod_b"][li:li + 1, :], w=["mb"])
                for t in range(6):
                    p = pm[t % 2]
                    for k in range(8):
                        self.mm(p[:], condT[:, k, :], mw[:, k, t * 512:(t + 1) * 512], start=(k == 0), stop=(k == 7),
                                r=["condT", f"mw{k}"], w=[f"pm{t % 2}"])
                    self.tt("dve", msb[:, t * 512:(t + 1) * 512], p[:], mb[:, t * 512:(t + 1) * 512], ALU.add,
                            r=[f"pm{t % 2}", "mb"], w=["msb"])
                self.ts("dve", msb[:, D:2 * D], msb[:, D:2 * D], 1.0, None, ALU.add, r=["msb"], w=["msb"])
                self.dma(self.mod_scr[li], msb[:], r=["msb"])

    def pass_A(self, g, li, X, col):
        if self.skip():
            return
        with self.phase():
            sc = self.sb([128, D], F32, "sc")
            sh = self.sb([128, D], F32, "sh")
            self.dma(sh[:], self.mod_scr[li, g.cond, 0:D].partition_broadcast(128), w=["sh"])
            self.dma(sc[:], self.mod_scr[li, g.cond, D:2 * D].partition_broadcast(128), w=["sc"])
            xr = Rot([self.sb([128, D], F32, "xA") for _ in range(2)], "xA")
            hr = Rot([self.sb([128, D], BF16, "hA") for _ in range(2)], "hA")
            tr_ = Rot([self.sb([128, 8, 128], BF16, "hTA") for _ in range(2)], "hTA")
            pr = Rot([self.ps([128, 8, 128], BF16, "pA") for _ in range(2)], "pA")
            nchunk = g.L // 128
            for s in range(g.nseq):
                base = s * (g.L + 3)
                self.dma(g.hT[:, :, base:base + 1].rearrange("k p t -> p k t"), self.zero_b[:, :, 0:1], r=["zerob"],
                         allow_slow_non_contiguous=True)
                self.dma(g.hT[:, :, base + g.L + 1:base + g.L + 3].rearrange("k p t -> p k t"),
                         self.zero_b[:, :, 0:2], r=["zerob"], allow_slow_non_contiguous=True)
                for c in range(nchunk):
                    xt, xn = xr.next()
                    for (src, p0, n) in self.xrows(g, X, s, c, col):
                        self.dma(xt[p0:p0 + n, :], src, w=[xn])
                    ht, hn = hr.next()
                    self.tt("dve", xt[:], xt[:], sc[:], ALU.mult, r=[xn, "sc"], w=[xn])
                    self.tt("dve", ht[:], xt[:], sh[:], ALU.add, r=[xn, "sh"], w=[hn])
                    pt, pn = pr.next()
                    for k in range(8):
                        self.tr(pt[:, k, :], ht[:, k * 128:(k + 1) * 128], self.ident_b[:], r=[hn, "identb"], w=[pn])
                    tt_, tn = tr_.next()
                    self.cp("act", tt_[:], pt[:], r=[pn], w=[tn])
                    c0 = base + 1 + c * 128
                    self.dma(g.hT[:, :, c0:c0 + 128].rearrange("k p t -> p k t"), tt_[:], r=[tn])

    def pass_E(self, g, li, X, Xo, col, wout, KC):
        if self.skip():
            return
        nc = self.nc
        with self.phase():
            wo = self.sb([128, KC, D], BF16, "wo")
            for k in range(KC):
                self.dma(wo[:, k, :], wout[k * 128:(k + 1) * 128, :], w=[f"wo{k}"], q="pool")
            gt = self.sb([128, D], F32, "gate")
            lg = self.sb([128, D], F32, "lng")
            lb = self.sb([128, D], F32, "lnb")
            self.dma(gt[:], self.mod_scr[li, g.cond, 2 * D:3 * D].partition_broadcast(128), w=["gate"])
            self.dma(lg[:], self.I["ln_g"][li].partition_broadcast(128), w=["lng"])
            self.dma(lb[:], self.I["ln_b"][li].partition_broadcast(128), w=["lnb"])
            yr = Rot([self.sb([128, KC, 128], BF16, "yE") for _ in range(2)], "yE")
            xr = Rot([self.sb([128, D], F32, "xE") for _ in range(2)], "xE")
            rr = Rot([self.sb([128, D], F32, "rE") for _ in range(2)], "rE")
            sr = Rot([self.sb([128, 16], F32, "sE") for _ in range(2)], "sE")
            pr = Rot([self.ps([128, D], F32, "pE") for _ in range(2)], "pE")
            nchunk = g.L // 128
            for s in range(g.nseq):
                for c in range(nchunk):
                    t0 = s * g.L + c * 128
                    yt, yn = yr.next()
                    self.dma(yt[:], g.yT[0:KC, :, t0:t0 + 128].rearrange("k p t -> p k t"), w=[yn])
                    xt, xn = xr.next()
                    for (src, p0, n) in self.xrows(g, X, s, c, col):
                        self.dma(xt[p0:p0 + n, :], src, w=[xn])
                    pt, pn = pr.next()
                    for hlf in range(2):
                        for k in range(KC):
                            self.mm(pt[:, hlf * 512:(hlf + 1) * 512], yt[:, k, :], wo[:, k, hlf * 512:(hlf + 1) * 512],
                                    start=(k == 0), stop=(k == KC - 1), r=[yn, f"wo{k}"], w=[pn + str(hlf)])
                    rt, rn = rr.next()
                    stt_, sn = sr.next()
                    for hlf in range(2):
                        sl = slice(hlf * 512, (hlf + 1) * 512)
                        self.tt("dve", rt[:, sl], pt[:, sl], gt[:, sl], ALU.mult, r=[pn + str(hlf), "gate"], w=[rn])
                    self.stt(rt[:], xt[:], ALPHA, rt[:], ALU.mult, ALU.add, r=[xn, rn], w=[rn])
                    self.S.op("dve", lambda stt_=stt_, rt=rt: nc.vector.bn_stats(out=stt_[:, 0:6], in_=rt[:, 0:512]),
                              reads=[rn], writes=[sn])
                    self.S.op("dve", lambda stt_=stt_, rt=rt: nc.vector.bn_stats(out=stt_[:, 6:12], in_=rt[:, 512:1024]),
                              reads=[rn], writes=[sn])
                    self.S.op("dve", lambda stt_=stt_: nc.vector.bn_aggr(out=stt_[:, 12:14], in_=stt_[:, 0:12]),
                              reads=[sn], writes=[sn])
                    self.ts("dve", stt_[:, 14:15], stt_[:, 13:14], LN_EPS, None, ALU.add, r=[sn], w=[sn])
                    self.act(stt_[:, 14:15], stt_[:, 14:15], AF.Sqrt, r=[sn], w=[sn])
                    self.S.op("dve", lambda stt_=stt_: nc.vector.reciprocal(out=stt_[:, 15:16], in_=stt_[:, 14:15]),
                              reads=[sn], writes=[sn])
                    self.ts("dve", rt[:], rt[:], stt_[:, 12:13], stt_[:, 15:16], ALU.subtract, ALU.mult,
                            r=[rn, sn], w=[rn])
                    self.tt("pool", rt[:], rt[:], lg[:], ALU.mult, r=[rn, "lng"], w=[rn])
                    self.tt("pool", rt[:], rt[:], lb[:], ALU.add, r=[rn, "lnb"], w=[rn])
                    for (dst, p0, n) in self.xrows(g, Xo, s, c, col):
                        self.dma(dst, rt[p0:p0 + n, :], r=[rn])

    def conv_fm(self, pp, pn, acc, an, dst, dn, cw, cb, fc, K, silu, woff):
        self.act(acc[:], pp[:, woff:woff + 256], AF.Identity, r=[pn, "cw", "cb"], w=[an],
                 scale=cw[:, 0, fc:fc + 1], bias=cb[:, fc:fc + 1])
        for k in range(1, K):
            self.stt(acc[:], pp[:, woff + k:woff + k + 256], cw[:, k, fc:fc + 1], acc[:], ALU.mult, ALU.add,
                     r=[pn, an, "cw"], w=[an])
        if silu:
            self.act(dst, acc[:], AF.Silu, r=[an], w=[dn])
        else:
            self.cp("act", dst, acc[:], r=[an], w=[dn])

    def ssd_layer(self, g, slot, li):
        self.ssd_p1(g, slot)
        if _os.environ.get("DEBUG") == "dta":
            with self.phase():
                t = self.sb([128, 8, 128], F32, "dbg")
                self.dma(t[:], g.dta[0:1024, :].rearrange("(c p) f -> p c f", p=128), w=["dbg"])
                self.dma(self.O["yp"][:, 0:128].rearrange("(c p) f -> p c f", p=128), t[:], r=["dbg"])
        self.ssd_p2a(g, slot)
        self.ssd_pR(g, slot)
        self.ssd_p2b(g, slot)

    def ssd_p1(self, g, slot):
        if self.skip():
            return
        nc = self.nc
        with self.phase():
            w_in = self.I["ssd_in_w"][slot]
            wx = self.sb([128, 8, 4096], BF16, "wx")
            wd = self.sb([128, 8, 64], BF16, "wd")
            for k in range(8):
                self.dma(wx[:, k, :], w_in[k * 128:(k + 1) * 128, 2048:6144], w=[f"wx{k}"], q="pool")
                self.dma(wd[:, k, :], w_in[k * 128:(k + 1) * 128, 6144:6208], w=["wd"], q="pool")
            cw = self.sb([128, 4, 32], F32, "cw")
            cb = self.sb([128, 32], F32, "cb")
            self.load_T(cw[:].rearrange("p k c -> p (k c)"),
                        self.I["ssd_conv_w"][slot].rearrange("k (c p) -> (k c) p", p=128), 128, "cw")
            self.load_T(cb[:], self.I["ssd_conv_b"][slot].rearrange("(c p) -> c p", p=128), 32, "cb")
            dtb = self.sb([128, 64], F32, "dtb")
            abc = self.sb([128, 64], F32, "abc")
            self.dma(dtb[:], self.I["ssd_dt_bias"][slot].partition_broadcast(128), w=["dtb"])
            self.dma(abc[:], self.I["ssd_a_log"][slot].partition_broadcast(128), w=["abc"])
            self.act(abc[:], abc[:], AF.Exp, r=["abc"], w=["abc"])
            self.ts("dve", abc[:], abc[:], -1.0, None, ALU.mult, r=["abc"], w=["abc"])

            hwr = Rot([self.sb([128, 8, 260], BF16, "hw") for _ in range(2)], "hw")
            xbr = Rot([self.sb([128, 32, 256], BF16, "xbc") for _ in range(2)], "xbc")
            accr = Rot([self.sb([128, 256], F32, "acc") for _ in range(3)], "acc")
            tmr = Rot([self.sb([128, 1024], BF16, "tm") for _ in range(3)], "tm")
            dtr = Rot([self.sb([128, 128], F32, "dta") for _ in range(2)], "dta")
            ppr = Rot([self.ps([128, 512], F32, "pp") for _ in range(3)], "pp")
            ptr = Rot([self.ps([128, 8, 128], BF16, "ptr") for _ in range(2)], "ptr")
            pdr = Rot([self.ps([128, 64], F32, "pd") for _ in range(1)], "pd")
            for s in range(g.nseq):
                for j in range(g.L // 256):
                    c0 = s * (g.L + 3) + 256 * j
                    t0 = s * g.L + 256 * j
                    hw, hn = hwr.next()
                    self.dma(hw[:, :, 0:259], g.hT[:, :, c0:c0 + 259].rearrange("k p t -> p k t"), w=[hn])
                    xb, xn = xbr.next()
                    for fc in range(32):
                        pp, pn = ppr.next()
                        for k in range(8):
                            self.mm(pp[:, 0:259], wx[:, k, fc * 128:(fc + 1) * 128], hw[:, k, 0:259],
                                    start=(k == 0), stop=(k == 7), r=[hn, f"wx{k}"], w=[pn])
                        acc, an = accr.next()
                        self.conv_fm(pp, pn, acc, an, xb[:, fc, :], f"{xn}_{fc}", cw, cb, fc, 4, True, 0)
                    self.dma(g.bcT[:, :, t0:t0 + 256].rearrange("c p t -> p c t"), xb[:, 16:32, :],
                             r=[f"{xn}_{fc}" for fc in range(16, 32)])
                    for tcn in range(2):
                        for blk in range(3):
                            pt, ptn = ptr.next()
                            for i in range(8):
                                fc = blk * 8 + i
                                self.tr(pt[:, i, :], xb[:, fc, tcn * 128:(tcn + 1) * 128], self.ident_b[:],
                                        r=[f"{xn}_{fc}", "identb"], w=[ptn])
                            tm, tn = tmr.next()
                            self.cp("act" if blk % 2 == 0 else "dve", tm[:], pt[:].rearrange("p a b -> p (a b)"),
                                    r=[ptn], w=[tn])
                            r0 = t0 + tcn * 128
                            if blk < 2:
                                self.dma(g.xs_tm[r0:r0 + 128, blk * 1024:(blk + 1) * 1024], tm[:], r=[tn])
                            else:
                                self.dma(g.b_tm[r0:r0 + 128, :], tm[:], r=[tn])
                        pd, pdn = pdr.next()
                        for k in range(8):
                            self.mm(pd[:], hw[:, k, 1 + tcn * 128:1 + (tcn + 1) * 128], wd[:, k, :],
                                    start=(k == 0), stop=(k == 7), r=[hn, "wd"], w=[pdn])
                        dt, dn = dtr.next()
                        self.tt("dve", dt[:, 0:64], pd[:], dtb[:], ALU.add, r=[pdn, "dtb"], w=[dn])
                        self.act(dt[:, 0:64], dt[:, 0:64], AF.Exp, r=[dn], w=[dn])
                        self.act(dt[:, 0:64], dt[:, 0:64], AF.Ln, r=[dn], w=[dn], bias=1.0)
                        self.tt("dve", dt[:, 64:128], dt[:, 0:64], abc[:], ALU.mult, r=[dn, "abc"], w=[dn])
                        self.dma(g.dta[r0:r0 + 128, :], dt[:], r=[dn])

    def ssd_p2a(self, g, slot):
        if self.skip():
            return
        nc = self.nc
        with self.phase():
            xsr = Rot([self.sb([128, 2048], BF16, "xs") for _ in range(2)], "xs")
            btr = Rot([self.sb([128, 1024], BF16, "bt") for _ in range(2)], "bt")
            dtr = Rot([self.sb([128, 128], F32, "dta") for _ in range(2)], "dta")
            der = Rot([self.sb([128, 64], F32, "de") for _ in range(2)], "de")
            cdr = Rot([self.sb([128, 64], F32, "cd") for _ in range(2)], "cd")
            wdr = Rot([self.sb([128, 2048], BF16, "wdd") for _ in range(2)], "wdd")
            ssr = Rot([self.sb([128, 1024], F32, "ss") for _ in range(3)], "ss")
            pcr = Rot([self.ps([128, 128], F32, "pc") for _ in range(2)], "pc")
            psr = Rot([self.ps([128, 1024], F32, "psS") for _ in range(2)], "psS")
            arr = Rot([self.sb([128, 64], F32R, "ar") for _ in range(2)], "ar")
            if _os.environ.get("DEBUG") == "alloc":
                for t in pcr.tiles + psr.tiles + der.tiles:
                    print("ALLOC", t.name, self.nc.lookup_mloc(t))
            for ch in range(g.T // 128):
                r0 = ch * 128
                xs, xn = xsr.next(); bt, bn = btr.next(); dt, dn = dtr.next()
                self.dma(xs[:], g.xs_tm[r0:r0 + 128, :], w=[xn])
                self.dma(bt[:], g.b_tm[r0:r0 + 128, :], w=[bn])
                self.dma(dt[:], g.dta[r0:r0 + 128, :], w=[dn])
                pc, pcn = pcr.next()
                ar, arn = arr.next()
                self.cp("dve", ar[:], dt[:, 64:128], r=[dn], w=[arn])
                self.mm(pc[:, 0:32], self.masks_r[:, 2, :], ar[:, 0:32], r=["masksr", arn], w=[pcn])
                self.mm(pc[:, 32:64], self.masks_r[:, 3, :], ar[:, 32:64], r=["masksr", arn], w=[pcn])
                self.mm(pc[:, 64:128], self.ones_r[:], ar[:, 0:64], r=["onesr", arn], w=[pcn])
                de, den = der.next(); cd, cdn = cdr.next()
                self.act(de[:], pc[:, 0:64], AF.Exp, r=[pcn], w=[den])
                self.act(cd[:], pc[:, 64:128], AF.Exp, r=[pcn], w=[cdn])
                self.dma(g.cdec[ch], cd[:], r=[cdn])
                self.tt("dve", de[:], de[:], dt[:, 0:64], ALU.mult, r=[den, dn], w=[den])
                for d in range(2):
                    wdd, wn = wdr.next()
                    self.tt("dve", wdd[:].rearrange("p (h e) -> p h e", h=32), xs[:].rearrange("p (h e) -> p h e", h=32),
                            de[:, d * 32:(d + 1) * 32].unsqueeze(2).to_broadcast([128, 32, 64]), ALU.mult,
                            r=[xn, den], w=[wn])
                    for hlf in range(2):
                        pS, psn = psr.next()
                        for gg in range(4):
                            G8 = hlf * 4 + gg
                            self.mm(pS[:, gg * 256:(gg + 1) * 256], bt[:, G8 * 128:(G8 + 1) * 128],
                                    wdd[:, G8 * 256:(G8 + 1) * 256], r=[bn, wn], w=[psn + str(gg // 2)])
                        ss, sn = ssr.next()
                        for q in range(2):
                            self.cp("act" if q == 0 else "dve", ss[:, q * 512:(q + 1) * 512], pS[:, q * 512:(q + 1) * 512],
                                    r=[psn + str(q)], w=[sn])
                        self.dma(g.sloc[ch, d, :, hlf * 1024:(hlf + 1) * 1024], ss[:], r=[sn])

    def ssd_pR(self, g, slot):
        if self.skip():
            return
        nc = self.nc
        nchunk = g.L // 128
        with self.phase():
            hst = [self.sb([128, 2048], F32, "hst") for _ in range(2)]
            hbr = [Rot([self.sb([128, 2048], BF16, "hb") for _ in range(2)], f"hb{d}") for d in range(2)]
            slr = [Rot([self.sb([128, 2048], F32, "sl") for _ in range(2)], f"sl{d}") for d in range(2)]
            cdr = [Rot([self.sb([128, 64], F32, "cd") for _ in range(2)], f"cd{d}") for d in range(2)]
            stg = Rot([self.sb([128, 128], F32, "stg") for _ in range(3)], "stg")
            ptr = Rot([self.ps([128, 128], F32, "ptR") for _ in range(2)], "ptR")
            eng = ["dve", "pool"]
            for s in range(g.nseq):
                for d in range(2):
                    hn = f"hst{d}"
                    if g.latent:
                        for t in range(16):
                            sg, sgn = stg.next()
                            self.dma(sg[:], self.I["st_ssd"][slot, d, t * 128:(t + 1) * 128, :], w=[sgn])
                            pt, ptn = ptr.next()
                            self.tr(pt[:], sg[:], self.ident_f[:], r=[sgn, "identf"], w=[ptn])
                            self.cp("act", hst[d][:, t * 128:(t + 1) * 128], pt[:], r=[ptn], w=[hn])
                    else:
                        self.memset(eng[d], hst[d][:], 0.0, w=[hn])
                order = [list(range(nchunk)), list(range(nchunk - 1, -1, -1))]
                for i in range(nchunk):
                    for d in range(2):
                        c = order[d][i]
                        ch = s * nchunk + c
                        hn = f"hst{d}"
                        hb, hbn = hbr[d].next()
                        self.cp("act", hb[:], hst[d][:], r=[hn], w=[hbn])
                        self.dma(g.hprev[ch, d], hb[:], r=[hbn])
                        sl, sln = slr[d].next(); cd, cdn = cdr[d].next()
                        self.dma(sl[:], g.sloc[ch, d], w=[sln])
                        self.dma(cd[:], g.cdec[ch], w=[cdn])
                        self.tt(eng[d], hst[d][:].rearrange("p (h e) -> p h e", h=32),
                                hst[d][:].rearrange("p (h e) -> p h e", h=32),
                                cd[:, d * 32:(d + 1) * 32].unsqueeze(2).to_broadcast([128, 32, 64]), ALU.mult,
                                r=[hn, cdn], w=[hn])
                        self.tt(eng[d], hst[d][:], hst[d][:], sl[:], ALU.add, r=[hn, sln], w=[hn])
                if not g.latent:
                    for d in range(2):
                        hn = f"hst{d}"
                        for t in range(16):
                            pt, ptn = ptr.next()
                            self.tr(pt[:], hst[d][:, t * 128:(t + 1) * 128], self.ident_f[:], r=[hn, "identf"], w=[ptn])
                            sg, sgn = stg.next()
                            self.cp("act", sg[:], pt[:], r=[ptn], w=[sgn])
                            self.dma(self.O["nss"][s, slot, d, t * 128:(t + 1) * 128, :], sg[:], r=[sgn])

    def ssd_p2b(self, g, slot):
        if self.skip():
            return
        nc = self.nc
        with self.phase():
            w_in = self.I["ssd_in_w"][slot]
            wz = self.sb([128, 8, 2048], BF16, "wz")
            for k in range(8):
                self.dma(wz[:, k, :], w_in[k * 128:(k + 1) * 128, 0:2048], w=[f"wz{k}"], q="pool")
            ng = self.sb([128, 2048], F32, "ng")
            self.dma(ng[:], self.I["ssd_norm_g"][slot].partition_broadcast(128), w=["ng"])
            dsk = self.sb([128, 32], F32, "dsk")
            self.dma(dsk[:], self.I["ssd_d"][slot].partition_broadcast(128), w=["dsk"])
            mgt_r = self.sb([128, 128], F32R, "mgtr")
            mlt_r = self.sb([128, 128], F32R, "mltr")
            self.cp("dve", mgt_r[:], self.masks[:, 2, :], r=["masks"], w=["mgtr"])
            self.cp("dve", mlt_r[:], self.masks[:, 3, :], r=["masks"], w=["mltr"])
            Xl = [mgt_r, mlt_r]
            Xn = ["mgtr", "mltr"]
            Ym = [0, 1]
            Sm = [0, 1]

            hTr = Rot([self.sb([128, 8, 128], BF16, "hT") for _ in range(2)], "hT")
            xsr = Rot([self.sb([128, 2048], BF16, "xs") for _ in range(2)], "xs")
            bcr = Rot([self.sb([128, 16, 128], BF16, "bc") for _ in range(2)], "bc")
            dtr = Rot([self.sb([128, 128], F32, "dta") for _ in range(2)], "dta")
            hpr = Rot([self.sb([128, 2, 2048], BF16, "hp") for _ in range(2)], "hp")
            zsr = Rot([self.sb([128, 2048], BF16, "zs") for _ in range(2)], "zs")
            xdr = Rot([self.sb([128, 2, 2048], BF16, "xd") for _ in range(2)], "xd")
            ear = Rot([self.sb([128, 64], F32, "ea") for _ in range(2)], "ea")
            scr_ = Rot([self.sb([128, 2, 128], BF16, "scm") for _ in range(2)], "scm")
            Yr = Rot([self.sb([128, 4, 128], F32R, "Y") for _ in range(3)], "Y")
            Er = Rot([self.sb([128, 4, 128], BF16, "Ee") for _ in range(3)], "Ee")
            Mr = Rot([self.sb([128, 2, 4, 128], BF16, "Mm") for _ in range(2)], "Mm")
            t1r = Rot([self.sb([128, 256], F32, "t1") for _ in range(2)], "t1")
            t2r = Rot([self.sb([128, 256], F32, "t2") for _ in range(2)], "t2")
            ypr = Rot([self.sb([128, 2048], F32, "ypre") for _ in range(2)], "ypre")
            ynr = Rot([self.sb([128, 2048], BF16, "yn") for _ in range(2)], "yn")
            ssr = Rot([self.sb([128, 4], F32, "ssq") for _ in range(2)], "ssq")
            junk = self.sb([128, 2048], BF16, "junk")
            yTr = Rot([self.sb([128, 16, 128], BF16, "yT") for _ in range(2)], "yT")

            pzr = Rot([self.ps([128, 1024], F32, "pz") for _ in range(1)], "pz")
            pcs = Rot([self.ps([128, 512], F32, "pcs") for _ in range(1)], "pcs")
            psg = Rot([self.ps([128, 512], F32, "psg") for _ in range(2)], "psg")
            pyr = Rot([self.ps([128, 1024], F32, "py") for _ in range(1)], "py")
            ptr = Rot([self.ps([128, 8, 128], BF16, "ptb") for _ in range(1)], "ptb")
            nchunk = g.L // 128
            for s in range(g.nseq):
                for c in range(nchunk):
                    ch = s * nchunk + c
                    r0 = ch * 128
                    c0 = s * (g.L + 3) + 1 + c * 128
                    hT, hTn = hTr.next(); xs, xn = xsr.next(); bc, bcn = bcr.next(); dt, dn = dtr.next()
                    hp, hpn = hpr.next()
                    self.dma(hT[:], g.hT[:, :, c0:c0 + 128].rearrange("k p t -> p k t"), w=[hTn])
                    self.dma(xs[:], g.xs_tm[r0:r0 + 128, :], w=[xn])
                    self.dma(bc[:], g.bcT[:, :, r0:r0 + 128].rearrange("c p t -> p c t"), w=[bcn])
                    self.dma(dt[:], g.dta[r0:r0 + 128, :], w=[dn])
                    self.dma(hp[:], g.hprev[ch].rearrange("d p f -> p d f"), w=[hpn])
                    zs, zn = zsr.next()
                    for hlf in range(2):
                        pz, pzn = pzr.next()
                        for q in range(2):
                            for k in range(8):
                                col0 = hlf * 1024 + q * 512
                                self.mm(pz[:, q * 512:(q + 1) * 512], hT[:, k, :], wz[:, k, col0:col0 + 512],
                                        start=(k == 0), stop=(k == 7), r=[hTn, f"wz{k}"], w=[pzn + str(q)])
                        for q in range(2):
                            self.act(zs[:, hlf * 1024 + q * 512:hlf * 1024 + (q + 1) * 512], pz[:, q * 512:(q + 1) * 512],
                                     AF.Silu, r=[pzn + str(q)], w=[zn])
                    xd, xdn = xdr.next()
                    for d in range(2):
                        self.tt("dve" if d == 0 else "pool", xd[:, d, :].rearrange("p (h e) -> p h e", h=32),
                                xs[:].rearrange("p (h e) -> p h e", h=32),
                                dt[:, d * 32:(d + 1) * 32].unsqueeze(2).to_broadcast([128, 32, 64]), ALU.mult,
                                r=[xn, dn], w=[xdn + str(d)])
                    pc, pcn = pcs.next()
                    self.mm(pc[:, 0:32], self.masks[:, 0, :], dt[:, 64:96], r=["masks", dn], w=[pcn])
                    self.mm(pc[:, 32:64], self.masks[:, 1, :], dt[:, 96:128], r=["masks", dn], w=[pcn])
                    ea, ean = ear.next()
                    self.act(ea[:], pc[:, 0:64], AF.Exp, r=[pcn], w=[ean])
                    yp, ypn = ypr.next()
                    for G8 in range(8):
                        self.mm(pc[:, 128:256], bc[:, G8, :], bc[:, 8 + G8, :], r=[bcn], w=[pcn])
                        sm, smn = scr_.next()
                        for d in range(2):
                            self.tt("dve", sm[:, d, :], pc[:, 128:256], self.masks[:, Sm[d], :], ALU.mult,
                                    r=[pcn, "masks"], w=[smn + str(d)])
                        Mt, Mn = Mr.next()
                        for d in range(2):
                            Y, Yn = Yr.next()
                            self.tt("pool", Y[:], self.masks[:, Ym[d], :].unsqueeze(1).to_broadcast([128, 4, 128]),
                                    dt[:, 64 + d * 32 + G8 * 4:64 + d * 32 + G8 * 4 + 4].unsqueeze(2).to_broadcast([128, 4, 128]),
                                    ALU.mult, r=["masks", dn], w=[Yn])
                            pg, pgn = psg.next()
                            self.mm(pg[:], Xl[d][:], Y[:].rearrange("p a b -> p (a b)"), r=[Xn[d], Yn], w=[pgn])
                            Et, En = Er.next()
                            self.act(Et[:].rearrange("p a b -> p (a b)"), pg[:], AF.Exp, r=[pgn], w=[En])
                            self.tt("dve", Mt[:, d, :, :], Et[:], sm[:, d, :].unsqueeze(1).to_broadcast([128, 4, 128]),
                                    ALU.mult, r=[En, smn + str(d)], w=[Mn + str(d)])
                        py, pyn = pyr.next()
                        for h in range(4):
                            H = G8 * 4 + h
                            for d in range(2):
                                self.mm(py[:, h * 64:(h + 1) * 64], Mt[:, d, h, :], xd[:, d, H * 64:(H + 1) * 64],
                                        start=(d == 0), stop=(d == 1), r=[Mn + str(d), xdn + str(d)], w=[pyn + "d"])
                        for d in range(2):
                            self.mm(py[:, 512 + d * 256:512 + (d + 1) * 256], bc[:, 8 + G8, :],
                                    hp[:, d, G8 * 256:(G8 + 1) * 256], r=[bcn, hpn], w=[pyn + "o"])
                        t1, t1n = t1r.next(); t2, t2n = t2r.next()
                        v3 = lambda ap: ap.rearrange("p (h e) -> p h e", h=4)
                        for d, (tt_, tn_) in enumerate(((t1, t1n), (t2, t2n))):
                            self.tt("dve", v3(tt_[:]), v3(py[:, 512 + d * 256:512 + (d + 1) * 256]),
                                    ea[:, d * 32 + G8 * 4:d * 32 + G8 * 4 + 4].unsqueeze(2).to_broadcast([128, 4, 64]),
                                    ALU.mult, r=[pyn + "o", ean], w=[tn_])
                        self.tt("pool", t1[:], t1[:], t2[:], ALU.add, r=[t1n, t2n], w=[t1n])
                        self.tt("dve", t2[:], py[:, 0:256], t1[:], ALU.add, r=[pyn + "d", t1n], w=[t2n])
                        self.tt("pool", v3(t1[:]), v3(xs[:, G8 * 256:(G8 + 1) * 256]),
                                dsk[:, G8 * 4:G8 * 4 + 4].unsqueeze(2).to_broadcast([128, 4, 64]), ALU.mult,
                                r=[xn, "dsk", t1n], w=[t1n])
                        self.tt("pool", yp[:, G8 * 256:(G8 + 1) * 256], t1[:], t2[:], ALU.add, r=[t1n, t2n],
                                w=[ypn + str(G8)])
                    ypa = [ypn + str(i) for i in range(8)]
                    self.tt("dve", yp[:], yp[:], zs[:], ALU.mult, r=ypa + [zn], w=ypa)
                    sq, sqn = ssr.next()
                    self.act(junk[:], yp[:], AF.Square, r=ypa, w=["junk", sqn], accum_out=sq[:, 0:1])
                    self.ts("dve", sq[:, 1:2], sq[:, 0:1], 1.0 / 2048.0, RMS_EPS, ALU.mult, ALU.add, r=[sqn], w=[sqn])
                    self.act(sq[:, 1:2], sq[:, 1:2], AF.Sqrt, r=[sqn], w=[sqn])
                    self.S.op("dve", lambda sq=sq: nc.vector.reciprocal(out=sq[:, 2:3], in_=sq[:, 1:2]),
                              reads=[sqn], writes=[sqn])
                    yn_, ynn = ynr.next()
                    self.stt(yn_[:], yp[:], sq[:, 2:3], ng[:], ALU.mult, ALU.mult, r=ypa + [sqn, "ng"], w=[ynn])
                    yT, yTn = yTr.next()
                    for blk in range(2):
                        pt, ptn = ptr.next()
                        for i in range(8):
                            fc = blk * 8 + i
                            self.tr(pt[:, i, :], yn_[:, fc * 128:(fc + 1) * 128], self.ident_b[:], r=[ynn, "identb"], w=[ptn])
                        self.cp("act", yT[:, blk * 8:(blk + 1) * 8, :], pt[:], r=[ptn], w=[yTn])
                    self.dma(g.yT[:, :, r0:r0 + 128].rearrange("k p t -> p k t"), yT[:], r=[yTn])

    def lru_layer(self, g):
        self.lru_p1(g)
        self.lru_p2(g)

    def lru_p1(self, g):
        if self.skip():
            return
        nc = self.nc
        with self.phase():
            win = self.sb([128, 8, 2048], BF16, "win")
            for k in range(8):
                self.dma(win[:, k, :], self.I["lru_in_w"][k * 128:(k + 1) * 128, :], w=[f"win{k}"], q="pool")
            gw = self.sb([128, 32, 256], BF16, "gw")
            gsrc = self.I["lru_gate_w"].rearrange("d g n (kc p) j -> (d g n kc) p j", p=128)
            for i in range(32):
                self.dma(gw[:, i, :], gsrc[i], w=["gw"], q="pool")
            cw = self.sb([128, 4, 8], F32, "cw")
            cb = self.sb([128, 8], F32, "cb")
            gb = self.sb([128, 4, 8], F32, "gb")
            nsp = self.sb([128, 2, 8], F32, "nsp")
            h0 = self.lru_h0
            self.load_T(cw[:].rearrange("p k c -> p (k c)"),
                        self.I["lru_conv_w"].rearrange("k (c p) -> (k c) p", p=128), 32, "cw")
            self.load_T(cb[:], self.I["lru_conv_b"].rearrange("(c p) -> c p", p=128), 8, "cb")
            self.load_T(gb[:].rearrange("p k c -> p (k c)"),
                        self.I["lru_gate_b"].rearrange("k (c p) -> (k c) p", p=128), 32, "gb")
            self.load_T(nsp[:].rearrange("p k c -> p (k c)"),
                        self.I["lru_a_param"].rearrange("k (c p) -> (k c) p", p=128), 16, "nsp")
            self.act(nsp[:], nsp[:], AF.Exp, r=["nsp"], w=["nsp"], scale=-1.0)
            self.act(nsp[:], nsp[:], AF.Ln, r=["nsp"], w=["nsp"], bias=1.0)
            self.ts("dve", nsp[:], nsp[:], -8.0, None, ALU.mult, r=["nsp"], w=["nsp"])
            if g.latent:
                self.load_T(h0[:].rearrange("p k c -> p (k c)"),
                            self.I["st_lru"].rearrange("k (c p) -> (k c) p", p=128), 16, "h0")
            else:
                self.memset("dve", h0[:], 0.0, w=["h0"])
            hwr = Rot([self.sb([128, 8, 260], BF16, "hw") for _ in range(2)], "hw")
            xrr = Rot([self.sb([128, 8, 256], F32, "xr") for _ in range(1)], "xr")
            xbr = Rot([self.sb([128, 8, 256], BF16, "xrb") for _ in range(1)], "xrb")
            zsr = Rot([self.sb([128, 8, 256], F32, "zs") for _ in range(2)], "zs")
            gtr = [Rot([self.sb([128, 8, 256], F32, "gt") for _ in range(1)], f"gt{i}") for i in range(4)]
            aar = [Rot([self.sb([128, 8, 256], F32, "aa") for _ in range(1 + d)], f"aa{d}") for d in range(2)]
            bxr = [Rot([self.sb([128, 8, 256], F32, "bx") for _ in range(1 + d)], f"bx{d}") for d in range(2)]
            tmr = Rot([self.sb([128, 8, 256], F32, "tmpl") for _ in range(1)], "tmpl")
            yfr = Rot([self.sb([128, 8, 256], F32, "yf") for _ in range(2)], "yf")
            accr = Rot([self.sb([128, 256], F32, "acc") for _ in range(3)], "acc")
            fin = self.sb([128, 8], F32, "fin")
            ppr = Rot([self.ps([128, 512], F32, "pp") for _ in range(3)], "pp")
            pgr = Rot([self.ps([128, 256], F32, "pg") for _ in range(3)], "pg")
            for s in range(g.nseq):
                yf_prev = None
                ntile = g.L // 256
                for j in range(ntile):
                    c0 = s * (g.L + 3) + 256 * j
                    t0 = s * g.L + 256 * j
                    hw, hn = hwr.next()
                    self.dma(hw[:, :, 0:259], g.hT[:, :, c0:c0 + 259].rearrange("k p t -> p k t"), w=[hn])
                    xr, xn = xrr.next(); xb, xbn = xbr.next(); zs, zn = zsr.next()
                    for fc in range(16):
                        pp, pn = ppr.next()
                        for k in range(8):
                            self.mm(pp[:, 0:259], win[:, k, fc * 128:(fc + 1) * 128], hw[:, k, 0:259],
                                    start=(k == 0), stop=(k == 7), r=[hn, f"win{k}"], w=[pn])
                        if fc < 8:
                            acc, an = accr.next()
                            self.conv_fm(pp, pn, acc, an, xr[:, fc, :], f"{xn}_{fc}", cw, cb, fc, 4, False, 0)
                            self.cp("pool", xb[:, fc, :], xr[:, fc, :], r=[f"{xn}_{fc}"], w=[f"{xbn}_{fc}"])
                        else:
                            self.act(zs[:, fc - 8, :], pp[:, 1:257], AF.Silu, r=[pn], w=[zn])
                    gts = [gtr[i].next() for i in range(4)]
                    for dg in range(4):
                        gt, gn = gts[dg]
                        for n in range(4):
                            for jc in range(2):
                                pg, pgn = pgr.next()
                                for kc in range(2):
                                    self.mm(pg[:], gw[:, (dg * 4 + n) * 2 + kc, jc * 128:(jc + 1) * 128], xb[:, n * 2 + kc, :],
                                            start=(kc == 0), stop=(kc == 1), r=["gw", f"{xbn}_{n * 2 + kc}"], w=[pgn])
                                ch = n * 2 + jc
                                self.act(gt[:, ch, :], pg[:], AF.Sigmoid, r=[pgn, "gb"], w=[gn], bias=gb[:, dg, ch:ch + 1])
                    xall = [f"{xn}_{fc}" for fc in range(8)]
                    ab = []
                    for d in range(2):
                        aa, aan = aar[d].next(); bx, bxn = bxr[d].next(); tm, tmn = tmr.next()
                        rt, rn = gts[d * 2]; it, itn = gts[d * 2 + 1]
                        for ch in range(8):
                            self.act(aa[:, ch, :], rt[:, ch, :], AF.Exp, r=[rn, "nsp"], w=[aan], scale=nsp[:, d, ch:ch + 1])
                        self.tt("dve", tm[:], aa[:], aa[:], ALU.mult, r=[aan], w=[tmn])
                        self.ts("dve", tm[:], tm[:], -1.0, 1.0, ALU.mult, ALU.add, r=[tmn], w=[tmn])
                        self.ts("dve", tm[:], tm[:], 0.0, None, ALU.max, r=[tmn], w=[tmn])
                        self.act(tm[:], tm[:], AF.Sqrt, r=[tmn], w=[tmn])
                        self.tt("pool", tm[:], tm[:], it[:], ALU.mult, r=[tmn, itn], w=[tmn])
                        self.tt("pool", bx[:], tm[:], xr[:], ALU.mult, r=[tmn] + xall, w=[bxn])
                        ab.append((aa, aan, bx, bxn))
                    yf, yfn = yfr.next()
                    aa, aan, bx, bxn = ab[0]
                    for ch in range(8):
                        init = h0[:, 0, ch:ch + 1] if yf_prev is None else yf_prev[0][:, ch, 255:256]
                        rr = [aan, bxn, "h0"] + ([yf_prev[1]] if yf_prev is not None else [])
                        self.S.op("dve", lambda yf=yf, aa=aa, bx=bx, ch=ch, init=init: nc.vector.tensor_tensor_scan(
                            out=yf[:, ch, :], data0=aa[:, ch, :], data1=bx[:, ch, :], initial=init,
                            op0=ALU.mult, op1=ALU.add), reads=rr, writes=[yfn])
                    yf_prev = (yf, yfn)
                    fmv = lambda slot: g.fm[slot, :, :, t0:t0 + 256].rearrange("c p t -> p c t")
                    self.dma(fmv(0), yf[:], r=[yfn])
                    self.dma(fmv(1), ab[1][0][:], r=[ab[1][1]])
                    self.dma(fmv(2), ab[1][2][:], r=[ab[1][3]])
                    self.dma(fmv(3), zs[:], r=[zn])
                if not g.latent:
                    self.cp("dve", fin[:], yf_prev[0][:, :, 255], r=[yf_prev[1]], w=["fin"])
                    self.dma(self.O["nsl"][s, 0, :].rearrange("(c p) -> p c", p=128), fin[:], r=["fin"],
                             allow_slow_non_contiguous=True)

    def lru_p2(self, g):
        if self.skip():
            return
        nc = self.nc
        with self.phase():
            h0 = self.lru_h0
            ldr = [Rot([self.sb([128, 8, 256], F32, "ld") for _ in range(2)], f"ld{i}") for i in range(4)]
            ybr = Rot([self.sb([128, 8, 256], F32, "yb") for _ in range(2)], "yb")
            yTr = Rot([self.sb([128, 8, 256], BF16, "yTl") for _ in range(2)], "yTl")
            fin = self.sb([128, 8], F32, "fin")
            for s in range(g.nseq):
                yb_prev = None
                ntile = g.L // 256
                for j in range(ntile - 1, -1, -1):
                    t0 = s * g.L + 256 * j
                    lt = []
                    for i in range(4):
                        t, tn = ldr[i].next()
                        self.dma(t[:], g.fm[i, :, :, t0:t0 + 256].rearrange("c p t -> p c t"), w=[tn])
                        lt.append((t, tn))
                    (yf, yfn), (aa, aan), (bx, bxn), (zs, zn) = lt
                    yb, ybn = ybr.next()
                    for ch in range(8):
                        init = h0[:, 1, ch:ch + 1] if yb_prev is None else yb_prev[0][:, ch, 0:1]
                        rr = [aan, bxn, "h0"] + ([yb_prev[1]] if yb_prev is not None else [])
                        self.S.op("dve", lambda yb=yb, aa=aa, bx=bx, ch=ch, init=init: nc.vector.tensor_tensor_scan(
                            out=yb[:, ch, ::-1], data0=aa[:, ch, ::-1], data1=bx[:, ch, ::-1], initial=init,
                            op0=ALU.mult, op1=ALU.add), reads=rr, writes=[ybn])
                    yb_prev = (yb, ybn)
                    self.tt("pool", yf[:], yf[:], yb[:], ALU.add, r=[yfn, ybn], w=[yfn])
                    yT, yTn = yTr.next()
                    self.tt("pool", yT[:], yf[:], zs[:], ALU.mult, r=[yfn, zn], w=[yTn])
                    self.dma(g.yT[0:8, :, t0:t0 + 256].rearrange("c p t -> p c t"), yT[:], r=[yTn])
                if not g.latent:
                    self.cp("dve", fin[:], yb_prev[0][:, :, 0], r=[yb_prev[1]], w=["fin"])
                    self.dma(self.O["nsl"][s, 1, :].rearrange("(c p) -> p c", p=128), fin[:], r=["fin"],
                             allow_slow_non_contiguous=True)

    def dbg_rows(self, dst_row0, src2d, nrows):
        with self.phase():
            for r0 in range(0, nrows, 128):
                t = self.sb([128, 1024], F32, "dbg")
                self.dma(t[:], src2d[r0:r0 + 128, :], w=["dbg"])
                self.dma(self.O["yp"][dst_row0 + r0:dst_row0 + r0 + 128, :], t[:], r=["dbg"])

    def hyena_layer(self, g):
        self.hy_filters(g)
        if _os.environ.get("DEBUG") == "hyk" and g.name == "p":
            self.dbg_rows(0, g.khat2[0, 0], 256)
            self.dbg_rows(256, g.khat2[0, 1], 256)
            self.dbg_rows(512, g.kw[:, 0:1024], 256)
            self.dbg_rows(768, g.kw[:, 1024:2048], 256)
            return
        self.hy_p1(g)
        for o in range(2):
            for s in range(g.nseq):
                self.hy_fwd(g, o, s)
                self.hy_inv(g, o, s)

    def _vec(self, dst, src1d, n, name):
        self.dma(dst, src1d.rearrange("(p o) -> p o", o=1), w=[name], allow_slow_non_contiguous=True)

    def _hy_mlp(self, g, hd2, w3r):
        nc = self.nc
        L = g.L
        nm = g.name
        TWO_PI = 2.0 * math.pi
        MAGIC = 12582912.0
        with self.phase():
            zT = self.sb([33, L], F32, "zT")
            self.dma(zT[:], self.I[f"hz_{nm}"], w=["zT"])
            w1 = self.sb([33, 64], F32, "w1"); w2 = self.sb([64, 64], F32, "w2")
            self.dma(w1[:], self.I["hy_f_w1"], w=["w1"]); self.dma(w2[:], self.I["hy_f_w2"], w=["w2"])
            w3 = self.sb([64, 4096], F32, "w3")
            self.dma(w3[:], self.I["hy_f_w3"], w=["w3"])
            self.cp("pool", w3r[:], w3[:], r=["w3"], w=["w3r"])
            pv = self.sb([64, 6], F32, "pv")
            self._vec(pv[:, 0:1], self.I["hy_f_b1"], 64, "pv"); self._vec(pv[:, 1:2], self.I["hy_f_b2"], 64, "pv")
            self._vec(pv[:, 2:3], self.I["hy_f_freq"][0], 64, "pv"); self._vec(pv[:, 3:4], self.I["hy_f_freq"][1], 64, "pv")
            self.tt("dve", pv[:, 4:6], pv[:, 0:2], pv[:, 2:4], ALU.mult, r=["pv"], w=["pv"])
            hd1 = self.sb([64, L], F32, "hd1")
            argr = Rot([self.sb([64, 512], F32, "arg") for _ in range(2)], "arg")
            nr = Rot([self.sb([64, 512], F32, "nq") for _ in range(2)], "nq")
            phr = Rot([self.ps([64, 512], F32, "ph") for _ in range(2)], "ph")
            TW = min(512, L)
            for layer in range(2):
                for ti in range(L // TW):
                    sl = slice(ti * TW, (ti + 1) * TW)
                    ph, phn = phr.next()
                    if layer == 0:
                        self.mm(ph[:, 0:TW], w1[:], zT[:, sl], r=["w1", "zT"], w=[phn])
                    else:
                        self.mm(ph[:, 0:TW], w2[:], hd1[:, sl], r=["w2", "hd1"], w=[phn])
                    arg, an = argr.next(); nq, nn = nr.next()
                    self.act(arg[:, 0:TW], ph[:, 0:TW], AF.Identity, r=[phn, "pv"], w=[an],
                             scale=pv[:, 2 + layer:3 + layer], bias=pv[:, 4 + layer:5 + layer])
                    self.ts("dve", nq[:, 0:TW], arg[:, 0:TW], 1.0 / TWO_PI, MAGIC, ALU.mult, ALU.add, r=[an], w=[nn])
                    self.ts("dve", nq[:, 0:TW], nq[:, 0:TW], MAGIC, None, ALU.subtract, r=[nn], w=[nn])
                    self.stt(arg[:, 0:TW], nq[:, 0:TW], -TWO_PI, arg[:, 0:TW], ALU.mult, ALU.add, r=[nn, an], w=[an])
                    self.ts("dve", arg[:, 0:TW], arg[:, 0:TW], math.pi, -math.pi, ALU.min, ALU.max, r=[an], w=[an])
                    if layer == 0:
                        self.act(hd1[:, sl], arg[:, 0:TW], AF.Sin, r=[an], w=["hd1"])
                    else:
                        self.act(hd2[:, sl], arg[:, 0:TW], AF.Sin, r=[an], w=["hd2"])

    def hy_filters(self, g):
        if self.skip():
            return
        nc = self.nc
        L = g.L
        nL = L // 128
        nm = g.name
        TWO_PI = 2.0 * math.pi
        MAGIC = 12582912.0
        with self.phase():
            hd2 = self.sb([64, L], F32R, "hd2")
            w3r = self.sb([64, 4096], F32R, "w3r")
            self._hy_mlp(g, hd2, w3r)
            asum = self.sb([128, 4096], F32, "asum")
            self.memset("pool", asum[:], 0.0, w=["asum"])
            winr = Rot([self.sb([128, 1024], F32, "winw") for _ in range(2)], "winw")
            kwr = Rot([self.sb([128, 4096], F32, "kwt") for _ in range(2)], "kwt")
            kabs = self.sb([128, 4096], F32, "kabs")
            pkr = Rot([self.ps([128, 512], F32, "pk") for _ in range(3)], "pk")
            for tc in range(nL):
                wt, wn = winr.next()
                self.dma(wt[:], self.I[f"win_{nm}"][tc * 128:(tc + 1) * 128, :], w=[wn])
                kt, kn = kwr.next()
                for ti in range(8):
                    pk, pkn = pkr.next()
                    self.mm(pk[:], hd2[:, tc * 128:(tc + 1) * 128], w3r[:, ti * 512:(ti + 1) * 512], r=["hd2", "w3r"], w=[pkn])
                    self.tt("dve", kt[:, ti * 512:(ti + 1) * 512], pk[:], wt[:, (ti % 2) * 512:(ti % 2 + 1) * 512], ALU.mult,
                            r=[pkn, wn], w=[kn])
                self.act(kabs[:], kt[:], AF.Abs, r=[kn], w=["kabs"])
                self.tt("pool", asum[:], asum[:], kabs[:], ALU.add, r=["kabs", "asum"], w=["asum"])
                self.dma(g.kw[tc * 128:(tc + 1) * 128, :], kt[:], r=[kn])
            asr = self.sb([128, 4096], F32R, "asr")
            self.cp("dve", asr[:], asum[:], r=["asum"], w=["asr"])
            tot = self.sb([128, 4096], F32, "tot")
            for ti in range(8):
                pk, pkn = pkr.next()
                self.mm(pk[:], self.ones_r[:], asr[:, ti * 512:(ti + 1) * 512], r=["onesr", "asr"], w=[pkn])
                self.cp("act", tot[:, ti * 512:(ti + 1) * 512], pk[:], r=[pkn], w=["tot"])
            rn = self.sb([128, 2, 1024], F32, "rn")
            tv = tot[:].rearrange("p (o d c) -> p o d c", o=2, d=2)
            self.tt("dve", rn[:], tv[:, :, 0, :], tv[:, :, 1, :], ALU.add, r=["tot"], w=["rn"])
            rn0 = self.sb([128, 2, 1024], F32, "rn0")
            self.act(rn0[:], rn[:], AF.Ln, r=["rn"], w=["rn0"])
            self.act(rn[:], rn0[:], AF.Exp, r=["rn0"], w=["rn"], scale=-1.0)
            self.ts("dve", rn[:], rn[:], 2.0 / (2 * L), None, ALU.mult, r=["rn"], w=["rn"])
            self.dma(g.khat[0, 0, 0:128, :], rn[:, 0, :], r=["rn"])
            self.dma(g.khat[0, 1, 0:128, :], rn[:, 1, :], r=["rn"])
        with self.phase():
            rn = self.sb([128, 2, 1024], F32, "rn")
            self.dma(rn[:, 0, :], g.khat[0, 0, 0:128, :], w=["rn"])
            self.dma(rn[:, 1, :], g.khat[0, 1, 0:128, :], w=["rn"])
            self.S.barrier()
            ks = self.sb([128, nL, 1024], BF16, "ks")
            kldr = Rot([self.sb([128, 2, 1024], F32, "kld") for _ in range(2)], "kld")
            mbr = Rot([self.sb([128, nL, 128], BF16, "mb") for _ in range(2)], "mb")
            kor = Rot([self.sb([128, 1024], F32, "ko") for _ in range(2)], "ko")
            pur = Rot([self.ps([128, 1024], F32, "pu") for _ in range(2)], "pu")
            for o in range(2):
                for cs in range(2):
                    for tc in range(nL):
                        kl, kln = kldr.next()
                        self.dma(kl[:], g.kw[tc * 128:(tc + 1) * 128, o * 2048:(o + 1) * 2048].rearrange("p (d c) -> p d c", d=2),
                                 w=[kln])
                        if tc == 0:
                            self.memset("dve", kl[0:1, 1, :], 0.0, w=[kln])
                        self.tt("dve", kl[:, 0, :], kl[:, 0, :], kl[:, 1, :], ALU.add if cs == 0 else ALU.subtract,
                                r=[kln], w=[kln])
                        self.tt("pool", ks[:, tc, :], kl[:, 0, :], rn[:, o, :], ALU.mult, r=[kln, "rn"], w=[f"ks{tc}"])
                    for fc in range(nL):
                        mb, mbn = mbr.next()
                        self.dma(mb[:], self.I[f"dft_{nm}"][2 + cs, fc], w=[mbn])
                        pu, pun = pur.next()
                        for tc in range(nL):
                            for hlf in range(2):
                                self.mm(pu[:, hlf * 512:(hlf + 1) * 512], mb[:, tc, :], ks[:, tc, hlf * 512:(hlf + 1) * 512],
                                        start=(tc == 0), stop=(tc == nL - 1), r=[mbn, f"ks{tc}"],
                                        w=[pun + str(hlf)])
                        ko, kon = kor.next()
                        for hlf in range(2):
                            self.cp("act", ko[:, hlf * 512:(hlf + 1) * 512], pu[:, hlf * 512:(hlf + 1) * 512],
                                    r=[pun + str(hlf)], w=[kon])
                        self.dma(g.khat2[o, cs, fc * 128:(fc + 1) * 128, :], ko[:], r=[kon])

    def hy_p1(self, g):
        if self.skip():
            return
        nc = self.nc
        with self.phase():
            win = self.sb([128, 8, 4096], BF16, "win")
            for k in range(8):
                self.dma(win[:, k, :], self.I["hy_in_w"][k * 128:(k + 1) * 128, :], w=[f"win{k}"], q="pool")
            cw = self.sb([128, 3, 24], F32, "cw")
            cb = self.sb([128, 24], F32, "cb")
            self.load_T(cw[:].rearrange("p k c -> p (k c)"),
                        self.I["hy_conv_w"].rearrange("k (c p) -> (k c) p", p=128), 72, "cw")
            self.load_T(cb[:], self.I["hy_conv_b"].rearrange("(c p) -> c p", p=128), 24, "cb")
            hwr = Rot([self.sb([128, 8, 260], BF16, "hw") for _ in range(2)], "hw")
            fmr = [Rot([self.sb([128, 8, 256], F32, "fmt") for _ in range(2)], f"fmt{i}") for i in range(4)]
            vbr = Rot([self.sb([128, 8, 256], BF16, "vb") for _ in range(2)], "vb")
            utr = Rot([self.sb([128, 1024], BF16, "ut") for _ in range(2)], "ut")
            accr = Rot([self.sb([128, 256], F32, "acc") for _ in range(3)], "acc")
            ppr = Rot([self.ps([128, 512], F32, "pp") for _ in range(3)], "pp")
            ptr = Rot([self.ps([128, 8, 128], BF16, "ptr") for _ in range(2)], "ptr")
            for s in range(g.nseq):
                for j in range(g.L // 256):
                    c0 = s * (g.L + 3) + 256 * j
                    t0 = s * g.L + 256 * j
                    hw, hn = hwr.next()
                    self.dma(hw[:, :, 0:259], g.hT[:, :, c0:c0 + 259].rearrange("k p t -> p k t"), w=[hn])
                    fts = [fmr[i].next() for i in range(4)]
                    vb, vbn = vbr.next()
                    for fc in range(32):
                        pp, pn = ppr.next()
                        for k in range(8):
                            self.mm(pp[:, 0:259], win[:, k, fc * 128:(fc + 1) * 128], hw[:, k, 0:259],
                                    start=(k == 0), stop=(k == 7), r=[hn, f"win{k}"], w=[pn])
                        ft, fn = fts[fc // 8]
                        if fc < 24:
                            acc, an = accr.next()
                            self.conv_fm(pp, pn, acc, an, ft[:, fc % 8, :], f"{fn}_{fc % 8}", cw, cb, fc, 3, False, 0)
                            if fc < 8:
                                self.cp("pool", vb[:, fc, :], ft[:, fc, :], r=[f"{fn}_{fc}"], w=[f"{vbn}_{fc}"])
                        else:
                            self.act(ft[:, fc % 8, :], pp[:, 1:257], AF.Silu, r=[pn], w=[f"{fn}_{fc % 8}"])
                    for i in range(4):
                        ft, fn = fts[i]
                        self.dma(g.fm[i, :, :, t0:t0 + 256].rearrange("c p t -> p c t"), ft[:],
                                 r=[f"{fn}_{c}" for c in range(8)])
                    for tcn in range(2):
                        pt, ptn = ptr.next()
                        for c in range(8):
                            self.tr(pt[:, c, :], vb[:, c, tcn * 128:(tcn + 1) * 128], self.ident_b[:],
                                    r=[f"{vbn}_{c}", "identb"], w=[ptn])
                        ut, utn = utr.next()
                        self.cp("dve", ut[:], pt[:].rearrange("p a b -> p (a b)"), r=[ptn], w=[utn])
                        self.dma(g.utm[t0 + tcn * 128:t0 + (tcn + 1) * 128, :], ut[:], r=[utn])

    def hy_fwd(self, g, o, s):
        if self.skip():
            return
        nc = self.nc
        L = g.L
        nL = L // 128
        nm = g.name
        with self.phase():
            u = self.sb([128, nL, 1024], BF16, "u")
            usrc = g.utm[s * L:(s + 1) * L, :].rearrange("(tc p) c -> p tc c", p=128)
            for q in range(0, nL, 8):
                qe = min(q + 8, nL)
                self.dma(u[:, q:qe, :], usrc[:, q:qe, :], w=[f"u{q}"])
            cbr = Rot([self.sb([128, nL, 128], BF16, "cbk") for _ in range(2)], "cbk")
            sbr = Rot([self.sb([128, nL, 128], BF16, "sbk") for _ in range(2)], "sbk")
            kcr = Rot([self.sb([128, 1024], F32, "kc") for _ in range(2)], "kc")
            ksr = Rot([self.sb([128, 1024], F32, "ksp") for _ in range(2)], "ksp")
            a1r = Rot([self.sb([128, 1024], F32, "a1") for _ in range(2)], "a1")
            a2r = Rot([self.sb([128, 1024], F32, "a2") for _ in range(2)], "a2")
            yor = Rot([self.sb([128, 2, 1024], BF16, "yo") for _ in range(2)], "yo")
            puc = Rot([self.ps([128, 1024], F32, "puc") for _ in range(1)], "puc")
            pus = Rot([self.ps([128, 1024], F32, "pus") for _ in range(1)], "pus")
            for fc in range(nL):
                cb_, cbn = cbr.next(); sb_, sbn = sbr.next()
                self.dma(cb_[:], self.I[f"dft_{nm}"][0, fc], w=[cbn])
                self.dma(sb_[:], self.I[f"dft_{nm}"][1, fc], w=[sbn])
                kc, kcn = kcr.next(); ksp, ksn = ksr.next()
                self.dma(kc[:], g.khat2[o, 0, fc * 128:(fc + 1) * 128, :], w=[kcn])
                self.dma(ksp[:], g.khat2[o, 1, fc * 128:(fc + 1) * 128, :], w=[ksn])
                pc_, pcn = puc.next(); ps_, psn = pus.next()
                for tc in range(nL):
                    q8 = (tc // 8) * 8
                    for hlf in range(2):
                        hs = slice(hlf * 512, (hlf + 1) * 512)
                        self.mm(pc_[:, hs], cb_[:, tc, :], u[:, tc, hs], start=(tc == 0), stop=(tc == nL - 1),
                                r=[cbn, f"u{q8}"], w=[pcn + str(hlf)])
                        self.mm(ps_[:, hs], sb_[:, tc, :], u[:, tc, hs], start=(tc == 0), stop=(tc == nL - 1),
                                r=[sbn, f"u{q8}"], w=[psn + str(hlf)])
                a1, a1n = a1r.next(); a2, a2n = a2r.next(); yo, yon = yor.next()
                for hlf in range(2):
                    hs = slice(hlf * 512, (hlf + 1) * 512)
                    self.tt("dve", a1[:, hs], pc_[:, hs], kc[:, hs], ALU.mult, r=[pcn + str(hlf), kcn], w=[a1n])
                    self.tt("dve", a2[:, hs], ps_[:, hs], ksp[:, hs], ALU.mult, r=[psn + str(hlf), ksn], w=[a2n])
                self.tt("pool", yo[:, 0, :], a1[:], a2[:], ALU.subtract, r=[a1n, a2n], w=[yon + "c"])
                a1, a1n = a1r.next(); a2, a2n = a2r.next()
                for hlf in range(2):
                    hs = slice(hlf * 512, (hlf + 1) * 512)
                    self.tt("dve", a1[:, hs], pc_[:, hs], ksp[:, hs], ALU.mult, r=[pcn + str(hlf), ksn], w=[a1n])
                    self.tt("dve", a2[:, hs], ps_[:, hs], kc[:, hs], ALU.mult, r=[psn + str(hlf), kcn], w=[a2n])
                self.tt("pool", yo[:, 1, :], a1[:], a2[:], ALU.add, r=[a1n, a2n], w=[yon + "s"])
                self.dma(g.yspec[:, :, :, fc, :].rearrange("a cc p j -> p a cc j"),
                         yo[:].rearrange("p a (cc j) -> p a cc j", j=128), r=[yon + "c", yon + "s"])

    def hy_inv(self, g, o, s):
        if self.skip():
            return
        nc = self.nc
        L = g.L
        nL = L // 128
        nm = g.name
        TT = min(512, L)
        nq = TT // 128
        with self.phase():
            fb = self.sb([128, 2, 8], F32, "fb")
            self.load_T(fb[:].rearrange("p k c -> p (k c)"),
                        self.I["hy_f_bias"].rearrange("k (c p) -> (k c) p", p=128), 16, "fb")
            cm = self.sb([128, nL, TT], BF16, "cm")
            sm = self.sb([128, nL, TT], BF16, "sm")
            ycr = Rot([self.sb([128, nL, 128], BF16, "yc") for _ in range(2)], "yc")
            ysr = Rot([self.sb([128, nL, 128], BF16, "ys") for _ in range(2)], "ys")
            utr = Rot([self.sb([128, TT], F32, "uti") for _ in range(2)], "uti")
            xgr = Rot([self.sb([128, TT], F32, "xg") for _ in range(2)], "xg")
            zgr = Rot([self.sb([128, TT], F32, "zg") for _ in range(2)], "zg")
            unr = Rot([self.sb([128, TT], F32, "un") for _ in range(2)], "un")
            ubr = Rot([self.sb([128, TT], BF16, "ub") for _ in range(2)], "ub")
            uor = Rot([self.sb([128, 8, 128], BF16, "uo") for _ in range(2)], "uo")
            par = Rot([self.ps([128, 512], F32, "pa") for _ in range(2)], "pa")
            ptq = [self.ps([128, 8, 128], BF16, "ptq") for _ in range(nq)] if o == 0 else []
            for tt in range(L // TT):
                t0 = s * L + tt * TT
                for q in range(0, nL, 8):
                    qe = min(q + 8, nL)
                    self.dma(cm[:, q:qe, :], self.I[f"dfti_{nm}"][0, tt, :, q:qe, :], w=[f"cm{q}"])
                    self.dma(sm[:, q:qe, :], self.I[f"dfti_{nm}"][1, tt, :, q:qe, :], w=[f"sm{q}"])
                for cc in range(8):
                    yc, ycn = ycr.next(); ys_, ysn = ysr.next()
                    self.dma(yc[:], g.yspec[0, cc], w=[ycn])
                    self.dma(ys_[:], g.yspec[1, cc], w=[ysn])
                    ut, utn = utr.next(); xg, xgn = xgr.next()
                    self.dma(ut[:], g.fm[0, cc, :, t0:t0 + TT], w=[utn])
                    self.dma(xg[:], g.fm[1 + o, cc, :, t0:t0 + TT], w=[xgn])
                    pa, pan = par.next()
                    for fc in range(nL):
                        q8 = (fc // 8) * 8
                        self.mm(pa[:, 0:TT], yc[:, fc, :], cm[:, fc, :], start=(fc == 0), stop=False,
                                r=[ycn, f"cm{q8}"], w=[pan])
                        self.mm(pa[:, 0:TT], ys_[:, fc, :], sm[:, fc, :], start=False, stop=(fc == nL - 1),
                                r=[ysn, f"sm{q8}"], w=[pan])
                    un, unn = unr.next()
                    self.stt(un[:], ut[:], fb[:, o, cc:cc + 1], pa[:, 0:TT], ALU.mult, ALU.add, r=[utn, "fb", pan], w=[unn])
                    self.tt("pool", un[:], un[:], xg[:], ALU.mult, r=[unn, xgn], w=[unn])
                    if o == 0:
                        self.dma(g.fm[0, cc, :, t0:t0 + TT], un[:], r=[unn])
                        ub, ubn = ubr.next()
                        self.cp("act", ub[:], un[:], r=[unn], w=[ubn])
                        for q in range(nq):
                            self.tr(ptq[q][:, cc, :], ub[:, q * 128:(q + 1) * 128], self.ident_b[:],
                                    r=[ubn, "identb"], w=[f"ptq{q}"])
                    else:
                        zg, zgn = zgr.next()
                        self.dma(zg[:], g.fm[3, cc, :, t0:t0 + TT], w=[zgn])
                        ub, ubn = ubr.next()
                        self.tt("dve", ub[:], un[:], zg[:], ALU.mult, r=[unn, zgn], w=[ubn])
                        self.dma(g.yT[cc, :, t0:t0 + TT], ub[:], r=[ubn])
                if o == 0:
                    for q in range(nq):
                        uo, uon = uor.next()
                        self.cp("act" if q % 2 == 0 else "dve", uo[:], ptq[q][:], r=[f"ptq{q}"], w=[uon])
                        self.dma(g.utm[t0 + q * 128:t0 + (q + 1) * 128, :], uo[:].rearrange("p a b -> p (a b)"), r=[uon])


def _consts():
    k = np.arange(128)[:, None]
    i = np.arange(128)[None, :]
    masks = np.stack([(k <= i), (k >= i), (k > i), (k < i)]).astype(np.float32)
    out = {"masks": masks}
    for nm, L in (("p", LP), ("s", LS)):
        n = 2 * L
        t = np.arange(L, dtype=np.float64)
        f = np.arange(L, dtype=np.float64)
        th = 2.0 * np.pi * (f + 0.5) / n
        a2 = np.outer(t + 0.5, th)
        am = np.outer(t, th)
        M = np.stack([np.cos(a2), np.sin(a2), np.cos(am), np.sin(am)]).astype(np.float32).astype(ml_dtypes.bfloat16)
        nL = L // 128
        TT = min(512, L)
        out[f"dft_{nm}"] = np.ascontiguousarray(M.reshape(4, nL, 128, nL, 128).transpose(0, 3, 2, 1, 4))
        out[f"dfti_{nm}"] = np.ascontiguousarray(M[0:2].reshape(2, nL, 128, L // TT, TT).transpose(0, 3, 2, 1, 4))
        tt = (np.arange(L, dtype=np.float32) / np.float32(L)).astype(np.float32)
        w = (np.float32(2.0 * math.pi) * np.arange(L, dtype=np.float32) / np.float32(L)).astype(np.float32)
        fr = np.linspace(1e-4, 15, 16, dtype=np.float32)
        ang = w[:, None] * fr
        z = np.concatenate([tt[:, None], np.cos(ang), np.sin(ang)], axis=-1).astype(np.float32)
        out[f"hz_{nm}"] = np.ascontiguousarray(z.T)
        deltas = np.linspace(math.log(HY_T) / 1.5, math.log(HY_T) / 0.3, D, dtype=np.float32)
        out[f"win_{nm}"] = np.exp(-tt[:, None] * np.abs(deltas)).astype(np.float32)
    return out


_CACHE = {}


def nc_inputs(nc):
    return _CACHE["in_names"]


def kernel(**inputs):
    f = lambda a: np.ascontiguousarray(np.asarray(a, dtype=np.float32))
    if "nc" not in _CACHE:
        kb = KB()
        _CACHE["nc"] = kb.build()
        _CACHE["in_names"] = set(kb.I.keys())
        _CACHE["consts"] = _consts()
    nc = _CACHE["nc"]
    consts = _CACHE["consts"]
    shared = {
        "mod_w": f(inputs["mod_w"]), "mod_b": f(inputs["mod_b"]), "ln_g": f(inputs["ln_g"]), "ln_b": f(inputs["ln_b"]),
        "ssd_in_w": f(inputs["ssd_in_w"]), "ssd_conv_w": f(inputs["ssd_conv_w"]), "ssd_conv_b": f(inputs["ssd_conv_b"]),
        "ssd_dt_bias": f(inputs["ssd_dt_bias"]).reshape(2, 64), "ssd_a_log": f(inputs["ssd_a_log"]).reshape(2, 64),
        "ssd_d": f(inputs["ssd_d"]), "ssd_norm_g": f(inputs["ssd_norm_g"]), "ssd_out_w": f(inputs["ssd_out_w"]),
        "hy_in_w": f(inputs["hy_in_w"])[0], "hy_conv_w": f(inputs["hy_conv_w"])[0], "hy_conv_b": f(inputs["hy_conv_b"])[0],
        "hy_f_w1": f(inputs["hy_f_w1"])[0], "hy_f_b1": f(inputs["hy_f_b1"])[0], "hy_f_w2": f(inputs["hy_f_w2"])[0],
        "hy_f_b2": f(inputs["hy_f_b2"])[0], "hy_f_w3": f(inputs["hy_f_w3"])[0], "hy_f_freq": f(inputs["hy_f_freq"])[0],
        "hy_f_bias": f(inputs["hy_f_bias"])[0], "hy_out_w": f(inputs["hy_out_w"])[0],
        "lru_in_w": f(inputs["lru_in_w"])[0], "lru_conv_w": f(inputs["lru_conv_w"])[0], "lru_conv_b": f(inputs["lru_conv_b"])[0],
        "lru_gate_w": f(inputs["lru_gate_w"])[0], "lru_gate_b": f(inputs["lru_gate_b"])[0].reshape(4, D),
        "lru_a_param": f(inputs["lru_a_param"])[0], "lru_out_w": f(inputs["lru_out_w"])[0],
    }
    shared.update(consts)
    shared = {k: v for k, v in shared.items() if k in nc_inputs(nc)}
    xp = f(inputs["x_prompt"]); xs = f(inputs["x_sample"])
    sts = f(inputs["state_ssd"]); stl = f(inputs["state_lru"])
    c = f(inputs["c"]); cc = f(inputs["c_ctx"])
    in_maps = []
    for core in range(8):
        b = core // 2
        m = dict(shared)
        m["xp"] = np.ascontiguousarray(xp[core * NPS:(core + 1) * NPS].reshape(NPS * LP, D))
        m["xs"] = np.ascontiguousarray(xs[b])
        m["st_ssd"] = np.ascontiguousarray(sts[b].reshape(2, 2, 2048, 128))
        m["st_lru"] = np.ascontiguousarray(stl[b].reshape(2, D))
        m["cond"] = np.ascontiguousarray(np.stack([cc, c[b]]))
        in_maps.append(m)
    res = run_bass_kernel_spmd(nc, in_maps, core_ids=list(range(8)))
    r = res.results
    y_prompt = np.concatenate([r[i]["yp"].reshape(NPS, LP, D) for i in range(8)], axis=0)
    y_sample = np.stack([r[2 * b]["ys"] for b in range(4)], axis=0)
    nss = np.concatenate([r[i]["nss"].reshape(NPS, 2, 2, 32, 64, 128) for i in range(8)], axis=0)
    nsl = np.concatenate([r[i]["nsl"].reshape(NPS, 1, 2, D) for i in range(8)], axis=0)
    return (y_prompt.astype(np.float32), y_sample.astype(np.float32), nss.astype(np.float32), nsl.astype(np.float32))
```

```python
import contextlib
import math
import numpy as np
import ml_dtypes
import concourse.bass as bass
import concourse.mybir as mybir
from concourse.bass_utils import run_bass_kernel_spmd

F32 = mybir.dt.float32
F32R = mybir.dt.float32r
BF16 = mybir.dt.bfloat16
AF = mybir.ActivationFunctionType
ALU = mybir.AluOpType

EPOCH = 8000
NDMA = 24
import os as _os
MAXOPS = int(_os.environ.get("MAXOPS", "1000000000"))

D = 1024
NPS = 4
LP = 256
LS = 4096
DEPTH = 4
ALPHA = (2.0 * DEPTH) ** 0.25
LN_EPS = 1e-5
RMS_EPS = 1e-5
SSD_PROJ = 6208
HY_T = 1e-2


class Sched:
    ENG = ("pe", "act", "dve", "pool")

    def __init__(self, nc, stack):
        self.nc = nc
        self.stack = stack
        self.eng = {"pe": nc.tensor, "act": nc.scalar, "dve": nc.vector,
                    "pool": nc.gpsimd, "sp": nc.sync}
        self.ops = {e: [] for e in self.eng}
        self.cnt = {e: 0 for e in self.ENG}
        self.esems = {e: [] for e in self.ENG}
        self.dsems = [stack.enter_context(nc.semaphore(f"dma{i}")) for i in range(NDMA)]
        self.dval = [0] * NDMA
        self.dnext = 0
        self.waited = {e: {} for e in self.eng}
        self.lastw = {}
        self.readers = {}
        self.n_inst = 0

    def _esem(self, e, count):
        k = (count - 1) // EPOCH
        while len(self.esems[e]) <= k:
            self.esems[e].append(self.stack.enter_context(
                self.nc.semaphore(f"s_{e}_{len(self.esems[e])}")))
        return self.esems[e][k], (count - 1) % EPOCH + 1, k

    def _emit_wait(self, e, ev):
        if ev[0] == "e":
            _, src, count = ev
            if src == e and e == "pe":
                return
            sem, val, k = self._esem(src, count)
            key = ("e", src, k)
        else:
            _, idx, val = ev
            sem = self.dsems[idx]
            key = ("d", idx)
        if self.waited[e].get(key, 0) >= val:
            return
        self.waited[e][key] = val
        engobj = self.eng[e]
        self.ops[e].append(lambda engobj=engobj, sem=sem, val=val: engobj.wait_ge(sem, val))

    def _deps(self, e, reads, writes):
        evs = []
        for r in reads:
            if r in self.lastw:
                evs.append(self.lastw[r])
        for w in writes:
            if w in self.lastw:
                evs.append(self.lastw[w])
            evs.extend(self.readers.get(w, ()))
        for ev in evs:
            self._emit_wait(e, ev)

    def _commit(self, ev, reads, writes):
        for r in reads:
            self.readers.setdefault(r, []).append(ev)
        for w in writes:
            self.lastw[w] = ev
            self.readers[w] = []

    def op(self, e, fn, reads=(), writes=()):
        if self.n_inst >= MAXOPS:
            return
        self._deps(e, reads, writes)
        self.cnt[e] += 1
        count = self.cnt[e]
        sem, val, k = self._esem(e, count)
        self.ops[e].append(lambda fn=fn, sem=sem: fn().then_inc(sem, 1))
        self._commit(("e", e, count), reads, writes)
        self.n_inst += 1

    def dma(self, q, fn, reads=(), writes=()):
        if self.n_inst >= MAXOPS:
            return
        idx = self.dnext
        self.dnext = (self.dnext + 1) % NDMA
        if self.dval[idx] > 0:
            self._emit_wait(q, ("d", idx, self.dval[idx]))
        self._deps(q, reads, writes)
        self.dval[idx] += 16
        val = self.dval[idx]
        sem = self.dsems[idx]
        self.ops[q].append(lambda fn=fn, sem=sem: fn().then_inc(sem, 16))
        self._commit(("d", idx, val), reads, writes)
        self.n_inst += 1

    def barrier(self):
        for e in self.eng:
            for en in self.ENG:
                if self.cnt[en] and not (en == e):
                    self._emit_wait(e, ("e", en, self.cnt[en]))
                elif self.cnt[en] and e != "pe":
                    self._emit_wait(e, ("e", en, self.cnt[en]))
            for i in range(NDMA):
                if self.dval[i]:
                    self._emit_wait(e, ("d", i, self.dval[i]))
        self.lastw = {}
        self.readers = {}

    def finish(self):
        self.barrier()
        nc = self.nc
        with nc.Block() as block:
            @block.tensor
            def _(t):
                for f in self.ops["pe"]:
                    f()

            @block.scalar
            def _(t):
                for f in self.ops["act"]:
                    f()

            @block.vector
            def _(t):
                for f in self.ops["dve"]:
                    f()

            @block.gpsimd
            def _(t):
                for f in self.ops["pool"]:
                    f()

            @block.sync
            def _(t):
                for f in self.ops["sp"]:
                    f()


class Rot:
    def __init__(self, tiles, name):
        self.tiles = tiles
        self.name = name
        self.i = -1

    def next(self):
        self.i += 1
        k = self.i % len(self.tiles)
        return self.tiles[k], f"{self.name}{k}"


class Grp:
    pass


class KB:
    def __init__(self, layers=(0, 1, 2, 3), do_prompt=True, do_sample=True):
        self.layers = layers
        self.do_prompt = do_prompt
        self.do_sample = do_sample
        self.nc = bass.Bass("TRN2", target_bir_lowering=False)
        self.I = {}
        self.O = {}
        self.uid = 0
        self.nphase = 0
        self.max_phase = 10 ** 9

    def skip(self):
        self.nphase += 1
        return self.nphase > self.max_phase

    def inp(self, name, shape, dt=F32):
        self.I[name] = self.nc.dram_tensor(name, list(shape), dt, kind="ExternalInput").ap()
        return self.I[name]

    def outp(self, name, shape, dt=F32):
        self.O[name] = self.nc.dram_tensor(name, list(shape), dt, kind="ExternalOutput").ap()
        return self.O[name]

    def scr(self, name, shape, dt):
        return self.nc.dram_tensor(name, list(shape), dt, kind="Internal").ap()

    def nm(self, p):
        self.uid += 1
        return f"{p}_{self.uid}"

    def sb(self, shape, dt, name="t"):
        return self.ph.enter_context(self.nc.sbuf_tensor(self.nm(name), list(shape), dt))

    def ps(self, shape, dt, name="p"):
        return self.ph.enter_context(self.nc.psum_tensor(self.nm(name), list(shape), dt))

    @contextlib.contextmanager
    def phase(self):
        with contextlib.ExitStack() as ph:
            old = getattr(self, "ph", None)
            self.ph = ph
            yield
            self.S.barrier()
            self.ph = old

    def dma(self, out, in_, r=(), w=(), q="sp", **kw):
        eng = self.nc.sync if q == "sp" else self.nc.gpsimd
        self.S.dma(q, lambda: eng.dma_start(out=out, in_=in_, **kw), reads=r, writes=w)

    def mm(self, out, lhsT, rhs, start=True, stop=True, r=(), w=()):
        self.S.op("pe", lambda: self.nc.tensor.matmul(out, lhsT=lhsT, rhs=rhs, start=start, stop=stop),
                  reads=r, writes=w)

    def tr(self, out, in_, ident, r=(), w=()):
        self.S.op("pe", lambda: self.nc.tensor.transpose(out=out, in_=in_, identity=ident), reads=r, writes=w)

    def act(self, out, in_, func, r=(), w=(), **kw):
        self.S.op("act", lambda: self.nc.scalar.activation(out=out, in_=in_, func=func, **kw), reads=r, writes=w)

    def E(self, e):
        return self.nc.vector if e == "dve" else self.nc.gpsimd

    def tt(self, e, out, in0, in1, op, r=(), w=()):
        self.S.op(e, lambda: self.E(e).tensor_tensor(out=out, in0=in0, in1=in1, op=op), reads=r, writes=w)

    def ts(self, e, out, in0, s1, s2, op0, op1=None, r=(), w=()):
        if op1 is None:
            self.S.op(e, lambda: self.E(e).tensor_scalar(out=out, in0=in0, scalar1=s1, scalar2=None, op0=op0),
                      reads=r, writes=w)
        else:
            self.S.op(e, lambda: self.E(e).tensor_scalar(out=out, in0=in0, scalar1=s1, scalar2=s2, op0=op0, op1=op1),
                      reads=r, writes=w)

    def stt(self, out, in0, scalar, in1, op0, op1, r=(), w=()):
        self.S.op("dve", lambda: self.nc.vector.scalar_tensor_tensor(out=out, in0=in0, scalar=scalar, in1=in1,
                                                                     op0=op0, op1=op1), reads=r, writes=w)

    def cp(self, e, out, in_, r=(), w=()):
        if e == "act":
            self.S.op("act", lambda: self.nc.scalar.copy(out=out, in_=in_), reads=r, writes=w)
        else:
            self.S.op(e, lambda: self.E(e).tensor_copy(out=out, in_=in_), reads=r, writes=w)

    def memset(self, e, ap, val, w=()):
        self.S.op(e, lambda: self.E(e).memset(ap, val), writes=w)

    def declare(self):
        inp = self.inp
        inp("xp", [NPS * LP, D]); inp("xs", [LS, D])
        inp("st_ssd", [2, 2, 2048, 128]); inp("st_lru", [2, D]); inp("cond", [2, D])
        inp("mod_w", [4, D, 3 * D]); inp("mod_b", [4, 3 * D]); inp("ln_g", [4, D]); inp("ln_b", [4, D])
        inp("ssd_in_w", [2, D, SSD_PROJ]); inp("ssd_conv_w", [2, 4, 4096]); inp("ssd_conv_b", [2, 4096])
        inp("ssd_dt_bias", [2, 64]); inp("ssd_a_log", [2, 64]); inp("ssd_d", [2, 32])
        inp("ssd_norm_g", [2, 2048]); inp("ssd_out_w", [2, 2048, D])
        inp("hy_in_w", [D, 4096]); inp("hy_conv_w", [3, 3072]); inp("hy_conv_b", [3072])
        inp("hy_f_w1", [33, 64]); inp("hy_f_b1", [64]); inp("hy_f_w2", [64, 64]); inp("hy_f_b2", [64])
        inp("hy_f_w3", [64, 4096]); inp("hy_f_freq", [2, 64]); inp("hy_f_bias", [2, D]); inp("hy_out_w", [D, D])
        inp("lru_in_w", [D, 2048]); inp("lru_conv_w", [4, D]); inp("lru_conv_b", [D])
        inp("lru_gate_w", [2, 2, 4, 256, 256]); inp("lru_gate_b", [4, D]); inp("lru_a_param", [2, D])
        inp("lru_out_w", [D, D])
        inp("masks", [4, 128, 128])
        if 1 in self.layers:
            inp("dft_p", [4, LP // 128, 128, LP // 128, 128], BF16)
            inp("dft_s", [4, LS // 128, 128, LS // 128, 128], BF16)
            inp("dfti_p", [2, 1, 128, LP // 128, 256], BF16)
            inp("dfti_s", [2, LS // 512, 128, LS // 128, 512], BF16)
            inp("hz_p", [33, LP]); inp("hz_s", [33, LS])
            inp("win_p", [LP, D]); inp("win_s", [LS, D])
        self.outp("yp", [NPS * LP, D]); self.outp("ys", [LS, D])
        self.outp("nss", [NPS, 2, 2, 2048, 128]); self.outp("nsl", [NPS, 2, D])

    def build(self):
        nc = self.nc
        self.declare()
        with contextlib.ExitStack() as st:
            self.S = Sched(nc, st)
            self.ph = st
            self.ident_f = self.sb([128, 128], F32, "identf")
            self.ident_b = self.sb([128, 128], BF16, "identb")
            self.masks = self.sb([128, 4, 128], F32, "masks")
            self.ones_f = self.sb([128, 128], F32, "ones")
            self.zero_b = self.sb([128, 8, 4], BF16, "zerob")
            self.dma(self.masks[:], self.I["masks"].rearrange("m p i -> p m i"), w=["masks"])
            self.memset("pool", self.ident_f[:], 1.0, w=["identf"])
            self.S.op("pool", lambda: nc.gpsimd.affine_select(
                out=self.ident_f[:], in_=self.ident_f[:], pattern=[[-1, 128]], compare_op=ALU.is_equal,
                fill=0.0, base=0, channel_multiplier=1), reads=["identf"], writes=["identf"])
            self.cp("dve", self.ident_b[:], self.ident_f[:], r=["identf"], w=["identb"])
            self.memset("dve", self.ones_f[:], 1.0, w=["ones"])
            self.masks_r = self.sb([128, 4, 128], F32R, "masksr")
            self.ones_r = self.sb([128, 128], F32R, "onesr")
            self.cp("dve", self.masks_r[:], self.masks[:], r=["masks"], w=["masksr"])
            self.cp("dve", self.ones_r[:], self.ones_f[:], r=["ones"], w=["onesr"])
            self.memset("dve", self.zero_b[:], 0.0, w=["zerob"])
            self.lru_h0 = self.sb([128, 2, 8], F32, "lruh0")
            self.S.barrier()

            self.mod_scr = self.scr("mod_scr", [4, 2, 3 * D], F32)
            groups = []
            if self.do_prompt:
                g = Grp(); g.name = "p"; g.nseq = NPS; g.L = LP; g.cond = 0; g.latent = False
                g.x_in = self.I["xp"]; g.x_out = self.O["yp"]
                groups.append(g)
            if self.do_sample:
                g = Grp(); g.name = "s"; g.nseq = 1; g.L = LS; g.cond = 1; g.latent = True
                g.x_in = self.I["xs"]; g.x_out = self.O["ys"]
                groups.append(g)
            for g in groups:
                g.T = g.nseq * g.L
                g.W = g.nseq * (g.L + 3)
                g.xa = self.scr(f"xa_{g.name}", [g.T, D], F32)
                g.xb = self.scr(f"xb_{g.name}", [g.T, D], F32)
                g.hT = self.scr(f"hT_{g.name}", [8, 128, g.W], BF16)
                g.yT = self.scr(f"yT_{g.name}", [16, 128, g.T], BF16)
                g.xs_tm = self.scr(f"xstm_{g.name}", [g.T, 2048], BF16)
                g.b_tm = self.scr(f"btm_{g.name}", [g.T, 1024], BF16)
                g.bcT = self.scr(f"bcT_{g.name}", [16, 128, g.T], BF16)
                g.dta = self.scr(f"dta_{g.name}", [g.T, 128], F32)
                g.sloc = self.scr(f"sloc_{g.name}", [g.T // 128, 2, 128, 2048], F32)
                g.cdec = self.scr(f"cdec_{g.name}", [g.T // 128, 128, 64], F32)
                g.hprev = self.scr(f"hprev_{g.name}", [g.T // 128, 2, 128, 2048], BF16)
                g.fm = self.scr(f"fm_{g.name}", [4, 8, 128, g.T], F32)
                g.utm = self.scr(f"utm_{g.name}", [g.T, D], BF16)
                g.kw = self.scr(f"kw_{g.name}", [g.L, 4096], F32)
                g.khat = self.scr(f"khat_{g.name}", [2, 2, g.L, D], F32)
                g.yspec = self.scr(f"ysp_{g.name}", [2, 8, 128, g.L // 128, 128], BF16)
                g.khat2 = self.scr(f"khat2_{g.name}", [2, 2, g.L, D], F32)
            self.groups = groups

            self.modulation()
            for li in self.layers:
                last = (li == self.layers[-1])
                for g in groups:
                    X_in = g.x_in if li == self.layers[0] else (g.xa if (li % 2 == 1) else g.xb)
                    X_out = g.x_out if last else (g.xa if (li % 2 == 0) else g.xb)
                    col = g.latent and li == 3
                    self.pass_A(g, li, X_in, col)
                    kind = li % 3
                    if kind == 0:
                        self.ssd_layer(g, li // 3, li)
                        KC = 16
                    elif kind == 1:
                        self.hyena_layer(g)
                        KC = 8
                    else:
                        self.lru_layer(g)
                        KC = 8
                    wname = {0: "ssd_out_w", 1: "hy_out_w", 2: "lru_out_w"}[kind]
                    wout = self.I[wname][li // 3] if kind == 0 else self.I[wname]
                    self.pass_E(g, li, X_in, X_out, col, wout, KC)
            self.S.finish()
        return nc

    def xrows(self, g, X, s, c, col):
        if not col:
            r0 = s * g.L + c * 128
            return [(X[r0:r0 + 128, :], 0, 128)]
        Xv = X.rearrange("(r w) f -> w r f", w=64)
        return [(Xv[2 * c + wo], wo * 64, 64) for wo in range(2)]

    def load_T(self, dst, src2d, R, name):
        stg = self.sb([128, 128], F32, "ldT")
        if getattr(self, "_ldT_ph", None) is not self.ph:
            self._ldT_ph = self.ph
            self._ldT_ps = self.ps([128, 128], F32, "ldTp")
        pt = self._ldT_ps
        k = self.nm("ldT")
        self.dma(stg[0:R, :], src2d, w=[k])
        self.tr(pt[:, 0:R], stg[0:R, :], self.ident_f[0:R, 0:R], r=[k, "identf"], w=["ldTp"])
        self.cp("dve", dst, pt[:, 0:R], r=["ldTp"], w=[name])

    def modulation(self):
        if self.skip():
            return
        nc = self.nc
        with self.phase():
            cond = self.sb([2, D], F32, "cond")
            cs = self.sb([2, D], F32, "cs")
            condT = self.sb([128, 8, 2], BF16, "condT")
            pT = self.ps([128, 8, 2], F32, "pT")
            self.dma(cond[:], self.I["cond"], w=["cond"])
            self.act(cs[:], cond[:], AF.Silu, r=["cond"], w=["cs"])
            for k in range(8):
                self.tr(pT[:, k, :], cs[0:2, k * 128:(k + 1) * 128], self.ident_f[0:2, 0:2],
                        r=["cs", "identf"], w=["pT"])
            self.cp("dve", condT[:], pT[:], r=["pT"], w=["condT"])
            mw = self.sb([128, 8, 3 * D], BF16, "mw")
            mb = self.sb([2, 3 * D], F32, "mb")
            msb = self.sb([2, 3 * D], F32, "msb")
            pm = [self.ps([2, 512], F32, "pm") for _ in range(2)]
            for li in self.layers:
                for k in range(8):
                    self.dma(mw[:, k, :], self.I["mod_w"][li, k * 128:(k + 1) * 128, :], w=[f"mw{k}"], q="pool")
                for c in range(2):
                    self.dma(mb[c:c + 1, :], self.I["mod_b"][li:li + 1, :], w=["mb"])
                for t in range(6):
                    p = pm[t % 2]
                    for k in range(8):
                        self.mm(p[:], condT[:, k, :], mw[:, k, t * 512:(t + 1) * 512], start=(k == 0), stop=(k == 7),
                                r=["condT", f"mw{k}"], w=[f"pm{t % 2}"])
                    self.tt("dve", msb[:, t * 512:(t + 1) * 512], p[:], mb[:, t * 512:(t + 1) * 512], ALU.add,
                            r=[f"pm{t % 2}", "mb"], w=["msb"])
                self.ts("dve", msb[:, D:2 * D], msb[:, D:2 * D], 1.0, None, ALU.add, r=["msb"], w=["msb"])
                self.dma(self.mod_scr[li], msb[:], r=["msb"])

    def pass_A(self, g, li, X, col):
        if self.skip():
            return
        with self.phase():
            sc = self.sb([128, D], F32, "sc")
            sh = self.sb([128, D], F32, "sh")
            self.dma(sh[:], self.mod_scr[li, g.cond, 0:D].partition_broadcast(128), w=["sh"])
            self.dma(sc[:], self.mod_scr[li, g.cond, D:2 * D].partition_broadcast(128), w=["sc"])
            xr = Rot([self.sb([128, D], F32, "xA") for _ in range(2)], "xA")
            hr = Rot([self.sb([128, D], BF16, "hA") for _ in range(2)], "hA")
            tr_ = Rot([self.sb([128, 8, 128], BF16, "hTA") for _ in range(2)], "hTA")
            pr = Rot([self.ps([128, 8, 128], BF16, "pA") for _ in range(2)], "pA")
            nchunk = g.L // 128
            for s in range(g.nseq):
                base = s * (g.L + 3)
                self.dma(g.hT[:, :, base:base + 1].rearrange("k p t -> p k t"), self.zero_b[:, :, 0:1], r=["zerob"],
                         allow_slow_non_contiguous=True)
                self.dma(g.hT[:, :, base + g.L + 1:base + g.L + 3].rearrange("k p t -> p k t"),
                         self.zero_b[:, :, 0:2], r=["zerob"], allow_slow_non_contiguous=True)
                for c in range(nchunk):
                    xt, xn = xr.next()
                    for (src, p0, n) in self.xrows(g, X, s, c, col):
                        self.dma(xt[p0:p0 + n, :], src, w=[xn])
                    ht, hn = hr.next()
                    self.tt("dve", xt[:], xt[:], sc[:], ALU.mult, r=[xn, "sc"], w=[xn])
                    self.tt("dve", ht[:], xt[:], sh[:], ALU.add, r=[xn, "sh"], w=[hn])
                    pt, pn = pr.next()
                    for k in range(8):
                        self.tr(pt[:, k, :], ht[:, k * 128:(k + 1) * 128], self.ident_b[:], r=[hn, "identb"], w=[pn])
                    tt_, tn = tr_.next()
                    self.cp("act", tt_[:], pt[:], r=[pn], w=[tn])
                    c0 = base + 1 + c * 128
                    self.dma(g.hT[:, :, c0:c0 + 128].rearrange("k p t -> p k t"), tt_[:], r=[tn])

    def pass_E(self, g, li, X, Xo, col, wout, KC):
        if self.skip():
            return
        nc = self.nc
        with self.phase():
            wo = self.sb([128, KC, D], BF16, "wo")
            for k in range(KC):
                self.dma(wo[:, k, :], wout[k * 128:(k + 1) * 128, :], w=[f"wo{k}"], q="pool")
            gt = self.sb([128, D], F32, "gate")
            lg = self.sb([128, D], F32, "lng")
            lb = self.sb([128, D], F32, "lnb")
            self.dma(gt[:], self.mod_scr[li, g.cond, 2 * D:3 * D].partition_broadcast(128), w=["gate"])
            self.dma(lg[:], self.I["ln_g"][li].partition_broadcast(128), w=["lng"])
            self.dma(lb[:], self.I["ln_b"][li].partition_broadcast(128), w=["lnb"])
            yr = Rot([self.sb([128, KC, 128], BF16, "yE") for _ in range(2)], "yE")
            xr = Rot([self.sb([128, D], F32, "xE") for _ in range(2)], "xE")
            rr = Rot([self.sb([128, D], F32, "rE") for _ in range(2)], "rE")
            sr = Rot([self.sb([128, 16], F32, "sE") for _ in range(2)], "sE")
            pr = Rot([self.ps([128, D], F32, "pE") for _ in range(2)], "pE")
            nchunk = g.L // 128
            for s in range(g.nseq):
                for c in range(nchunk):
                    t0 = s * g.L + c * 128
                    yt, yn = yr.next()
                    self.dma(yt[:], g.yT[0:KC, :, t0:t0 + 128].rearrange("k p t -> p k t"), w=[yn])
                    xt, xn = xr.next()
                    for (src, p0, n) in self.xrows(g, X, s, c, col):
                        self.dma(xt[p0:p0 + n, :], src, w=[xn])
                    pt, pn = pr.next()
                    for hlf in range(2):
                        for k in range(KC):
                            self.mm(pt[:, hlf * 512:(hlf + 1) * 512], yt[:, k, :], wo[:, k, hlf * 512:(hlf + 1) * 512],
                                    start=(k == 0), stop=(k == KC - 1), r=[yn, f"wo{k}"], w=[pn + str(hlf)])
                    rt, rn = rr.next()
                    stt_, sn = sr.next()
                    for hlf in range(2):
                        sl = slice(hlf * 512, (hlf + 1) * 512)
                        self.tt("dve", rt[:, sl], pt[:, sl], gt[:, sl], ALU.mult, r=[pn + str(hlf), "gate"], w=[rn])
                    self.stt(rt[:], xt[:], ALPHA, rt[:], ALU.mult, ALU.add, r=[xn, rn], w=[rn])
                    self.S.op("dve", lambda stt_=stt_, rt=rt: nc.vector.bn_stats(out=stt_[:, 0:6], in_=rt[:, 0:512]),
                              reads=[rn], writes=[sn])
                    self.S.op("dve", lambda stt_=stt_, rt=rt: nc.vector.bn_stats(out=stt_[:, 6:12], in_=rt[:, 512:1024]),
                              reads=[rn], writes=[sn])
                    self.S.op("dve", lambda stt_=stt_: nc.vector.bn_aggr(out=stt_[:, 12:14], in_=stt_[:, 0:12]),
                              reads=[sn], writes=[sn])
                    self.ts("dve", stt_[:, 14:15], stt_[:, 13:14], LN_EPS, None, ALU.add, r=[sn], w=[sn])
                    self.act(stt_[:, 14:15], stt_[:, 14:15], AF.Sqrt, r=[sn], w=[sn])
                    self.S.op("dve", lambda stt_=stt_: nc.vector.reciprocal(out=stt_[:, 15:16], in_=stt_[:, 14:15]),
                              reads=[sn], writes=[sn])
                    self.ts("dve", rt[:], rt[:], stt_[:, 12:13], stt_[:, 15:16], ALU.subtract, ALU.mult,
                            r=[rn, sn], w=[rn])
                    self.tt("pool", rt[:], rt[:], lg[:], ALU.mult, r=[rn, "lng"], w=[rn])
                    self.tt("pool", rt[:], rt[:], lb[:], ALU.add, r=[rn, "lnb"], w=[rn])
                    for (dst, p0, n) in self.xrows(g, Xo, s, c, col):
                        self.dma(dst, rt[p0:p0 + n, :], r=[rn])

    def conv_fm(self, pp, pn, acc, an, dst, dn, cw, cb, fc, K, silu, woff):
        self.act(acc[:], pp[:, woff:woff + 256], AF.Identity, r=[pn, "cw", "cb"], w=[an],
                 scale=cw[:, 0, fc:fc + 1], bias=cb[:, fc:fc + 1])
        for k in range(1, K):
            self.stt(acc[:], pp[:, woff + k:woff + k + 256], cw[:, k, fc:fc + 1], acc[:], ALU.mult, ALU.add,
                     r=[pn, an, "cw"], w=[an])
        if silu:
            self.act(dst, acc[:], AF.Silu, r=[an], w=[dn])
        else:
            self.cp("act", dst, acc[:], r=[an], w=[dn])

    def ssd_layer(self, g, slot, li):
        self.ssd_p1(g, slot)
        if _os.environ.get("DEBUG") == "dta":
            with self.phase():
                t = self.sb([128, 8, 128], F32, "dbg")
                self.dma(t[:], g.dta[0:1024, :].rearrange("(c p) f -> p c f", p=128), w=["dbg"])
                self.dma(self.O["yp"][:, 0:128].rearrange("(c p) f -> p c f", p=128), t[:], r=["dbg"])
        self.ssd_p2a(g, slot)
        self.ssd_pR(g, slot)
        self.ssd_p2b(g, slot)

    def ssd_p1(self, g, slot):
        if self.skip():
            return
        nc = self.nc
        with self.phase():
            w_in = self.I["ssd_in_w"][slot]
            wx = self.sb([128, 8, 4096], BF16, "wx")
            wd = self.sb([128, 8, 64], BF16, "wd")
            for k in range(8):
                self.dma(wx[:, k, :], w_in[k * 128:(k + 1) * 128, 2048:6144], w=[f"wx{k}"], q="pool")
                self.dma(wd[:, k, :], w_in[k * 128:(k + 1) * 128, 6144:6208], w=["wd"], q="pool")
            cw = self.sb([128, 4, 32], F32, "cw")
            cb = self.sb([128, 32], F32, "cb")
            self.load_T(cw[:].rearrange("p k c -> p (k c)"),
                        self.I["ssd_conv_w"][slot].rearrange("k (c p) -> (k c) p", p=128), 128, "cw")
            self.load_T(cb[:], self.I["ssd_conv_b"][slot].rearrange("(c p) -> c p", p=128), 32, "cb")
            dtb = self.sb([128, 64], F32, "dtb")
            abc = self.sb([128, 64], F32, "abc")
            self.dma(dtb[:], self.I["ssd_dt_bias"][slot].partition_broadcast(128), w=["dtb"])
            self.dma(abc[:], self.I["ssd_a_log"][slot].partition_broadcast(128), w=["abc"])
            self.act(abc[:], abc[:], AF.Exp, r=["abc"], w=["abc"])
            self.ts("dve", abc[:], abc[:], -1.0, None, ALU.mult, r=["abc"], w=["abc"])

            hwr = Rot([self.sb([128, 8, 260], BF16, "hw") for _ in range(2)], "hw")
            xbr = Rot([self.sb([128, 32, 256], BF16, "xbc") for _ in range(2)], "xbc")
            accr = Rot([self.sb([128, 256], F32, "acc") for _ in range(3)], "acc")
            tmr = Rot([self.sb([128, 1024], BF16, "tm") for _ in range(3)], "tm")
            dtr = Rot([self.sb([128, 128], F32, "dta") for _ in range(2)], "dta")
            ppr = Rot([self.ps([128, 512], F32, "pp") for _ in range(3)], "pp")
            ptr = Rot([self.ps([128, 8, 128], BF16, "ptr") for _ in range(2)], "ptr")
            pdr = Rot([self.ps([128, 64], F32, "pd") for _ in range(1)], "pd")
            for s in range(g.nseq):
                for j in range(g.L // 256):
                    c0 = s * (g.L + 3) + 256 * j
                    t0 = s * g.L + 256 * j
                    hw, hn = hwr.next()
                    self.dma(hw[:, :, 0:259], g.hT[:, :, c0:c0 + 259].rearrange("k p t -> p k t"), w=[hn])
                    xb, xn = xbr.next()
                    for fc in range(32):
                        pp, pn = ppr.next()
                        for k in range(8):
                            self.mm(pp[:, 0:259], wx[:, k, fc * 128:(fc + 1) * 128], hw[:, k, 0:259],
                                    start=(k == 0), stop=(k == 7), r=[hn, f"wx{k}"], w=[pn])
                        acc, an = accr.next()
                        self.conv_fm(pp, pn, acc, an, xb[:, fc, :], f"{xn}_{fc}", cw, cb, fc, 4, True, 0)
                    self.dma(g.bcT[:, :, t0:t0 + 256].rearrange("c p t -> p c t"), xb[:, 16:32, :],
                             r=[f"{xn}_{fc}" for fc in range(16, 32)])
                    for tcn in range(2):
                        for blk in range(3):
                            pt, ptn = ptr.next()
                            for i in range(8):
                                fc = blk * 8 + i
                                self.tr(pt[:, i, :], xb[:, fc, tcn * 128:(tcn + 1) * 128], self.ident_b[:],
                                        r=[f"{xn}_{fc}", "identb"], w=[ptn])
                            tm, tn = tmr.next()
                            self.cp("act" if blk % 2 == 0 else "dve", tm[:], pt[:].rearrange("p a b -> p (a b)"),
                                    r=[ptn], w=[tn])
                            r0 = t0 + tcn * 128
                            if blk < 2:
                                self.dma(g.xs_tm[r0:r0 + 128, blk * 1024:(blk + 1) * 1024], tm[:], r=[tn])
                            else:
                                self.dma(g.b_tm[r0:r0 + 128, :], tm[:], r=[tn])
                        pd, pdn = pdr.next()
                        for k in range(8):
                            self.mm(pd[:], hw[:, k, 1 + tcn * 128:1 + (tcn + 1) * 128], wd[:, k, :],
                                    start=(k == 0), stop=(k == 7), r=[hn, "wd"], w=[pdn])
                        dt, dn = dtr.next()
                        self.tt("dve", dt[:, 0:64], pd[:], dtb[:], ALU.add, r=[pdn, "dtb"], w=[dn])
                        self.act(dt[:, 0:64], dt[:, 0:64], AF.Exp, r=[dn], w=[dn])
                        self.act(dt[:, 0:64], dt[:, 0:64], AF.Ln, r=[dn], w=[dn], bias=1.0)
                        self.tt("dve", dt[:, 64:128], dt[:, 0:64], abc[:], ALU.mult, r=[dn, "abc"], w=[dn])
                        self.dma(g.dta[r0:r0 + 128, :], dt[:], r=[dn])

    def ssd_p2a(self, g, slot):
        if self.skip():
            return
        nc = self.nc
        with self.phase():
            xsr = Rot([self.sb([128, 2048], BF16, "xs") for _ in range(2)], "xs")
            btr = Rot([self.sb([128, 1024], BF16, "bt") for _ in range(2)], "bt")
            dtr = Rot([self.sb([128, 128], F32, "dta") for _ in range(2)], "dta")
            der = Rot([self.sb([128, 64], F32, "de") for _ in range(2)], "de")
            cdr = Rot([self.sb([128, 64], F32, "cd") for _ in range(2)], "cd")
            wdr = Rot([self.sb([128, 2048], BF16, "wdd") for _ in range(2)], "wdd")
            ssr = Rot([self.sb([128, 1024], F32, "ss") for _ in range(3)], "ss")
            pcr = Rot([self.ps([128, 128], F32, "pc") for _ in range(2)], "pc")
            psr = Rot([self.ps([128, 1024], F32, "psS") for _ in range(2)], "psS")
            arr = Rot([self.sb([128, 64], F32R, "ar") for _ in range(2)], "ar")
            if _os.environ.get("DEBUG") == "alloc":
                for t in pcr.tiles + psr.tiles + der.tiles:
                    print("ALLOC", t.name, self.nc.lookup_mloc(t))
            for ch in range(g.T // 128):
                r0 = ch * 128
                xs, xn = xsr.next(); bt, bn = btr.next(); dt, dn = dtr.next()
                self.dma(xs[:], g.xs_tm[r0:r0 + 128, :], w=[xn])
                self.dma(bt[:], g.b_tm[r0:r0 + 128, :], w=[bn])
                self.dma(dt[:], g.dta[r0:r0 + 128, :], w=[dn])
                pc, pcn = pcr.next()
                ar, arn = arr.next()
                self.cp("dve", ar[:], dt[:, 64:128], r=[dn], w=[arn])
                self.mm(pc[:, 0:32], self.masks_r[:, 2, :], ar[:, 0:32], r=["masksr", arn], w=[pcn])
                self.mm(pc[:, 32:64], self.masks_r[:, 3, :], ar[:, 32:64], r=["masksr", arn], w=[pcn])
                self.mm(pc[:, 64:128], self.ones_r[:], ar[:, 0:64], r=["onesr", arn], w=[pcn])
                de, den = der.next(); cd, cdn = cdr.next()
                self.act(de[:], pc[:, 0:64], AF.Exp, r=[pcn], w=[den])
                self.act(cd[:], pc[:, 64:128], AF.Exp, r=[pcn], w=[cdn])
                self.dma(g.cdec[ch], cd[:], r=[cdn])
                self.tt("dve", de[:], de[:], dt[:, 0:64], ALU.mult, r=[den, dn], w=[den])
                for d in range(2):
                    wdd, wn = wdr.next()
                    self.tt("dve", wdd[:].rearrange("p (h e) -> p h e", h=32), xs[:].rearrange("p (h e) -> p h e", h=32),
                            de[:, d * 32:(d + 1) * 32].unsqueeze(2).to_broadcast([128, 32, 64]), ALU.mult,
                            r=[xn, den], w=[wn])
                    for hlf in range(2):
                        pS, psn = psr.next()
                        for gg in range(4):
                            G8 = hlf * 4 + gg
                            self.mm(pS[:, gg * 256:(gg + 1) * 256], bt[:, G8 * 128:(G8 + 1) * 128],
                                    wdd[:, G8 * 256:(G8 + 1) * 256], r=[bn, wn], w=[psn + str(gg // 2)])
                        ss, sn = ssr.next()
                        for q in range(2):
                            self.cp("act" if q == 0 else "dve", ss[:, q * 512:(q + 1) * 512], pS[:, q * 512:(q + 1) * 512],
                                    r=[psn + str(q)], w=[sn])
                        self.dma(g.sloc[ch, d, :, hlf * 1024:(hlf + 1) * 1024], ss[:], r=[sn])

    def ssd_pR(self, g, slot):
        if self.skip():
            return
        nc = self.nc
        nchunk = g.L // 128
        with self.phase():
            hst = [self.sb([128, 2048], F32, "hst") for _ in range(2)]
            hbr = [Rot([self.sb([128, 2048], BF16, "hb") for _ in range(2)], f"hb{d}") for d in range(2)]
            slr = [Rot([self.sb([128, 2048], F32, "sl") for _ in range(2)], f"sl{d}") for d in range(2)]
            cdr = [Rot([self.sb([128, 64], F32, "cd") for _ in range(2)], f"cd{d}") for d in range(2)]
            stg = Rot([self.sb([128, 128], F32, "stg") for _ in range(3)], "stg")
            ptr = Rot([self.ps([128, 128], F32, "ptR") for _ in range(2)], "ptR")
            eng = ["dve", "pool"]
            for s in range(g.nseq):
                for d in range(2):
                    hn = f"hst{d}"
                    if g.latent:
                        for t in range(16):
                            sg, sgn = stg.next()
                            self.dma(sg[:], self.I["st_ssd"][slot, d, t * 128:(t + 1) * 128, :], w=[sgn])
                            pt, ptn = ptr.next()
                            self.tr(pt[:], sg[:], self.ident_f[:], r=[sgn, "identf"], w=[ptn])
                            self.cp("act", hst[d][:, t * 128:(t + 1) * 128], pt[:], r=[ptn], w=[hn])
                    else:
                        self.memset(eng[d], hst[d][:], 0.0, w=[hn])
                order = [list(range(nchunk)), list(range(nchunk - 1, -1, -1))]
                for i in range(nchunk):
                    for d in range(2):
                        c = order[d][i]
                        ch = s * nchunk + c
                        hn = f"hst{d}"
                        hb, hbn = hbr[d].next()
                        self.cp("act", hb[:], hst[d][:], r=[hn], w=[hbn])
                        self.dma(g.hprev[ch, d], hb[:], r=[hbn])
                        sl, sln = slr[d].next(); cd, cdn = cdr[d].next()
                        self.dma(sl[:], g.sloc[ch, d], w=[sln])
                        self.dma(cd[:], g.cdec[ch], w=[cdn])
                        self.tt(eng[d], hst[d][:].rearrange("p (h e) -> p h e", h=32),
                                hst[d][:].rearrange("p (h e) -> p h e", h=32),
                                cd[:, d * 32:(d + 1) * 32].unsqueeze(2).to_broadcast([128, 32, 64]), ALU.mult,
                                r=[hn, cdn], w=[hn])
                        self.tt(eng[d], hst[d][:], hst[d][:], sl[:], ALU.add, r=[hn, sln], w=[hn])
                if not g.latent:
                    for d in range(2):
                        hn = f"hst{d}"
                        for t in range(16):
                            pt, ptn = ptr.next()
                            self.tr(pt[:], hst[d][:, t * 128:(t + 1) * 128], self.ident_f[:], r=[hn, "identf"], w=[ptn])
                            sg, sgn = stg.next()
                            self.cp("act", sg[:], pt[:], r=[ptn], w=[sgn])
                            self.dma(self.O["nss"][s, slot, d, t * 128:(t + 1) * 128, :], sg[:], r=[sgn])

    def ssd_p2b(self, g, slot):
        if self.skip():
            return
        nc = self.nc
        with self.phase():
            w_in = self.I["ssd_in_w"][slot]
            wz = self.sb([128, 8, 2048], BF16, "wz")
            for k in range(8):
                self.dma(wz[:, k, :], w_in[k * 128:(k + 1) * 128, 0:2048], w=[f"wz{k}"], q="pool")
            ng = self.sb([128, 2048], F32, "ng")
            self.dma(ng[:], self.I["ssd_norm_g"][slot].partition_broadcast(128), w=["ng"])
            dsk = self.sb([128, 32], F32, "dsk")
            self.dma(dsk[:], self.I["ssd_d"][slot].partition_broadcast(128), w=["dsk"])
            mgt_r = self.sb([128, 128], F32R, "mgtr")
            mlt_r = self.sb([128, 128], F32R, "mltr")
            self.cp("dve", mgt_r[:], self.masks[:, 2, :], r=["masks"], w=["mgtr"])
            self.cp("dve", mlt_r[:], self.masks[:, 3, :], r=["masks"], w=["mltr"])
            Xl = [mgt_r, mlt_r]
            Xn = ["mgtr", "mltr"]
            Ym = [0, 1]
            Sm = [0, 1]

            hTr = Rot([self.sb([128, 8, 128], BF16, "hT") for _ in range(2)], "hT")
            xsr = Rot([self.sb([128, 2048], BF16, "xs") for _ in range(2)], "xs")
            bcr = Rot([self.sb([128, 16, 128], BF16, "bc") for _ in range(2)], "bc")
            dtr = Rot([self.sb([128, 128], F32, "dta") for _ in range(2)], "dta")
            hpr = Rot([self.sb([128, 2, 2048], BF16, "hp") for _ in range(2)], "hp")
            zsr = Rot([self.sb([128, 2048], BF16, "zs") for _ in range(2)], "zs")
            xdr = Rot([self.sb([128, 2, 2048], BF16, "xd") for _ in range(2)], "xd")
            ear = Rot([self.sb([128, 64], F32, "ea") for _ in range(2)], "ea")
            scr_ = Rot([self.sb([128, 2, 128], BF16, "scm") for _ in range(2)], "scm")
            Yr = Rot([self.sb([128, 4, 128], F32R, "Y") for _ in range(3)], "Y")
            Er = Rot([self.sb([128, 4, 128], BF16, "Ee") for _ in range(3)], "Ee")
            Mr = Rot([self.sb([128, 2, 4, 128], BF16, "Mm") for _ in range(2)], "Mm")
            t1r = Rot([self.sb([128, 256], F32, "t1") for _ in range(2)], "t1")
            t2r = Rot([self.sb([128, 256], F32, "t2") for _ in range(2)], "t2")
            ypr = Rot([self.sb([128, 2048], F32, "ypre") for _ in range(2)], "ypre")
            ynr = Rot([self.sb([128, 2048], BF16, "yn") for _ in range(2)], "yn")
            ssr = Rot([self.sb([128, 4], F32, "ssq") for _ in range(2)], "ssq")
            junk = self.sb([128, 2048], BF16, "junk")
            yTr = Rot([self.sb([128, 16, 128], BF16, "yT") for _ in range(2)], "yT")

            pzr = Rot([self.ps([128, 1024], F32, "pz") for _ in range(1)], "pz")
            pcs = Rot([self.ps([128, 512], F32, "pcs") for _ in range(1)], "pcs")
            psg = Rot([self.ps([128, 512], F32, "psg") for _ in range(2)], "psg")
            pyr = Rot([self.ps([128, 1024], F32, "py") for _ in range(1)], "py")
            ptr = Rot([self.ps([128, 8, 128], BF16, "ptb") for _ in range(1)], "ptb")
            nchunk = g.L // 128
            for s in range(g.nseq):
                for c in range(nchunk):
                    ch = s * nchunk + c
                    r0 = ch * 128
                    c0 = s * (g.L + 3) + 1 + c * 128
                    hT, hTn = hTr.next(); xs, xn = xsr.next(); bc, bcn = bcr.next(); dt, dn = dtr.next()
                    hp, hpn = hpr.next()
                    self.dma(hT[:], g.hT[:, :, c0:c0 + 128].rearrange("k p t -> p k t"), w=[hTn])
                    self.dma(xs[:], g.xs_tm[r0:r0 + 128, :], w=[xn])
                    self.dma(bc[:], g.bcT[:, :, r0:r0 + 128].rearrange("c p t -> p c t"), w=[bcn])
                    self.dma(dt[:], g.dta[r0:r0 + 128, :], w=[dn])
                    self.dma(hp[:], g.hprev[ch].rearrange("d p f -> p d f"), w=[hpn])
                    zs, zn = zsr.next()
                    for hlf in range(2):
                        pz, pzn = pzr.next()
                        for q in range(2):
                            for k in range(8):
                                col0 = hlf * 1024 + q * 512
                                self.mm(pz[:, q * 512:(q + 1) * 512], hT[:, k, :], wz[:, k, col0:col0 + 512],
                                        start=(k == 0), stop=(k == 7), r=[hTn, f"wz{k}"], w=[pzn + str(q)])
                        for q in range(2):
                            self.act(zs[:, hlf * 1024 + q * 512:hlf * 1024 + (q + 1) * 512], pz[:, q * 512:(q + 1) * 512],
                                     AF.Silu, r=[pzn + str(q)], w=[zn])
                    xd, xdn = xdr.next()
                    for d in range(2):
                        self.tt("dve" if d == 0 else "pool", xd[:, d, :].rearrange("p (h e) -> p h e", h=32),
                                xs[:].rearrange("p (h e) -> p h e", h=32),
                                dt[:, d * 32:(d + 1) * 32].unsqueeze(2).to_broadcast([128, 32, 64]), ALU.mult,
                                r=[xn, dn], w=[xdn + str(d)])
                    pc, pcn = pcs.next()
                    self.mm(pc[:, 0:32], self.masks[:, 0, :], dt[:, 64:96], r=["masks", dn], w=[pcn])
                    self.mm(pc[:, 32:64], self.masks[:, 1, :], dt[:, 96:128], r=["masks", dn], w=[pcn])
                    ea, ean = ear.next()
                    self.act(ea[:], pc[:, 0:64], AF.Exp, r=[pcn], w=[ean])
                    yp, ypn = ypr.next()
                    v3 = lambda ap: ap.rearrange("p (h e) -> p h e", h=4)

                    def stA(G8):
                        self.mm(pc[:, 128:256], bc[:, G8, :], bc[:, 8 + G8, :], r=[bcn], w=[pcn])
                        sm, smn = scr_.next()
                        for d in range(2):
                            self.tt("dve", sm[:, d, :], pc[:, 128:256], self.masks[:, Sm[d], :], ALU.mult,
                                    r=[pcn, "masks"], w=[smn + str(d)])
                        Mt, Mn = Mr.next()
                        for d in range(2):
                            Y, Yn = Yr.next()
                            self.tt("pool", Y[:], self.masks[:, Ym[d], :].unsqueeze(1).to_broadcast([128, 4, 128]),
                                    dt[:, 64 + d * 32 + G8 * 4:64 + d * 32 + G8 * 4 + 4].unsqueeze(2).to_broadcast([128, 4, 128]),
                                    ALU.mult, r=["masks", dn], w=[Yn])
                            pg, pgn = psg.next()
                            self.mm(pg[:], Xl[d][:], Y[:].rearrange("p a b -> p (a b)"), r=[Xn[d], Yn], w=[pgn])
                            Et, En = Er.next()
                            self.act(Et[:].rearrange("p a b -> p (a b)"), pg[:], AF.Exp, r=[pgn], w=[En])
                            self.tt("dve", Mt[:, d, :, :], Et[:], sm[:, d, :].unsqueeze(1).to_broadcast([128, 4, 128]),
                                    ALU.mult, r=[En, smn + str(d)], w=[Mn + str(d)])
                        return Mt, Mn

                    def stB(G8, Mt, Mn):
                        py, pyn = pyr.next()
                        for h in range(4):
                            H = G8 * 4 + h
                            for d in range(2):
                                self.mm(py[:, h * 64:(h + 1) * 64], Mt[:, d, h, :], xd[:, d, H * 64:(H + 1) * 64],
                                        start=(d == 0), stop=(d == 1), r=[Mn + str(d), xdn + str(d)], w=[pyn + "d"])
                        for d in range(2):
                            self.mm(py[:, 512 + d * 256:512 + (d + 1) * 256], bc[:, 8 + G8, :],
                                    hp[:, d, G8 * 256:(G8 + 1) * 256], r=[bcn, hpn], w=[pyn + "o"])
                        return py, pyn

                    def stC(G8, py, pyn):
                        t1, t1n = t1r.next(); t2, t2n = t2r.next()
                        for d, (tt_, tn_) in enumerate(((t1, t1n), (t2, t2n))):
                            self.tt("dve", v3(tt_[:]), v3(py[:, 512 + d * 256:512 + (d + 1) * 256]),
                                    ea[:, d * 32 + G8 * 4:d * 32 + G8 * 4 + 4].unsqueeze(2).to_broadcast([128, 4, 64]),
                                    ALU.mult, r=[pyn + "o", ean], w=[tn_])
                        self.tt("pool", t1[:], t1[:], t2[:], ALU.add, r=[t1n, t2n], w=[t1n])
                        self.tt("dve", t2[:], py[:, 0:256], t1[:], ALU.add, r=[pyn + "d", t1n], w=[t2n])
                        self.tt("pool", v3(t1[:]), v3(xs[:, G8 * 256:(G8 + 1) * 256]),
                                dsk[:, G8 * 4:G8 * 4 + 4].unsqueeze(2).to_broadcast([128, 4, 64]), ALU.mult,
                                r=[xn, "dsk", t1n], w=[t1n])
                        self.tt("pool", yp[:, G8 * 256:(G8 + 1) * 256], t1[:], t2[:], ALU.add, r=[t1n, t2n],
                                w=[ypn + str(G8)])

                    MA = stA(0)
                    for G8 in range(8):
                        PB = stB(G8, *MA)
                        if G8 < 7:
                            MA = stA(G8 + 1)
                        stC(G8, *PB)
                    ypa = [ypn + str(i) for i in range(8)]
                    self.tt("dve", yp[:], yp[:], zs[:], ALU.mult, r=ypa + [zn], w=ypa)
                    sq, sqn = ssr.next()
                    self.act(junk[:], yp[:], AF.Square, r=ypa, w=["junk", sqn], accum_out=sq[:, 0:1])
                    self.ts("dve", sq[:, 1:2], sq[:, 0:1], 1.0 / 2048.0, RMS_EPS, ALU.mult, ALU.add, r=[sqn], w=[sqn])
                    self.act(sq[:, 1:2], sq[:, 1:2], AF.Sqrt, r=[sqn], w=[sqn])
                    self.S.op("dve", lambda sq=sq: nc.vector.reciprocal(out=sq[:, 2:3], in_=sq[:, 1:2]),
                              reads=[sqn], writes=[sqn])
                    yn_, ynn = ynr.next()
                    self.stt(yn_[:], yp[:], sq[:, 2:3], ng[:], ALU.mult, ALU.mult, r=ypa + [sqn, "ng"], w=[ynn])
                    yT, yTn = yTr.next()
                    for blk in range(2):
                        pt, ptn = ptr.next()
                        for i in range(8):
                            fc = blk * 8 + i
                            self.tr(pt[:, i, :], yn_[:, fc * 128:(fc + 1) * 128], self.ident_b[:], r=[ynn, "identb"], w=[ptn])
                        self.cp("act", yT[:, blk * 8:(blk + 1) * 8, :], pt[:], r=[ptn], w=[yTn])
                    self.dma(g.yT[:, :, r0:r0 + 128].rearrange("k p t -> p k t"), yT[:], r=[yTn])

    def lru_layer(self, g):
        self.lru_p1(g)
        self.lru_p2(g)

    def lru_p1(self, g):
        if self.skip():
            return
        nc = self.nc
        with self.phase():
            win = self.sb([128, 8, 2048], BF16, "win")
            for k in range(8):
                self.dma(win[:, k, :], self.I["lru_in_w"][k * 128:(k + 1) * 128, :], w=[f"win{k}"], q="pool")
            gw = self.sb([128, 32, 256], BF16, "gw")
            gsrc = self.I["lru_gate_w"].rearrange("d g n (kc p) j -> (d g n kc) p j", p=128)
            for i in range(32):
                self.dma(gw[:, i, :], gsrc[i], w=["gw"], q="pool")
            cw = self.sb([128, 4, 8], F32, "cw")
            cb = self.sb([128, 8], F32, "cb")
            gb = self.sb([128, 4, 8], F32, "gb")
            nsp = self.sb([128, 2, 8], F32, "nsp")
            h0 = self.lru_h0
            self.load_T(cw[:].rearrange("p k c -> p (k c)"),
                        self.I["lru_conv_w"].rearrange("k (c p) -> (k c) p", p=128), 32, "cw")
            self.load_T(cb[:], self.I["lru_conv_b"].rearrange("(c p) -> c p", p=128), 8, "cb")
            self.load_T(gb[:].rearrange("p k c -> p (k c)"),
                        self.I["lru_gate_b"].rearrange("k (c p) -> (k c) p", p=128), 32, "gb")
            self.load_T(nsp[:].rearrange("p k c -> p (k c)"),
                        self.I["lru_a_param"].rearrange("k (c p) -> (k c) p", p=128), 16, "nsp")
            self.act(nsp[:], nsp[:], AF.Exp, r=["nsp"], w=["nsp"], scale=-1.0)
            self.act(nsp[:], nsp[:], AF.Ln, r=["nsp"], w=["nsp"], bias=1.0)
            self.ts("dve", nsp[:], nsp[:], -8.0, None, ALU.mult, r=["nsp"], w=["nsp"])
            if g.latent:
                self.load_T(h0[:].rearrange("p k c -> p (k c)"),
                            self.I["st_lru"].rearrange("k (c p) -> (k c) p", p=128), 16, "h0")
            else:
                self.memset("dve", h0[:], 0.0, w=["h0"])
            hwr = Rot([self.sb([128, 8, 260], BF16, "hw") for _ in range(2)], "hw")
            xrr = Rot([self.sb([128, 8, 256], F32, "xr") for _ in range(1)], "xr")
            xbr = Rot([self.sb([128, 8, 256], BF16, "xrb") for _ in range(1)], "xrb")
            zsr = Rot([self.sb([128, 8, 256], F32, "zs") for _ in range(2)], "zs")
            gtr = [Rot([self.sb([128, 8, 256], F32, "gt") for _ in range(1)], f"gt{i}") for i in range(4)]
            aar = [Rot([self.sb([128, 8, 256], F32, "aa") for _ in range(1 + d)], f"aa{d}") for d in range(2)]
            bxr = [Rot([self.sb([128, 8, 256], F32, "bx") for _ in range(1 + d)], f"bx{d}") for d in range(2)]
            tmr = Rot([self.sb([128, 8, 256], F32, "tmpl") for _ in range(1)], "tmpl")
            yfr = Rot([self.sb([128, 8, 256], F32, "yf") for _ in range(2)], "yf")
            accr = Rot([self.sb([128, 256], F32, "acc") for _ in range(3)], "acc")
            fin = self.sb([128, 8], F32, "fin")
            ppr = Rot([self.ps([128, 512], F32, "pp") for _ in range(3)], "pp")
            pgr = Rot([self.ps([128, 256], F32, "pg") for _ in range(3)], "pg")
            for s in range(g.nseq):
                yf_prev = None
                ntile = g.L // 256
                for j in range(ntile):
                    c0 = s * (g.L + 3) + 256 * j
                    t0 = s * g.L + 256 * j
                    hw, hn = hwr.next()
                    self.dma(hw[:, :, 0:259], g.hT[:, :, c0:c0 + 259].rearrange("k p t -> p k t"), w=[hn])
                    xr, xn = xrr.next(); xb, xbn = xbr.next(); zs, zn = zsr.next()
                    for fc in range(16):
                        pp, pn = ppr.next()
                        for k in range(8):
                            self.mm(pp[:, 0:259], win[:, k, fc * 128:(fc + 1) * 128], hw[:, k, 0:259],
                                    start=(k == 0), stop=(k == 7), r=[hn, f"win{k}"], w=[pn])
                        if fc < 8:
                            acc, an = accr.next()
                            self.conv_fm(pp, pn, acc, an, xr[:, fc, :], f"{xn}_{fc}", cw, cb, fc, 4, False, 0)
                            self.cp("pool", xb[:, fc, :], xr[:, fc, :], r=[f"{xn}_{fc}"], w=[f"{xbn}_{fc}"])
                        else:
                            self.act(zs[:, fc - 8, :], pp[:, 1:257], AF.Silu, r=[pn], w=[zn])
                    gts = [gtr[i].next() for i in range(4)]
                    for dg in range(4):
                        gt, gn = gts[dg]
                        for n in range(4):
                            for jc in range(2):
                                pg, pgn = pgr.next()
                                for kc in range(2):
                                    self.mm(pg[:], gw[:, (dg * 4 + n) * 2 + kc, jc * 128:(jc + 1) * 128], xb[:, n * 2 + kc, :],
                                            start=(kc == 0), stop=(kc == 1), r=["gw", f"{xbn}_{n * 2 + kc}"], w=[pgn])
                                ch = n * 2 + jc
                                self.act(gt[:, ch, :], pg[:], AF.Sigmoid, r=[pgn, "gb"], w=[gn], bias=gb[:, dg, ch:ch + 1])
                    xall = [f"{xn}_{fc}" for fc in range(8)]
                    ab = []
                    for d in range(2):
                        aa, aan = aar[d].next(); bx, bxn = bxr[d].next(); tm, tmn = tmr.next()
                        rt, rn = gts[d * 2]; it, itn = gts[d * 2 + 1]
                        for ch in range(8):
                            self.act(aa[:, ch, :], rt[:, ch, :], AF.Exp, r=[rn, "nsp"], w=[aan], scale=nsp[:, d, ch:ch + 1])
                        self.tt("dve", tm[:], aa[:], aa[:], ALU.mult, r=[aan], w=[tmn])
                        self.ts("dve", tm[:], tm[:], -1.0, 1.0, ALU.mult, ALU.add, r=[tmn], w=[tmn])
                        self.ts("dve", tm[:], tm[:], 0.0, None, ALU.max, r=[tmn], w=[tmn])
                        self.act(tm[:], tm[:], AF.Sqrt, r=[tmn], w=[tmn])
                        self.tt("pool", tm[:], tm[:], it[:], ALU.mult, r=[tmn, itn], w=[tmn])
                        self.tt("pool", bx[:], tm[:], xr[:], ALU.mult, r=[tmn] + xall, w=[bxn])
                        ab.append((aa, aan, bx, bxn))
                    yf, yfn = yfr.next()
                    aa, aan, bx, bxn = ab[0]
                    for ch in range(8):
                        init = h0[:, 0, ch:ch + 1] if yf_prev is None else yf_prev[0][:, ch, 255:256]
                        rr = [aan, bxn, "h0"] + ([yf_prev[1]] if yf_prev is not None else [])
                        self.S.op("dve", lambda yf=yf, aa=aa, bx=bx, ch=ch, init=init: nc.vector.tensor_tensor_scan(
                            out=yf[:, ch, :], data0=aa[:, ch, :], data1=bx[:, ch, :], initial=init,
                            op0=ALU.mult, op1=ALU.add), reads=rr, writes=[yfn])
                    yf_prev = (yf, yfn)
                    fmv = lambda slot: g.fm[slot, :, :, t0:t0 + 256].rearrange("c p t -> p c t")
                    self.dma(fmv(0), yf[:], r=[yfn])
                    self.dma(fmv(1), ab[1][0][:], r=[ab[1][1]])
                    self.dma(fmv(2), ab[1][2][:], r=[ab[1][3]])
                    self.dma(fmv(3), zs[:], r=[zn])
                if not g.latent:
                    self.cp("dve", fin[:], yf_prev[0][:, :, 255], r=[yf_prev[1]], w=["fin"])
                    self.dma(self.O["nsl"][s, 0, :].rearrange("(c p) -> p c", p=128), fin[:], r=["fin"],
                             allow_slow_non_contiguous=True)

    def lru_p2(self, g):
        if self.skip():
            return
        nc = self.nc
        with self.phase():
            h0 = self.lru_h0
            ldr = [Rot([self.sb([128, 8, 256], F32, "ld") for _ in range(2)], f"ld{i}") for i in range(4)]
            ybr = Rot([self.sb([128, 8, 256], F32, "yb") for _ in range(2)], "yb")
            yTr = Rot([self.sb([128, 8, 256], BF16, "yTl") for _ in range(2)], "yTl")
            fin = self.sb([128, 8], F32, "fin")
            for s in range(g.nseq):
                yb_prev = None
                ntile = g.L // 256
                for j in range(ntile - 1, -1, -1):
                    t0 = s * g.L + 256 * j
                    lt = []
                    for i in range(4):
                        t, tn = ldr[i].next()
                        self.dma(t[:], g.fm[i, :, :, t0:t0 + 256].rearrange("c p t -> p c t"), w=[tn])
                        lt.append((t, tn))
                    (yf, yfn), (aa, aan), (bx, bxn), (zs, zn) = lt
                    yb, ybn = ybr.next()
                    for ch in range(8):
                        init = h0[:, 1, ch:ch + 1] if yb_prev is None else yb_prev[0][:, ch, 0:1]
                        rr = [aan, bxn, "h0"] + ([yb_prev[1]] if yb_prev is not None else [])
                        self.S.op("dve", lambda yb=yb, aa=aa, bx=bx, ch=ch, init=init: nc.vector.tensor_tensor_scan(
                            out=yb[:, ch, ::-1], data0=aa[:, ch, ::-1], data1=bx[:, ch, ::-1], initial=init,
                            op0=ALU.mult, op1=ALU.add), reads=rr, writes=[ybn])
                    yb_prev = (yb, ybn)
                    self.tt("pool", yf[:], yf[:], yb[:], ALU.add, r=[yfn, ybn], w=[yfn])
                    yT, yTn = yTr.next()
                    self.tt("pool", yT[:], yf[:], zs[:], ALU.mult, r=[yfn, zn], w=[yTn])
                    self.dma(g.yT[0:8, :, t0:t0 + 256].rearrange("c p t -> p c t"), yT[:], r=[yTn])
                if not g.latent:
                    self.cp("dve", fin[:], yb_prev[0][:, :, 0], r=[yb_prev[1]], w=["fin"])
                    self.dma(self.O["nsl"][s, 1, :].rearrange("(c p) -> p c", p=128), fin[:], r=["fin"],
                             allow_slow_non_contiguous=True)

    def dbg_rows(self, dst_row0, src2d, nrows):
        with self.phase():
            for r0 in range(0, nrows, 128):
                t = self.sb([128, 1024], F32, "dbg")
                self.dma(t[:], src2d[r0:r0 + 128, :], w=["dbg"])
                self.dma(self.O["yp"][dst_row0 + r0:dst_row0 + r0 + 128, :], t[:], r=["dbg"])

    def hyena_layer(self, g):
        self.hy_filters(g)
        if _os.environ.get("DEBUG") == "hyk" and g.name == "p":
            self.dbg_rows(0, g.khat2[0, 0], 256)
            self.dbg_rows(256, g.khat2[0, 1], 256)
            self.dbg_rows(512, g.kw[:, 0:1024], 256)
            self.dbg_rows(768, g.kw[:, 1024:2048], 256)
            return
        self.hy_p1(g)
        for o in range(2):
            for s in range(g.nseq):
                self.hy_fwd(g, o, s)
                self.hy_inv(g, o, s)

    def _vec(self, dst, src1d, n, name):
        self.dma(dst, src1d.rearrange("(p o) -> p o", o=1), w=[name], allow_slow_non_contiguous=True)

    def _hy_mlp(self, g, hd2, w3r):
        nc = self.nc
        L = g.L
        nm = g.name
        TWO_PI = 2.0 * math.pi
        MAGIC = 12582912.0
        with self.phase():
            zT = self.sb([33, L], F32, "zT")
            self.dma(zT[:], self.I[f"hz_{nm}"], w=["zT"])
            w1 = self.sb([33, 64], F32, "w1"); w2 = self.sb([64, 64], F32, "w2")
            self.dma(w1[:], self.I["hy_f_w1"], w=["w1"]); self.dma(w2[:], self.I["hy_f_w2"], w=["w2"])
            w3 = self.sb([64, 4096], F32, "w3")
            self.dma(w3[:], self.I["hy_f_w3"], w=["w3"])
            self.cp("pool", w3r[:], w3[:], r=["w3"], w=["w3r"])
            pv = self.sb([64, 6], F32, "pv")
            self._vec(pv[:, 0:1], self.I["hy_f_b1"], 64, "pv"); self._vec(pv[:, 1:2], self.I["hy_f_b2"], 64, "pv")
            self._vec(pv[:, 2:3], self.I["hy_f_freq"][0], 64, "pv"); self._vec(pv[:, 3:4], self.I["hy_f_freq"][1], 64, "pv")
            self.tt("dve", pv[:, 4:6], pv[:, 0:2], pv[:, 2:4], ALU.mult, r=["pv"], w=["pv"])
            hd1 = self.sb([64, L], F32, "hd1")
            argr = Rot([self.sb([64, 512], F32, "arg") for _ in range(2)], "arg")
            nr = Rot([self.sb([64, 512], F32, "nq") for _ in range(2)], "nq")
            phr = Rot([self.ps([64, 512], F32, "ph") for _ in range(2)], "ph")
            TW = min(512, L)
            for layer in range(2):
                for ti in range(L // TW):
                    sl = slice(ti * TW, (ti + 1) * TW)
                    ph, phn = phr.next()
                    if layer == 0:
                        self.mm(ph[:, 0:TW], w1[:], zT[:, sl], r=["w1", "zT"], w=[phn])
                    else:
                        self.mm(ph[:, 0:TW], w2[:], hd1[:, sl], r=["w2", "hd1"], w=[phn])
                    arg, an = argr.next(); nq, nn = nr.next()
                    self.act(arg[:, 0:TW], ph[:, 0:TW], AF.Identity, r=[phn, "pv"], w=[an],
                             scale=pv[:, 2 + layer:3 + layer], bias=pv[:, 4 + layer:5 + layer])
                    self.ts("dve", nq[:, 0:TW], arg[:, 0:TW], 1.0 / TWO_PI, MAGIC, ALU.mult, ALU.add, r=[an], w=[nn])
                    self.ts("dve", nq[:, 0:TW], nq[:, 0:TW], MAGIC, None, ALU.subtract, r=[nn], w=[nn])
                    self.stt(arg[:, 0:TW], nq[:, 0:TW], -TWO_PI, arg[:, 0:TW], ALU.mult, ALU.add, r=[nn, an], w=[an])
                    self.ts("dve", arg[:, 0:TW], arg[:, 0:TW], math.pi, -math.pi, ALU.min, ALU.max, r=[an], w=[an])
                    if layer == 0:
                        self.act(hd1[:, sl], arg[:, 0:TW], AF.Sin, r=[an], w=["hd1"])
                    else:
                        self.act(hd2[:, sl], arg[:, 0:TW], AF.Sin, r=[an], w=["hd2"])

    def hy_filters(self, g):
        if self.skip():
            return
        nc = self.nc
        L = g.L
        nL = L // 128
        nm = g.name
        TWO_PI = 2.0 * math.pi
        MAGIC = 12582912.0
        with self.phase():
            hd2 = self.sb([64, L], F32R, "hd2")
            w3r = self.sb([64, 4096], F32R, "w3r")
            self._hy_mlp(g, hd2, w3r)
            asum = self.sb([128, 4096], F32, "asum")
            self.memset("pool", asum[:], 0.0, w=["asum"])
            winr = Rot([self.sb([128, 1024], F32, "winw") for _ in range(2)], "winw")
            kwr = Rot([self.sb([128, 4096], F32, "kwt") for _ in range(2)], "kwt")
            kabs = self.sb([128, 4096], F32, "kabs")
            pkr = Rot([self.ps([128, 512], F32, "pk") for _ in range(3)], "pk")
            for tc in range(nL):
                wt, wn = winr.next()
                self.dma(wt[:], self.I[f"win_{nm}"][tc * 128:(tc + 1) * 128, :], w=[wn])
                kt, kn = kwr.next()
                for ti in range(8):
                    pk, pkn = pkr.next()
                    self.mm(pk[:], hd2[:, tc * 128:(tc + 1) * 128], w3r[:, ti * 512:(ti + 1) * 512], r=["hd2", "w3r"], w=[pkn])
                    self.tt("dve", kt[:, ti * 512:(ti + 1) * 512], pk[:], wt[:, (ti % 2) * 512:(ti % 2 + 1) * 512], ALU.mult,
                            r=[pkn, wn], w=[kn])
                self.act(kabs[:], kt[:], AF.Abs, r=[kn], w=["kabs"])
                self.tt("pool", asum[:], asum[:], kabs[:], ALU.add, r=["kabs", "asum"], w=["asum"])
                self.dma(g.kw[tc * 128:(tc + 1) * 128, :], kt[:], r=[kn])
            asr = self.sb([128, 4096], F32R, "asr")
            self.cp("dve", asr[:], asum[:], r=["asum"], w=["asr"])
            tot = self.sb([128, 4096], F32, "tot")
            for ti in range(8):
                pk, pkn = pkr.next()
                self.mm(pk[:], self.ones_r[:], asr[:, ti * 512:(ti + 1) * 512], r=["onesr", "asr"], w=[pkn])
                self.cp("act", tot[:, ti * 512:(ti + 1) * 512], pk[:], r=[pkn], w=["tot"])
            rn = self.sb([128, 2, 1024], F32, "rn")
            tv = tot[:].rearrange("p (o d c) -> p o d c", o=2, d=2)
            self.tt("dve", rn[:], tv[:, :, 0, :], tv[:, :, 1, :], ALU.add, r=["tot"], w=["rn"])
            rn0 = self.sb([128, 2, 1024], F32, "rn0")
            self.act(rn0[:], rn[:], AF.Ln, r=["rn"], w=["rn0"])
            self.act(rn[:], rn0[:], AF.Exp, r=["rn0"], w=["rn"], scale=-1.0)
            self.ts("dve", rn[:], rn[:], 2.0 / (2 * L), None, ALU.mult, r=["rn"], w=["rn"])
            self.dma(g.khat[0, 0, 0:128, :], rn[:, 0, :], r=["rn"])
            self.dma(g.khat[0, 1, 0:128, :], rn[:, 1, :], r=["rn"])
        with self.phase():
            rn = self.sb([128, 2, 1024], F32, "rn")
            self.dma(rn[:, 0, :], g.khat[0, 0, 0:128, :], w=["rn"])
            self.dma(rn[:, 1, :], g.khat[0, 1, 0:128, :], w=["rn"])
            self.S.barrier()
            ks = self.sb([128, nL, 1024], BF16, "ks")
            kldr = Rot([self.sb([128, 2, 1024], F32, "kld") for _ in range(2)], "kld")
            mbr = Rot([self.sb([128, nL, 128], BF16, "mb") for _ in range(2)], "mb")
            kor = Rot([self.sb([128, 1024], F32, "ko") for _ in range(2)], "ko")
            pur = Rot([self.ps([128, 1024], F32, "pu") for _ in range(2)], "pu")
            for o in range(2):
                for cs in range(2):
                    for tc in range(nL):
                        kl, kln = kldr.next()
                        self.dma(kl[:], g.kw[tc * 128:(tc + 1) * 128, o * 2048:(o + 1) * 2048].rearrange("p (d c) -> p d c", d=2),
                                 w=[kln])
                        if tc == 0:
                            self.memset("dve", kl[0:1, 1, :], 0.0, w=[kln])
                        self.tt("dve", kl[:, 0, :], kl[:, 0, :], kl[:, 1, :], ALU.add if cs == 0 else ALU.subtract,
                                r=[kln], w=[kln])
                        self.tt("pool", ks[:, tc, :], kl[:, 0, :], rn[:, o, :], ALU.mult, r=[kln, "rn"], w=[f"ks{tc}"])
                    for fc in range(nL):
                        mb, mbn = mbr.next()
                        self.dma(mb[:], self.I[f"dft_{nm}"][2 + cs, fc], w=[mbn])
                        pu, pun = pur.next()
                        for tc in range(nL):
                            for hlf in range(2):
                                self.mm(pu[:, hlf * 512:(hlf + 1) * 512], mb[:, tc, :], ks[:, tc, hlf * 512:(hlf + 1) * 512],
                                        start=(tc == 0), stop=(tc == nL - 1), r=[mbn, f"ks{tc}"],
                                        w=[pun + str(hlf)])
                        ko, kon = kor.next()
                        for hlf in range(2):
                            self.cp("act", ko[:, hlf * 512:(hlf + 1) * 512], pu[:, hlf * 512:(hlf + 1) * 512],
                                    r=[pun + str(hlf)], w=[kon])
                        self.dma(g.khat2[o, cs, fc * 128:(fc + 1) * 128, :], ko[:], r=[kon])

    def hy_p1(self, g):
        if self.skip():
            return
        nc = self.nc
        with self.phase():
            win = self.sb([128, 8, 4096], BF16, "win")
            for k in range(8):
                self.dma(win[:, k, :], self.I["hy_in_w"][k * 128:(k + 1) * 128, :], w=[f"win{k}"], q="pool")
            cw = self.sb([128, 3, 24], F32, "cw")
            cb = self.sb([128, 24], F32, "cb")
            self.load_T(cw[:].rearrange("p k c -> p (k c)"),
                        self.I["hy_conv_w"].rearrange("k (c p) -> (k c) p", p=128), 72, "cw")
            self.load_T(cb[:], self.I["hy_conv_b"].rearrange("(c p) -> c p", p=128), 24, "cb")
            hwr = Rot([self.sb([128, 8, 260], BF16, "hw") for _ in range(2)], "hw")
            fmr = [Rot([self.sb([128, 8, 256], F32, "fmt") for _ in range(2)], f"fmt{i}") for i in range(4)]
            vbr = Rot([self.sb([128, 8, 256], BF16, "vb") for _ in range(2)], "vb")
            utr = Rot([self.sb([128, 1024], BF16, "ut") for _ in range(2)], "ut")
            accr = Rot([self.sb([128, 256], F32, "acc") for _ in range(3)], "acc")
            ppr = Rot([self.ps([128, 512], F32, "pp") for _ in range(3)], "pp")
            ptr = Rot([self.ps([128, 8, 128], BF16, "ptr") for _ in range(2)], "ptr")
            for s in range(g.nseq):
                for j in range(g.L // 256):
                    c0 = s * (g.L + 3) + 256 * j
                    t0 = s * g.L + 256 * j
                    hw, hn = hwr.next()
                    self.dma(hw[:, :, 0:259], g.hT[:, :, c0:c0 + 259].rearrange("k p t -> p k t"), w=[hn])
                    fts = [fmr[i].next() for i in range(4)]
                    vb, vbn = vbr.next()
                    for fc in range(32):
                        pp, pn = ppr.next()
                        for k in range(8):
                            self.mm(pp[:, 0:259], win[:, k, fc * 128:(fc + 1) * 128], hw[:, k, 0:259],
                                    start=(k == 0), stop=(k == 7), r=[hn, f"win{k}"], w=[pn])
                        ft, fn = fts[fc // 8]
                        if fc < 24:
                            acc, an = accr.next()
                            self.conv_fm(pp, pn, acc, an, ft[:, fc % 8, :], f"{fn}_{fc % 8}", cw, cb, fc, 3, False, 0)
                            if fc < 8:
                                self.cp("pool", vb[:, fc, :], ft[:, fc, :], r=[f"{fn}_{fc}"], w=[f"{vbn}_{fc}"])
                        else:
                            self.act(ft[:, fc % 8, :], pp[:, 1:257], AF.Silu, r=[pn], w=[f"{fn}_{fc % 8}"])
                    for i in range(4):
                        ft, fn = fts[i]
                        self.dma(g.fm[i, :, :, t0:t0 + 256].rearrange("c p t -> p c t"), ft[:],
                                 r=[f"{fn}_{c}" for c in range(8)])
                    for tcn in range(2):
                        pt, ptn = ptr.next()
                        for c in range(8):
                            self.tr(pt[:, c, :], vb[:, c, tcn * 128:(tcn + 1) * 128], self.ident_b[:],
                                    r=[f"{vbn}_{c}", "identb"], w=[ptn])
                        ut, utn = utr.next()
                        self.cp("dve", ut[:], pt[:].rearrange("p a b -> p (a b)"), r=[ptn], w=[utn])
                        self.dma(g.utm[t0 + tcn * 128:t0 + (tcn + 1) * 128, :], ut[:], r=[utn])

    def hy_fwd(self, g, o, s):
        if self.skip():
            return
        nc = self.nc
        L = g.L
        nL = L // 128
        nm = g.name
        with self.phase():
            u = self.sb([128, nL, 1024], BF16, "u")
            usrc = g.utm[s * L:(s + 1) * L, :].rearrange("(tc p) c -> p tc c", p=128)
            for q in range(0, nL, 8):
                qe = min(q + 8, nL)
                self.dma(u[:, q:qe, :], usrc[:, q:qe, :], w=[f"u{q}"])
            cbr = Rot([self.sb([128, nL, 128], BF16, "cbk") for _ in range(2)], "cbk")
            sbr = Rot([self.sb([128, nL, 128], BF16, "sbk") for _ in range(2)], "sbk")
            kcr = Rot([self.sb([128, 1024], F32, "kc") for _ in range(2)], "kc")
            ksr = Rot([self.sb([128, 1024], F32, "ksp") for _ in range(2)], "ksp")
            a1r = Rot([self.sb([128, 1024], F32, "a1") for _ in range(2)], "a1")
            a2r = Rot([self.sb([128, 1024], F32, "a2") for _ in range(2)], "a2")
            yor = Rot([self.sb([128, 2, 1024], BF16, "yo") for _ in range(2)], "yo")
            puc = Rot([self.ps([128, 1024], F32, "puc") for _ in range(2)], "puc")
            pus = Rot([self.ps([128, 1024], F32, "pus") for _ in range(2)], "pus")
            for fc in range(nL):
                cb_, cbn = cbr.next(); sb_, sbn = sbr.next()
                self.dma(cb_[:], self.I[f"dft_{nm}"][0, fc], w=[cbn])
                self.dma(sb_[:], self.I[f"dft_{nm}"][1, fc], w=[sbn])
                kc, kcn = kcr.next(); ksp, ksn = ksr.next()
                self.dma(kc[:], g.khat2[o, 0, fc * 128:(fc + 1) * 128, :], w=[kcn])
                self.dma(ksp[:], g.khat2[o, 1, fc * 128:(fc + 1) * 128, :], w=[ksn])
                pc_, pcn = puc.next(); ps_, psn = pus.next()
                for tc in range(nL):
                    q8 = (tc // 8) * 8
                    for hlf in range(2):
                        hs = slice(hlf * 512, (hlf + 1) * 512)
                        self.mm(pc_[:, hs], cb_[:, tc, :], u[:, tc, hs], start=(tc == 0), stop=(tc == nL - 1),
                                r=[cbn, f"u{q8}"], w=[pcn + str(hlf)])
                        self.mm(ps_[:, hs], sb_[:, tc, :], u[:, tc, hs], start=(tc == 0), stop=(tc == nL - 1),
                                r=[sbn, f"u{q8}"], w=[psn + str(hlf)])
                a1, a1n = a1r.next(); a2, a2n = a2r.next(); yo, yon = yor.next()
                for hlf in range(2):
                    hs = slice(hlf * 512, (hlf + 1) * 512)
                    self.tt("dve", a1[:, hs], pc_[:, hs], kc[:, hs], ALU.mult, r=[pcn + str(hlf), kcn], w=[a1n])
                    self.tt("dve", a2[:, hs], ps_[:, hs], ksp[:, hs], ALU.mult, r=[psn + str(hlf), ksn], w=[a2n])
                self.tt("pool", yo[:, 0, :], a1[:], a2[:], ALU.subtract, r=[a1n, a2n], w=[yon + "c"])
                a1, a1n = a1r.next(); a2, a2n = a2r.next()
                for hlf in range(2):
                    hs = slice(hlf * 512, (hlf + 1) * 512)
                    self.tt("dve", a1[:, hs], pc_[:, hs], ksp[:, hs], ALU.mult, r=[pcn + str(hlf), ksn], w=[a1n])
                    self.tt("dve", a2[:, hs], ps_[:, hs], kc[:, hs], ALU.mult, r=[psn + str(hlf), kcn], w=[a2n])
                self.tt("pool", yo[:, 1, :], a1[:], a2[:], ALU.add, r=[a1n, a2n], w=[yon + "s"])
                self.dma(g.yspec[:, :, :, fc, :].rearrange("a cc p j -> p a cc j"),
                         yo[:].rearrange("p a (cc j) -> p a cc j", j=128), r=[yon + "c", yon + "s"])

    def hy_inv(self, g, o, s):
        if self.skip():
            return
        nc = self.nc
        L = g.L
        nL = L // 128
        nm = g.name
        TT = min(512, L)
        nq = TT // 128
        with self.phase():
            fb = self.sb([128, 2, 8], F32, "fb")
            self.load_T(fb[:].rearrange("p k c -> p (k c)"),
                        self.I["hy_f_bias"].rearrange("k (c p) -> (k c) p", p=128), 16, "fb")
            cm = self.sb([128, nL, TT], BF16, "cm")
            sm = self.sb([128, nL, TT], BF16, "sm")
            ycr = Rot([self.sb([128, nL, 128], BF16, "yc") for _ in range(2)], "yc")
            ysr = Rot([self.sb([128, nL, 128], BF16, "ys") for _ in range(2)], "ys")
            utr = Rot([self.sb([128, TT], F32, "uti") for _ in range(2)], "uti")
            xgr = Rot([self.sb([128, TT], F32, "xg") for _ in range(2)], "xg")
            zgr = Rot([self.sb([128, TT], F32, "zg") for _ in range(2)], "zg")
            unr = Rot([self.sb([128, TT], F32, "un") for _ in range(2)], "un")
            ubr = Rot([self.sb([128, TT], BF16, "ub") for _ in range(2)], "ub")
            uor = Rot([self.sb([128, 8, 128], BF16, "uo") for _ in range(2)], "uo")
            par = Rot([self.ps([128, 512], F32, "pa") for _ in range(2)], "pa")
            ptq = [self.ps([128, 8, 128], BF16, "ptq") for _ in range(nq)] if o == 0 else []
            for tt in range(L // TT):
                t0 = s * L + tt * TT
                for q in range(0, nL, 8):
                    qe = min(q + 8, nL)
                    self.dma(cm[:, q:qe, :], self.I[f"dfti_{nm}"][0, tt, :, q:qe, :], w=[f"cm{q}"])
                    self.dma(sm[:, q:qe, :], self.I[f"dfti_{nm}"][1, tt, :, q:qe, :], w=[f"sm{q}"])
                for cc in range(8):
                    yc, ycn = ycr.next(); ys_, ysn = ysr.next()
                    self.dma(yc[:], g.yspec[0, cc], w=[ycn])
                    self.dma(ys_[:], g.yspec[1, cc], w=[ysn])
                    ut, utn = utr.next(); xg, xgn = xgr.next()
                    self.dma(ut[:], g.fm[0, cc, :, t0:t0 + TT], w=[utn])
                    self.dma(xg[:], g.fm[1 + o, cc, :, t0:t0 + TT], w=[xgn])
                    pa, pan = par.next()
                    for fc in range(nL):
                        q8 = (fc // 8) * 8
                        self.mm(pa[:, 0:TT], yc[:, fc, :], cm[:, fc, :], start=(fc == 0), stop=False,
                                r=[ycn, f"cm{q8}"], w=[pan])
                        self.mm(pa[:, 0:TT], ys_[:, fc, :], sm[:, fc, :], start=False, stop=(fc == nL - 1),
                                r=[ysn, f"sm{q8}"], w=[pan])
                    un, unn = unr.next()
                    self.stt(un[:], ut[:], fb[:, o, cc:cc + 1], pa[:, 0:TT], ALU.mult, ALU.add, r=[utn, "fb", pan], w=[unn])
                    self.tt("pool", un[:], un[:], xg[:], ALU.mult, r=[unn, xgn], w=[unn])
                    if o == 0:
                        self.dma(g.fm[0, cc, :, t0:t0 + TT], un[:], r=[unn])
                        ub, ubn = ubr.next()
                        self.cp("act", ub[:], un[:], r=[unn], w=[ubn])
                        for q in range(nq):
                            self.tr(ptq[q][:, cc, :], ub[:, q * 128:(q + 1) * 128], self.ident_b[:],
                                    r=[ubn, "identb"], w=[f"ptq{q}"])
                    else:
                        zg, zgn = zgr.next()
                        self.dma(zg[:], g.fm[3, cc, :, t0:t0 + TT], w=[zgn])
                        ub, ubn = ubr.next()
                        self.tt("dve", ub[:], un[:], zg[:], ALU.mult, r=[unn, zgn], w=[ubn])
                        self.dma(g.yT[cc, :, t0:t0 + TT], ub[:], r=[ubn])
                if o == 0:
                    for q in range(nq):
                        uo, uon = uor.next()
                        self.cp("act" if q % 2 == 0 else "dve", uo[:], ptq[q][:], r=[f"ptq{q}"], w=[uon])
                        self.dma(g.utm[t0 + q * 128:t0 + (q + 1) * 128, :], uo[:].rearrange("p a b -> p (a b)"), r=[uon])


def _consts():
    k = np.arange(128)[:, None]
    i = np.arange(128)[None, :]
    masks = np.stack([(k <= i), (k >= i), (k > i), (k < i)]).astype(np.float32)
    out = {"masks": masks}
    for nm, L in (("p", LP), ("s", LS)):
        n = 2 * L
        t = np.arange(L, dtype=np.float64)
        f = np.arange(L, dtype=np.float64)
        th = 2.0 * np.pi * (f + 0.5) / n
        a2 = np.outer(t + 0.5, th)
        am = np.outer(t, th)
        M = np.stack([np.cos(a2), np.sin(a2), np.cos(am), np.sin(am)]).astype(np.float32).astype(ml_dtypes.bfloat16)
        nL = L // 128
        TT = min(512, L)
        out[f"dft_{nm}"] = np.ascontiguousarray(M.reshape(4, nL, 128, nL, 128).transpose(0, 3, 2, 1, 4))
        out[f"dfti_{nm}"] = np.ascontiguousarray(M[0:2].reshape(2, nL, 128, L // TT, TT).transpose(0, 3, 2, 1, 4))
        tt = (np.arange(L, dtype=np.float32) / np.float32(L)).astype(np.float32)
        w = (np.float32(2.0 * math.pi) * np.arange(L, dtype=np.float32) / np.float32(L)).astype(np.float32)
        fr = np.linspace(1e-4, 15, 16, dtype=np.float32)
        ang = w[:, None] * fr
        z = np.concatenate([tt[:, None], np.cos(ang), np.sin(ang)], axis=-1).astype(np.float32)
        out[f"hz_{nm}"] = np.ascontiguousarray(z.T)
        deltas = np.linspace(math.log(HY_T) / 1.5, math.log(HY_T) / 0.3, D, dtype=np.float32)
        out[f"win_{nm}"] = np.exp(-tt[:, None] * np.abs(deltas)).astype(np.float32)
    return out


_CACHE = {}


def nc_inputs(nc):
    return _CACHE["in_names"]


def kernel(**inputs):
    f = lambda a: np.ascontiguousarray(np.asarray(a, dtype=np.float32))
    if "nc" not in _CACHE:
        kb = KB()
        _CACHE["nc"] = kb.build()
        _CACHE["in_names"] = set(kb.I.keys())
        _CACHE["consts"] = _consts()
    nc = _CACHE["nc"]
    consts = _CACHE["consts"]
    shared = {
        "mod_w": f(inputs["mod_w"]), "mod_b": f(inputs["mod_b"]), "ln_g": f(inputs["ln_g"]), "ln_b": f(inputs["ln_b"]),
        "ssd_in_w": f(inputs["ssd_in_w"]), "ssd_conv_w": f(inputs["ssd_conv_w"]), "ssd_conv_b": f(inputs["ssd_conv_b"]),
        "ssd_dt_bias": f(inputs["ssd_dt_bias"]).reshape(2, 64), "ssd_a_log": f(inputs["ssd_a_log"]).reshape(2, 64),
        "ssd_d": f(inputs["ssd_d"]), "ssd_norm_g": f(inputs["ssd_norm_g"]), "ssd_out_w": f(inputs["ssd_out_w"]),
        "hy_in_w": f(inputs["hy_in_w"])[0], "hy_conv_w": f(inputs["hy_conv_w"])[0], "hy_conv_b": f(inputs["hy_conv_b"])[0],
        "hy_f_w1": f(inputs["hy_f_w1"])[0], "hy_f_b1": f(inputs["hy_f_b1"])[0], "hy_f_w2": f(inputs["hy_f_w2"])[0],
        "hy_f_b2": f(inputs["hy_f_b2"])[0], "hy_f_w3": f(inputs["hy_f_w3"])[0], "hy_f_freq": f(inputs["hy_f_freq"])[0],
        "hy_f_bias": f(inputs["hy_f_bias"])[0], "hy_out_w": f(inputs["hy_out_w"])[0],
        "lru_in_w": f(inputs["lru_in_w"])[0], "lru_conv_w": f(inputs["lru_conv_w"])[0], "lru_conv_b": f(inputs["lru_conv_b"])[0],
        "lru_gate_w": f(inputs["lru_gate_w"])[0], "lru_gate_b": f(inputs["lru_gate_b"])[0].reshape(4, D),
        "lru_a_param": f(inputs["lru_a_param"])[0], "lru_out_w": f(inputs["lru_out_w"])[0],
    }
    shared.update(consts)
    shared = {k: v for k, v in shared.items() if k in nc_inputs(nc)}
    xp = f(inputs["x_prompt"]); xs = f(inputs["x_sample"])
    sts = f(inputs["state_ssd"]); stl = f(inputs["state_lru"])
    c = f(inputs["c"]); cc = f(inputs["c_ctx"])
    in_maps = []
    for core in range(8):
        b = core // 2
        m = dict(shared)
        m["xp"] = np.ascontiguousarray(xp[core * NPS:(core + 1) * NPS].reshape(NPS * LP, D))
        m["xs"] = np.ascontiguousarray(xs[b])
        m["st_ssd"] = np.ascontiguousarray(sts[b].reshape(2, 2, 2048, 128))
        m["st_lru"] = np.ascontiguousarray(stl[b].reshape(2, D))
        m["cond"] = np.ascontiguousarray(np.stack([cc, c[b]]))
        in_maps.append(m)
    res = run_bass_kernel_spmd(nc, in_maps, core_ids=list(range(8)))
    r = res.results
    y_prompt = np.concatenate([r[i]["yp"].reshape(NPS, LP, D) for i in range(8)], axis=0)
    y_sample = np.stack([r[2 * b]["ys"] for b in range(4)], axis=0)
    nss = np.concatenate([r[i]["nss"].reshape(NPS, 2, 2, 32, 64, 128) for i in range(8)], axis=0)
    nsl = np.concatenate([r[i]["nsl"].reshape(NPS, 1, 2, D) for i in range(8)], axis=0)
    return (y_prompt.astype(np.float32), y_sample.astype(np.float32), nss.astype(np.float32), nsl.astype(np.float32))
```

```python
import contextlib
import math
import numpy as np
import ml_dtypes
import concourse.bass as bass
import concourse.mybir as mybir
from concourse.bass_utils import run_bass_kernel_spmd

F32 = mybir.dt.float32
F32R = mybir.dt.float32r
BF16 = mybir.dt.bfloat16
AF = mybir.ActivationFunctionType
ALU = mybir.AluOpType

EPOCH = 8000
NDMA = 28
NDMA_HW = 20
import os as _os
MAXOPS = int(_os.environ.get("MAXOPS", "1000000000"))
NOSELF = _os.environ.get("NOSELF", "0") == "1"

D = 1024
NPS = 4
LP = 256
LS = 4096
DEPTH = 4
ALPHA = (2.0 * DEPTH) ** 0.25
LN_EPS = 1e-5
RMS_EPS = 1e-5
SSD_PROJ = 6208
HY_T = 1e-2


class Sched:
    ENG = ("pe", "act", "dve", "pool")

    def __init__(self, nc, stack):
        self.nc = nc
        self.stack = stack
        self.eng = {"pe": nc.tensor, "act": nc.scalar, "dve": nc.vector,
                    "pool": nc.gpsimd, "sp": nc.sync}
        self.ops = {e: [] for e in self.eng}
        self.cnt = {e: 0 for e in self.ENG}
        self.esems = {e: [] for e in self.ENG}
        self.dsems = [stack.enter_context(nc.semaphore(f"dma{i}")) for i in range(NDMA)]
        self.dval = [0] * NDMA
        self.dnext = 0
        self.dnext_sw = 0
        self.waited = {e: {} for e in self.eng}
        self.lastw = {}
        self.readers = {}
        self.n_inst = 0

    def _esem(self, e, count):
        k = (count - 1) // EPOCH
        while len(self.esems[e]) <= k:
            self.esems[e].append(self.stack.enter_context(
                self.nc.semaphore(f"s_{e}_{len(self.esems[e])}")))
        return self.esems[e][k], (count - 1) % EPOCH + 1, k

    def _emit_wait(self, e, ev):
        if ev[0] == "e":
            _, src, count = ev
            if src == e and (e == "pe" or NOSELF):
                return
            sem, val, k = self._esem(src, count)
            key = ("e", src, k)
        else:
            _, idx, val = ev
            sem = self.dsems[idx]
            key = ("d", idx)
        if self.waited[e].get(key, 0) >= val:
            return
        self.waited[e][key] = val
        engobj = self.eng[e]
        self.ops[e].append(lambda engobj=engobj, sem=sem, val=val: engobj.wait_ge(sem, val))

    def _deps(self, e, reads, writes):
        evs = []
        for r in reads:
            if r in self.lastw:
                evs.append(self.lastw[r])
        for w in writes:
            if w in self.lastw:
                evs.append(self.lastw[w])
            evs.extend(self.readers.get(w, ()))
        for ev in evs:
            self._emit_wait(e, ev)

    def _commit(self, ev, reads, writes):
        for r in reads:
            self.readers.setdefault(r, []).append(ev)
        for w in writes:
            self.lastw[w] = ev
            self.readers[w] = []

    def op(self, e, fn, reads=(), writes=()):
        if self.n_inst >= MAXOPS:
            return
        self._deps(e, reads, writes)
        self.cnt[e] += 1
        count = self.cnt[e]
        sem, val, k = self._esem(e, count)
        self.ops[e].append(lambda fn=fn, sem=sem: fn().then_inc(sem, 1))
        self._commit(("e", e, count), reads, writes)
        self.n_inst += 1

    def dma(self, q, fn, reads=(), writes=()):
        if self.n_inst >= MAXOPS:
            return
        if q == "sp":
            idx = self.dnext
            self.dnext = (self.dnext + 1) % NDMA_HW
        else:
            idx = NDMA_HW + self.dnext_sw
            self.dnext_sw = (self.dnext_sw + 1) % (NDMA - NDMA_HW)
        if self.dval[idx] > 0:
            self._emit_wait(q, ("d", idx, self.dval[idx]))
        self._deps(q, reads, writes)
        self.dval[idx] += 16
        val = self.dval[idx]
        sem = self.dsems[idx]
        self.ops[q].append(lambda fn=fn, sem=sem: fn().then_inc(sem, 16))
        self._commit(("d", idx, val), reads, writes)
        self.n_inst += 1

    def barrier(self):
        for e in self.eng:
            for en in self.ENG:
                if self.cnt[en] and not (en == e):
                    self._emit_wait(e, ("e", en, self.cnt[en]))
                elif self.cnt[en] and e != "pe":
                    self._emit_wait(e, ("e", en, self.cnt[en]))
            for i in range(NDMA):
                if self.dval[i]:
                    self._emit_wait(e, ("d", i, self.dval[i]))
        self.lastw = {}
        self.readers = {}

    def finish(self):
        self.barrier()
        nc = self.nc
        with nc.Block() as block:
            @block.tensor
            def _(t):
                for f in self.ops["pe"]:
                    f()

            @block.scalar
            def _(t):
                for f in self.ops["act"]:
                    f()

            @block.vector
            def _(t):
                for f in self.ops["dve"]:
                    f()

            @block.gpsimd
            def _(t):
                for f in self.ops["pool"]:
                    f()

            @block.sync
            def _(t):
                for f in self.ops["sp"]:
                    f()


class Rot:
    def __init__(self, tiles, name):
        self.tiles = tiles
        self.name = name
        self.i = -1

    def next(self):
        self.i += 1
        k = self.i % len(self.tiles)
        return self.tiles[k], f"{self.name}{k}"


class Grp:
    pass


class KB:
    def __init__(self, layers=(0, 1, 2, 3), do_prompt=True, do_sample=True):
        self.layers = layers
        self.do_prompt = do_prompt
        self.do_sample = do_sample
        self.nc = bass.Bass("TRN2", target_bir_lowering=False)
        self.I = {}
        self.O = {}
        self.uid = 0
        self.nphase = 0
        self.max_phase = 10 ** 9

    def skip(self):
        self.nphase += 1
        return self.nphase > self.max_phase

    def inp(self, name, shape, dt=F32):
        self.I[name] = self.nc.dram_tensor(name, list(shape), dt, kind="ExternalInput").ap()
        return self.I[name]

    def outp(self, name, shape, dt=F32):
        self.O[name] = self.nc.dram_tensor(name, list(shape), dt, kind="ExternalOutput").ap()
        return self.O[name]

    def scr(self, name, shape, dt):
        return self.nc.dram_tensor(name, list(shape), dt, kind="Internal").ap()

    def nm(self, p):
        self.uid += 1
        return f"{p}_{self.uid}"

    def sb(self, shape, dt, name="t"):
        return self.ph.enter_context(self.nc.sbuf_tensor(self.nm(name), list(shape), dt))

    def ps(self, shape, dt, name="p"):
        return self.ph.enter_context(self.nc.psum_tensor(self.nm(name), list(shape), dt))

    @contextlib.contextmanager
    def phase(self):
        with contextlib.ExitStack() as ph:
            old = getattr(self, "ph", None)
            self.ph = ph
            yield
            self.S.barrier()
            self.ph = old

    def dma(self, out, in_, r=(), w=(), q="sp", **kw):
        eng = self.nc.sync if q == "sp" else self.nc.gpsimd
        self.S.dma(q, lambda: eng.dma_start(out=out, in_=in_, **kw), reads=r, writes=w)

    def mm(self, out, lhsT, rhs, start=True, stop=True, r=(), w=()):
        self.S.op("pe", lambda: self.nc.tensor.matmul(out, lhsT=lhsT, rhs=rhs, start=start, stop=stop),
                  reads=r, writes=w)

    def tr(self, out, in_, ident, r=(), w=()):
        self.S.op("pe", lambda: self.nc.tensor.transpose(out=out, in_=in_, identity=ident), reads=r, writes=w)

    def act(self, out, in_, func, r=(), w=(), **kw):
        self.S.op("act", lambda: self.nc.scalar.activation(out=out, in_=in_, func=func, **kw), reads=r, writes=w)

    def E(self, e):
        return self.nc.vector if e == "dve" else self.nc.gpsimd

    def tt(self, e, out, in0, in1, op, r=(), w=()):
        self.S.op(e, lambda: self.E(e).tensor_tensor(out=out, in0=in0, in1=in1, op=op), reads=r, writes=w)

    def ts(self, e, out, in0, s1, s2, op0, op1=None, r=(), w=()):
        if op1 is None:
            self.S.op(e, lambda: self.E(e).tensor_scalar(out=out, in0=in0, scalar1=s1, scalar2=None, op0=op0),
                      reads=r, writes=w)
        else:
            self.S.op(e, lambda: self.E(e).tensor_scalar(out=out, in0=in0, scalar1=s1, scalar2=s2, op0=op0, op1=op1),
                      reads=r, writes=w)

    def stt(self, out, in0, scalar, in1, op0, op1, r=(), w=()):
        self.S.op("dve", lambda: self.nc.vector.scalar_tensor_tensor(out=out, in0=in0, scalar=scalar, in1=in1,
                                                                     op0=op0, op1=op1), reads=r, writes=w)

    def cp(self, e, out, in_, r=(), w=()):
        if e == "act":
            self.S.op("act", lambda: self.nc.scalar.copy(out=out, in_=in_), reads=r, writes=w)
        else:
            self.S.op(e, lambda: self.E(e).tensor_copy(out=out, in_=in_), reads=r, writes=w)

    def memset(self, e, ap, val, w=()):
        self.S.op(e, lambda: self.E(e).memset(ap, val), writes=w)

    def declare(self):
        inp = self.inp
        inp("xp", [NPS * LP, D]); inp("xs", [LS, D])
        inp("st_ssd", [2, 2, 2048, 128]); inp("st_lru", [2, D]); inp("cond", [2, D])
        inp("mod_w", [4, D, 3 * D]); inp("mod_b", [4, 3 * D]); inp("ln_g", [4, D]); inp("ln_b", [4, D])
        inp("ssd_in_w", [2, D, SSD_PROJ]); inp("ssd_conv_w", [2, 4, 4096]); inp("ssd_conv_b", [2, 4096])
        inp("ssd_dt_bias", [2, 64]); inp("ssd_a_log", [2, 64]); inp("ssd_d", [2, 32])
        inp("ssd_norm_g", [2, 2048]); inp("ssd_out_w", [2, 2048, D])
        inp("hy_in_w", [D, 4096]); inp("hy_conv_w", [3, 3072]); inp("hy_conv_b", [3072])
        inp("hy_f_w1", [33, 64]); inp("hy_f_b1", [64]); inp("hy_f_w2", [64, 64]); inp("hy_f_b2", [64])
        inp("hy_f_w3", [64, 4096]); inp("hy_f_freq", [2, 64]); inp("hy_f_bias", [2, D]); inp("hy_out_w", [D, D])
        inp("lru_in_w", [D, 2048]); inp("lru_conv_w", [4, D]); inp("lru_conv_b", [D])
        inp("lru_gate_w", [2, 2, 4, 256, 256]); inp("lru_gate_b", [4, D]); inp("lru_a_param", [2, D])
        inp("lru_out_w", [D, D])
        inp("masks", [4, 128, 128])
        if 1 in self.layers:
            inp("dft_p", [4, LP // 128, 128, LP // 128, 128], BF16)
            inp("dft_s", [4, LS // 128, 128, LS // 128, 128], BF16)
            inp("dfti_p", [2, 1, 128, LP // 128, 256], BF16)
            inp("dfti_s", [2, LS // 512, 128, LS // 128, 512], BF16)
            inp("hz_p", [33, LP]); inp("hz_s", [33, LS])
            inp("win_p", [LP, D]); inp("win_s", [LS, D])
        self.outp("yp", [NPS * LP, D]); self.outp("ys", [LS, D])
        self.outp("nss", [NPS, 2, 2, 2048, 128]); self.outp("nsl", [NPS, 2, D])

    def build(self):
        nc = self.nc
        self.declare()
        with contextlib.ExitStack() as st:
            self.S = Sched(nc, st)
            self.ph = st
            self.ident_f = self.sb([128, 128], F32, "identf")
            self.ident_b = self.sb([128, 128], BF16, "identb")
            self.masks = self.sb([128, 4, 128], F32, "masks")
            self.ones_f = self.sb([128, 128], F32, "ones")
            self.zero_b = self.sb([128, 8, 4], BF16, "zerob")
            self.dma(self.masks[:], self.I["masks"].rearrange("m p i -> p m i"), w=["masks"])
            self.memset("pool", self.ident_f[:], 1.0, w=["identf"])
            self.S.op("pool", lambda: nc.gpsimd.affine_select(
                out=self.ident_f[:], in_=self.ident_f[:], pattern=[[-1, 128]], compare_op=ALU.is_equal,
                fill=0.0, base=0, channel_multiplier=1), reads=["identf"], writes=["identf"])
            self.cp("dve", self.ident_b[:], self.ident_f[:], r=["identf"], w=["identb"])
            self.memset("dve", self.ones_f[:], 1.0, w=["ones"])
            self.masks_r = self.sb([128, 4, 128], F32R, "masksr")
            self.ones_r = self.sb([128, 128], F32R, "onesr")
            self.cp("dve", self.masks_r[:], self.masks[:], r=["masks"], w=["masksr"])
            self.cp("dve", self.ones_r[:], self.ones_f[:], r=["ones"], w=["onesr"])
            self.memset("dve", self.zero_b[:], 0.0, w=["zerob"])
            self.lru_h0 = self.sb([128, 2, 8], F32, "lruh0")
            self.S.barrier()

            self.mod_scr = self.scr("mod_scr", [4, 2, 3 * D], F32)
            groups = []
            if self.do_prompt:
                g = Grp(); g.name = "p"; g.nseq = NPS; g.L = LP; g.cond = 0; g.latent = False
                g.x_in = self.I["xp"]; g.x_out = self.O["yp"]
                groups.append(g)
            if self.do_sample:
                g = Grp(); g.name = "s"; g.nseq = 1; g.L = LS; g.cond = 1; g.latent = True
                g.x_in = self.I["xs"]; g.x_out = self.O["ys"]
                groups.append(g)
            for g in groups:
                g.T = g.nseq * g.L
                g.W = g.nseq * (g.L + 3)
                g.xa = self.scr(f"xa_{g.name}", [g.T, D], F32)
                g.xb = self.scr(f"xb_{g.name}", [g.T, D], F32)
                g.hT = self.scr(f"hT_{g.name}", [8, 128, g.W], BF16)
                g.yT = self.scr(f"yT_{g.name}", [16, 128, g.T], BF16)
                g.xs_tm = self.scr(f"xstm_{g.name}", [g.T, 2048], BF16)
                g.b_tm = self.scr(f"btm_{g.name}", [g.T, 1024], BF16)
                g.bcT = self.scr(f"bcT_{g.name}", [16, 128, g.T], BF16)
                g.dta = self.scr(f"dta_{g.name}", [g.T, 128], F32)
                g.sloc = self.scr(f"sloc_{g.name}", [g.T // 128, 2, 128, 2048], F32)
                g.cdec = self.scr(f"cdec_{g.name}", [g.T // 128, 128, 64], F32)
                g.hprev = self.scr(f"hprev_{g.name}", [g.T // 128, 2, 128, 2048], BF16)
                g.fm = self.scr(f"fm_{g.name}", [4, 8, 128, g.T], F32)
                g.utm = self.scr(f"utm_{g.name}", [g.T, D], BF16)
                g.kw = self.scr(f"kw_{g.name}", [g.L, 4096], F32)
                g.khat = self.scr(f"khat_{g.name}", [2, 2, g.L, D], F32)
                g.yspec = self.scr(f"ysp_{g.name}", [2, 8, 128, g.L // 128, 128], BF16)
                g.khat2 = self.scr(f"khat2_{g.name}", [2, 2, g.L, D], F32)
            self.groups = groups

            self.modulation()
            for li in self.layers:
                last = (li == self.layers[-1])
                for g in groups:
                    X_in = g.x_in if li == self.layers[0] else (g.xa if (li % 2 == 1) else g.xb)
                    X_out = g.x_out if last else (g.xa if (li % 2 == 0) else g.xb)
                    col = g.latent and li == 3
                    self.pass_A(g, li, X_in, col)
                    kind = li % 3
                    if kind == 0:
                        self.ssd_layer(g, li // 3, li)
                        KC = 16
                    elif kind == 1:
                        self.hyena_layer(g)
                        KC = 8
                    else:
                        self.lru_layer(g)
                        KC = 8
                    wname = {0: "ssd_out_w", 1: "hy_out_w", 2: "lru_out_w"}[kind]
                    wout = self.I[wname][li // 3] if kind == 0 else self.I[wname]
                    self.pass_E(g, li, X_in, X_out, col, wout, KC)
            self.S.finish()
        return nc

    def xrows(self, g, X, s, c, col):
        if not col:
            r0 = s * g.L + c * 128
            return [(X[r0:r0 + 128, :], 0, 128)]
        Xv = X.rearrange("(r w) f -> w r f", w=64)
        return [(Xv[2 * c + wo], wo * 64, 64) for wo in range(2)]

    def load_T(self, dst, src2d, R, name):
        stg = self.sb([128, 128], F32, "ldT")
        if getattr(self, "_ldT_ph", None) is not self.ph:
            self._ldT_ph = self.ph
            self._ldT_ps = self.ps([128, 128], F32, "ldTp")
        pt = self._ldT_ps
        k = self.nm("ldT")
        self.dma(stg[0:R, :], src2d, w=[k])
        self.tr(pt[:, 0:R], stg[0:R, :], self.ident_f[0:R, 0:R], r=[k, "identf"], w=["ldTp"])
        self.cp("dve", dst, pt[:, 0:R], r=["ldTp"], w=[name])

    def modulation(self):
        if self.skip():
            return
        nc = self.nc
        with self.phase():
            cond = self.sb([2, D], F32, "cond")
            cs = self.sb([2, D], F32, "cs")
            condT = self.sb([128, 8, 2], BF16, "condT")
            pT = self.ps([128, 8, 2], F32, "pT")
            self.dma(cond[:], self.I["cond"], w=["cond"])
            self.act(cs[:], cond[:], AF.Silu, r=["cond"], w=["cs"])
            for k in range(8):
                self.tr(pT[:, k, :], cs[0:2, k * 128:(k + 1) * 128], self.ident_f[0:2, 0:2],
                        r=["cs", "identf"], w=["pT"])
            self.cp("dve", condT[:], pT[:], r=["pT"], w=["condT"])
            mw = self.sb([128, 8, 3 * D], BF16, "mw")
            mb = self.sb([2, 3 * D], F32, "mb")
            msb = self.sb([2, 3 * D], F32, "msb")
            pm = [self.ps([2, 512], F32, "pm") for _ in range(2)]
            for li in self.layers:
                for k in range(8):
                    self.dma(mw[:, k, :], self.I["mod_w"][li, k * 128:(k + 1) * 128, :], w=[f"mw{k}"], q="pool")
                for c in range(2):
                    self.dma(mb[c:c + 1, :], self.I["mod_b"][li:li + 1, :], w=["mb"])
                for t in range(6):
                    p = pm[t % 2]
                    for k in range(8):
                        self.mm(p[:], condT[:, k, :], mw[:, k, t * 512:(t + 1) * 512], start=(k == 0), stop=(k == 7),
                                r=["condT", f"mw{k}"], w=[f"pm{t % 2}"])
                    self.tt("dve", msb[:, t * 512:(t + 1) * 512], p[:], mb[:, t * 512:(t + 1) * 512], ALU.add,
                            r=[f"pm{t % 2}", "mb"], w=["msb"])
                self.ts("dve", msb[:, D:2 * D], msb[:, D:2 * D], 1.0, None, ALU.add, r=["msb"], w=["msb"])
                self.dma(self.mod_scr[li], msb[:], r=["msb"])

    def pass_A(self, g, li, X, col):
        if self.skip():
            return
        with self.phase():
            sc = self.sb([128, D], F32, "sc")
            sh = self.sb([128, D], F32, "sh")
            self.dma(sh[:], self.mod_scr[li, g.cond, 0:D].partition_broadcast(128), w=["sh"])
            self.dma(sc[:], self.mod_scr[li, g.cond, D:2 * D].partition_broadcast(128), w=["sc"])
            xr = Rot([self.sb([128, D], F32, "xA") for _ in range(2)], "xA")
            hr = Rot([self.sb([128, D], BF16, "hA") for _ in range(2)], "hA")
            tr_ = Rot([self.sb([128, 8, 128], BF16, "hTA") for _ in range(2)], "hTA")
            pr = Rot([self.ps([128, 8, 128], BF16, "pA") for _ in range(2)], "pA")
            nchunk = g.L // 128
            for s in range(g.nseq):
                base = s * (g.L + 3)
                self.dma(g.hT[:, :, base:base + 1].rearrange("k p t -> p k t"), self.zero_b[:, :, 0:1], r=["zerob"],
                         allow_slow_non_contiguous=True)
                self.dma(g.hT[:, :, base + g.L + 1:base + g.L + 3].rearrange("k p t -> p k t"),
                         self.zero_b[:, :, 0:2], r=["zerob"], allow_slow_non_contiguous=True)
                for c in range(nchunk):
                    xt, xn = xr.next()
                    for (src, p0, n) in self.xrows(g, X, s, c, col):
                        self.dma(xt[p0:p0 + n, :], src, w=[xn])
                    ht, hn = hr.next()
                    self.tt("dve", xt[:], xt[:], sc[:], ALU.mult, r=[xn, "sc"], w=[xn])
                    self.tt("dve", ht[:], xt[:], sh[:], ALU.add, r=[xn, "sh"], w=[hn])
                    pt, pn = pr.next()
                    for k in range(8):
                        self.tr(pt[:, k, :], ht[:, k * 128:(k + 1) * 128], self.ident_b[:], r=[hn, "identb"], w=[pn])
                    tt_, tn = tr_.next()
                    self.cp("act", tt_[:], pt[:], r=[pn], w=[tn])
                    c0 = base + 1 + c * 128
                    self.dma(g.hT[:, :, c0:c0 + 128].rearrange("k p t -> p k t"), tt_[:], r=[tn])

    def pass_E(self, g, li, X, Xo, col, wout, KC):
        if self.skip():
            return
        nc = self.nc
        with self.phase():
            wo = self.sb([128, KC, D], BF16, "wo")
            for k in range(KC):
                self.dma(wo[:, k, :], wout[k * 128:(k + 1) * 128, :], w=[f"wo{k}"], q="pool")
            gt = self.sb([128, D], F32, "gate")
            lg = self.sb([128, D], F32, "lng")
            lb = self.sb([128, D], F32, "lnb")
            self.dma(gt[:], self.mod_scr[li, g.cond, 2 * D:3 * D].partition_broadcast(128), w=["gate"])
            self.dma(lg[:], self.I["ln_g"][li].partition_broadcast(128), w=["lng"])
            self.dma(lb[:], self.I["ln_b"][li].partition_broadcast(128), w=["lnb"])
            yr = Rot([self.sb([128, KC, 128], BF16, "yE") for _ in range(2)], "yE")
            xr = Rot([self.sb([128, D], F32, "xE") for _ in range(2)], "xE")
            rr = Rot([self.sb([128, D], F32, "rE") for _ in range(2)], "rE")
            sr = Rot([self.sb([128, 16], F32, "sE") for _ in range(2)], "sE")
            pr = Rot([self.ps([128, D], F32, "pE") for _ in range(2)], "pE")
            nchunk = g.L // 128
            for s in range(g.nseq):
                for c in range(nchunk):
                    t0 = s * g.L + c * 128
                    yt, yn = yr.next()
                    self.dma(yt[:], g.yT[0:KC, :, t0:t0 + 128].rearrange("k p t -> p k t"), w=[yn])
                    xt, xn = xr.next()
                    for (src, p0, n) in self.xrows(g, X, s, c, col):
                        self.dma(xt[p0:p0 + n, :], src, w=[xn])
                    pt, pn = pr.next()
                    for hlf in range(2):
                        for k in range(KC):
                            self.mm(pt[:, hlf * 512:(hlf + 1) * 512], yt[:, k, :], wo[:, k, hlf * 512:(hlf + 1) * 512],
                                    start=(k == 0), stop=(k == KC - 1), r=[yn, f"wo{k}"], w=[pn + str(hlf)])
                    rt, rn = rr.next()
                    stt_, sn = sr.next()
                    for hlf in range(2):
                        sl = slice(hlf * 512, (hlf + 1) * 512)
                        self.tt("dve", rt[:, sl], pt[:, sl], gt[:, sl], ALU.mult, r=[pn + str(hlf), "gate"], w=[rn])
                    self.stt(rt[:], xt[:], ALPHA, rt[:], ALU.mult, ALU.add, r=[xn, rn], w=[rn])
                    self.S.op("dve", lambda stt_=stt_, rt=rt: nc.vector.bn_stats(out=stt_[:, 0:6], in_=rt[:, 0:512]),
                              reads=[rn], writes=[sn])
                    self.S.op("dve", lambda stt_=stt_, rt=rt: nc.vector.bn_stats(out=stt_[:, 6:12], in_=rt[:, 512:1024]),
                              reads=[rn], writes=[sn])
                    self.S.op("dve", lambda stt_=stt_: nc.vector.bn_aggr(out=stt_[:, 12:14], in_=stt_[:, 0:12]),
                              reads=[sn], writes=[sn])
                    self.ts("dve", stt_[:, 14:15], stt_[:, 13:14], LN_EPS, None, ALU.add, r=[sn], w=[sn])
                    self.act(stt_[:, 14:15], stt_[:, 14:15], AF.Sqrt, r=[sn], w=[sn])
                    self.S.op("dve", lambda stt_=stt_: nc.vector.reciprocal(out=stt_[:, 15:16], in_=stt_[:, 14:15]),
                              reads=[sn], writes=[sn])
                    self.ts("dve", rt[:], rt[:], stt_[:, 12:13], stt_[:, 15:16], ALU.subtract, ALU.mult,
                            r=[rn, sn], w=[rn])
                    self.tt("pool", rt[:], rt[:], lg[:], ALU.mult, r=[rn, "lng"], w=[rn])
                    self.tt("pool", rt[:], rt[:], lb[:], ALU.add, r=[rn, "lnb"], w=[rn])
                    for (dst, p0, n) in self.xrows(g, Xo, s, c, col):
                        self.dma(dst, rt[p0:p0 + n, :], r=[rn])

    def conv_fm(self, pp, pn, acc, an, dst, dn, cw, cb, fc, K, silu, woff):
        self.act(acc[:], pp[:, woff:woff + 256], AF.Identity, r=[pn, "cw", "cb"], w=[an],
                 scale=cw[:, 0, fc:fc + 1], bias=cb[:, fc:fc + 1])
        for k in range(1, K):
            self.stt(acc[:], pp[:, woff + k:woff + k + 256], cw[:, k, fc:fc + 1], acc[:], ALU.mult, ALU.add,
                     r=[pn, an, "cw"], w=[an])
        def fin():
            if silu:
                self.act(dst, acc[:], AF.Silu, r=[an], w=[dn])
            else:
                self.cp("act", dst, acc[:], r=[an], w=[dn])
        return fin

    def ssd_layer(self, g, slot, li):
        self.ssd_p1(g, slot)
        if _os.environ.get("DEBUG") == "dta":
            with self.phase():
                t = self.sb([128, 8, 128], F32, "dbg")
                self.dma(t[:], g.dta[0:1024, :].rearrange("(c p) f -> p c f", p=128), w=["dbg"])
                self.dma(self.O["yp"][:, 0:128].rearrange("(c p) f -> p c f", p=128), t[:], r=["dbg"])
        self.ssd_p2a(g, slot)
        self.ssd_pR(g, slot)
        self.ssd_p2b(g, slot)

    def ssd_p1(self, g, slot):
        if self.skip():
            return
        nc = self.nc
        with self.phase():
            w_in = self.I["ssd_in_w"][slot]
            wx = self.sb([128, 8, 4096], BF16, "wx")
            wd = self.sb([128, 8, 64], BF16, "wd")
            for k in range(8):
                self.dma(wx[:, k, :], w_in[k * 128:(k + 1) * 128, 2048:6144], w=[f"wx{k}"], q="pool")
                self.dma(wd[:, k, :], w_in[k * 128:(k + 1) * 128, 6144:6208], w=["wd"], q="pool")
            cw = self.sb([128, 4, 32], F32, "cw")
            cb = self.sb([128, 32], F32, "cb")
            self.load_T(cw[:].rearrange("p k c -> p (k c)"),
                        self.I["ssd_conv_w"][slot].rearrange("k (c p) -> (k c) p", p=128), 128, "cw")
            self.load_T(cb[:], self.I["ssd_conv_b"][slot].rearrange("(c p) -> c p", p=128), 32, "cb")
            dtb = self.sb([128, 64], F32, "dtb")
            abc = self.sb([128, 64], F32, "abc")
            self.dma(dtb[:], self.I["ssd_dt_bias"][slot].partition_broadcast(128), w=["dtb"])
            self.dma(abc[:], self.I["ssd_a_log"][slot].partition_broadcast(128), w=["abc"])
            self.act(abc[:], abc[:], AF.Exp, r=["abc"], w=["abc"])
            self.ts("dve", abc[:], abc[:], -1.0, None, ALU.mult, r=["abc"], w=["abc"])

            hwr = Rot([self.sb([128, 8, 260], BF16, "hw") for _ in range(2)], "hw")
            xbr = Rot([self.sb([128, 32, 256], BF16, "xbc") for _ in range(2)], "xbc")
            accr = Rot([self.sb([128, 256], F32, "acc") for _ in range(3)], "acc")
            tmr = Rot([self.sb([128, 1024], BF16, "tm") for _ in range(3)], "tm")
            dtr = Rot([self.sb([128, 128], F32, "dta") for _ in range(2)], "dta")
            ppr = Rot([self.ps([128, 512], F32, "pp") for _ in range(3)], "pp")
            ptr = Rot([self.ps([128, 8, 128], BF16, "ptr") for _ in range(2)], "ptr")
            pdr = Rot([self.ps([128, 64], F32, "pd") for _ in range(1)], "pd")
            for s in range(g.nseq):
                for j in range(g.L // 256):
                    c0 = s * (g.L + 3) + 256 * j
                    t0 = s * g.L + 256 * j
                    hw, hn = hwr.next()
                    self.dma(hw[:, :, 0:259], g.hT[:, :, c0:c0 + 259].rearrange("k p t -> p k t"), w=[hn])
                    xb, xn = xbr.next()
                    pend = None
                    for fc in range(32):
                        pp, pn = ppr.next()
                        for k in range(8):
                            self.mm(pp[:, 0:259], wx[:, k, fc * 128:(fc + 1) * 128], hw[:, k, 0:259],
                                    start=(k == 0), stop=(k == 7), r=[hn, f"wx{k}"], w=[pn])
                        acc, an = accr.next()
                        fin = self.conv_fm(pp, pn, acc, an, xb[:, fc, :], f"{xn}_{fc}", cw, cb, fc, 4, True, 0)
                        if pend is not None:
                            pend()
                        pend = fin
                    pend()
                    self.dma(g.bcT[:, :, t0:t0 + 256].rearrange("c p t -> p c t"), xb[:, 16:32, :],
                             r=[f"{xn}_{fc}" for fc in range(16, 32)])
                    for tcn in range(2):
                        for blk in range(3):
                            pt, ptn = ptr.next()
                            for i in range(8):
                                fc = blk * 8 + i
                                self.tr(pt[:, i, :], xb[:, fc, tcn * 128:(tcn + 1) * 128], self.ident_b[:],
                                        r=[f"{xn}_{fc}", "identb"], w=[ptn])
                            tm, tn = tmr.next()
                            self.cp("act" if blk % 2 == 0 else "dve", tm[:], pt[:].rearrange("p a b -> p (a b)"),
                                    r=[ptn], w=[tn])
                            r0 = t0 + tcn * 128
                            if blk < 2:
                                self.dma(g.xs_tm[r0:r0 + 128, blk * 1024:(blk + 1) * 1024], tm[:], r=[tn])
                            else:
                                self.dma(g.b_tm[r0:r0 + 128, :], tm[:], r=[tn])
                        pd, pdn = pdr.next()
                        for k in range(8):
                            self.mm(pd[:], hw[:, k, 1 + tcn * 128:1 + (tcn + 1) * 128], wd[:, k, :],
                                    start=(k == 0), stop=(k == 7), r=[hn, "wd"], w=[pdn])
                        dt, dn = dtr.next()
                        self.tt("dve", dt[:, 0:64], pd[:], dtb[:], ALU.add, r=[pdn, "dtb"], w=[dn])
                        self.act(dt[:, 0:64], dt[:, 0:64], AF.Exp, r=[dn], w=[dn])
                        self.act(dt[:, 0:64], dt[:, 0:64], AF.Ln, r=[dn], w=[dn], bias=1.0)
                        self.tt("dve", dt[:, 64:128], dt[:, 0:64], abc[:], ALU.mult, r=[dn, "abc"], w=[dn])
                        self.dma(g.dta[r0:r0 + 128, :], dt[:], r=[dn])

    def ssd_p2a(self, g, slot):
        if self.skip():
            return
        nc = self.nc
        with self.phase():
            xsr = Rot([self.sb([128, 2048], BF16, "xs") for _ in range(2)], "xs")
            btr = Rot([self.sb([128, 1024], BF16, "bt") for _ in range(2)], "bt")
            dtr = Rot([self.sb([128, 128], F32, "dta") for _ in range(2)], "dta")
            der = Rot([self.sb([128, 64], F32, "de") for _ in range(2)], "de")
            cdr = Rot([self.sb([128, 64], F32, "cd") for _ in range(2)], "cd")
            wdr = Rot([self.sb([128, 2048], BF16, "wdd") for _ in range(2)], "wdd")
            ssr = Rot([self.sb([128, 1024], F32, "ss") for _ in range(3)], "ss")
            pcr = Rot([self.ps([128, 128], F32, "pc") for _ in range(2)], "pc")
            psr = Rot([self.ps([128, 1024], F32, "psS") for _ in range(2)], "psS")
            arr = Rot([self.sb([128, 64], F32R, "ar") for _ in range(2)], "ar")
            if _os.environ.get("DEBUG") == "alloc":
                for t in pcr.tiles + psr.tiles + der.tiles:
                    print("ALLOC", t.name, self.nc.lookup_mloc(t))
            for ch in range(g.T // 128):
                r0 = ch * 128
                xs, xn = xsr.next(); bt, bn = btr.next(); dt, dn = dtr.next()
                self.dma(xs[:], g.xs_tm[r0:r0 + 128, :], w=[xn])
                self.dma(bt[:], g.b_tm[r0:r0 + 128, :], w=[bn])
                self.dma(dt[:], g.dta[r0:r0 + 128, :], w=[dn])
                pc, pcn = pcr.next()
                ar, arn = arr.next()
                self.cp("dve", ar[:], dt[:, 64:128], r=[dn], w=[arn])
                self.mm(pc[:, 0:32], self.masks_r[:, 2, :], ar[:, 0:32], r=["masksr", arn], w=[pcn])
                self.mm(pc[:, 32:64], self.masks_r[:, 3, :], ar[:, 32:64], r=["masksr", arn], w=[pcn])
                self.mm(pc[:, 64:128], self.ones_r[:], ar[:, 0:64], r=["onesr", arn], w=[pcn])
                de, den = der.next(); cd, cdn = cdr.next()
                self.act(de[:], pc[:, 0:64], AF.Exp, r=[pcn], w=[den])
                self.act(cd[:], pc[:, 64:128], AF.Exp, r=[pcn], w=[cdn])
                self.dma(g.cdec[ch], cd[:], r=[cdn])
                self.tt("dve", de[:], de[:], dt[:, 0:64], ALU.mult, r=[den, dn], w=[den])
                for d in range(2):
                    wdd, wn = wdr.next()
                    self.tt("dve", wdd[:].rearrange("p (h e) -> p h e", h=32), xs[:].rearrange("p (h e) -> p h e", h=32),
                            de[:, d * 32:(d + 1) * 32].unsqueeze(2).to_broadcast([128, 32, 64]), ALU.mult,
                            r=[xn, den], w=[wn])
                    for hlf in range(2):
                        pS, psn = psr.next()
                        for gg in range(4):
                            G8 = hlf * 4 + gg
                            self.mm(pS[:, gg * 256:(gg + 1) * 256], bt[:, G8 * 128:(G8 + 1) * 128],
                                    wdd[:, G8 * 256:(G8 + 1) * 256], r=[bn, wn], w=[psn + str(gg // 2)])
                        ss, sn = ssr.next()
                        for q in range(2):
                            self.cp("act" if q == 0 else "dve", ss[:, q * 512:(q + 1) * 512], pS[:, q * 512:(q + 1) * 512],
                                    r=[psn + str(q)], w=[sn])
                        self.dma(g.sloc[ch, d, :, hlf * 1024:(hlf + 1) * 1024], ss[:], r=[sn])

    def ssd_pR(self, g, slot):
        if self.skip():
            return
        nc = self.nc
        nchunk = g.L // 128
        with self.phase():
            hst = [self.sb([128, 2048], F32, "hst") for _ in range(2)]
            hbr = [Rot([self.sb([128, 2048], BF16, "hb") for _ in range(2)], f"hb{d}") for d in range(2)]
            slr = [Rot([self.sb([128, 2048], F32, "sl") for _ in range(2)], f"sl{d}") for d in range(2)]
            cdr = [Rot([self.sb([128, 64], F32, "cd") for _ in range(2)], f"cd{d}") for d in range(2)]
            stg = Rot([self.sb([128, 128], F32, "stg") for _ in range(3)], "stg")
            ptr = Rot([self.ps([128, 128], F32, "ptR") for _ in range(2)], "ptR")
            eng = ["dve", "pool"]
            for s in range(g.nseq):
                for d in range(2):
                    hn = f"hst{d}"
                    if g.latent:
                        for t in range(16):
                            sg, sgn = stg.next()
                            self.dma(sg[:], self.I["st_ssd"][slot, d, t * 128:(t + 1) * 128, :], w=[sgn])
                            pt, ptn = ptr.next()
                            self.tr(pt[:], sg[:], self.ident_f[:], r=[sgn, "identf"], w=[ptn])
                            self.cp("act", hst[d][:, t * 128:(t + 1) * 128], pt[:], r=[ptn], w=[hn])
                    else:
                        self.memset(eng[d], hst[d][:], 0.0, w=[hn])
                order = [list(range(nchunk)), list(range(nchunk - 1, -1, -1))]
                for i in range(nchunk):
                    for d in range(2):
                        c = order[d][i]
                        ch = s * nchunk + c
                        hn = f"hst{d}"
                        hb, hbn = hbr[d].next()
                        self.cp("act", hb[:], hst[d][:], r=[hn], w=[hbn])
                        self.dma(g.hprev[ch, d], hb[:], r=[hbn])
                        sl, sln = slr[d].next(); cd, cdn = cdr[d].next()
                        self.dma(sl[:], g.sloc[ch, d], w=[sln])
                        self.dma(cd[:], g.cdec[ch], w=[cdn])
                        self.tt(eng[d], hst[d][:].rearrange("p (h e) -> p h e", h=32),
                                hst[d][:].rearrange("p (h e) -> p h e", h=32),
                                cd[:, d * 32:(d + 1) * 32].unsqueeze(2).to_broadcast([128, 32, 64]), ALU.mult,
                                r=[hn, cdn], w=[hn])
                        self.tt(eng[d], hst[d][:], hst[d][:], sl[:], ALU.add, r=[hn, sln], w=[hn])
                if not g.latent:
                    for d in range(2):
                        hn = f"hst{d}"
                        for t in range(16):
                            pt, ptn = ptr.next()
                            self.tr(pt[:], hst[d][:, t * 128:(t + 1) * 128], self.ident_f[:], r=[hn, "identf"], w=[ptn])
                            sg, sgn = stg.next()
                            self.cp("act", sg[:], pt[:], r=[ptn], w=[sgn])
                            self.dma(self.O["nss"][s, slot, d, t * 128:(t + 1) * 128, :], sg[:], r=[sgn])

    def ssd_p2b(self, g, slot):
        if self.skip():
            return
        nc = self.nc
        with self.phase():
            w_in = self.I["ssd_in_w"][slot]
            wz = self.sb([128, 8, 2048], BF16, "wz")
            for k in range(8):
                self.dma(wz[:, k, :], w_in[k * 128:(k + 1) * 128, 0:2048], w=[f"wz{k}"], q="pool")
            ng = self.sb([128, 2048], F32, "ng")
            self.dma(ng[:], self.I["ssd_norm_g"][slot].partition_broadcast(128), w=["ng"])
            dsk = self.sb([128, 32], F32, "dsk")
            self.dma(dsk[:], self.I["ssd_d"][slot].partition_broadcast(128), w=["dsk"])
            mgt_r = self.sb([128, 128], F32R, "mgtr")
            mlt_r = self.sb([128, 128], F32R, "mltr")
            self.cp("dve", mgt_r[:], self.masks[:, 2, :], r=["masks"], w=["mgtr"])
            self.cp("dve", mlt_r[:], self.masks[:, 3, :], r=["masks"], w=["mltr"])
            Xl = [mgt_r, mlt_r]
            Xn = ["mgtr", "mltr"]
            Ym = [0, 1]
            Sm = [0, 1]

            hTr = Rot([self.sb([128, 8, 128], BF16, "hT") for _ in range(2)], "hT")
            xsr = Rot([self.sb([128, 2048], BF16, "xs") for _ in range(2)], "xs")
            bcr = Rot([self.sb([128, 16, 128], BF16, "bc") for _ in range(2)], "bc")
            dtr = Rot([self.sb([128, 128], F32, "dta") for _ in range(2)], "dta")
            hpr = Rot([self.sb([128, 2, 2048], BF16, "hp") for _ in range(2)], "hp")
            zsr = Rot([self.sb([128, 2048], BF16, "zs") for _ in range(2)], "zs")
            xdr = Rot([self.sb([128, 2, 2048], BF16, "xd") for _ in range(2)], "xd")
            ear = Rot([self.sb([128, 64], F32, "ea") for _ in range(2)], "ea")
            scr_ = Rot([self.sb([128, 2, 128], BF16, "scm") for _ in range(2)], "scm")
            Yr = Rot([self.sb([128, 4, 128], F32R, "Y") for _ in range(3)], "Y")
            Er = Rot([self.sb([128, 4, 128], BF16, "Ee") for _ in range(3)], "Ee")
            Mr = Rot([self.sb([128, 2, 4, 128], BF16, "Mm") for _ in range(2)], "Mm")
            t1r = Rot([self.sb([128, 256], F32, "t1") for _ in range(2)], "t1")
            t2r = Rot([self.sb([128, 256], F32, "t2") for _ in range(2)], "t2")
            ypr = Rot([self.sb([128, 2048], F32, "ypre") for _ in range(2)], "ypre")
            ynr = Rot([self.sb([128, 2048], BF16, "yn") for _ in range(2)], "yn")
            ssr = Rot([self.sb([128, 4], F32, "ssq") for _ in range(2)], "ssq")
            junk = self.sb([128, 2048], BF16, "junk")
            yTr = Rot([self.sb([128, 16, 128], BF16, "yT") for _ in range(2)], "yT")

            pzr = Rot([self.ps([128, 1024], F32, "pz") for _ in range(1)], "pz")
            pcs = Rot([self.ps([128, 512], F32, "pcs") for _ in range(1)], "pcs")
            psg = Rot([self.ps([128, 512], F32, "psg") for _ in range(2)], "psg")
            pyr = Rot([self.ps([128, 1024], F32, "py") for _ in range(1)], "py")
            ptr = Rot([self.ps([128, 8, 128], BF16, "ptb") for _ in range(1)], "ptb")
            nchunk = g.L // 128
            for s in range(g.nseq):
                for c in range(nchunk):
                    ch = s * nchunk + c
                    r0 = ch * 128
                    c0 = s * (g.L + 3) + 1 + c * 128
                    hT, hTn = hTr.next(); xs, xn = xsr.next(); bc, bcn = bcr.next(); dt, dn = dtr.next()
                    hp, hpn = hpr.next()
                    self.dma(hT[:], g.hT[:, :, c0:c0 + 128].rearrange("k p t -> p k t"), w=[hTn])
                    self.dma(xs[:], g.xs_tm[r0:r0 + 128, :], w=[xn])
                    self.dma(bc[:], g.bcT[:, :, r0:r0 + 128].rearrange("c p t -> p c t"), w=[bcn])
                    self.dma(dt[:], g.dta[r0:r0 + 128, :], w=[dn])
                    self.dma(hp[:], g.hprev[ch].rearrange("d p f -> p d f"), w=[hpn])
                    zs, zn = zsr.next()
                    for hlf in range(2):
                        pz, pzn = pzr.next()
                        for q in range(2):
                            for k in range(8):
                                col0 = hlf * 1024 + q * 512
                                self.mm(pz[:, q * 512:(q + 1) * 512], hT[:, k, :], wz[:, k, col0:col0 + 512],
                                        start=(k == 0), stop=(k == 7), r=[hTn, f"wz{k}"], w=[pzn + str(q)])
                        for q in range(2):
                            self.act(zs[:, hlf * 1024 + q * 512:hlf * 1024 + (q + 1) * 512], pz[:, q * 512:(q + 1) * 512],
                                     AF.Silu, r=[pzn + str(q)], w=[zn])
                    xd, xdn = xdr.next()
                    for d in range(2):
                        self.tt("dve" if d == 0 else "pool", xd[:, d, :].rearrange("p (h e) -> p h e", h=32),
                                xs[:].rearrange("p (h e) -> p h e", h=32),
                                dt[:, d * 32:(d + 1) * 32].unsqueeze(2).to_broadcast([128, 32, 64]), ALU.mult,
                                r=[xn, dn], w=[xdn + str(d)])
                    pc, pcn = pcs.next()
                    self.mm(pc[:, 0:32], self.masks[:, 0, :], dt[:, 64:96], r=["masks", dn], w=[pcn])
                    self.mm(pc[:, 32:64], self.masks[:, 1, :], dt[:, 96:128], r=["masks", dn], w=[pcn])
                    ea, ean = ear.next()
                    self.act(ea[:], pc[:, 0:64], AF.Exp, r=[pcn], w=[ean])
                    yp, ypn = ypr.next()
                    v3 = lambda ap: ap.rearrange("p (h e) -> p h e", h=4)

                    def stA(G8):
                        self.mm(pc[:, 128:256], bc[:, G8, :], bc[:, 8 + G8, :], r=[bcn], w=[pcn])
                        sm, smn = scr_.next()
                        for d in range(2):
                            self.tt("dve", sm[:, d, :], pc[:, 128:256], self.masks[:, Sm[d], :], ALU.mult,
                                    r=[pcn, "masks"], w=[smn + str(d)])
                        Mt, Mn = Mr.next()
                        for d in range(2):
                            Y, Yn = Yr.next()
                            self.tt("pool", Y[:], self.masks[:, Ym[d], :].unsqueeze(1).to_broadcast([128, 4, 128]),
                                    dt[:, 64 + d * 32 + G8 * 4:64 + d * 32 + G8 * 4 + 4].unsqueeze(2).to_broadcast([128, 4, 128]),
                                    ALU.mult, r=["masks", dn], w=[Yn])
                            pg, pgn = psg.next()
                            self.mm(pg[:], Xl[d][:], Y[:].rearrange("p a b -> p (a b)"), r=[Xn[d], Yn], w=[pgn])
                            Et, En = Er.next()
                            self.act(Et[:].rearrange("p a b -> p (a b)"), pg[:], AF.Exp, r=[pgn], w=[En])
                            self.tt("dve", Mt[:, d, :, :], Et[:], sm[:, d, :].unsqueeze(1).to_broadcast([128, 4, 128]),
                                    ALU.mult, r=[En, smn + str(d)], w=[Mn + str(d)])
                        return Mt, Mn

                    def stB(G8, Mt, Mn):
                        py, pyn = pyr.next()
                        for h in range(4):
                            H = G8 * 4 + h
                            for d in range(2):
                                self.mm(py[:, h * 64:(h + 1) * 64], Mt[:, d, h, :], xd[:, d, H * 64:(H + 1) * 64],
                                        start=(d == 0), stop=(d == 1), r=[Mn + str(d), xdn + str(d)], w=[pyn + "d"])
                        for d in range(2):
                            self.mm(py[:, 512 + d * 256:512 + (d + 1) * 256], bc[:, 8 + G8, :],
                                    hp[:, d, G8 * 256:(G8 + 1) * 256], r=[bcn, hpn], w=[pyn + "o"])
                        return py, pyn

                    def stC(G8, py, pyn):
                        t1, t1n = t1r.next(); t2, t2n = t2r.next()
                        for d, (tt_, tn_) in enumerate(((t1, t1n), (t2, t2n))):
                            self.tt("dve", v3(tt_[:]), v3(py[:, 512 + d * 256:512 + (d + 1) * 256]),
                                    ea[:, d * 32 + G8 * 4:d * 32 + G8 * 4 + 4].unsqueeze(2).to_broadcast([128, 4, 64]),
                                    ALU.mult, r=[pyn + "o", ean], w=[tn_])
                        self.tt("pool", t1[:], t1[:], t2[:], ALU.add, r=[t1n, t2n], w=[t1n])
                        self.tt("dve", t2[:], py[:, 0:256], t1[:], ALU.add, r=[pyn + "d", t1n], w=[t2n])
                        self.tt("pool", v3(t1[:]), v3(xs[:, G8 * 256:(G8 + 1) * 256]),
                                dsk[:, G8 * 4:G8 * 4 + 4].unsqueeze(2).to_broadcast([128, 4, 64]), ALU.mult,
                                r=[xn, "dsk", t1n], w=[t1n])
                        self.tt("pool", yp[:, G8 * 256:(G8 + 1) * 256], t1[:], t2[:], ALU.add, r=[t1n, t2n],
                                w=[ypn + str(G8)])

                    MA = stA(0)
                    for G8 in range(8):
                        PB = stB(G8, *MA)
                        if G8 < 7:
                            MA = stA(G8 + 1)
                        stC(G8, *PB)
                    ypa = [ypn + str(i) for i in range(8)]
                    self.tt("dve", yp[:], yp[:], zs[:], ALU.mult, r=ypa + [zn], w=ypa)
                    sq, sqn = ssr.next()
                    self.act(junk[:], yp[:], AF.Square, r=ypa, w=["junk", sqn], accum_out=sq[:, 0:1])
                    self.ts("dve", sq[:, 1:2], sq[:, 0:1], 1.0 / 2048.0, RMS_EPS, ALU.mult, ALU.add, r=[sqn], w=[sqn])
                    self.act(sq[:, 1:2], sq[:, 1:2], AF.Sqrt, r=[sqn], w=[sqn])
                    self.S.op("dve", lambda sq=sq: nc.vector.reciprocal(out=sq[:, 2:3], in_=sq[:, 1:2]),
                              reads=[sqn], writes=[sqn])
                    yn_, ynn = ynr.next()
                    self.stt(yn_[:], yp[:], sq[:, 2:3], ng[:], ALU.mult, ALU.mult, r=ypa + [sqn, "ng"], w=[ynn])
                    yT, yTn = yTr.next()
                    for blk in range(2):
                        pt, ptn = ptr.next()
                        for i in range(8):
                            fc = blk * 8 + i
                            self.tr(pt[:, i, :], yn_[:, fc * 128:(fc + 1) * 128], self.ident_b[:], r=[ynn, "identb"], w=[ptn])
                        self.cp("act", yT[:, blk * 8:(blk + 1) * 8, :], pt[:], r=[ptn], w=[yTn])
                    self.dma(g.yT[:, :, r0:r0 + 128].rearrange("k p t -> p k t"), yT[:], r=[yTn])

    def lru_layer(self, g):
        self.lru_p1(g)
        self.lru_p2(g)

    def lru_p1(self, g):
        if self.skip():
            return
        nc = self.nc
        with self.phase():
            win = self.sb([128, 8, 2048], BF16, "win")
            for k in range(8):
                self.dma(win[:, k, :], self.I["lru_in_w"][k * 128:(k + 1) * 128, :], w=[f"win{k}"], q="pool")
            gw = self.sb([128, 32, 256], BF16, "gw")
            gsrc = self.I["lru_gate_w"].rearrange("d g n (kc p) j -> (d g n kc) p j", p=128)
            for i in range(32):
                self.dma(gw[:, i, :], gsrc[i], w=["gw"], q="pool")
            cw = self.sb([128, 4, 8], F32, "cw")
            cb = self.sb([128, 8], F32, "cb")
            gb = self.sb([128, 4, 8], F32, "gb")
            nsp = self.sb([128, 2, 8], F32, "nsp")
            h0 = self.lru_h0
            self.load_T(cw[:].rearrange("p k c -> p (k c)"),
                        self.I["lru_conv_w"].rearrange("k (c p) -> (k c) p", p=128), 32, "cw")
            self.load_T(cb[:], self.I["lru_conv_b"].rearrange("(c p) -> c p", p=128), 8, "cb")
            self.load_T(gb[:].rearrange("p k c -> p (k c)"),
                        self.I["lru_gate_b"].rearrange("k (c p) -> (k c) p", p=128), 32, "gb")
            self.load_T(nsp[:].rearrange("p k c -> p (k c)"),
                        self.I["lru_a_param"].rearrange("k (c p) -> (k c) p", p=128), 16, "nsp")
            self.act(nsp[:], nsp[:], AF.Exp, r=["nsp"], w=["nsp"], scale=-1.0)
            self.act(nsp[:], nsp[:], AF.Ln, r=["nsp"], w=["nsp"], bias=1.0)
            self.ts("dve", nsp[:], nsp[:], -8.0, None, ALU.mult, r=["nsp"], w=["nsp"])
            if g.latent:
                self.load_T(h0[:].rearrange("p k c -> p (k c)"),
                            self.I["st_lru"].rearrange("k (c p) -> (k c) p", p=128), 16, "h0")
            else:
                self.memset("dve", h0[:], 0.0, w=["h0"])
            hwr = Rot([self.sb([128, 8, 260], BF16, "hw") for _ in range(2)], "hw")
            xrr = Rot([self.sb([128, 8, 256], F32, "xr") for _ in range(1)], "xr")
            xbr = Rot([self.sb([128, 8, 256], BF16, "xrb") for _ in range(1)], "xrb")
            zsr = Rot([self.sb([128, 8, 256], F32, "zs") for _ in range(2)], "zs")
            gtr = [Rot([self.sb([128, 8, 256], F32, "gt") for _ in range(1)], f"gt{i}") for i in range(4)]
            aar = [Rot([self.sb([128, 8, 256], F32, "aa") for _ in range(1 + d)], f"aa{d}") for d in range(2)]
            bxr = [Rot([self.sb([128, 8, 256], F32, "bx") for _ in range(1 + d)], f"bx{d}") for d in range(2)]
            tmr = Rot([self.sb([128, 8, 256], F32, "tmpl") for _ in range(1)], "tmpl")
            yfr = Rot([self.sb([128, 8, 256], F32, "yf") for _ in range(2)], "yf")
            accr = Rot([self.sb([128, 256], F32, "acc") for _ in range(3)], "acc")
            fin = self.sb([128, 8], F32, "fin")
            ppr = Rot([self.ps([128, 512], F32, "pp") for _ in range(3)], "pp")
            pgr = Rot([self.ps([128, 256], F32, "pg") for _ in range(3)], "pg")
            for s in range(g.nseq):
                yf_prev = None
                ntile = g.L // 256
                for j in range(ntile):
                    c0 = s * (g.L + 3) + 256 * j
                    t0 = s * g.L + 256 * j
                    hw, hn = hwr.next()
                    self.dma(hw[:, :, 0:259], g.hT[:, :, c0:c0 + 259].rearrange("k p t -> p k t"), w=[hn])
                    xr, xn = xrr.next(); xb, xbn = xbr.next(); zs, zn = zsr.next()
                    pend = None
                    for fc in range(16):
                        pp, pn = ppr.next()
                        for k in range(8):
                            self.mm(pp[:, 0:259], win[:, k, fc * 128:(fc + 1) * 128], hw[:, k, 0:259],
                                    start=(k == 0), stop=(k == 7), r=[hn, f"win{k}"], w=[pn])
                        if fc < 8:
                            acc, an = accr.next()
                            cfin = self.conv_fm(pp, pn, acc, an, xr[:, fc, :], f"{xn}_{fc}", cw, cb, fc, 4, False, 0)

                            def fin2(cfin=cfin, fc=fc):
                                cfin()
                                self.cp("pool", xb[:, fc, :], xr[:, fc, :], r=[f"{xn}_{fc}"], w=[f"{xbn}_{fc}"])
                            if pend is not None:
                                pend()
                            pend = fin2
                        else:
                            if pend is not None:
                                pend()
                                pend = None
                            self.act(zs[:, fc - 8, :], pp[:, 1:257], AF.Silu, r=[pn], w=[zn])
                    gts = [gtr[i].next() for i in range(4)]
                    for dg in range(4):
                        gt, gn = gts[dg]
                        for n in range(4):
                            for jc in range(2):
                                pg, pgn = pgr.next()
                                for kc in range(2):
                                    self.mm(pg[:], gw[:, (dg * 4 + n) * 2 + kc, jc * 128:(jc + 1) * 128], xb[:, n * 2 + kc, :],
                                            start=(kc == 0), stop=(kc == 1), r=["gw", f"{xbn}_{n * 2 + kc}"], w=[pgn])
                                ch = n * 2 + jc
                                self.act(gt[:, ch, :], pg[:], AF.Sigmoid, r=[pgn, "gb"], w=[gn], bias=gb[:, dg, ch:ch + 1])
                    xall = [f"{xn}_{fc}" for fc in range(8)]
                    ab = []
                    for d in range(2):
                        aa, aan = aar[d].next(); bx, bxn = bxr[d].next(); tm, tmn = tmr.next()
                        rt, rn = gts[d * 2]; it, itn = gts[d * 2 + 1]
                        for ch in range(8):
                            self.act(aa[:, ch, :], rt[:, ch, :], AF.Exp, r=[rn, "nsp"], w=[aan], scale=nsp[:, d, ch:ch + 1])
                        self.tt("dve", tm[:], aa[:], aa[:], ALU.mult, r=[aan], w=[tmn])
                        self.ts("dve", tm[:], tm[:], -1.0, 1.0, ALU.mult, ALU.add, r=[tmn], w=[tmn])
                        self.ts("dve", tm[:], tm[:], 0.0, None, ALU.max, r=[tmn], w=[tmn])
                        self.act(tm[:], tm[:], AF.Sqrt, r=[tmn], w=[tmn])
                        self.tt("pool", tm[:], tm[:], it[:], ALU.mult, r=[tmn, itn], w=[tmn])
                        self.tt("pool", bx[:], tm[:], xr[:], ALU.mult, r=[tmn] + xall, w=[bxn])
                        ab.append((aa, aan, bx, bxn))
                    yf, yfn = yfr.next()
                    aa, aan, bx, bxn = ab[0]
                    for ch in range(8):
                        init = h0[:, 0, ch:ch + 1] if yf_prev is None else yf_prev[0][:, ch, 255:256]
                        rr = [aan, bxn, "h0"] + ([yf_prev[1]] if yf_prev is not None else [])
                        self.S.op("dve", lambda yf=yf, aa=aa, bx=bx, ch=ch, init=init: nc.vector.tensor_tensor_scan(
                            out=yf[:, ch, :], data0=aa[:, ch, :], data1=bx[:, ch, :], initial=init,
                            op0=ALU.mult, op1=ALU.add), reads=rr, writes=[yfn])
                    yf_prev = (yf, yfn)
                    fmv = lambda slot: g.fm[slot, :, :, t0:t0 + 256].rearrange("c p t -> p c t")
                    self.dma(fmv(0), yf[:], r=[yfn])
                    self.dma(fmv(1), ab[1][0][:], r=[ab[1][1]])
                    self.dma(fmv(2), ab[1][2][:], r=[ab[1][3]])
                    self.dma(fmv(3), zs[:], r=[zn])
                if not g.latent:
                    self.cp("dve", fin[:], yf_prev[0][:, :, 255], r=[yf_prev[1]], w=["fin"])
                    self.dma(self.O["nsl"][s, 0, :].rearrange("(c p) -> p c", p=128), fin[:], r=["fin"],
                             allow_slow_non_contiguous=True)

    def lru_p2(self, g):
        if self.skip():
            return
        nc = self.nc
        with self.phase():
            h0 = self.lru_h0
            ldr = [Rot([self.sb([128, 8, 256], F32, "ld") for _ in range(2)], f"ld{i}") for i in range(4)]
            ybr = Rot([self.sb([128, 8, 256], F32, "yb") for _ in range(2)], "yb")
            yTr = Rot([self.sb([128, 8, 256], BF16, "yTl") for _ in range(2)], "yTl")
            fin = self.sb([128, 8], F32, "fin")
            for s in range(g.nseq):
                yb_prev = None
                ntile = g.L // 256
                for j in range(ntile - 1, -1, -1):
                    t0 = s * g.L + 256 * j
                    lt = []
                    for i in range(4):
                        t, tn = ldr[i].next()
                        self.dma(t[:], g.fm[i, :, :, t0:t0 + 256].rearrange("c p t -> p c t"), w=[tn])
                        lt.append((t, tn))
                    (yf, yfn), (aa, aan), (bx, bxn), (zs, zn) = lt
                    yb, ybn = ybr.next()
                    for ch in range(8):
                        init = h0[:, 1, ch:ch + 1] if yb_prev is None else yb_prev[0][:, ch, 0:1]
                        rr = [aan, bxn, "h0"] + ([yb_prev[1]] if yb_prev is not None else [])
                        self.S.op("dve", lambda yb=yb, aa=aa, bx=bx, ch=ch, init=init: nc.vector.tensor_tensor_scan(
                            out=yb[:, ch, ::-1], data0=aa[:, ch, ::-1], data1=bx[:, ch, ::-1], initial=init,
                            op0=ALU.mult, op1=ALU.add), reads=rr, writes=[ybn])
                    yb_prev = (yb, ybn)
                    self.tt("pool", yf[:], yf[:], yb[:], ALU.add, r=[yfn, ybn], w=[yfn])
                    yT, yTn = yTr.next()
                    self.tt("pool", yT[:], yf[:], zs[:], ALU.mult, r=[yfn, zn], w=[yTn])
                    self.dma(g.yT[0:8, :, t0:t0 + 256].rearrange("c p t -> p c t"), yT[:], r=[yTn])
                if not g.latent:
                    self.cp("dve", fin[:], yb_prev[0][:, :, 0], r=[yb_prev[1]], w=["fin"])
                    self.dma(self.O["nsl"][s, 1, :].rearrange("(c p) -> p c", p=128), fin[:], r=["fin"],
                             allow_slow_non_contiguous=True)

    def dbg_rows(self, dst_row0, src2d, nrows):
        with self.phase():
            for r0 in range(0, nrows, 128):
                t = self.sb([128, 1024], F32, "dbg")
                self.dma(t[:], src2d[r0:r0 + 128, :], w=["dbg"])
                self.dma(self.O["yp"][dst_row0 + r0:dst_row0 + r0 + 128, :], t[:], r=["dbg"])

    def hyena_layer(self, g):
        self.hy_filters(g)
        if _os.environ.get("DEBUG") == "hyk" and g.name == "p":
            self.dbg_rows(0, g.khat2[0, 0], 256)
            self.dbg_rows(256, g.khat2[0, 1], 256)
            self.dbg_rows(512, g.kw[:, 0:1024], 256)
            self.dbg_rows(768, g.kw[:, 1024:2048], 256)
            return
        self.hy_p1(g)
        for o in range(2):
            for s in range(g.nseq):
                self.hy_fwd(g, o, s)
                self.hy_inv(g, o, s)

    def _vec(self, dst, src1d, n, name):
        self.dma(dst, src1d.rearrange("(p o) -> p o", o=1), w=[name], allow_slow_non_contiguous=True)

    def _hy_mlp(self, g, hd2, w3r):
        nc = self.nc
        L = g.L
        nm = g.name
        TWO_PI = 2.0 * math.pi
        MAGIC = 12582912.0
        with self.phase():
            zT = self.sb([33, L], F32, "zT")
            self.dma(zT[:], self.I[f"hz_{nm}"], w=["zT"])
            w1 = self.sb([33, 64], F32, "w1"); w2 = self.sb([64, 64], F32, "w2")
            self.dma(w1[:], self.I["hy_f_w1"], w=["w1"]); self.dma(w2[:], self.I["hy_f_w2"], w=["w2"])
            w3 = self.sb([64, 4096], F32, "w3")
            self.dma(w3[:], self.I["hy_f_w3"], w=["w3"])
            self.cp("pool", w3r[:], w3[:], r=["w3"], w=["w3r"])
            pv = self.sb([64, 6], F32, "pv")
            self._vec(pv[:, 0:1], self.I["hy_f_b1"], 64, "pv"); self._vec(pv[:, 1:2], self.I["hy_f_b2"], 64, "pv")
            self._vec(pv[:, 2:3], self.I["hy_f_freq"][0], 64, "pv"); self._vec(pv[:, 3:4], self.I["hy_f_freq"][1], 64, "pv")
            self.tt("dve", pv[:, 4:6], pv[:, 0:2], pv[:, 2:4], ALU.mult, r=["pv"], w=["pv"])
            hd1 = self.sb([64, L], F32, "hd1")
            argr = Rot([self.sb([64, 512], F32, "arg") for _ in range(2)], "arg")
            nr = Rot([self.sb([64, 512], F32, "nq") for _ in range(2)], "nq")
            phr = Rot([self.ps([64, 512], F32, "ph") for _ in range(2)], "ph")
            TW = min(512, L)
            for layer in range(2):
                for ti in range(L // TW):
                    sl = slice(ti * TW, (ti + 1) * TW)
                    ph, phn = phr.next()
                    if layer == 0:
                        self.mm(ph[:, 0:TW], w1[:], zT[:, sl], r=["w1", "zT"], w=[phn])
                    else:
                        self.mm(ph[:, 0:TW], w2[:], hd1[:, sl], r=["w2", "hd1"], w=[phn])
                    arg, an = argr.next(); nq, nn = nr.next()
                    self.act(arg[:, 0:TW], ph[:, 0:TW], AF.Identity, r=[phn, "pv"], w=[an],
                             scale=pv[:, 2 + layer:3 + layer], bias=pv[:, 4 + layer:5 + layer])
                    self.ts("dve", nq[:, 0:TW], arg[:, 0:TW], 1.0 / TWO_PI, MAGIC, ALU.mult, ALU.add, r=[an], w=[nn])
                    self.ts("dve", nq[:, 0:TW], nq[:, 0:TW], MAGIC, None, ALU.subtract, r=[nn], w=[nn])
                    self.stt(arg[:, 0:TW], nq[:, 0:TW], -TWO_PI, arg[:, 0:TW], ALU.mult, ALU.add, r=[nn, an], w=[an])
                    self.ts("dve", arg[:, 0:TW], arg[:, 0:TW], math.pi, -math.pi, ALU.min, ALU.max, r=[an], w=[an])
                    if layer == 0:
                        self.act(hd1[:, sl], arg[:, 0:TW], AF.Sin, r=[an], w=["hd1"])
                    else:
                        self.act(hd2[:, sl], arg[:, 0:TW], AF.Sin, r=[an], w=["hd2"])

    def hy_filters(self, g):
        if self.skip():
            return
        nc = self.nc
        L = g.L
        nL = L // 128
        nm = g.name
        TWO_PI = 2.0 * math.pi
        MAGIC = 12582912.0
        with self.phase():
            hd2 = self.sb([64, L], F32R, "hd2")
            w3r = self.sb([64, 4096], F32R, "w3r")
            self._hy_mlp(g, hd2, w3r)
            asum = self.sb([128, 4096], F32, "asum")
            self.memset("pool", asum[:], 0.0, w=["asum"])
            winr = Rot([self.sb([128, 1024], F32, "winw") for _ in range(2)], "winw")
            kwr = Rot([self.sb([128, 4096], F32, "kwt") for _ in range(2)], "kwt")
            kabs = self.sb([128, 4096], F32, "kabs")
            pkr = Rot([self.ps([128, 512], F32, "pk") for _ in range(3)], "pk")
            for tc in range(nL):
                wt, wn = winr.next()
                self.dma(wt[:], self.I[f"win_{nm}"][tc * 128:(tc + 1) * 128, :], w=[wn])
                kt, kn = kwr.next()
                for ti in range(8):
                    pk, pkn = pkr.next()
                    self.mm(pk[:], hd2[:, tc * 128:(tc + 1) * 128], w3r[:, ti * 512:(ti + 1) * 512], r=["hd2", "w3r"], w=[pkn])
                    self.tt("dve", kt[:, ti * 512:(ti + 1) * 512], pk[:], wt[:, (ti % 2) * 512:(ti % 2 + 1) * 512], ALU.mult,
                            r=[pkn, wn], w=[kn])
                self.act(kabs[:], kt[:], AF.Abs, r=[kn], w=["kabs"])
                self.tt("pool", asum[:], asum[:], kabs[:], ALU.add, r=["kabs", "asum"], w=["asum"])
                self.dma(g.kw[tc * 128:(tc + 1) * 128, :], kt[:], r=[kn])
            asr = self.sb([128, 4096], F32R, "asr")
            self.cp("dve", asr[:], asum[:], r=["asum"], w=["asr"])
            tot = self.sb([128, 4096], F32, "tot")
            for ti in range(8):
                pk, pkn = pkr.next()
                self.mm(pk[:], self.ones_r[:], asr[:, ti * 512:(ti + 1) * 512], r=["onesr", "asr"], w=[pkn])
                self.cp("act", tot[:, ti * 512:(ti + 1) * 512], pk[:], r=[pkn], w=["tot"])
            rn = self.sb([128, 2, 1024], F32, "rn")
            tv = tot[:].rearrange("p (o d c) -> p o d c", o=2, d=2)
            self.tt("dve", rn[:], tv[:, :, 0, :], tv[:, :, 1, :], ALU.add, r=["tot"], w=["rn"])
            rn0 = self.sb([128, 2, 1024], F32, "rn0")
            self.act(rn0[:], rn[:], AF.Ln, r=["rn"], w=["rn0"])
            self.act(rn[:], rn0[:], AF.Exp, r=["rn0"], w=["rn"], scale=-1.0)
            self.ts("dve", rn[:], rn[:], 2.0 / (2 * L), None, ALU.mult, r=["rn"], w=["rn"])
            self.dma(g.khat[0, 0, 0:128, :], rn[:, 0, :], r=["rn"])
            self.dma(g.khat[0, 1, 0:128, :], rn[:, 1, :], r=["rn"])
        with self.phase():
            rn = self.sb([128, 2, 1024], F32, "rn")
            self.dma(rn[:, 0, :], g.khat[0, 0, 0:128, :], w=["rn"])
            self.dma(rn[:, 1, :], g.khat[0, 1, 0:128, :], w=["rn"])
            self.S.barrier()
            ks = self.sb([128, nL, 1024], BF16, "ks")
            kldr = Rot([self.sb([128, 2, 1024], F32, "kld") for _ in range(2)], "kld")
            mbr = Rot([self.sb([128, nL, 128], BF16, "mb") for _ in range(2)], "mb")
            kor = Rot([self.sb([128, 1024], F32, "ko") for _ in range(2)], "ko")
            pur = Rot([self.ps([128, 1024], F32, "pu") for _ in range(2)], "pu")
            for o in range(2):
                for cs in range(2):
                    for tc in range(nL):
                        kl, kln = kldr.next()
                        self.dma(kl[:], g.kw[tc * 128:(tc + 1) * 128, o * 2048:(o + 1) * 2048].rearrange("p (d c) -> p d c", d=2),
                                 w=[kln])
                        if tc == 0:
                            self.memset("dve", kl[0:1, 1, :], 0.0, w=[kln])
                        self.tt("dve", kl[:, 0, :], kl[:, 0, :], kl[:, 1, :], ALU.add if cs == 0 else ALU.subtract,
                                r=[kln], w=[kln])
                        self.tt("pool", ks[:, tc, :], kl[:, 0, :], rn[:, o, :], ALU.mult, r=[kln, "rn"], w=[f"ks{tc}"])
                    for fc in range(nL):
                        mb, mbn = mbr.next()
                        self.dma(mb[:], self.I[f"dft_{nm}"][2 + cs, fc], w=[mbn])
                        pu, pun = pur.next()
                        for tc in range(nL):
                            for hlf in range(2):
                                self.mm(pu[:, hlf * 512:(hlf + 1) * 512], mb[:, tc, :], ks[:, tc, hlf * 512:(hlf + 1) * 512],
                                        start=(tc == 0), stop=(tc == nL - 1), r=[mbn, f"ks{tc}"],
                                        w=[pun + str(hlf)])
                        ko, kon = kor.next()
                        for hlf in range(2):
                            self.cp("act", ko[:, hlf * 512:(hlf + 1) * 512], pu[:, hlf * 512:(hlf + 1) * 512],
                                    r=[pun + str(hlf)], w=[kon])
                        self.dma(g.khat2[o, cs, fc * 128:(fc + 1) * 128, :], ko[:], r=[kon])

    def hy_p1(self, g):
        if self.skip():
            return
        nc = self.nc
        with self.phase():
            win = self.sb([128, 8, 4096], BF16, "win")
            for k in range(8):
                self.dma(win[:, k, :], self.I["hy_in_w"][k * 128:(k + 1) * 128, :], w=[f"win{k}"], q="pool")
            cw = self.sb([128, 3, 24], F32, "cw")
            cb = self.sb([128, 24], F32, "cb")
            self.load_T(cw[:].rearrange("p k c -> p (k c)"),
                        self.I["hy_conv_w"].rearrange("k (c p) -> (k c) p", p=128), 72, "cw")
            self.load_T(cb[:], self.I["hy_conv_b"].rearrange("(c p) -> c p", p=128), 24, "cb")
            hwr = Rot([self.sb([128, 8, 260], BF16, "hw") for _ in range(2)], "hw")
            fmr = [Rot([self.sb([128, 8, 256], F32, "fmt") for _ in range(2)], f"fmt{i}") for i in range(4)]
            vbr = Rot([self.sb([128, 8, 256], BF16, "vb") for _ in range(2)], "vb")
            utr = Rot([self.sb([128, 1024], BF16, "ut") for _ in range(2)], "ut")
            accr = Rot([self.sb([128, 256], F32, "acc") for _ in range(3)], "acc")
            ppr = Rot([self.ps([128, 512], F32, "pp") for _ in range(3)], "pp")
            ptr = Rot([self.ps([128, 8, 128], BF16, "ptr") for _ in range(2)], "ptr")
            for s in range(g.nseq):
                for j in range(g.L // 256):
                    c0 = s * (g.L + 3) + 256 * j
                    t0 = s * g.L + 256 * j
                    hw, hn = hwr.next()
                    self.dma(hw[:, :, 0:259], g.hT[:, :, c0:c0 + 259].rearrange("k p t -> p k t"), w=[hn])
                    fts = [fmr[i].next() for i in range(4)]
                    vb, vbn = vbr.next()
                    pend = None
                    for fc in range(32):
                        pp, pn = ppr.next()
                        for k in range(8):
                            self.mm(pp[:, 0:259], win[:, k, fc * 128:(fc + 1) * 128], hw[:, k, 0:259],
                                    start=(k == 0), stop=(k == 7), r=[hn, f"win{k}"], w=[pn])
                        ft, fn = fts[fc // 8]
                        if fc < 24:
                            acc, an = accr.next()
                            fin = self.conv_fm(pp, pn, acc, an, ft[:, fc % 8, :], f"{fn}_{fc % 8}", cw, cb, fc, 3, False, 0)

                            def fin2(fin=fin, fc=fc, ft=ft, fn=fn):
                                fin()
                                if fc < 8:
                                    self.cp("pool", vb[:, fc, :], ft[:, fc, :], r=[f"{fn}_{fc}"], w=[f"{vbn}_{fc}"])
                            if pend is not None:
                                pend()
                            pend = fin2
                        else:
                            if pend is not None:
                                pend()
                                pend = None
                            self.act(ft[:, fc % 8, :], pp[:, 1:257], AF.Silu, r=[pn], w=[f"{fn}_{fc % 8}"])
                    for i in range(4):
                        ft, fn = fts[i]
                        self.dma(g.fm[i, :, :, t0:t0 + 256].rearrange("c p t -> p c t"), ft[:],
                                 r=[f"{fn}_{c}" for c in range(8)])
                    for tcn in range(2):
                        pt, ptn = ptr.next()
                        for c in range(8):
                            self.tr(pt[:, c, :], vb[:, c, tcn * 128:(tcn + 1) * 128], self.ident_b[:],
                                    r=[f"{vbn}_{c}", "identb"], w=[ptn])
                        ut, utn = utr.next()
                        self.cp("dve", ut[:], pt[:].rearrange("p a b -> p (a b)"), r=[ptn], w=[utn])
                        self.dma(g.utm[t0 + tcn * 128:t0 + (tcn + 1) * 128, :], ut[:], r=[utn])

    def hy_fwd(self, g, o, s):
        if self.skip():
            return
        nc = self.nc
        L = g.L
        nL = L // 128
        nm = g.name
        with self.phase():
            u = self.sb([128, nL, 1024], BF16, "u")
            usrc = g.utm[s * L:(s + 1) * L, :].rearrange("(tc p) c -> p tc c", p=128)
            for q in range(0, nL, 8):
                qe = min(q + 8, nL)
                self.dma(u[:, q:qe, :], usrc[:, q:qe, :], w=[f"u{q}"])
            cbr = Rot([self.sb([128, nL, 128], BF16, "cbk") for _ in range(2)], "cbk")
            sbr = Rot([self.sb([128, nL, 128], BF16, "sbk") for _ in range(2)], "sbk")
            kcr = Rot([self.sb([128, 1024], F32, "kc") for _ in range(2)], "kc")
            ksr = Rot([self.sb([128, 1024], F32, "ksp") for _ in range(2)], "ksp")
            a1r = Rot([self.sb([128, 1024], F32, "a1") for _ in range(2)], "a1")
            a2r = Rot([self.sb([128, 1024], F32, "a2") for _ in range(2)], "a2")
            yor = Rot([self.sb([128, 2, 1024], BF16, "yo") for _ in range(2)], "yo")
            puc = Rot([self.ps([128, 1024], F32, "puc") for _ in range(2)], "puc")
            pus = Rot([self.ps([128, 1024], F32, "pus") for _ in range(2)], "pus")
            for fc in range(nL):
                cb_, cbn = cbr.next(); sb_, sbn = sbr.next()
                self.dma(cb_[:], self.I[f"dft_{nm}"][0, fc], w=[cbn])
                self.dma(sb_[:], self.I[f"dft_{nm}"][1, fc], w=[sbn])
                kc, kcn = kcr.next(); ksp, ksn = ksr.next()
                self.dma(kc[:], g.khat2[o, 0, fc * 128:(fc + 1) * 128, :], w=[kcn])
                self.dma(ksp[:], g.khat2[o, 1, fc * 128:(fc + 1) * 128, :], w=[ksn])
                pc_, pcn = puc.next(); ps_, psn = pus.next()
                for tc in range(nL):
                    q8 = (tc // 8) * 8
                    for hlf in range(2):
                        hs = slice(hlf * 512, (hlf + 1) * 512)
                        self.mm(pc_[:, hs], cb_[:, tc, :], u[:, tc, hs], start=(tc == 0), stop=(tc == nL - 1),
                                r=[cbn, f"u{q8}"], w=[pcn + str(hlf)])
                        self.mm(ps_[:, hs], sb_[:, tc, :], u[:, tc, hs], start=(tc == 0), stop=(tc == nL - 1),
                                r=[sbn, f"u{q8}"], w=[psn + str(hlf)])
                a1, a1n = a1r.next(); a2, a2n = a2r.next(); yo, yon = yor.next()
                for hlf in range(2):
                    hs = slice(hlf * 512, (hlf + 1) * 512)
                    self.tt("dve", a1[:, hs], pc_[:, hs], kc[:, hs], ALU.mult, r=[pcn + str(hlf), kcn], w=[a1n])
                    self.tt("dve", a2[:, hs], ps_[:, hs], ksp[:, hs], ALU.mult, r=[psn + str(hlf), ksn], w=[a2n])
                self.tt("pool", yo[:, 0, :], a1[:], a2[:], ALU.subtract, r=[a1n, a2n], w=[yon + "c"])
                a1, a1n = a1r.next(); a2, a2n = a2r.next()
                for hlf in range(2):
                    hs = slice(hlf * 512, (hlf + 1) * 512)
                    self.tt("dve", a1[:, hs], pc_[:, hs], ksp[:, hs], ALU.mult, r=[pcn + str(hlf), ksn], w=[a1n])
                    self.tt("dve", a2[:, hs], ps_[:, hs], kc[:, hs], ALU.mult, r=[psn + str(hlf), kcn], w=[a2n])
                self.tt("pool", yo[:, 1, :], a1[:], a2[:], ALU.add, r=[a1n, a2n], w=[yon + "s"])
                self.dma(g.yspec[:, :, :, fc, :].rearrange("a cc p j -> p a cc j"),
                         yo[:].rearrange("p a (cc j) -> p a cc j", j=128), r=[yon + "c", yon + "s"])

    def hy_inv(self, g, o, s):
        if self.skip():
            return
        nc = self.nc
        L = g.L
        nL = L // 128
        nm = g.name
        TT = min(512, L)
        nq = TT // 128
        with self.phase():
            fb = self.sb([128, 2, 8], F32, "fb")
            self.load_T(fb[:].rearrange("p k c -> p (k c)"),
                        self.I["hy_f_bias"].rearrange("k (c p) -> (k c) p", p=128), 16, "fb")
            cm = self.sb([128, nL, TT], BF16, "cm")
            sm = self.sb([128, nL, TT], BF16, "sm")
            ycr = Rot([self.sb([128, nL, 128], BF16, "yc") for _ in range(2)], "yc")
            ysr = Rot([self.sb([128, nL, 128], BF16, "ys") for _ in range(2)], "ys")
            utr = Rot([self.sb([128, TT], F32, "uti") for _ in range(2)], "uti")
            xgr = Rot([self.sb([128, TT], F32, "xg") for _ in range(2)], "xg")
            zgr = Rot([self.sb([128, TT], F32, "zg") for _ in range(2)], "zg")
            unr = Rot([self.sb([128, TT], F32, "un") for _ in range(2)], "un")
            ubr = Rot([self.sb([128, TT], BF16, "ub") for _ in range(2)], "ub")
            uor = Rot([self.sb([128, 8, 128], BF16, "uo") for _ in range(2)], "uo")
            par = Rot([self.ps([128, 512], F32, "pa") for _ in range(2)], "pa")
            ptq = [self.ps([128, 8, 128], BF16, "ptq") for _ in range(nq)] if o == 0 else []
            for tt in range(L // TT):
                t0 = s * L + tt * TT
                for q in range(0, nL, 8):
                    qe = min(q + 8, nL)
                    self.dma(cm[:, q:qe, :], self.I[f"dfti_{nm}"][0, tt, :, q:qe, :], w=[f"cm{q}"])
                    self.dma(sm[:, q:qe, :], self.I[f"dfti_{nm}"][1, tt, :, q:qe, :], w=[f"sm{q}"])
                for cc in range(8):
                    yc, ycn = ycr.next(); ys_, ysn = ysr.next()
                    self.dma(yc[:], g.yspec[0, cc], w=[ycn])
                    self.dma(ys_[:], g.yspec[1, cc], w=[ysn])
                    ut, utn = utr.next(); xg, xgn = xgr.next()
                    self.dma(ut[:], g.fm[0, cc, :, t0:t0 + TT], w=[utn])
                    self.dma(xg[:], g.fm[1 + o, cc, :, t0:t0 + TT], w=[xgn])
                    pa, pan = par.next()
                    for fc in range(nL):
                        q8 = (fc // 8) * 8
                        self.mm(pa[:, 0:TT], yc[:, fc, :], cm[:, fc, :], start=(fc == 0), stop=False,
                                r=[ycn, f"cm{q8}"], w=[pan])
                        self.mm(pa[:, 0:TT], ys_[:, fc, :], sm[:, fc, :], start=False, stop=(fc == nL - 1),
                                r=[ysn, f"sm{q8}"], w=[pan])
                    un, unn = unr.next()
                    self.stt(un[:], ut[:], fb[:, o, cc:cc + 1], pa[:, 0:TT], ALU.mult, ALU.add, r=[utn, "fb", pan], w=[unn])
                    self.tt("pool", un[:], un[:], xg[:], ALU.mult, r=[unn, xgn], w=[unn])
                    if o == 0:
                        self.dma(g.fm[0, cc, :, t0:t0 + TT], un[:], r=[unn])
                        ub, ubn = ubr.next()
                        self.cp("act", ub[:], un[:], r=[unn], w=[ubn])
                        for q in range(nq):
                            self.tr(ptq[q][:, cc, :], ub[:, q * 128:(q + 1) * 128], self.ident_b[:],
                                    r=[ubn, "identb"], w=[f"ptq{q}"])
                    else:
                        zg, zgn = zgr.next()
                        self.dma(zg[:], g.fm[3, cc, :, t0:t0 + TT], w=[zgn])
                        ub, ubn = ubr.next()
                        self.tt("dve", ub[:], un[:], zg[:], ALU.mult, r=[unn, zgn], w=[ubn])
                        self.dma(g.yT[cc, :, t0:t0 + TT], ub[:], r=[ubn])
                if o == 0:
                    for q in range(nq):
                        uo, uon = uor.next()
                        self.cp("act" if q % 2 == 0 else "dve", uo[:], ptq[q][:], r=[f"ptq{q}"], w=[uon])
                        self.dma(g.utm[t0 + q * 128:t0 + (q + 1) * 128, :], uo[:].rearrange("p a b -> p (a b)"), r=[uon])


def _consts():
    k = np.arange(128)[:, None]
    i = np.arange(128)[None, :]
    masks = np.stack([(k <= i), (k >= i), (k > i), (k < i)]).astype(np.float32)
    out = {"masks": masks}
    for nm, L in (("p", LP), ("s", LS)):
        n = 2 * L
        t = np.arange(L, dtype=np.float64)
        f = np.arange(L, dtype=np.float64)
        th = 2.0 * np.pi * (f + 0.5) / n
        a2 = np.outer(t + 0.5, th)
        am = np.outer(t, th)
        M = np.stack([np.cos(a2), np.sin(a2), np.cos(am), np.sin(am)]).astype(np.float32).astype(ml_dtypes.bfloat16)
        nL = L // 128
        TT = min(512, L)
        out[f"dft_{nm}"] = np.ascontiguousarray(M.reshape(4, nL, 128, nL, 128).transpose(0, 3, 2, 1, 4))
        out[f"dfti_{nm}"] = np.ascontiguousarray(M[0:2].reshape(2, nL, 128, L // TT, TT).transpose(0, 3, 2, 1, 4))
        tt = (np.arange(L, dtype=np.float32) / np.float32(L)).astype(np.float32)
        w = (np.float32(2.0 * math.pi) * np.arange(L, dtype=np.float32) / np.float32(L)).astype(np.float32)
        fr = np.linspace(1e-4, 15, 16, dtype=np.float32)
        ang = w[:, None] * fr
        z = np.concatenate([tt[:, None], np.cos(ang), np.sin(ang)], axis=-1).astype(np.float32)
        out[f"hz_{nm}"] = np.ascontiguousarray(z.T)
        deltas = np.linspace(math.log(HY_T) / 1.5, math.log(HY_T) / 0.3, D, dtype=np.float32)
        out[f"win_{nm}"] = np.exp(-tt[:, None] * np.abs(deltas)).astype(np.float32)
    return out


_CACHE = {}


def nc_inputs(nc):
    return _CACHE["in_names"]


def kernel(**inputs):
    f = lambda a: np.ascontiguousarray(np.asarray(a, dtype=np.float32))
    if "nc" not in _CACHE:
        kb = KB()
        _CACHE["nc"] = kb.build()
        _CACHE["in_names"] = set(kb.I.keys())
        _CACHE["consts"] = _consts()
    nc = _CACHE["nc"]
    consts = _CACHE["consts"]
    shared = {
        "mod_w": f(inputs["mod_w"]), "mod_b": f(inputs["mod_b"]), "ln_g": f(inputs["ln_g"]), "ln_b": f(inputs["ln_b"]),
        "ssd_in_w": f(inputs["ssd_in_w"]), "ssd_conv_w": f(inputs["ssd_conv_w"]), "ssd_conv_b": f(inputs["ssd_conv_b"]),
        "ssd_dt_bias": f(inputs["ssd_dt_bias"]).reshape(2, 64), "ssd_a_log": f(inputs["ssd_a_log"]).reshape(2, 64),
        "ssd_d": f(inputs["ssd_d"]), "ssd_norm_g": f(inputs["ssd_norm_g"]), "ssd_out_w": f(inputs["ssd_out_w"]),
        "hy_in_w": f(inputs["hy_in_w"])[0], "hy_conv_w": f(inputs["hy_conv_w"])[0], "hy_conv_b": f(inputs["hy_conv_b"])[0],
        "hy_f_w1": f(inputs["hy_f_w1"])[0], "hy_f_b1": f(inputs["hy_f_b1"])[0], "hy_f_w2": f(inputs["hy_f_w2"])[0],
        "hy_f_b2": f(inputs["hy_f_b2"])[0], "hy_f_w3": f(inputs["hy_f_w3"])[0], "hy_f_freq": f(inputs["hy_f_freq"])[0],
        "hy_f_bias": f(inputs["hy_f_bias"])[0], "hy_out_w": f(inputs["hy_out_w"])[0],
        "lru_in_w": f(inputs["lru_in_w"])[0], "lru_conv_w": f(inputs["lru_conv_w"])[0], "lru_conv_b": f(inputs["lru_conv_b"])[0],
        "lru_gate_w": f(inputs["lru_gate_w"])[0], "lru_gate_b": f(inputs["lru_gate_b"])[0].reshape(4, D),
        "lru_a_param": f(inputs["lru_a_param"])[0], "lru_out_w": f(inputs["lru_out_w"])[0],
    }
    shared.update(consts)
    shared = {k: v for k, v in shared.items() if k in nc_inputs(nc)}
    xp = f(inputs["x_prompt"]); xs = f(inputs["x_sample"])
    sts = f(inputs["state_ssd"]); stl = f(inputs["state_lru"])
    c = f(inputs["c"]); cc = f(inputs["c_ctx"])
    in_maps = []
    for core in range(8):
        b = core // 2
        m = dict(shared)
        m["xp"] = np.ascontiguousarray(xp[core * NPS:(core + 1) * NPS].reshape(NPS * LP, D))
        m["xs"] = np.ascontiguousarray(xs[b])
        m["st_ssd"] = np.ascontiguousarray(sts[b].reshape(2, 2, 2048, 128))
        m["st_lru"] = np.ascontiguousarray(stl[b].reshape(2, D))
        m["cond"] = np.ascontiguousarray(np.stack([cc, c[b]]))
        in_maps.append(m)
    res = run_bass_kernel_spmd(nc, in_maps, core_ids=list(range(8)))
    r = res.results
    y_prompt = np.concatenate([r[i]["yp"].reshape(NPS, LP, D) for i in range(8)], axis=0)
    y_sample = np.stack([r[2 * b]["ys"] for b in range(4)], axis=0)
    nss = np.concatenate([r[i]["nss"].reshape(NPS, 2, 2, 32, 64, 128) for i in range(8)], axis=0)
    nsl = np.concatenate([r[i]["nsl"].reshape(NPS, 1, 2, D) for i in range(8)], axis=0)
    return (y_prompt.astype(np.float32), y_sample.astype(np.float32), nss.astype(np.float32), nsl.astype(np.float32))
```

```python
import contextlib
import math
import numpy as np
import ml_dtypes
import concourse.bass as bass
import concourse.mybir as mybir
from concourse.bass_utils import run_bass_kernel_spmd

F32 = mybir.dt.float32
F32R = mybir.dt.float32r
BF16 = mybir.dt.bfloat16
AF = mybir.ActivationFunctionType
ALU = mybir.AluOpType

EPOCH = 8000
NDMA = 28
NDMA_HW = 20
import os as _os
MAXOPS = int(_os.environ.get("MAXOPS", "1000000000"))
NOSELF = _os.environ.get("NOSELF", "0") == "1"

D = 1024
NPS = 4
LP = 256
LS = 4096
DEPTH = 4
ALPHA = (2.0 * DEPTH) ** 0.25
LN_EPS = 1e-5
RMS_EPS = 1e-5
SSD_PROJ = 6208
HY_T = 1e-2


class Sched:
    ENG = ("pe", "act", "dve", "pool")

    def __init__(self, nc, stack):
        self.nc = nc
        self.stack = stack
        self.eng = {"pe": nc.tensor, "act": nc.scalar, "dve": nc.vector,
                    "pool": nc.gpsimd, "sp": nc.sync}
        self.ops = {e: [] for e in self.eng}
        self.cnt = {e: 0 for e in self.ENG}
        self.esems = {e: [] for e in self.ENG}
        self.dsems = [stack.enter_context(nc.semaphore(f"dma{i}")) for i in range(NDMA)]
        self.dval = [0] * NDMA
        self.dnext = 0
        self.dnext_sw = 0
        self.waited = {e: {} for e in self.eng}
        self.lastw = {}
        self.readers = {}
        self.n_inst = 0

    def _esem(self, e, count):
        k = (count - 1) // EPOCH
        while len(self.esems[e]) <= k:
            self.esems[e].append(self.stack.enter_context(
                self.nc.semaphore(f"s_{e}_{len(self.esems[e])}")))
        return self.esems[e][k], (count - 1) % EPOCH + 1, k

    def _emit_wait(self, e, ev):
        if ev[0] == "e":
            _, src, count = ev
            if src == e and (e == "pe" or NOSELF):
                return
            sem, val, k = self._esem(src, count)
            key = ("e", src, k)
        else:
            _, idx, val = ev
            sem = self.dsems[idx]
            key = ("d", idx)
        if self.waited[e].get(key, 0) >= val:
            return
        self.waited[e][key] = val
        engobj = self.eng[e]
        self.ops[e].append(lambda engobj=engobj, sem=sem, val=val: engobj.wait_ge(sem, val))

    def _deps(self, e, reads, writes):
        evs = []
        for r in reads:
            if r in self.lastw:
                evs.append(self.lastw[r])
        for w in writes:
            if w in self.lastw:
                evs.append(self.lastw[w])
            evs.extend(self.readers.get(w, ()))
        for ev in evs:
            self._emit_wait(e, ev)

    def _commit(self, ev, reads, writes):
        for r in reads:
            self.readers.setdefault(r, []).append(ev)
        for w in writes:
            self.lastw[w] = ev
            self.readers[w] = []

    def op(self, e, fn, reads=(), writes=()):
        if self.n_inst >= MAXOPS:
            return
        self._deps(e, reads, writes)
        self.cnt[e] += 1
        count = self.cnt[e]
        sem, val, k = self._esem(e, count)
        self.ops[e].append(lambda fn=fn, sem=sem: fn().then_inc(sem, 1))
        self._commit(("e", e, count), reads, writes)
        self.n_inst += 1

    def dma(self, q, fn, reads=(), writes=()):
        if self.n_inst >= MAXOPS:
            return
        if q == "sp":
            idx = self.dnext
            self.dnext = (self.dnext + 1) % NDMA_HW
        else:
            idx = NDMA_HW + self.dnext_sw
            self.dnext_sw = (self.dnext_sw + 1) % (NDMA - NDMA_HW)
        if self.dval[idx] > 0:
            self._emit_wait(q, ("d", idx, self.dval[idx]))
        self._deps(q, reads, writes)
        self.dval[idx] += 16
        val = self.dval[idx]
        sem = self.dsems[idx]
        self.ops[q].append(lambda fn=fn, sem=sem: fn().then_inc(sem, 16))
        self._commit(("d", idx, val), reads, writes)
        self.n_inst += 1

    def barrier(self):
        for e in self.eng:
            for en in self.ENG:
                if self.cnt[en] and not (en == e):
                    self._emit_wait(e, ("e", en, self.cnt[en]))
                elif self.cnt[en] and e != "pe":
                    self._emit_wait(e, ("e", en, self.cnt[en]))
            for i in range(NDMA):
                if self.dval[i]:
                    self._emit_wait(e, ("d", i, self.dval[i]))
        self.lastw = {}
        self.readers = {}

    def finish(self):
        self.barrier()
        nc = self.nc
        with nc.Block() as block:
            @block.tensor
            def _(t):
                for f in self.ops["pe"]:
                    f()

            @block.scalar
            def _(t):
                for f in self.ops["act"]:
                    f()

            @block.vector
            def _(t):
                for f in self.ops["dve"]:
                    f()

            @block.gpsimd
            def _(t):
                for f in self.ops["pool"]:
                    f()

            @block.sync
            def _(t):
                for f in self.ops["sp"]:
                    f()


class Rot:
    def __init__(self, tiles, name):
        self.tiles = tiles
        self.name = name
        self.i = -1

    def next(self):
        self.i += 1
        k = self.i % len(self.tiles)
        return self.tiles[k], f"{self.name}{k}"


class Grp:
    pass


class KB:
    def __init__(self, layers=(0, 1, 2, 3), do_prompt=True, do_sample=True):
        self.layers = layers
        self.do_prompt = do_prompt
        self.do_sample = do_sample
        self.nc = bass.Bass("TRN2", target_bir_lowering=False)
        self.I = {}
        self.O = {}
        self.uid = 0
        self.nphase = 0
        self.max_phase = 10 ** 9

    def skip(self):
        self.nphase += 1
        return self.nphase > self.max_phase

    def inp(self, name, shape, dt=F32):
        self.I[name] = self.nc.dram_tensor(name, list(shape), dt, kind="ExternalInput").ap()
        return self.I[name]

    def outp(self, name, shape, dt=F32):
        self.O[name] = self.nc.dram_tensor(name, list(shape), dt, kind="ExternalOutput").ap()
        return self.O[name]

    def scr(self, name, shape, dt):
        return self.nc.dram_tensor(name, list(shape), dt, kind="Internal").ap()

    def nm(self, p):
        self.uid += 1
        return f"{p}_{self.uid}"

    def sb(self, shape, dt, name="t"):
        return self.ph.enter_context(self.nc.sbuf_tensor(self.nm(name), list(shape), dt))

    def ps(self, shape, dt, name="p"):
        return self.ph.enter_context(self.nc.psum_tensor(self.nm(name), list(shape), dt))

    @contextlib.contextmanager
    def phase(self):
        with contextlib.ExitStack() as ph:
            old = getattr(self, "ph", None)
            self.ph = ph
            yield
            self.S.barrier()
            self.ph = old

    def dma(self, out, in_, r=(), w=(), q="sp", **kw):
        eng = self.nc.sync if q == "sp" else self.nc.gpsimd
        self.S.dma(q, lambda: eng.dma_start(out=out, in_=in_, **kw), reads=r, writes=w)

    def mm(self, out, lhsT, rhs, start=True, stop=True, r=(), w=()):
        self.S.op("pe", lambda: self.nc.tensor.matmul(out, lhsT=lhsT, rhs=rhs, start=start, stop=stop),
                  reads=r, writes=w)

    def tr(self, out, in_, ident, r=(), w=()):
        self.S.op("pe", lambda: self.nc.tensor.transpose(out=out, in_=in_, identity=ident), reads=r, writes=w)

    def act(self, out, in_, func, r=(), w=(), **kw):
        self.S.op("act", lambda: self.nc.scalar.activation(out=out, in_=in_, func=func, **kw), reads=r, writes=w)

    def E(self, e):
        return self.nc.vector if e == "dve" else self.nc.gpsimd

    def tt(self, e, out, in0, in1, op, r=(), w=()):
        self.S.op(e, lambda: self.E(e).tensor_tensor(out=out, in0=in0, in1=in1, op=op), reads=r, writes=w)

    def ts(self, e, out, in0, s1, s2, op0, op1=None, r=(), w=()):
        if op1 is None:
            self.S.op(e, lambda: self.E(e).tensor_scalar(out=out, in0=in0, scalar1=s1, scalar2=None, op0=op0),
                      reads=r, writes=w)
        else:
            self.S.op(e, lambda: self.E(e).tensor_scalar(out=out, in0=in0, scalar1=s1, scalar2=s2, op0=op0, op1=op1),
                      reads=r, writes=w)

    def stt(self, out, in0, scalar, in1, op0, op1, r=(), w=()):
        self.S.op("dve", lambda: self.nc.vector.scalar_tensor_tensor(out=out, in0=in0, scalar=scalar, in1=in1,
                                                                     op0=op0, op1=op1), reads=r, writes=w)

    def cp(self, e, out, in_, r=(), w=()):
        if e == "act":
            self.S.op("act", lambda: self.nc.scalar.copy(out=out, in_=in_), reads=r, writes=w)
        else:
            self.S.op(e, lambda: self.E(e).tensor_copy(out=out, in_=in_), reads=r, writes=w)

    def memset(self, e, ap, val, w=()):
        self.S.op(e, lambda: self.E(e).memset(ap, val), writes=w)

    def declare(self):
        inp = self.inp
        inp("xp", [NPS * LP, D]); inp("xs", [LS, D])
        inp("st_ssd", [2, 2, 2048, 128]); inp("st_lru", [2, D]); inp("cond", [2, D])
        inp("mod_w", [4, D, 3 * D]); inp("mod_b", [4, 3 * D]); inp("ln_g", [4, D]); inp("ln_b", [4, D])
        inp("ssd_in_w", [2, D, SSD_PROJ]); inp("ssd_conv_w", [2, 4, 4096]); inp("ssd_conv_b", [2, 4096])
        inp("ssd_dt_bias", [2, 64]); inp("ssd_a_log", [2, 64]); inp("ssd_d", [2, 32])
        inp("ssd_norm_g", [2, 2048]); inp("ssd_out_w", [2, 2048, D])
        inp("hy_in_w", [D, 4096]); inp("hy_conv_w", [3, 3072]); inp("hy_conv_b", [3072])
        inp("hy_f_w1", [33, 64]); inp("hy_f_b1", [64]); inp("hy_f_w2", [64, 64]); inp("hy_f_b2", [64])
        inp("hy_f_w3", [64, 4096]); inp("hy_f_freq", [2, 64]); inp("hy_f_bias", [2, D]); inp("hy_out_w", [D, D])
        inp("lru_in_w", [D, 2048]); inp("lru_conv_w", [4, D]); inp("lru_conv_b", [D])
        inp("lru_gate_w", [2, 2, 4, 256, 256]); inp("lru_gate_b", [4, D]); inp("lru_a_param", [2, D])
        inp("lru_out_w", [D, D])
        inp("masks", [4, 128, 128])
        if 1 in self.layers:
            inp("dft_p", [4, LP // 128, 128, LP // 128, 128], BF16)
            inp("dft_s", [4, LS // 128, 128, LS // 128, 128], BF16)
            inp("dfti_p", [2, 1, 128, LP // 128, 256], BF16)
            inp("dfti_s", [2, LS // 512, 128, LS // 128, 512], BF16)
            inp("hz_p", [33, LP]); inp("hz_s", [33, LS])
            inp("win_p", [LP, D]); inp("win_s", [LS, D])
        self.outp("yp", [NPS * LP, D]); self.outp("ys", [LS, D])
        self.outp("nss", [NPS, 2, 2, 2048, 128]); self.outp("nsl", [NPS, 2, D])

    def build(self):
        nc = self.nc
        self.declare()
        with contextlib.ExitStack() as st:
            self.S = Sched(nc, st)
            self.ph = st
            self.ident_f = self.sb([128, 128], F32, "identf")
            self.ident_b = self.sb([128, 128], BF16, "identb")
            self.masks = self.sb([128, 4, 128], F32, "masks")
            self.ones_f = self.sb([128, 128], F32, "ones")
            self.zero_b = self.sb([128, 8, 4], BF16, "zerob")
            self.dma(self.masks[:], self.I["masks"].rearrange("m p i -> p m i"), w=["masks"])
            self.memset("pool", self.ident_f[:], 1.0, w=["identf"])
            self.S.op("pool", lambda: nc.gpsimd.affine_select(
                out=self.ident_f[:], in_=self.ident_f[:], pattern=[[-1, 128]], compare_op=ALU.is_equal,
                fill=0.0, base=0, channel_multiplier=1), reads=["identf"], writes=["identf"])
            self.cp("dve", self.ident_b[:], self.ident_f[:], r=["identf"], w=["identb"])
            self.memset("dve", self.ones_f[:], 1.0, w=["ones"])
            self.masks_r = self.sb([128, 4, 128], F32R, "masksr")
            self.ones_r = self.sb([128, 128], F32R, "onesr")
            self.cp("dve", self.masks_r[:], self.masks[:], r=["masks"], w=["masksr"])
            self.cp("dve", self.ones_r[:], self.ones_f[:], r=["ones"], w=["onesr"])
            self.memset("dve", self.zero_b[:], 0.0, w=["zerob"])
            self.lru_h0 = self.sb([128, 2, 8], F32, "lruh0")
            self.S.barrier()

            self.mod_scr = self.scr("mod_scr", [4, 2, 3 * D], F32)
            groups = []
            if self.do_prompt:
                g = Grp(); g.name = "p"; g.nseq = NPS; g.L = LP; g.cond = 0; g.latent = False
                g.x_in = self.I["xp"]; g.x_out = self.O["yp"]
                groups.append(g)
            if self.do_sample:
                g = Grp(); g.name = "s"; g.nseq = 1; g.L = LS; g.cond = 1; g.latent = True
                g.x_in = self.I["xs"]; g.x_out = self.O["ys"]
                groups.append(g)
            for g in groups:
                g.T = g.nseq * g.L
                g.W = g.nseq * (g.L + 3)
                g.xa = self.scr(f"xa_{g.name}", [g.T, D], F32)
                g.xb = self.scr(f"xb_{g.name}", [g.T, D], F32)
                g.hT = self.scr(f"hT_{g.name}", [8, 128, g.W], BF16)
                g.yT = self.scr(f"yT_{g.name}", [16, 128, g.T], BF16)
                g.xs_tm = self.scr(f"xstm_{g.name}", [g.T, 2048], BF16)
                g.b_tm = self.scr(f"btm_{g.name}", [g.T, 1024], BF16)
                g.bcT = self.scr(f"bcT_{g.name}", [16, 128, g.T], BF16)
                g.dta = self.scr(f"dta_{g.name}", [g.T, 128], F32)
                g.sloc = self.scr(f"sloc_{g.name}", [g.T // 128, 2, 128, 2048], F32)
                g.cdec = self.scr(f"cdec_{g.name}", [g.T // 128, 128, 64], F32)
                g.hprev = self.scr(f"hprev_{g.name}", [g.T // 128, 2, 128, 2048], BF16)
                g.fm = self.scr(f"fm_{g.name}", [4, 8, 128, g.T], F32)
                g.utm = self.scr(f"utm_{g.name}", [g.T, D], BF16)
                g.kw = self.scr(f"kw_{g.name}", [g.L, 4096], F32)
                g.khat = self.scr(f"khat_{g.name}", [2, 2, g.L, D], F32)
                g.yspec = self.scr(f"ysp_{g.name}", [2, 8, 128, g.L // 128, 128], BF16)
                g.khat2 = self.scr(f"khat2_{g.name}", [2, 2, g.L, D], F32)
            self.groups = groups

            self.modulation()
            for li in self.layers:
                last = (li == self.layers[-1])
                for g in groups:
                    X_in = g.x_in if li == self.layers[0] else (g.xa if (li % 2 == 1) else g.xb)
                    X_out = g.x_out if last else (g.xa if (li % 2 == 0) else g.xb)
                    col = g.latent and li == 3
                    self.pass_A(g, li, X_in, col)
                    kind = li % 3
                    if kind == 0:
                        self.ssd_layer(g, li // 3, li)
                        KC = 16
                    elif kind == 1:
                        self.hyena_layer(g)
                        KC = 8
                    else:
                        self.lru_layer(g)
                        KC = 8
                    wname = {0: "ssd_out_w", 1: "hy_out_w", 2: "lru_out_w"}[kind]
                    wout = self.I[wname][li // 3] if kind == 0 else self.I[wname]
                    self.pass_E(g, li, X_in, X_out, col, wout, KC)
            self.S.finish()
        return nc

    def xrows(self, g, X, s, c, col):
        if not col:
            r0 = s * g.L + c * 128
            return [(X[r0:r0 + 128, :], 0, 128)]
        Xv = X.rearrange("(r w) f -> w r f", w=64)
        return [(Xv[2 * c + wo], wo * 64, 64) for wo in range(2)]

    def load_T(self, dst, src2d, R, name):
        stg = self.sb([128, 128], F32, "ldT")
        if getattr(self, "_ldT_ph", None) is not self.ph:
            self._ldT_ph = self.ph
            self._ldT_ps = self.ps([128, 128], F32, "ldTp")
        pt = self._ldT_ps
        k = self.nm("ldT")
        self.dma(stg[0:R, :], src2d, w=[k])
        self.tr(pt[:, 0:R], stg[0:R, :], self.ident_f[0:R, 0:R], r=[k, "identf"], w=["ldTp"])
        self.cp("dve", dst, pt[:, 0:R], r=["ldTp"], w=[name])

    def modulation(self):
        if self.skip():
            return
        nc = self.nc
        with self.phase():
            cond = self.sb([2, D], F32, "cond")
            cs = self.sb([2, D], F32, "cs")
            condT = self.sb([128, 8, 2], BF16, "condT")
            pT = self.ps([128, 8, 2], F32, "pT")
            self.dma(cond[:], self.I["cond"], w=["cond"])
            self.act(cs[:], cond[:], AF.Silu, r=["cond"], w=["cs"])
            for k in range(8):
                self.tr(pT[:, k, :], cs[0:2, k * 128:(k + 1) * 128], self.ident_f[0:2, 0:2],
                        r=["cs", "identf"], w=["pT"])
            self.cp("dve", condT[:], pT[:], r=["pT"], w=["condT"])
            mw = self.sb([128, 8, 3 * D], BF16, "mw")
            mb = self.sb([2, 3 * D], F32, "mb")
            msb = self.sb([2, 3 * D], F32, "msb")
            pm = [self.ps([2, 512], F32, "pm") for _ in range(2)]
            for li in self.layers:
                for k in range(8):
                    self.dma(mw[:, k, :], self.I["mod_w"][li, k * 128:(k + 1) * 128, :], w=[f"mw{k}"], q="pool")
                for c in range(2):
                    self.dma(mb[c:c + 1, :], self.I["mod_b"][li:li + 1, :], w=["mb"])
                for t in range(6):
                    p = pm[t % 2]
                    for k in range(8):
                        self.mm(p[:], condT[:, k, :], mw[:, k, t * 512:(t + 1) * 512], start=(k == 0), stop=(k == 7),
                                r=["condT", f"mw{k}"], w=[f"pm{t % 2}"])
                    self.tt("dve", msb[:, t * 512:(t + 1) * 512], p[:], mb[:, t * 512:(t + 1) * 512], ALU.add,
                            r=[f"pm{t % 2}", "mb"], w=["msb"])
                self.ts("dve", msb[:, D:2 * D], msb[:, D:2 * D], 1.0, None, ALU.add, r=["msb"], w=["msb"])
                self.dma(self.mod_scr[li], msb[:], r=["msb"])

    def pass_A(self, g, li, X, col):
        if self.skip():
            return
        with self.phase():
            sc = self.sb([128, D], F32, "sc")
            sh = self.sb([128, D], F32, "sh")
            self.dma(sh[:], self.mod_scr[li, g.cond, 0:D].partition_broadcast(128), w=["sh"])
            self.dma(sc[:], self.mod_scr[li, g.cond, D:2 * D].partition_broadcast(128), w=["sc"])
            xr = Rot([self.sb([128, D], F32, "xA") for _ in range(2)], "xA")
            hr = Rot([self.sb([128, D], BF16, "hA") for _ in range(2)], "hA")
            tr_ = Rot([self.sb([128, 8, 128], BF16, "hTA") for _ in range(2)], "hTA")
            pr = Rot([self.ps([128, 8, 128], BF16, "pA") for _ in range(2)], "pA")
            nchunk = g.L // 128
            for s in range(g.nseq):
                base = s * (g.L + 3)
                self.dma(g.hT[:, :, base:base + 1].rearrange("k p t -> p k t"), self.zero_b[:, :, 0:1], r=["zerob"],
                         allow_slow_non_contiguous=True)
                self.dma(g.hT[:, :, base + g.L + 1:base + g.L + 3].rearrange("k p t -> p k t"),
                         self.zero_b[:, :, 0:2], r=["zerob"], allow_slow_non_contiguous=True)
                for c in range(nchunk):
                    xt, xn = xr.next()
                    for (src, p0, n) in self.xrows(g, X, s, c, col):
                        self.dma(xt[p0:p0 + n, :], src, w=[xn])
                    ht, hn = hr.next()
                    self.tt("dve", xt[:], xt[:], sc[:], ALU.mult, r=[xn, "sc"], w=[xn])
                    self.tt("dve", ht[:], xt[:], sh[:], ALU.add, r=[xn, "sh"], w=[hn])
                    pt, pn = pr.next()
                    for k in range(8):
                        self.tr(pt[:, k, :], ht[:, k * 128:(k + 1) * 128], self.ident_b[:], r=[hn, "identb"], w=[pn])
                    tt_, tn = tr_.next()
                    self.cp("act", tt_[:], pt[:], r=[pn], w=[tn])
                    c0 = base + 1 + c * 128
                    self.dma(g.hT[:, :, c0:c0 + 128].rearrange("k p t -> p k t"), tt_[:], r=[tn])

    def pass_E(self, g, li, X, Xo, col, wout, KC):
        if self.skip():
            return
        nc = self.nc
        with self.phase():
            wo = self.sb([128, KC, D], BF16, "wo")
            for k in range(KC):
                self.dma(wo[:, k, :], wout[k * 128:(k + 1) * 128, :], w=[f"wo{k}"], q="pool")
            gt = self.sb([128, D], F32, "gate")
            lg = self.sb([128, D], F32, "lng")
            lb = self.sb([128, D], F32, "lnb")
            self.dma(gt[:], self.mod_scr[li, g.cond, 2 * D:3 * D].partition_broadcast(128), w=["gate"])
            self.dma(lg[:], self.I["ln_g"][li].partition_broadcast(128), w=["lng"])
            self.dma(lb[:], self.I["ln_b"][li].partition_broadcast(128), w=["lnb"])
            yr = Rot([self.sb([128, KC, 128], BF16, "yE") for _ in range(2)], "yE")
            xr = Rot([self.sb([128, D], F32, "xE") for _ in range(2)], "xE")
            rr = Rot([self.sb([128, D], F32, "rE") for _ in range(2)], "rE")
            sr = Rot([self.sb([128, 16], F32, "sE") for _ in range(2)], "sE")
            pr = Rot([self.ps([128, D], F32, "pE") for _ in range(2)], "pE")
            nchunk = g.L // 128

            def chunk_gen(s, c):
                t0 = s * g.L + c * 128
                yt, yn = yr.next()
                self.dma(yt[:], g.yT[0:KC, :, t0:t0 + 128].rearrange("k p t -> p k t"), w=[yn])
                xt, xn = xr.next()
                for (src, p0, n) in self.xrows(g, X, s, c, col):
                    self.dma(xt[p0:p0 + n, :], src, w=[xn])
                pt, pn = pr.next()
                rt, rn = rr.next()
                stt_, sn = sr.next()
                yield
                for hlf in range(2):
                    for k in range(KC):
                        self.mm(pt[:, hlf * 512:(hlf + 1) * 512], yt[:, k, :], wo[:, k, hlf * 512:(hlf + 1) * 512],
                                start=(k == 0), stop=(k == KC - 1), r=[yn, f"wo{k}"], w=[pn + str(hlf)])
                yield
                for hlf in range(2):
                    sl = slice(hlf * 512, (hlf + 1) * 512)
                    self.tt("dve", rt[:, sl], pt[:, sl], gt[:, sl], ALU.mult, r=[pn + str(hlf), "gate"], w=[rn])
                yield
                self.stt(rt[:], xt[:], ALPHA, rt[:], ALU.mult, ALU.add, r=[xn, rn], w=[rn])
                yield
                self.S.op("dve", lambda: nc.vector.bn_stats(out=stt_[:, 0:6], in_=rt[:, 0:512]),
                          reads=[rn], writes=[sn])
                self.S.op("dve", lambda: nc.vector.bn_stats(out=stt_[:, 6:12], in_=rt[:, 512:1024]),
                          reads=[rn], writes=[sn])
                yield
                self.S.op("dve", lambda: nc.vector.bn_aggr(out=stt_[:, 12:14], in_=stt_[:, 0:12]),
                          reads=[sn], writes=[sn])
                yield
                self.ts("dve", stt_[:, 14:15], stt_[:, 13:14], LN_EPS, None, ALU.add, r=[sn], w=[sn])
                yield
                self.act(stt_[:, 14:15], stt_[:, 14:15], AF.Sqrt, r=[sn], w=[sn])
                yield
                self.S.op("dve", lambda: nc.vector.reciprocal(out=stt_[:, 15:16], in_=stt_[:, 14:15]),
                          reads=[sn], writes=[sn])
                yield
                self.ts("dve", rt[:], rt[:], stt_[:, 12:13], stt_[:, 15:16], ALU.subtract, ALU.mult,
                        r=[rn, sn], w=[rn])
                yield
                self.tt("pool", rt[:], rt[:], lg[:], ALU.mult, r=[rn, "lng"], w=[rn])
                yield
                self.tt("pool", rt[:], rt[:], lb[:], ALU.add, r=[rn, "lnb"], w=[rn])
                yield
                for (dst, p0, n) in self.xrows(g, Xo, s, c, col):
                    self.dma(dst, rt[p0:p0 + n, :], r=[rn])

            gens = [chunk_gen(s, c) for s in range(g.nseq) for c in range(nchunk)]
            for i in range(0, len(gens), 2):
                active = gens[i:i + 2]
                while active:
                    for gch in list(active):
                        try:
                            next(gch)
                        except StopIteration:
                            active.remove(gch)

    def conv_fm(self, pp, pn, acc, an, dst, dn, cw, cb, fc, K, silu, woff):
        self.act(acc[:], pp[:, woff:woff + 256], AF.Identity, r=[pn, "cw", "cb"], w=[an],
                 scale=cw[:, 0, fc:fc + 1], bias=cb[:, fc:fc + 1])
        for k in range(1, K):
            self.stt(acc[:], pp[:, woff + k:woff + k + 256], cw[:, k, fc:fc + 1], acc[:], ALU.mult, ALU.add,
                     r=[pn, an, "cw"], w=[an])
        def fin():
            if silu:
                self.act(dst, acc[:], AF.Silu, r=[an], w=[dn])
            else:
                self.cp("act", dst, acc[:], r=[an], w=[dn])
        return fin

    def ssd_layer(self, g, slot, li):
        self.ssd_p1(g, slot)
        if _os.environ.get("DEBUG") == "dta":
            with self.phase():
                t = self.sb([128, 8, 128], F32, "dbg")
                self.dma(t[:], g.dta[0:1024, :].rearrange("(c p) f -> p c f", p=128), w=["dbg"])
                self.dma(self.O["yp"][:, 0:128].rearrange("(c p) f -> p c f", p=128), t[:], r=["dbg"])
        self.ssd_p2a(g, slot)
        self.ssd_pR(g, slot)
        self.ssd_p2b(g, slot)

    def ssd_p1(self, g, slot):
        if self.skip():
            return
        nc = self.nc
        with self.phase():
            w_in = self.I["ssd_in_w"][slot]
            wx = self.sb([128, 8, 4096], BF16, "wx")
            wd = self.sb([128, 8, 64], BF16, "wd")
            for k in range(8):
                self.dma(wx[:, k, :], w_in[k * 128:(k + 1) * 128, 2048:6144], w=[f"wx{k}"], q="pool")
                self.dma(wd[:, k, :], w_in[k * 128:(k + 1) * 128, 6144:6208], w=["wd"], q="pool")
            cw = self.sb([128, 4, 32], F32, "cw")
            cb = self.sb([128, 32], F32, "cb")
            self.load_T(cw[:].rearrange("p k c -> p (k c)"),
                        self.I["ssd_conv_w"][slot].rearrange("k (c p) -> (k c) p", p=128), 128, "cw")
            self.load_T(cb[:], self.I["ssd_conv_b"][slot].rearrange("(c p) -> c p", p=128), 32, "cb")
            dtb = self.sb([128, 64], F32, "dtb")
            abc = self.sb([128, 64], F32, "abc")
            self.dma(dtb[:], self.I["ssd_dt_bias"][slot].partition_broadcast(128), w=["dtb"])
            self.dma(abc[:], self.I["ssd_a_log"][slot].partition_broadcast(128), w=["abc"])
            self.act(abc[:], abc[:], AF.Exp, r=["abc"], w=["abc"])
            self.ts("dve", abc[:], abc[:], -1.0, None, ALU.mult, r=["abc"], w=["abc"])

            hwr = Rot([self.sb([128, 8, 260], BF16, "hw") for _ in range(2)], "hw")
            xbr = Rot([self.sb([128, 32, 256], BF16, "xbc") for _ in range(2)], "xbc")
            accr = Rot([self.sb([128, 256], F32, "acc") for _ in range(3)], "acc")
            tmr = Rot([self.sb([128, 1024], BF16, "tm") for _ in range(3)], "tm")
            dtr = Rot([self.sb([128, 128], F32, "dta") for _ in range(2)], "dta")
            ppr = Rot([self.ps([128, 512], F32, "pp") for _ in range(3)], "pp")
            ptr = Rot([self.ps([128, 8, 128], BF16, "ptr") for _ in range(2)], "ptr")
            pdr = Rot([self.ps([128, 64], F32, "pd") for _ in range(1)], "pd")
            for s in range(g.nseq):
                for j in range(g.L // 256):
                    c0 = s * (g.L + 3) + 256 * j
                    t0 = s * g.L + 256 * j
                    hw, hn = hwr.next()
                    self.dma(hw[:, :, 0:259], g.hT[:, :, c0:c0 + 259].rearrange("k p t -> p k t"), w=[hn])
                    xb, xn = xbr.next()
                    pend = None
                    for fc in range(32):
                        pp, pn = ppr.next()
                        for k in range(8):
                            self.mm(pp[:, 0:259], wx[:, k, fc * 128:(fc + 1) * 128], hw[:, k, 0:259],
                                    start=(k == 0), stop=(k == 7), r=[hn, f"wx{k}"], w=[pn])
                        acc, an = accr.next()
                        fin = self.conv_fm(pp, pn, acc, an, xb[:, fc, :], f"{xn}_{fc}", cw, cb, fc, 4, True, 0)
                        if pend is not None:
                            pend()
                        pend = fin
                    pend()
                    self.dma(g.bcT[:, :, t0:t0 + 256].rearrange("c p t -> p c t"), xb[:, 16:32, :],
                             r=[f"{xn}_{fc}" for fc in range(16, 32)])
                    for tcn in range(2):
                        for blk in range(3):
                            pt, ptn = ptr.next()
                            for i in range(8):
                                fc = blk * 8 + i
                                self.tr(pt[:, i, :], xb[:, fc, tcn * 128:(tcn + 1) * 128], self.ident_b[:],
                                        r=[f"{xn}_{fc}", "identb"], w=[ptn])
                            tm, tn = tmr.next()
                            self.cp("act" if blk % 2 == 0 else "dve", tm[:], pt[:].rearrange("p a b -> p (a b)"),
                                    r=[ptn], w=[tn])
                            r0 = t0 + tcn * 128
                            if blk < 2:
                                self.dma(g.xs_tm[r0:r0 + 128, blk * 1024:(blk + 1) * 1024], tm[:], r=[tn])
                            else:
                                self.dma(g.b_tm[r0:r0 + 128, :], tm[:], r=[tn])
                        pd, pdn = pdr.next()
                        for k in range(8):
                            self.mm(pd[:], hw[:, k, 1 + tcn * 128:1 + (tcn + 1) * 128], wd[:, k, :],
                                    start=(k == 0), stop=(k == 7), r=[hn, "wd"], w=[pdn])
                        dt, dn = dtr.next()
                        self.tt("dve", dt[:, 0:64], pd[:], dtb[:], ALU.add, r=[pdn, "dtb"], w=[dn])
                        self.act(dt[:, 0:64], dt[:, 0:64], AF.Exp, r=[dn], w=[dn])
                        self.act(dt[:, 0:64], dt[:, 0:64], AF.Ln, r=[dn], w=[dn], bias=1.0)
                        self.tt("dve", dt[:, 64:128], dt[:, 0:64], abc[:], ALU.mult, r=[dn, "abc"], w=[dn])
                        self.dma(g.dta[r0:r0 + 128, :], dt[:], r=[dn])

    def ssd_p2a(self, g, slot):
        if self.skip():
            return
        nc = self.nc
        with self.phase():
            xsr = Rot([self.sb([128, 2048], BF16, "xs") for _ in range(2)], "xs")
            btr = Rot([self.sb([128, 1024], BF16, "bt") for _ in range(2)], "bt")
            dtr = Rot([self.sb([128, 128], F32, "dta") for _ in range(2)], "dta")
            der = Rot([self.sb([128, 64], F32, "de") for _ in range(2)], "de")
            cdr = Rot([self.sb([128, 64], F32, "cd") for _ in range(2)], "cd")
            wdr = Rot([self.sb([128, 2048], BF16, "wdd") for _ in range(2)], "wdd")
            ssr = Rot([self.sb([128, 1024], F32, "ss") for _ in range(3)], "ss")
            pcr = Rot([self.ps([128, 128], F32, "pc") for _ in range(2)], "pc")
            psr = Rot([self.ps([128, 1024], F32, "psS") for _ in range(2)], "psS")
            arr = Rot([self.sb([128, 64], F32R, "ar") for _ in range(2)], "ar")
            if _os.environ.get("DEBUG") == "alloc":
                for t in pcr.tiles + psr.tiles + der.tiles:
                    print("ALLOC", t.name, self.nc.lookup_mloc(t))
            def chunk_gen(ch):
                r0 = ch * 128
                xs, xn = xsr.next(); bt, bn = btr.next(); dt, dn = dtr.next()
                self.dma(xs[:], g.xs_tm[r0:r0 + 128, :], w=[xn])
                self.dma(bt[:], g.b_tm[r0:r0 + 128, :], w=[bn])
                self.dma(dt[:], g.dta[r0:r0 + 128, :], w=[dn])
                pc, pcn = pcr.next()
                ar, arn = arr.next()
                de, den = der.next(); cd, cdn = cdr.next()
                yield
                self.cp("dve", ar[:], dt[:, 64:128], r=[dn], w=[arn])
                yield
                self.mm(pc[:, 0:32], self.masks_r[:, 2, :], ar[:, 0:32], r=["masksr", arn], w=[pcn])
                self.mm(pc[:, 32:64], self.masks_r[:, 3, :], ar[:, 32:64], r=["masksr", arn], w=[pcn])
                self.mm(pc[:, 64:128], self.ones_r[:], ar[:, 0:64], r=["onesr", arn], w=[pcn])
                yield
                self.act(de[:], pc[:, 0:64], AF.Exp, r=[pcn], w=[den])
                self.act(cd[:], pc[:, 64:128], AF.Exp, r=[pcn], w=[cdn])
                yield
                self.dma(g.cdec[ch], cd[:], r=[cdn])
                self.tt("dve", de[:], de[:], dt[:, 0:64], ALU.mult, r=[den, dn], w=[den])
                yield
                for d in range(2):
                    wdd, wn = wdr.next()
                    self.tt("dve", wdd[:].rearrange("p (h e) -> p h e", h=32), xs[:].rearrange("p (h e) -> p h e", h=32),
                            de[:, d * 32:(d + 1) * 32].unsqueeze(2).to_broadcast([128, 32, 64]), ALU.mult,
                            r=[xn, den], w=[wn])
                    yield
                    for hlf in range(2):
                        pS, psn = psr.next()
                        for gg in range(4):
                            G8 = hlf * 4 + gg
                            self.mm(pS[:, gg * 256:(gg + 1) * 256], bt[:, G8 * 128:(G8 + 1) * 128],
                                    wdd[:, G8 * 256:(G8 + 1) * 256], r=[bn, wn], w=[psn + str(gg // 2)])
                        yield
                        ss, sn = ssr.next()
                        for q in range(2):
                            self.cp("act" if q == 0 else "dve", ss[:, q * 512:(q + 1) * 512], pS[:, q * 512:(q + 1) * 512],
                                    r=[psn + str(q)], w=[sn])
                        yield
                        self.dma(g.sloc[ch, d, :, hlf * 1024:(hlf + 1) * 1024], ss[:], r=[sn])

            gens = [chunk_gen(ch) for ch in range(g.T // 128)]
            for i in range(0, len(gens), 2):
                active = gens[i:i + 2]
                while active:
                    for gch in list(active):
                        try:
                            next(gch)
                        except StopIteration:
                            active.remove(gch)

    def ssd_pR(self, g, slot):
        if self.skip():
            return
        nc = self.nc
        nchunk = g.L // 128
        with self.phase():
            hst = [self.sb([128, 2048], F32, "hst") for _ in range(2)]
            hbr = [Rot([self.sb([128, 2048], BF16, "hb") for _ in range(2)], f"hb{d}") for d in range(2)]
            slr = [Rot([self.sb([128, 2048], F32, "sl") for _ in range(2)], f"sl{d}") for d in range(2)]
            cdr = [Rot([self.sb([128, 64], F32, "cd") for _ in range(2)], f"cd{d}") for d in range(2)]
            stg = Rot([self.sb([128, 128], F32, "stg") for _ in range(3)], "stg")
            ptr = Rot([self.ps([128, 128], F32, "ptR") for _ in range(2)], "ptR")
            eng = ["dve", "pool"]
            for s in range(g.nseq):
                for d in range(2):
                    hn = f"hst{d}"
                    if g.latent:
                        for t in range(16):
                            sg, sgn = stg.next()
                            self.dma(sg[:], self.I["st_ssd"][slot, d, t * 128:(t + 1) * 128, :], w=[sgn])
                            pt, ptn = ptr.next()
                            self.tr(pt[:], sg[:], self.ident_f[:], r=[sgn, "identf"], w=[ptn])
                            self.cp("act", hst[d][:, t * 128:(t + 1) * 128], pt[:], r=[ptn], w=[hn])
                    else:
                        self.memset(eng[d], hst[d][:], 0.0, w=[hn])
                order = [list(range(nchunk)), list(range(nchunk - 1, -1, -1))]
                for i in range(nchunk):
                    for d in range(2):
                        c = order[d][i]
                        ch = s * nchunk + c
                        hn = f"hst{d}"
                        hb, hbn = hbr[d].next()
                        self.cp("act", hb[:], hst[d][:], r=[hn], w=[hbn])
                        self.dma(g.hprev[ch, d], hb[:], r=[hbn])
                        sl, sln = slr[d].next(); cd, cdn = cdr[d].next()
                        self.dma(sl[:], g.sloc[ch, d], w=[sln])
                        self.dma(cd[:], g.cdec[ch], w=[cdn])
                        self.tt(eng[d], hst[d][:].rearrange("p (h e) -> p h e", h=32),
                                hst[d][:].rearrange("p (h e) -> p h e", h=32),
                                cd[:, d * 32:(d + 1) * 32].unsqueeze(2).to_broadcast([128, 32, 64]), ALU.mult,
                                r=[hn, cdn], w=[hn])
                        self.tt(eng[d], hst[d][:], hst[d][:], sl[:], ALU.add, r=[hn, sln], w=[hn])
                if not g.latent:
                    for d in range(2):
                        hn = f"hst{d}"
                        for t in range(16):
                            pt, ptn = ptr.next()
                            self.tr(pt[:], hst[d][:, t * 128:(t + 1) * 128], self.ident_f[:], r=[hn, "identf"], w=[ptn])
                            sg, sgn = stg.next()
                            self.cp("act", sg[:], pt[:], r=[ptn], w=[sgn])
                            self.dma(self.O["nss"][s, slot, d, t * 128:(t + 1) * 128, :], sg[:], r=[sgn])

    def ssd_p2b(self, g, slot):
        if self.skip():
            return
        nc = self.nc
        with self.phase():
            w_in = self.I["ssd_in_w"][slot]
            wz = self.sb([128, 8, 2048], BF16, "wz")
            for k in range(8):
                self.dma(wz[:, k, :], w_in[k * 128:(k + 1) * 128, 0:2048], w=[f"wz{k}"], q="pool")
            ng = self.sb([128, 2048], F32, "ng")
            self.dma(ng[:], self.I["ssd_norm_g"][slot].partition_broadcast(128), w=["ng"])
            dsk = self.sb([128, 32], F32, "dsk")
            self.dma(dsk[:], self.I["ssd_d"][slot].partition_broadcast(128), w=["dsk"])
            mgt_r = self.sb([128, 128], F32R, "mgtr")
            mlt_r = self.sb([128, 128], F32R, "mltr")
            self.cp("dve", mgt_r[:], self.masks[:, 2, :], r=["masks"], w=["mgtr"])
            self.cp("dve", mlt_r[:], self.masks[:, 3, :], r=["masks"], w=["mltr"])
            Xl = [mgt_r, mlt_r]
            Xn = ["mgtr", "mltr"]
            Ym = [0, 1]
            Sm = [0, 1]

            hTr = Rot([self.sb([128, 8, 128], BF16, "hT") for _ in range(2)], "hT")
            xsr = Rot([self.sb([128, 2048], BF16, "xs") for _ in range(2)], "xs")
            bcr = Rot([self.sb([128, 16, 128], BF16, "bc") for _ in range(2)], "bc")
            dtr = Rot([self.sb([128, 128], F32, "dta") for _ in range(2)], "dta")
            hpr = Rot([self.sb([128, 2, 2048], BF16, "hp") for _ in range(2)], "hp")
            zsr = Rot([self.sb([128, 2048], BF16, "zs") for _ in range(2)], "zs")
            xdr = Rot([self.sb([128, 2, 2048], BF16, "xd") for _ in range(2)], "xd")
            ear = Rot([self.sb([128, 64], F32, "ea") for _ in range(2)], "ea")
            scr_ = Rot([self.sb([128, 2, 128], BF16, "scm") for _ in range(2)], "scm")
            Yr = Rot([self.sb([128, 4, 128], F32R, "Y") for _ in range(3)], "Y")
            Er = Rot([self.sb([128, 4, 128], BF16, "Ee") for _ in range(3)], "Ee")
            Mr = Rot([self.sb([128, 2, 4, 128], BF16, "Mm") for _ in range(2)], "Mm")
            t1r = Rot([self.sb([128, 256], F32, "t1") for _ in range(2)], "t1")
            t2r = Rot([self.sb([128, 256], F32, "t2") for _ in range(2)], "t2")
            ypr = Rot([self.sb([128, 2048], F32, "ypre") for _ in range(2)], "ypre")
            ynr = Rot([self.sb([128, 2048], BF16, "yn") for _ in range(2)], "yn")
            ssr = Rot([self.sb([128, 4], F32, "ssq") for _ in range(2)], "ssq")
            junk = self.sb([128, 2048], BF16, "junk")
            yTr = Rot([self.sb([128, 16, 128], BF16, "yT") for _ in range(2)], "yT")

            pzr = Rot([self.ps([128, 1024], F32, "pz") for _ in range(1)], "pz")
            pcs = Rot([self.ps([128, 512], F32, "pcs") for _ in range(1)], "pcs")
            psg = Rot([self.ps([128, 512], F32, "psg") for _ in range(2)], "psg")
            pyr = Rot([self.ps([128, 1024], F32, "py") for _ in range(1)], "py")
            ptr = Rot([self.ps([128, 8, 128], BF16, "ptb") for _ in range(1)], "ptb")
            nchunk = g.L // 128
            for s in range(g.nseq):
                for c in range(nchunk):
                    ch = s * nchunk + c
                    r0 = ch * 128
                    c0 = s * (g.L + 3) + 1 + c * 128
                    hT, hTn = hTr.next(); xs, xn = xsr.next(); bc, bcn = bcr.next(); dt, dn = dtr.next()
                    hp, hpn = hpr.next()
                    self.dma(hT[:], g.hT[:, :, c0:c0 + 128].rearrange("k p t -> p k t"), w=[hTn])
                    self.dma(xs[:], g.xs_tm[r0:r0 + 128, :], w=[xn])
                    self.dma(bc[:], g.bcT[:, :, r0:r0 + 128].rearrange("c p t -> p c t"), w=[bcn])
                    self.dma(dt[:], g.dta[r0:r0 + 128, :], w=[dn])
                    self.dma(hp[:], g.hprev[ch].rearrange("d p f -> p d f"), w=[hpn])
                    zs, zn = zsr.next()
                    for hlf in range(2):
                        pz, pzn = pzr.next()
                        for q in range(2):
                            for k in range(8):
                                col0 = hlf * 1024 + q * 512
                                self.mm(pz[:, q * 512:(q + 1) * 512], hT[:, k, :], wz[:, k, col0:col0 + 512],
                                        start=(k == 0), stop=(k == 7), r=[hTn, f"wz{k}"], w=[pzn + str(q)])
                        for q in range(2):
                            self.act(zs[:, hlf * 1024 + q * 512:hlf * 1024 + (q + 1) * 512], pz[:, q * 512:(q + 1) * 512],
                                     AF.Silu, r=[pzn + str(q)], w=[zn])
                    xd, xdn = xdr.next()
                    for d in range(2):
                        self.tt("dve" if d == 0 else "pool", xd[:, d, :].rearrange("p (h e) -> p h e", h=32),
                                xs[:].rearrange("p (h e) -> p h e", h=32),
                                dt[:, d * 32:(d + 1) * 32].unsqueeze(2).to_broadcast([128, 32, 64]), ALU.mult,
                                r=[xn, dn], w=[xdn + str(d)])
                    pc, pcn = pcs.next()
                    self.mm(pc[:, 0:32], self.masks[:, 0, :], dt[:, 64:96], r=["masks", dn], w=[pcn])
                    self.mm(pc[:, 32:64], self.masks[:, 1, :], dt[:, 96:128], r=["masks", dn], w=[pcn])
                    ea, ean = ear.next()
                    self.act(ea[:], pc[:, 0:64], AF.Exp, r=[pcn], w=[ean])
                    yp, ypn = ypr.next()
                    v3 = lambda ap: ap.rearrange("p (h e) -> p h e", h=4)

                    def stA(G8):
                        self.mm(pc[:, 128:256], bc[:, G8, :], bc[:, 8 + G8, :], r=[bcn], w=[pcn])
                        sm, smn = scr_.next()
                        for d in range(2):
                            self.tt("dve", sm[:, d, :], pc[:, 128:256], self.masks[:, Sm[d], :], ALU.mult,
                                    r=[pcn, "masks"], w=[smn + str(d)])
                        Mt, Mn = Mr.next()
                        for d in range(2):
                            Y, Yn = Yr.next()
                            self.tt("pool", Y[:], self.masks[:, Ym[d], :].unsqueeze(1).to_broadcast([128, 4, 128]),
                                    dt[:, 64 + d * 32 + G8 * 4:64 + d * 32 + G8 * 4 + 4].unsqueeze(2).to_broadcast([128, 4, 128]),
                                    ALU.mult, r=["masks", dn], w=[Yn])
                            pg, pgn = psg.next()
                            self.mm(pg[:], Xl[d][:], Y[:].rearrange("p a b -> p (a b)"), r=[Xn[d], Yn], w=[pgn])
                            Et, En = Er.next()
                            self.act(Et[:].rearrange("p a b -> p (a b)"), pg[:], AF.Exp, r=[pgn], w=[En])
                            self.tt("dve", Mt[:, d, :, :], Et[:], sm[:, d, :].unsqueeze(1).to_broadcast([128, 4, 128]),
                                    ALU.mult, r=[En, smn + str(d)], w=[Mn + str(d)])
                        return Mt, Mn

                    def stB(G8, Mt, Mn):
                        py, pyn = pyr.next()
                        for h in range(4):
                            H = G8 * 4 + h
                            for d in range(2):
                                self.mm(py[:, h * 64:(h + 1) * 64], Mt[:, d, h, :], xd[:, d, H * 64:(H + 1) * 64],
                                        start=(d == 0), stop=(d == 1), r=[Mn + str(d), xdn + str(d)], w=[pyn + "d"])
                        for d in range(2):
                            self.mm(py[:, 512 + d * 256:512 + (d + 1) * 256], bc[:, 8 + G8, :],
                                    hp[:, d, G8 * 256:(G8 + 1) * 256], r=[bcn, hpn], w=[pyn + "o"])
                        return py, pyn

                    def stC(G8, py, pyn):
                        t1, t1n = t1r.next(); t2, t2n = t2r.next()
                        for d, (tt_, tn_) in enumerate(((t1, t1n), (t2, t2n))):
                            self.tt("dve", v3(tt_[:]), v3(py[:, 512 + d * 256:512 + (d + 1) * 256]),
                                    ea[:, d * 32 + G8 * 4:d * 32 + G8 * 4 + 4].unsqueeze(2).to_broadcast([128, 4, 64]),
                                    ALU.mult, r=[pyn + "o", ean], w=[tn_])
                        self.tt("pool", t1[:], t1[:], t2[:], ALU.add, r=[t1n, t2n], w=[t1n])
                        self.tt("dve", t2[:], py[:, 0:256], t1[:], ALU.add, r=[pyn + "d", t1n], w=[t2n])
                        self.tt("pool", v3(t1[:]), v3(xs[:, G8 * 256:(G8 + 1) * 256]),
                                dsk[:, G8 * 4:G8 * 4 + 4].unsqueeze(2).to_broadcast([128, 4, 64]), ALU.mult,
                                r=[xn, "dsk", t1n], w=[t1n])
                        self.tt("pool", yp[:, G8 * 256:(G8 + 1) * 256], t1[:], t2[:], ALU.add, r=[t1n, t2n],
                                w=[ypn + str(G8)])

                    MA = stA(0)
                    for G8 in range(8):
                        PB = stB(G8, *MA)
                        if G8 < 7:
                            MA = stA(G8 + 1)
                        stC(G8, *PB)
                    ypa = [ypn + str(i) for i in range(8)]
                    self.tt("dve", yp[:], yp[:], zs[:], ALU.mult, r=ypa + [zn], w=ypa)
                    sq, sqn = ssr.next()
                    self.act(junk[:], yp[:], AF.Square, r=ypa, w=["junk", sqn], accum_out=sq[:, 0:1])
                    self.ts("dve", sq[:, 1:2], sq[:, 0:1], 1.0 / 2048.0, RMS_EPS, ALU.mult, ALU.add, r=[sqn], w=[sqn])
                    self.act(sq[:, 1:2], sq[:, 1:2], AF.Sqrt, r=[sqn], w=[sqn])
                    self.S.op("dve", lambda sq=sq: nc.vector.reciprocal(out=sq[:, 2:3], in_=sq[:, 1:2]),
                              reads=[sqn], writes=[sqn])
                    yn_, ynn = ynr.next()
                    self.stt(yn_[:], yp[:], sq[:, 2:3], ng[:], ALU.mult, ALU.mult, r=ypa + [sqn, "ng"], w=[ynn])
                    yT, yTn = yTr.next()
                    for blk in range(2):
                        pt, ptn = ptr.next()
                        for i in range(8):
                            fc = blk * 8 + i
                            self.tr(pt[:, i, :], yn_[:, fc * 128:(fc + 1) * 128], self.ident_b[:], r=[ynn, "identb"], w=[ptn])
                        self.cp("act", yT[:, blk * 8:(blk + 1) * 8, :], pt[:], r=[ptn], w=[yTn])
                    self.dma(g.yT[:, :, r0:r0 + 128].rearrange("k p t -> p k t"), yT[:], r=[yTn])

    def lru_layer(self, g):
        self.lru_p1(g)
        self.lru_p2(g)

    def lru_p1(self, g):
        if self.skip():
            return
        nc = self.nc
        with self.phase():
            win = self.sb([128, 8, 2048], BF16, "win")
            for k in range(8):
                self.dma(win[:, k, :], self.I["lru_in_w"][k * 128:(k + 1) * 128, :], w=[f"win{k}"], q="pool")
            gw = self.sb([128, 32, 256], BF16, "gw")
            gsrc = self.I["lru_gate_w"].rearrange("d g n (kc p) j -> (d g n kc) p j", p=128)
            for i in range(32):
                self.dma(gw[:, i, :], gsrc[i], w=["gw"], q="pool")
            cw = self.sb([128, 4, 8], F32, "cw")
            cb = self.sb([128, 8], F32, "cb")
            gb = self.sb([128, 4, 8], F32, "gb")
            nsp = self.sb([128, 2, 8], F32, "nsp")
            h0 = self.lru_h0
            self.load_T(cw[:].rearrange("p k c -> p (k c)"),
                        self.I["lru_conv_w"].rearrange("k (c p) -> (k c) p", p=128), 32, "cw")
            self.load_T(cb[:], self.I["lru_conv_b"].rearrange("(c p) -> c p", p=128), 8, "cb")
            self.load_T(gb[:].rearrange("p k c -> p (k c)"),
                        self.I["lru_gate_b"].rearrange("k (c p) -> (k c) p", p=128), 32, "gb")
            self.load_T(nsp[:].rearrange("p k c -> p (k c)"),
                        self.I["lru_a_param"].rearrange("k (c p) -> (k c) p", p=128), 16, "nsp")
            self.act(nsp[:], nsp[:], AF.Exp, r=["nsp"], w=["nsp"], scale=-1.0)
            self.act(nsp[:], nsp[:], AF.Ln, r=["nsp"], w=["nsp"], bias=1.0)
            self.ts("dve", nsp[:], nsp[:], -8.0, None, ALU.mult, r=["nsp"], w=["nsp"])
            if g.latent:
                self.load_T(h0[:].rearrange("p k c -> p (k c)"),
                            self.I["st_lru"].rearrange("k (c p) -> (k c) p", p=128), 16, "h0")
            else:
                self.memset("dve", h0[:], 0.0, w=["h0"])
            hwr = Rot([self.sb([128, 8, 260], BF16, "hw") for _ in range(2)], "hw")
            xrr = Rot([self.sb([128, 8, 256], F32, "xr") for _ in range(1)], "xr")
            xbr = Rot([self.sb([128, 8, 256], BF16, "xrb") for _ in range(1)], "xrb")
            zsr = Rot([self.sb([128, 8, 256], F32, "zs") for _ in range(2)], "zs")
            gtr = [Rot([self.sb([128, 8, 256], F32, "gt") for _ in range(1)], f"gt{i}") for i in range(4)]
            aar = [Rot([self.sb([128, 8, 256], F32, "aa") for _ in range(1 + d)], f"aa{d}") for d in range(2)]
            bxr = [Rot([self.sb([128, 8, 256], F32, "bx") for _ in range(1 + d)], f"bx{d}") for d in range(2)]
            tmr = Rot([self.sb([128, 8, 256], F32, "tmpl") for _ in range(1)], "tmpl")
            yfr = Rot([self.sb([128, 8, 256], F32, "yf") for _ in range(2)], "yf")
            accr = Rot([self.sb([128, 256], F32, "acc") for _ in range(3)], "acc")
            fin = self.sb([128, 8], F32, "fin")
            ppr = Rot([self.ps([128, 512], F32, "pp") for _ in range(3)], "pp")
            pgr = Rot([self.ps([128, 256], F32, "pg") for _ in range(3)], "pg")
            for s in range(g.nseq):
                yf_prev = None
                ntile = g.L // 256
                for j in range(ntile):
                    c0 = s * (g.L + 3) + 256 * j
                    t0 = s * g.L + 256 * j
                    hw, hn = hwr.next()
                    self.dma(hw[:, :, 0:259], g.hT[:, :, c0:c0 + 259].rearrange("k p t -> p k t"), w=[hn])
                    xr, xn = xrr.next(); xb, xbn = xbr.next(); zs, zn = zsr.next()
                    pend = None
                    for fc in range(16):
                        pp, pn = ppr.next()
                        for k in range(8):
                            self.mm(pp[:, 0:259], win[:, k, fc * 128:(fc + 1) * 128], hw[:, k, 0:259],
                                    start=(k == 0), stop=(k == 7), r=[hn, f"win{k}"], w=[pn])
                        if fc < 8:
                            acc, an = accr.next()
                            cfin = self.conv_fm(pp, pn, acc, an, xr[:, fc, :], f"{xn}_{fc}", cw, cb, fc, 4, False, 0)

                            def fin2(cfin=cfin, fc=fc):
                                cfin()
                                self.cp("pool", xb[:, fc, :], xr[:, fc, :], r=[f"{xn}_{fc}"], w=[f"{xbn}_{fc}"])
                            if pend is not None:
                                pend()
                            pend = fin2
                        else:
                            if pend is not None:
                                pend()
                                pend = None
                            self.act(zs[:, fc - 8, :], pp[:, 1:257], AF.Silu, r=[pn], w=[zn])
                    gts = [gtr[i].next() for i in range(4)]
                    for dg in range(4):
                        gt, gn = gts[dg]
                        for n in range(4):
                            for jc in range(2):
                                pg, pgn = pgr.next()
                                for kc in range(2):
                                    self.mm(pg[:], gw[:, (dg * 4 + n) * 2 + kc, jc * 128:(jc + 1) * 128], xb[:, n * 2 + kc, :],
                                            start=(kc == 0), stop=(kc == 1), r=["gw", f"{xbn}_{n * 2 + kc}"], w=[pgn])
                                ch = n * 2 + jc
                                self.act(gt[:, ch, :], pg[:], AF.Sigmoid, r=[pgn, "gb"], w=[gn], bias=gb[:, dg, ch:ch + 1])
                    xall = [f"{xn}_{fc}" for fc in range(8)]
                    ab = []
                    for d in range(2):
                        aa, aan = aar[d].next(); bx, bxn = bxr[d].next(); tm, tmn = tmr.next()
                        rt, rn = gts[d * 2]; it, itn = gts[d * 2 + 1]
                        for ch in range(8):
                            self.act(aa[:, ch, :], rt[:, ch, :], AF.Exp, r=[rn, "nsp"], w=[aan], scale=nsp[:, d, ch:ch + 1])
                        self.tt("dve", tm[:], aa[:], aa[:], ALU.mult, r=[aan], w=[tmn])
                        self.ts("dve", tm[:], tm[:], -1.0, 1.0, ALU.mult, ALU.add, r=[tmn], w=[tmn])
                        self.ts("dve", tm[:], tm[:], 0.0, None, ALU.max, r=[tmn], w=[tmn])
                        self.act(tm[:], tm[:], AF.Sqrt, r=[tmn], w=[tmn])
                        self.tt("pool", tm[:], tm[:], it[:], ALU.mult, r=[tmn, itn], w=[tmn])
                        self.tt("pool", bx[:], tm[:], xr[:], ALU.mult, r=[tmn] + xall, w=[bxn])
                        ab.append((aa, aan, bx, bxn))
                    yf, yfn = yfr.next()
                    aa, aan, bx, bxn = ab[0]
                    for ch in range(8):
                        init = h0[:, 0, ch:ch + 1] if yf_prev is None else yf_prev[0][:, ch, 255:256]
                        rr = [aan, bxn, "h0"] + ([yf_prev[1]] if yf_prev is not None else [])
                        self.S.op("dve", lambda yf=yf, aa=aa, bx=bx, ch=ch, init=init: nc.vector.tensor_tensor_scan(
                            out=yf[:, ch, :], data0=aa[:, ch, :], data1=bx[:, ch, :], initial=init,
                            op0=ALU.mult, op1=ALU.add), reads=rr, writes=[yfn])
                    yf_prev = (yf, yfn)
                    fmv = lambda slot: g.fm[slot, :, :, t0:t0 + 256].rearrange("c p t -> p c t")
                    self.dma(fmv(0), yf[:], r=[yfn])
                    self.dma(fmv(1), ab[1][0][:], r=[ab[1][1]])
                    self.dma(fmv(2), ab[1][2][:], r=[ab[1][3]])
                    self.dma(fmv(3), zs[:], r=[zn])
                if not g.latent:
                    self.cp("dve", fin[:], yf_prev[0][:, :, 255], r=[yf_prev[1]], w=["fin"])
                    self.dma(self.O["nsl"][s, 0, :].rearrange("(c p) -> p c", p=128), fin[:], r=["fin"],
                             allow_slow_non_contiguous=True)

    def lru_p2(self, g):
        if self.skip():
            return
        nc = self.nc
        with self.phase():
            h0 = self.lru_h0
            ldr = [Rot([self.sb([128, 8, 256], F32, "ld") for _ in range(2)], f"ld{i}") for i in range(4)]
            ybr = Rot([self.sb([128, 8, 256], F32, "yb") for _ in range(2)], "yb")
            yTr = Rot([self.sb([128, 8, 256], BF16, "yTl") for _ in range(2)], "yTl")
            fin = self.sb([128, 8], F32, "fin")
            for s in range(g.nseq):
                yb_prev = None
                ntile = g.L // 256
                for j in range(ntile - 1, -1, -1):
                    t0 = s * g.L + 256 * j
                    lt = []
                    for i in range(4):
                        t, tn = ldr[i].next()
                        self.dma(t[:], g.fm[i, :, :, t0:t0 + 256].rearrange("c p t -> p c t"), w=[tn])
                        lt.append((t, tn))
                    (yf, yfn), (aa, aan), (bx, bxn), (zs, zn) = lt
                    yb, ybn = ybr.next()
                    for ch in range(8):
                        init = h0[:, 1, ch:ch + 1] if yb_prev is None else yb_prev[0][:, ch, 0:1]
                        rr = [aan, bxn, "h0"] + ([yb_prev[1]] if yb_prev is not None else [])
                        self.S.op("dve", lambda yb=yb, aa=aa, bx=bx, ch=ch, init=init: nc.vector.tensor_tensor_scan(
                            out=yb[:, ch, ::-1], data0=aa[:, ch, ::-1], data1=bx[:, ch, ::-1], initial=init,
                            op0=ALU.mult, op1=ALU.add), reads=rr, writes=[ybn])
                    yb_prev = (yb, ybn)
                    self.tt("pool", yf[:], yf[:], yb[:], ALU.add, r=[yfn, ybn], w=[yfn])
                    yT, yTn = yTr.next()
                    self.tt("pool", yT[:], yf[:], zs[:], ALU.mult, r=[yfn, zn], w=[yTn])
                    self.dma(g.yT[0:8, :, t0:t0 + 256].rearrange("c p t -> p c t"), yT[:], r=[yTn])
                if not g.latent:
                    self.cp("dve", fin[:], yb_prev[0][:, :, 0], r=[yb_prev[1]], w=["fin"])
                    self.dma(self.O["nsl"][s, 1, :].rearrange("(c p) -> p c", p=128), fin[:], r=["fin"],
                             allow_slow_non_contiguous=True)

    def dbg_rows(self, dst_row0, src2d, nrows):
        with self.phase():
            for r0 in range(0, nrows, 128):
                t = self.sb([128, 1024], F32, "dbg")
                self.dma(t[:], src2d[r0:r0 + 128, :], w=["dbg"])
                self.dma(self.O["yp"][dst_row0 + r0:dst_row0 + r0 + 128, :], t[:], r=["dbg"])

    def hyena_layer(self, g):
        self.hy_filters(g)
        if _os.environ.get("DEBUG") == "hyk" and g.name == "p":
            self.dbg_rows(0, g.khat2[0, 0], 256)
            self.dbg_rows(256, g.khat2[0, 1], 256)
            self.dbg_rows(512, g.kw[:, 0:1024], 256)
            self.dbg_rows(768, g.kw[:, 1024:2048], 256)
            return
        self.hy_p1(g)
        for o in range(2):
            for s in range(g.nseq):
                self.hy_fwd(g, o, s)
                self.hy_inv(g, o, s)

    def _vec(self, dst, src1d, n, name):
        self.dma(dst, src1d.rearrange("(p o) -> p o", o=1), w=[name], allow_slow_non_contiguous=True)

    def _hy_mlp(self, g, hd2, w3r):
        nc = self.nc
        L = g.L
        nm = g.name
        TWO_PI = 2.0 * math.pi
        MAGIC = 12582912.0
        with self.phase():
            zT = self.sb([33, L], F32, "zT")
            self.dma(zT[:], self.I[f"hz_{nm}"], w=["zT"])
            w1 = self.sb([33, 64], F32, "w1"); w2 = self.sb([64, 64], F32, "w2")
            self.dma(w1[:], self.I["hy_f_w1"], w=["w1"]); self.dma(w2[:], self.I["hy_f_w2"], w=["w2"])
            w3 = self.sb([64, 4096], F32, "w3")
            self.dma(w3[:], self.I["hy_f_w3"], w=["w3"])
            self.cp("pool", w3r[:], w3[:], r=["w3"], w=["w3r"])
            pv = self.sb([64, 6], F32, "pv")
            self._vec(pv[:, 0:1], self.I["hy_f_b1"], 64, "pv"); self._vec(pv[:, 1:2], self.I["hy_f_b2"], 64, "pv")
            self._vec(pv[:, 2:3], self.I["hy_f_freq"][0], 64, "pv"); self._vec(pv[:, 3:4], self.I["hy_f_freq"][1], 64, "pv")
            self.tt("dve", pv[:, 4:6], pv[:, 0:2], pv[:, 2:4], ALU.mult, r=["pv"], w=["pv"])
            hd1 = self.sb([64, L], F32, "hd1")
            argr = Rot([self.sb([64, 512], F32, "arg") for _ in range(2)], "arg")
            nr = Rot([self.sb([64, 512], F32, "nq") for _ in range(2)], "nq")
            phr = Rot([self.ps([64, 512], F32, "ph") for _ in range(2)], "ph")
            TW = min(512, L)
            for layer in range(2):
                for ti in range(L // TW):
                    sl = slice(ti * TW, (ti + 1) * TW)
                    ph, phn = phr.next()
                    if layer == 0:
                        self.mm(ph[:, 0:TW], w1[:], zT[:, sl], r=["w1", "zT"], w=[phn])
                    else:
                        self.mm(ph[:, 0:TW], w2[:], hd1[:, sl], r=["w2", "hd1"], w=[phn])
                    arg, an = argr.next(); nq, nn = nr.next()
                    self.act(arg[:, 0:TW], ph[:, 0:TW], AF.Identity, r=[phn, "pv"], w=[an],
                             scale=pv[:, 2 + layer:3 + layer], bias=pv[:, 4 + layer:5 + layer])
                    self.ts("dve", nq[:, 0:TW], arg[:, 0:TW], 1.0 / TWO_PI, MAGIC, ALU.mult, ALU.add, r=[an], w=[nn])
                    self.ts("dve", nq[:, 0:TW], nq[:, 0:TW], MAGIC, None, ALU.subtract, r=[nn], w=[nn])
                    self.stt(arg[:, 0:TW], nq[:, 0:TW], -TWO_PI, arg[:, 0:TW], ALU.mult, ALU.add, r=[nn, an], w=[an])
                    self.ts("dve", arg[:, 0:TW], arg[:, 0:TW], math.pi, -math.pi, ALU.min, ALU.max, r=[an], w=[an])
                    if layer == 0:
                        self.act(hd1[:, sl], arg[:, 0:TW], AF.Sin, r=[an], w=["hd1"])
                    else:
                        self.act(hd2[:, sl], arg[:, 0:TW], AF.Sin, r=[an], w=["hd2"])

    def hy_filters(self, g):
        if self.skip():
            return
        nc = self.nc
        L = g.L
        nL = L // 128
        nm = g.name
        TWO_PI = 2.0 * math.pi
        MAGIC = 12582912.0
        with self.phase():
            hd2 = self.sb([64, L], F32R, "hd2")
            w3r = self.sb([64, 4096], F32R, "w3r")
            self._hy_mlp(g, hd2, w3r)
            asum = self.sb([128, 4096], F32, "asum")
            self.memset("pool", asum[:], 0.0, w=["asum"])
            winr = Rot([self.sb([128, 1024], F32, "winw") for _ in range(2)], "winw")
            kwr = Rot([self.sb([128, 4096], F32, "kwt") for _ in range(2)], "kwt")
            kabs = self.sb([128, 4096], F32, "kabs")
            pkr = Rot([self.ps([128, 512], F32, "pk") for _ in range(3)], "pk")
            for tc in range(nL):
                wt, wn = winr.next()
                self.dma(wt[:], self.I[f"win_{nm}"][tc * 128:(tc + 1) * 128, :], w=[wn])
                kt, kn = kwr.next()
                for ti in range(8):
                    pk, pkn = pkr.next()
                    self.mm(pk[:], hd2[:, tc * 128:(tc + 1) * 128], w3r[:, ti * 512:(ti + 1) * 512], r=["hd2", "w3r"], w=[pkn])
                    self.tt("dve", kt[:, ti * 512:(ti + 1) * 512], pk[:], wt[:, (ti % 2) * 512:(ti % 2 + 1) * 512], ALU.mult,
                            r=[pkn, wn], w=[kn])
                self.act(kabs[:], kt[:], AF.Abs, r=[kn], w=["kabs"])
                self.tt("pool", asum[:], asum[:], kabs[:], ALU.add, r=["kabs", "asum"], w=["asum"])
                self.dma(g.kw[tc * 128:(tc + 1) * 128, :], kt[:], r=[kn])
            asr = self.sb([128, 4096], F32R, "asr")
            self.cp("dve", asr[:], asum[:], r=["asum"], w=["asr"])
            tot = self.sb([128, 4096], F32, "tot")
            for ti in range(8):
                pk, pkn = pkr.next()
                self.mm(pk[:], self.ones_r[:], asr[:, ti * 512:(ti + 1) * 512], r=["onesr", "asr"], w=[pkn])
                self.cp("act", tot[:, ti * 512:(ti + 1) * 512], pk[:], r=[pkn], w=["tot"])
            rn = self.sb([128, 2, 1024], F32, "rn")
            tv = tot[:].rearrange("p (o d c) -> p o d c", o=2, d=2)
            self.tt("dve", rn[:], tv[:, :, 0, :], tv[:, :, 1, :], ALU.add, r=["tot"], w=["rn"])
            rn0 = self.sb([128, 2, 1024], F32, "rn0")
            self.act(rn0[:], rn[:], AF.Ln, r=["rn"], w=["rn0"])
            self.act(rn[:], rn0[:], AF.Exp, r=["rn0"], w=["rn"], scale=-1.0)
            self.ts("dve", rn[:], rn[:], 2.0 / (2 * L), None, ALU.mult, r=["rn"], w=["rn"])
            self.dma(g.khat[0, 0, 0:128, :], rn[:, 0, :], r=["rn"])
            self.dma(g.khat[0, 1, 0:128, :], rn[:, 1, :], r=["rn"])
        with self.phase():
            rn = self.sb([128, 2, 1024], F32, "rn")
            self.dma(rn[:, 0, :], g.khat[0, 0, 0:128, :], w=["rn"])
            self.dma(rn[:, 1, :], g.khat[0, 1, 0:128, :], w=["rn"])
            self.S.barrier()
            ks = self.sb([128, nL, 1024], BF16, "ks")
            kldr = Rot([self.sb([128, 2, 1024], F32, "kld") for _ in range(2)], "kld")
            mbr = Rot([self.sb([128, nL, 128], BF16, "mb") for _ in range(2)], "mb")
            kor = Rot([self.sb([128, 1024], F32, "ko") for _ in range(2)], "ko")
            pur = Rot([self.ps([128, 1024], F32, "pu") for _ in range(2)], "pu")
            for o in range(2):
                for cs in range(2):
                    for tc in range(nL):
                        kl, kln = kldr.next()
                        self.dma(kl[:], g.kw[tc * 128:(tc + 1) * 128, o * 2048:(o + 1) * 2048].rearrange("p (d c) -> p d c", d=2),
                                 w=[kln])
                        if tc == 0:
                            self.memset("dve", kl[0:1, 1, :], 0.0, w=[kln])
                        self.tt("dve", kl[:, 0, :], kl[:, 0, :], kl[:, 1, :], ALU.add if cs == 0 else ALU.subtract,
                                r=[kln], w=[kln])
                        self.tt("pool", ks[:, tc, :], kl[:, 0, :], rn[:, o, :], ALU.mult, r=[kln, "rn"], w=[f"ks{tc}"])
                    for fc in range(nL):
                        mb, mbn = mbr.next()
                        self.dma(mb[:], self.I[f"dft_{nm}"][2 + cs, fc], w=[mbn])
                        pu, pun = pur.next()
                        for tc in range(nL):
                            for hlf in range(2):
                                self.mm(pu[:, hlf * 512:(hlf + 1) * 512], mb[:, tc, :], ks[:, tc, hlf * 512:(hlf + 1) * 512],
                                        start=(tc == 0), stop=(tc == nL - 1), r=[mbn, f"ks{tc}"],
                                        w=[pun + str(hlf)])
                        ko, kon = kor.next()
                        for hlf in range(2):
                            self.cp("act", ko[:, hlf * 512:(hlf + 1) * 512], pu[:, hlf * 512:(hlf + 1) * 512],
                                    r=[pun + str(hlf)], w=[kon])
                        self.dma(g.khat2[o, cs, fc * 128:(fc + 1) * 128, :], ko[:], r=[kon])

    def hy_p1(self, g):
        if self.skip():
            return
        nc = self.nc
        with self.phase():
            win = self.sb([128, 8, 4096], BF16, "win")
            for k in range(8):
                self.dma(win[:, k, :], self.I["hy_in_w"][k * 128:(k + 1) * 128, :], w=[f"win{k}"], q="pool")
            cw = self.sb([128, 3, 24], F32, "cw")
            cb = self.sb([128, 24], F32, "cb")
            self.load_T(cw[:].rearrange("p k c -> p (k c)"),
                        self.I["hy_conv_w"].rearrange("k (c p) -> (k c) p", p=128), 72, "cw")
            self.load_T(cb[:], self.I["hy_conv_b"].rearrange("(c p) -> c p", p=128), 24, "cb")
            hwr = Rot([self.sb([128, 8, 260], BF16, "hw") for _ in range(2)], "hw")
            fmr = [Rot([self.sb([128, 8, 256], F32, "fmt") for _ in range(2)], f"fmt{i}") for i in range(4)]
            vbr = Rot([self.sb([128, 8, 256], BF16, "vb") for _ in range(2)], "vb")
            utr = Rot([self.sb([128, 1024], BF16, "ut") for _ in range(2)], "ut")
            accr = Rot([self.sb([128, 256], F32, "acc") for _ in range(3)], "acc")
            ppr = Rot([self.ps([128, 512], F32, "pp") for _ in range(3)], "pp")
            ptr = Rot([self.ps([128, 8, 128], BF16, "ptr") for _ in range(2)], "ptr")
            for s in range(g.nseq):
                for j in range(g.L // 256):
                    c0 = s * (g.L + 3) + 256 * j
                    t0 = s * g.L + 256 * j
                    hw, hn = hwr.next()
                    self.dma(hw[:, :, 0:259], g.hT[:, :, c0:c0 + 259].rearrange("k p t -> p k t"), w=[hn])
                    fts = [fmr[i].next() for i in range(4)]
                    vb, vbn = vbr.next()
                    pend = None
                    for fc in range(32):
                        pp, pn = ppr.next()
                        for k in range(8):
                            self.mm(pp[:, 0:259], win[:, k, fc * 128:(fc + 1) * 128], hw[:, k, 0:259],
                                    start=(k == 0), stop=(k == 7), r=[hn, f"win{k}"], w=[pn])
                        ft, fn = fts[fc // 8]
                        if fc < 24:
                            acc, an = accr.next()
                            fin = self.conv_fm(pp, pn, acc, an, ft[:, fc % 8, :], f"{fn}_{fc % 8}", cw, cb, fc, 3, False, 0)

                            def fin2(fin=fin, fc=fc, ft=ft, fn=fn):
                                fin()
                                if fc < 8:
                                    self.cp("pool", vb[:, fc, :], ft[:, fc, :], r=[f"{fn}_{fc}"], w=[f"{vbn}_{fc}"])
                            if pend is not None:
                                pend()
                            pend = fin2
                        else:
                            if pend is not None:
                                pend()
                                pend = None
                            self.act(ft[:, fc % 8, :], pp[:, 1:257], AF.Silu, r=[pn], w=[f"{fn}_{fc % 8}"])
                    for i in range(4):
                        ft, fn = fts[i]
                        self.dma(g.fm[i, :, :, t0:t0 + 256].rearrange("c p t -> p c t"), ft[:],
                                 r=[f"{fn}_{c}" for c in range(8)])
                    for tcn in range(2):
                        pt, ptn = ptr.next()
                        for c in range(8):
                            self.tr(pt[:, c, :], vb[:, c, tcn * 128:(tcn + 1) * 128], self.ident_b[:],
                                    r=[f"{vbn}_{c}", "identb"], w=[ptn])
                        ut, utn = utr.next()
                        self.cp("dve", ut[:], pt[:].rearrange("p a b -> p (a b)"), r=[ptn], w=[utn])
                        self.dma(g.utm[t0 + tcn * 128:t0 + (tcn + 1) * 128, :], ut[:], r=[utn])

    def hy_fwd(self, g, o, s):
        if self.skip():
            return
        nc = self.nc
        L = g.L
        nL = L // 128
        nm = g.name
        with self.phase():
            u = self.sb([128, nL, 1024], BF16, "u")
            usrc = g.utm[s * L:(s + 1) * L, :].rearrange("(tc p) c -> p tc c", p=128)
            for q in range(0, nL, 8):
                qe = min(q + 8, nL)
                self.dma(u[:, q:qe, :], usrc[:, q:qe, :], w=[f"u{q}"])
            cbr = Rot([self.sb([128, nL, 128], BF16, "cbk") for _ in range(2)], "cbk")
            sbr = Rot([self.sb([128, nL, 128], BF16, "sbk") for _ in range(2)], "sbk")
            kcr = Rot([self.sb([128, 1024], F32, "kc") for _ in range(2)], "kc")
            ksr = Rot([self.sb([128, 1024], F32, "ksp") for _ in range(2)], "ksp")
            a1r = Rot([self.sb([128, 1024], F32, "a1") for _ in range(2)], "a1")
            a2r = Rot([self.sb([128, 1024], F32, "a2") for _ in range(2)], "a2")
            yor = Rot([self.sb([128, 2, 1024], BF16, "yo") for _ in range(2)], "yo")
            puc = Rot([self.ps([128, 1024], F32, "puc") for _ in range(2)], "puc")
            pus = Rot([self.ps([128, 1024], F32, "pus") for _ in range(2)], "pus")
            for fc in range(nL):
                cb_, cbn = cbr.next(); sb_, sbn = sbr.next()
                self.dma(cb_[:], self.I[f"dft_{nm}"][0, fc], w=[cbn])
                self.dma(sb_[:], self.I[f"dft_{nm}"][1, fc], w=[sbn])
                kc, kcn = kcr.next(); ksp, ksn = ksr.next()
                self.dma(kc[:], g.khat2[o, 0, fc * 128:(fc + 1) * 128, :], w=[kcn])
                self.dma(ksp[:], g.khat2[o, 1, fc * 128:(fc + 1) * 128, :], w=[ksn])
                pc_, pcn = puc.next(); ps_, psn = pus.next()
                for tc in range(nL):
                    q8 = (tc // 8) * 8
                    for hlf in range(2):
                        hs = slice(hlf * 512, (hlf + 1) * 512)
                        self.mm(pc_[:, hs], cb_[:, tc, :], u[:, tc, hs], start=(tc == 0), stop=(tc == nL - 1),
                                r=[cbn, f"u{q8}"], w=[pcn + str(hlf)])
                        self.mm(ps_[:, hs], sb_[:, tc, :], u[:, tc, hs], start=(tc == 0), stop=(tc == nL - 1),
                                r=[sbn, f"u{q8}"], w=[psn + str(hlf)])
                a1, a1n = a1r.next(); a2, a2n = a2r.next(); yo, yon = yor.next()
                for hlf in range(2):
                    hs = slice(hlf * 512, (hlf + 1) * 512)
                    self.tt("dve", a1[:, hs], pc_[:, hs], kc[:, hs], ALU.mult, r=[pcn + str(hlf), kcn], w=[a1n])
                    self.tt("dve", a2[:, hs], ps_[:, hs], ksp[:, hs], ALU.mult, r=[psn + str(hlf), ksn], w=[a2n])
                self.tt("pool", yo[:, 0, :], a1[:], a2[:], ALU.subtract, r=[a1n, a2n], w=[yon + "c"])
                a1, a1n = a1r.next(); a2, a2n = a2r.next()
                for hlf in range(2):
                    hs = slice(hlf * 512, (hlf + 1) * 512)
                    self.tt("dve", a1[:, hs], pc_[:, hs], ksp[:, hs], ALU.mult, r=[pcn + str(hlf), ksn], w=[a1n])
                    self.tt("dve", a2[:, hs], ps_[:, hs], kc[:, hs], ALU.mult, r=[psn + str(hlf), kcn], w=[a2n])
                self.tt("pool", yo[:, 1, :], a1[:], a2[:], ALU.add, r=[a1n, a2n], w=[yon + "s"])
                self.dma(g.yspec[:, :, :, fc, :].rearrange("a cc p j -> p a cc j"),
                         yo[:].rearrange("p a (cc j) -> p a cc j", j=128), r=[yon + "c", yon + "s"])

    def hy_inv(self, g, o, s):
        if self.skip():
            return
        nc = self.nc
        L = g.L
        nL = L // 128
        nm = g.name
        TT = min(512, L)
        nq = TT // 128
        with self.phase():
            fb = self.sb([128, 2, 8], F32, "fb")
            self.load_T(fb[:].rearrange("p k c -> p (k c)"),
                        self.I["hy_f_bias"].rearrange("k (c p) -> (k c) p", p=128), 16, "fb")
            cm = self.sb([128, nL, TT], BF16, "cm")
            sm = self.sb([128, nL, TT], BF16, "sm")
            ycr = Rot([self.sb([128, nL, 128], BF16, "yc") for _ in range(2)], "yc")
            ysr = Rot([self.sb([128, nL, 128], BF16, "ys") for _ in range(2)], "ys")
            utr = Rot([self.sb([128, TT], F32, "uti") for _ in range(2)], "uti")
            xgr = Rot([self.sb([128, TT], F32, "xg") for _ in range(2)], "xg")
            zgr = Rot([self.sb([128, TT], F32, "zg") for _ in range(2)], "zg")
            unr = Rot([self.sb([128, TT], F32, "un") for _ in range(2)], "un")
            ubr = Rot([self.sb([128, TT], BF16, "ub") for _ in range(2)], "ub")
            uor = Rot([self.sb([128, 8, 128], BF16, "uo") for _ in range(2)], "uo")
            par = Rot([self.ps([128, 512], F32, "pa") for _ in range(2)], "pa")
            ptq = [self.ps([128, 8, 128], BF16, "ptq") for _ in range(nq)] if o == 0 else []
            for tt in range(L // TT):
                t0 = s * L + tt * TT
                for q in range(0, nL, 8):
                    qe = min(q + 8, nL)
                    self.dma(cm[:, q:qe, :], self.I[f"dfti_{nm}"][0, tt, :, q:qe, :], w=[f"cm{q}"])
                    self.dma(sm[:, q:qe, :], self.I[f"dfti_{nm}"][1, tt, :, q:qe, :], w=[f"sm{q}"])
                for cc in range(8):
                    yc, ycn = ycr.next(); ys_, ysn = ysr.next()
                    self.dma(yc[:], g.yspec[0, cc], w=[ycn])
                    self.dma(ys_[:], g.yspec[1, cc], w=[ysn])
                    ut, utn = utr.next(); xg, xgn = xgr.next()
                    self.dma(ut[:], g.fm[0, cc, :, t0:t0 + TT], w=[utn])
                    self.dma(xg[:], g.fm[1 + o, cc, :, t0:t0 + TT], w=[xgn])
                    pa, pan = par.next()
                    for fc in range(nL):
                        q8 = (fc // 8) * 8
                        self.mm(pa[:, 0:TT], yc[:, fc, :], cm[:, fc, :], start=(fc == 0), stop=False,
                                r=[ycn, f"cm{q8}"], w=[pan])
                        self.mm(pa[:, 0:TT], ys_[:, fc, :], sm[:, fc, :], start=False, stop=(fc == nL - 1),
                                r=[ysn, f"sm{q8}"], w=[pan])
                    un, unn = unr.next()
                    self.stt(un[:], ut[:], fb[:, o, cc:cc + 1], pa[:, 0:TT], ALU.mult, ALU.add, r=[utn, "fb", pan], w=[unn])
                    self.tt("pool", un[:], un[:], xg[:], ALU.mult, r=[unn, xgn], w=[unn])
                    if o == 0:
                        self.dma(g.fm[0, cc, :, t0:t0 + TT], un[:], r=[unn])
                        ub, ubn = ubr.next()
                        self.cp("act", ub[:], un[:], r=[unn], w=[ubn])
                        for q in range(nq):
                            self.tr(ptq[q][:, cc, :], ub[:, q * 128:(q + 1) * 128], self.ident_b[:],
                                    r=[ubn, "identb"], w=[f"ptq{q}"])
                    else:
                        zg, zgn = zgr.next()
                        self.dma(zg[:], g.fm[3, cc, :, t0:t0 + TT], w=[zgn])
                        ub, ubn = ubr.next()
                        self.tt("dve", ub[:], un[:], zg[:], ALU.mult, r=[unn, zgn], w=[ubn])
                        self.dma(g.yT[cc, :, t0:t0 + TT], ub[:], r=[ubn])
                if o == 0:
                    for q in range(nq):
                        uo, uon = uor.next()
                        self.cp("act" if q % 2 == 0 else "dve", uo[:], ptq[q][:], r=[f"ptq{q}"], w=[uon])
                        self.dma(g.utm[t0 + q * 128:t0 + (q + 1) * 128, :], uo[:].rearrange("p a b -> p (a b)"), r=[uon])


def _consts():
    k = np.arange(128)[:, None]
    i = np.arange(128)[None, :]
    masks = np.stack([(k <= i), (k >= i), (k > i), (k < i)]).astype(np.float32)
    out = {"masks": masks}
    for nm, L in (("p", LP), ("s", LS)):
        n = 2 * L
        t = np.arange(L, dtype=np.float64)
        f = np.arange(L, dtype=np.float64)
        th = 2.0 * np.pi * (f + 0.5) / n
        a2 = np.outer(t + 0.5, th)
        am = np.outer(t, th)
        M = np.stack([np.cos(a2), np.sin(a2), np.cos(am), np.sin(am)]).astype(np.float32).astype(ml_dtypes.bfloat16)
        nL = L // 128
        TT = min(512, L)
        out[f"dft_{nm}"] = np.ascontiguousarray(M.reshape(4, nL, 128, nL, 128).transpose(0, 3, 2, 1, 4))
        out[f"dfti_{nm}"] = np.ascontiguousarray(M[0:2].reshape(2, nL, 128, L // TT, TT).transpose(0, 3, 2, 1, 4))
        tt = (np.arange(L, dtype=np.float32) / np.float32(L)).astype(np.float32)
        w = (np.float32(2.0 * math.pi) * np.arange(L, dtype=np.float32) / np.float32(L)).astype(np.float32)
        fr = np.linspace(1e-4, 15, 16, dtype=np.float32)
        ang = w[:, None] * fr
        z = np.concatenate([tt[:, None], np.cos(ang), np.sin(ang)], axis=-1).astype(np.float32)
        out[f"hz_{nm}"] = np.ascontiguousarray(z.T)
        deltas = np.linspace(math.log(HY_T) / 1.5, math.log(HY_T) / 0.3, D, dtype=np.float32)
        out[f"win_{nm}"] = np.exp(-tt[:, None] * np.abs(deltas)).astype(np.float32)
    return out


_CACHE = {}


def nc_inputs(nc):
    return _CACHE["in_names"]


def kernel(**inputs):
    f = lambda a: np.ascontiguousarray(np.asarray(a, dtype=np.float32))
    if "nc" not in _CACHE:
        kb = KB()
        _CACHE["nc"] = kb.build()
        _CACHE["in_names"] = set(kb.I.keys())
        _CACHE["consts"] = _consts()
    nc = _CACHE["nc"]
    consts = _CACHE["consts"]
    shared = {
        "mod_w": f(inputs["mod_w"]), "mod_b": f(inputs["mod_b"]), "ln_g": f(inputs["ln_g"]), "ln_b": f(inputs["ln_b"]),
        "ssd_in_w": f(inputs["ssd_in_w"]), "ssd_conv_w": f(inputs["ssd_conv_w"]), "ssd_conv_b": f(inputs["ssd_conv_b"]),
        "ssd_dt_bias": f(inputs["ssd_dt_bias"]).reshape(2, 64), "ssd_a_log": f(inputs["ssd_a_log"]).reshape(2, 64),
        "ssd_d": f(inputs["ssd_d"]), "ssd_norm_g": f(inputs["ssd_norm_g"]), "ssd_out_w": f(inputs["ssd_out_w"]),
        "hy_in_w": f(inputs["hy_in_w"])[0], "hy_conv_w": f(inputs["hy_conv_w"])[0], "hy_conv_b": f(inputs["hy_conv_b"])[0],
        "hy_f_w1": f(inputs["hy_f_w1"])[0], "hy_f_b1": f(inputs["hy_f_b1"])[0], "hy_f_w2": f(inputs["hy_f_w2"])[0],
        "hy_f_b2": f(inputs["hy_f_b2"])[0], "hy_f_w3": f(inputs["hy_f_w3"])[0], "hy_f_freq": f(inputs["hy_f_freq"])[0],
        "hy_f_bias": f(inputs["hy_f_bias"])[0], "hy_out_w": f(inputs["hy_out_w"])[0],
        "lru_in_w": f(inputs["lru_in_w"])[0], "lru_conv_w": f(inputs["lru_conv_w"])[0], "lru_conv_b": f(inputs["lru_conv_b"])[0],
        "lru_gate_w": f(inputs["lru_gate_w"])[0], "lru_gate_b": f(inputs["lru_gate_b"])[0].reshape(4, D),
        "lru_a_param": f(inputs["lru_a_param"])[0], "lru_out_w": f(inputs["lru_out_w"])[0],
    }
    shared.update(consts)
    shared = {k: v for k, v in shared.items() if k in nc_inputs(nc)}
    xp = f(inputs["x_prompt"]); xs = f(inputs["x_sample"])
    sts = f(inputs["state_ssd"]); stl = f(inputs["state_lru"])
    c = f(inputs["c"]); cc = f(inputs["c_ctx"])
    in_maps = []
    for core in range(8):
        b = core // 2
        m = dict(shared)
        m["xp"] = np.ascontiguousarray(xp[core * NPS:(core + 1) * NPS].reshape(NPS * LP, D))
        m["xs"] = np.ascontiguousarray(xs[b])
        m["st_ssd"] = np.ascontiguousarray(sts[b].reshape(2, 2, 2048, 128))
        m["st_lru"] = np.ascontiguousarray(stl[b].reshape(2, D))
        m["cond"] = np.ascontiguousarray(np.stack([cc, c[b]]))
        in_maps.append(m)
    res = run_bass_kernel_spmd(nc, in_maps, core_ids=list(range(8)))
    r = res.results
    y_prompt = np.concatenate([r[i]["yp"].reshape(NPS, LP, D) for i in range(8)], axis=0)
    y_sample = np.stack([r[2 * b]["ys"] for b in range(4)], axis=0)
    nss = np.concatenate([r[i]["nss"].reshape(NPS, 2, 2, 32, 64, 128) for i in range(8)], axis=0)
    nsl = np.concatenate([r[i]["nsl"].reshape(NPS, 1, 2, D) for i in range(8)], axis=0)
    return (y_prompt.astype(np.float32), y_sample.astype(np.float32), nss.astype(np.float32), nsl.astype(np.float32))
```

```python
import contextlib
import math
import numpy as np
import ml_dtypes
import concourse.bass as bass
import concourse.mybir as mybir
from concourse.bass_utils import run_bass_kernel_spmd

F32 = mybir.dt.float32
F32R = mybir.dt.float32r
BF16 = mybir.dt.bfloat16
AF = mybir.ActivationFunctionType
ALU = mybir.AluOpType

EPOCH = 8000
NDMA = 28
NDMA_HW = 20
import os as _os
MAXOPS = int(_os.environ.get("MAXOPS", "1000000000"))
NOSELF = _os.environ.get("NOSELF", "0") == "1"

D = 1024
NPS = 4
LP = 256
LS = 4096
DEPTH = 4
ALPHA = (2.0 * DEPTH) ** 0.25
LN_EPS = 1e-5
RMS_EPS = 1e-5
SSD_PROJ = 6208
HY_T = 1e-2


class Sched:
    ENG = ("pe", "act", "dve", "pool")

    def __init__(self, nc, stack):
        self.nc = nc
        self.stack = stack
        self.eng = {"pe": nc.tensor, "act": nc.scalar, "dve": nc.vector,
                    "pool": nc.gpsimd, "sp": nc.sync}
        self.ops = {e: [] for e in self.eng}
        self.cnt = {e: 0 for e in self.ENG}
        self.esems = {e: [] for e in self.ENG}
        self.dsems = [stack.enter_context(nc.semaphore(f"dma{i}")) for i in range(NDMA)]
        self.dval = [0] * NDMA
        self.dnext = 0
        self.dnext_sw = 0
        self.waited = {e: {} for e in self.eng}
        self.lastw = {}
        self.readers = {}
        self.n_inst = 0

    def _esem(self, e, count):
        k = (count - 1) // EPOCH
        while len(self.esems[e]) <= k:
            self.esems[e].append(self.stack.enter_context(
                self.nc.semaphore(f"s_{e}_{len(self.esems[e])}")))
        return self.esems[e][k], (count - 1) % EPOCH + 1, k

    def _emit_wait(self, e, ev):
        if ev[0] == "e":
            _, src, count = ev
            if src == e and (e == "pe" or NOSELF):
                return
            sem, val, k = self._esem(src, count)
            key = ("e", src, k)
        else:
            _, idx, val = ev
            sem = self.dsems[idx]
            key = ("d", idx)
        if self.waited[e].get(key, 0) >= val:
            return
        self.waited[e][key] = val
        engobj = self.eng[e]
        self.ops[e].append(lambda engobj=engobj, sem=sem, val=val: engobj.wait_ge(sem, val))

    def _deps(self, e, reads, writes):
        evs = []
        for r in reads:
            if r in self.lastw:
                evs.append(self.lastw[r])
        for w in writes:
            if w in self.lastw:
                evs.append(self.lastw[w])
            evs.extend(self.readers.get(w, ()))
        for ev in evs:
            self._emit_wait(e, ev)

    def _commit(self, ev, reads, writes):
        for r in reads:
            self.readers.setdefault(r, []).append(ev)
        for w in writes:
            self.lastw[w] = ev
            self.readers[w] = []

    def op(self, e, fn, reads=(), writes=()):
        if self.n_inst >= MAXOPS:
            return
        self._deps(e, reads, writes)
        self.cnt[e] += 1
        count = self.cnt[e]
        sem, val, k = self._esem(e, count)
        self.ops[e].append(lambda fn=fn, sem=sem: fn().then_inc(sem, 1))
        self._commit(("e", e, count), reads, writes)
        self.n_inst += 1

    def dma(self, q, fn, reads=(), writes=()):
        if self.n_inst >= MAXOPS:
            return
        if q == "sp":
            idx = self.dnext
            self.dnext = (self.dnext + 1) % NDMA_HW
        else:
            idx = NDMA_HW + self.dnext_sw
            self.dnext_sw = (self.dnext_sw + 1) % (NDMA - NDMA_HW)
        if self.dval[idx] > 0:
            self._emit_wait(q, ("d", idx, self.dval[idx]))
        self._deps(q, reads, writes)
        self.dval[idx] += 16
        val = self.dval[idx]
        sem = self.dsems[idx]
        self.ops[q].append(lambda fn=fn, sem=sem: fn().then_inc(sem, 16))
        self._commit(("d", idx, val), reads, writes)
        self.n_inst += 1

    def barrier(self):
        for e in self.eng:
            for en in self.ENG:
                if self.cnt[en] and not (en == e):
                    self._emit_wait(e, ("e", en, self.cnt[en]))
                elif self.cnt[en] and e != "pe":
                    self._emit_wait(e, ("e", en, self.cnt[en]))
            for i in range(NDMA):
                if self.dval[i]:
                    self._emit_wait(e, ("d", i, self.dval[i]))
        self.lastw = {}
        self.readers = {}

    def finish(self):
        self.barrier()
        nc = self.nc
        with nc.Block() as block:
            @block.tensor
            def _(t):
                for f in self.ops["pe"]:
                    f()

            @block.scalar
            def _(t):
                for f in self.ops["act"]:
                    f()

            @block.vector
            def _(t):
                for f in self.ops["dve"]:
                    f()

            @block.gpsimd
            def _(t):
                for f in self.ops["pool"]:
                    f()

            @block.sync
            def _(t):
                for f in self.ops["sp"]:
                    f()


class Rot:
    def __init__(self, tiles, name):
        self.tiles = tiles
        self.name = name
        self.i = -1

    def next(self):
        self.i += 1
        k = self.i % len(self.tiles)
        return self.tiles[k], f"{self.name}{k}"


class Grp:
    pass


class KB:
    def __init__(self, layers=(0, 1, 2, 3), do_prompt=True, do_sample=True):
        self.layers = layers
        self.do_prompt = do_prompt
        self.do_sample = do_sample
        self.nc = bass.Bass("TRN2", target_bir_lowering=False)
        self.I = {}
        self.O = {}
        self.uid = 0
        self.nphase = 0
        self.max_phase = 10 ** 9

    def skip(self):
        self.nphase += 1
        return self.nphase > self.max_phase

    def inp(self, name, shape, dt=F32):
        self.I[name] = self.nc.dram_tensor(name, list(shape), dt, kind="ExternalInput").ap()
        return self.I[name]

    def outp(self, name, shape, dt=F32):
        self.O[name] = self.nc.dram_tensor(name, list(shape), dt, kind="ExternalOutput").ap()
        return self.O[name]

    def scr(self, name, shape, dt):
        return self.nc.dram_tensor(name, list(shape), dt, kind="Internal").ap()

    def nm(self, p):
        self.uid += 1
        return f"{p}_{self.uid}"

    def sb(self, shape, dt, name="t"):
        return self.ph.enter_context(self.nc.sbuf_tensor(self.nm(name), list(shape), dt))

    def ps(self, shape, dt, name="p"):
        return self.ph.enter_context(self.nc.psum_tensor(self.nm(name), list(shape), dt))

    @contextlib.contextmanager
    def phase(self):
        with contextlib.ExitStack() as ph:
            old = getattr(self, "ph", None)
            self.ph = ph
            yield
            self.S.barrier()
            self.ph = old

    def dma(self, out, in_, r=(), w=(), q="sp", **kw):
        eng = self.nc.sync if q == "sp" else self.nc.gpsimd
        self.S.dma(q, lambda: eng.dma_start(out=out, in_=in_, **kw), reads=r, writes=w)

    def mm(self, out, lhsT, rhs, start=True, stop=True, r=(), w=()):
        self.S.op("pe", lambda: self.nc.tensor.matmul(out, lhsT=lhsT, rhs=rhs, start=start, stop=stop),
                  reads=r, writes=w)

    def tr(self, out, in_, ident, r=(), w=()):
        self.S.op("pe", lambda: self.nc.tensor.transpose(out=out, in_=in_, identity=ident), reads=r, writes=w)

    def act(self, out, in_, func, r=(), w=(), **kw):
        self.S.op("act", lambda: self.nc.scalar.activation(out=out, in_=in_, func=func, **kw), reads=r, writes=w)

    def E(self, e):
        return self.nc.vector if e == "dve" else self.nc.gpsimd

    def tt(self, e, out, in0, in1, op, r=(), w=()):
        self.S.op(e, lambda: self.E(e).tensor_tensor(out=out, in0=in0, in1=in1, op=op), reads=r, writes=w)

    def ts(self, e, out, in0, s1, s2, op0, op1=None, r=(), w=()):
        if op1 is None:
            self.S.op(e, lambda: self.E(e).tensor_scalar(out=out, in0=in0, scalar1=s1, scalar2=None, op0=op0),
                      reads=r, writes=w)
        else:
            self.S.op(e, lambda: self.E(e).tensor_scalar(out=out, in0=in0, scalar1=s1, scalar2=s2, op0=op0, op1=op1),
                      reads=r, writes=w)

    def stt(self, out, in0, scalar, in1, op0, op1, r=(), w=()):
        self.S.op("dve", lambda: self.nc.vector.scalar_tensor_tensor(out=out, in0=in0, scalar=scalar, in1=in1,
                                                                     op0=op0, op1=op1), reads=r, writes=w)

    def cp(self, e, out, in_, r=(), w=()):
        if e == "act":
            self.S.op("act", lambda: self.nc.scalar.copy(out=out, in_=in_), reads=r, writes=w)
        else:
            self.S.op(e, lambda: self.E(e).tensor_copy(out=out, in_=in_), reads=r, writes=w)

    def memset(self, e, ap, val, w=()):
        self.S.op(e, lambda: self.E(e).memset(ap, val), writes=w)

    def declare(self):
        inp = self.inp
        inp("xp", [NPS * LP, D]); inp("xs", [LS, D])
        inp("st_ssd", [2, 2, 2048, 128]); inp("st_lru", [2, D]); inp("cond", [2, D])
        inp("mod_w", [4, D, 3 * D]); inp("mod_b", [4, 3 * D]); inp("ln_g", [4, D]); inp("ln_b", [4, D])
        inp("ssd_in_w", [2, D, SSD_PROJ]); inp("ssd_conv_w", [2, 4, 4096]); inp("ssd_conv_b", [2, 4096])
        inp("ssd_dt_bias", [2, 64]); inp("ssd_a_log", [2, 64]); inp("ssd_d", [2, 32])
        inp("ssd_norm_g", [2, 2048]); inp("ssd_out_w", [2, 2048, D])
        inp("hy_in_w", [D, 4096]); inp("hy_conv_w", [3, 3072]); inp("hy_conv_b", [3072])
        inp("hy_f_w1", [33, 64]); inp("hy_f_b1", [64]); inp("hy_f_w2", [64, 64]); inp("hy_f_b2", [64])
        inp("hy_f_w3", [64, 4096]); inp("hy_f_freq", [2, 64]); inp("hy_f_bias", [2, D]); inp("hy_out_w", [D, D])
        inp("lru_in_w", [D, 2048]); inp("lru_conv_w", [4, D]); inp("lru_conv_b", [D])
        inp("lru_gate_w", [2, 2, 4, 256, 256]); inp("lru_gate_b", [4, D]); inp("lru_a_param", [2, D])
        inp("lru_out_w", [D, D])
        inp("masks", [4, 128, 128])
        if 1 in self.layers:
            inp("dft_p", [4, LP // 128, 128, LP // 128, 128], BF16)
            inp("dft_s", [4, LS // 128, 128, LS // 128, 128], BF16)
            inp("dfti_p", [2, 1, 128, LP // 128, 256], BF16)
            inp("dfti_s", [2, LS // 512, 128, LS // 128, 512], BF16)
            inp("hz_p", [33, LP]); inp("hz_s", [33, LS])
            inp("win_p", [LP, D]); inp("win_s", [LS, D])
        self.outp("yp", [NPS * LP, D]); self.outp("ys", [LS, D])
        self.outp("nss", [NPS, 2, 2, 2048, 128]); self.outp("nsl", [NPS, 2, D])

    def build(self):
        nc = self.nc
        self.declare()
        with contextlib.ExitStack() as st:
            self.S = Sched(nc, st)
            self.ph = st
            self.ident_f = self.sb([128, 128], F32, "identf")
            self.ident_b = self.sb([128, 128], BF16, "identb")
            self.masks = self.sb([128, 4, 128], F32, "masks")
            self.ones_f = self.sb([128, 128], F32, "ones")
            self.zero_b = self.sb([128, 8, 4], BF16, "zerob")
            self.dma(self.masks[:], self.I["masks"].rearrange("m p i -> p m i"), w=["masks"])
            self.memset("pool", self.ident_f[:], 1.0, w=["identf"])
            self.S.op("pool", lambda: nc.gpsimd.affine_select(
                out=self.ident_f[:], in_=self.ident_f[:], pattern=[[-1, 128]], compare_op=ALU.is_equal,
                fill=0.0, base=0, channel_multiplier=1), reads=["identf"], writes=["identf"])
            self.cp("dve", self.ident_b[:], self.ident_f[:], r=["identf"], w=["identb"])
            self.memset("dve", self.ones_f[:], 1.0, w=["ones"])
            self.masks_r = self.sb([128, 4, 128], F32R, "masksr")
            self.ones_r = self.sb([128, 128], F32R, "onesr")
            self.cp("dve", self.masks_r[:], self.masks[:], r=["masks"], w=["masksr"])
            self.cp("dve", self.ones_r[:], self.ones_f[:], r=["ones"], w=["onesr"])
            self.memset("dve", self.zero_b[:], 0.0, w=["zerob"])
            self.lru_h0 = self.sb([128, 2, 8], F32, "lruh0")
            self.S.barrier()

            self.mod_scr = self.scr("mod_scr", [4, 2, 3 * D], F32)
            groups = []
            if self.do_prompt:
                g = Grp(); g.name = "p"; g.nseq = NPS; g.L = LP; g.cond = 0; g.latent = False
                g.x_in = self.I["xp"]; g.x_out = self.O["yp"]
                groups.append(g)
            if self.do_sample:
                g = Grp(); g.name = "s"; g.nseq = 1; g.L = LS; g.cond = 1; g.latent = True
                g.x_in = self.I["xs"]; g.x_out = self.O["ys"]
                groups.append(g)
            for g in groups:
                g.T = g.nseq * g.L
                g.W = g.nseq * (g.L + 3)
                g.xa = self.scr(f"xa_{g.name}", [g.T, D], F32)
                g.xb = self.scr(f"xb_{g.name}", [g.T, D], F32)
                g.hT = self.scr(f"hT_{g.name}", [8, 128, g.W], BF16)
                g.yT = self.scr(f"yT_{g.name}", [16, 128, g.T], BF16)
                g.xs_tm = self.scr(f"xstm_{g.name}", [g.T, 2048], BF16)
                g.b_tm = self.scr(f"btm_{g.name}", [g.T, 1024], BF16)
                g.bcT = self.scr(f"bcT_{g.name}", [16, 128, g.T], BF16)
                g.dta = self.scr(f"dta_{g.name}", [g.T, 128], F32)
                g.sloc = self.scr(f"sloc_{g.name}", [g.T // 128, 2, 128, 2048], F32)
                g.cdec = self.scr(f"cdec_{g.name}", [g.T // 128, 128, 64], F32)
                g.hprev = self.scr(f"hprev_{g.name}", [g.T // 128, 2, 128, 2048], BF16)
                g.fm = self.scr(f"fm_{g.name}", [4, 8, 128, g.T], F32)
                g.utm = self.scr(f"utm_{g.name}", [g.T, D], BF16)
                g.kw = self.scr(f"kw_{g.name}", [g.L, 4096], F32)
                g.khat = self.scr(f"khat_{g.name}", [2, 2, g.L, D], F32)
                g.yspec = self.scr(f"ysp_{g.name}", [2, 8, 128, g.L // 128, 128], BF16)
                g.khat2 = self.scr(f"khat2_{g.name}", [2, 2, g.L, D], F32)
            self.groups = groups

            self.modulation()
            for li in self.layers:
                last = (li == self.layers[-1])
                for g in groups:
                    X_in = g.x_in if li == self.layers[0] else (g.xa if (li % 2 == 1) else g.xb)
                    X_out = g.x_out if last else (g.xa if (li % 2 == 0) else g.xb)
                    col = g.latent and li == 3
                    self.pass_A(g, li, X_in, col)
                    kind = li % 3
                    if kind == 0:
                        self.ssd_layer(g, li // 3, li)
                        KC = 16
                    elif kind == 1:
                        self.hyena_layer(g)
                        KC = 8
                    else:
                        self.lru_layer(g)
                        KC = 8
                    wname = {0: "ssd_out_w", 1: "hy_out_w", 2: "lru_out_w"}[kind]
                    wout = self.I[wname][li // 3] if kind == 0 else self.I[wname]
                    self.pass_E(g, li, X_in, X_out, col, wout, KC)
            self.S.finish()
        return nc

    def xrows(self, g, X, s, c, col):
        if not col:
            r0 = s * g.L + c * 128
            return [(X[r0:r0 + 128, :], 0, 128)]
        Xv = X.rearrange("(r w) f -> w r f", w=64)
        return [(Xv[2 * c + wo], wo * 64, 64) for wo in range(2)]

    def load_T(self, dst, src2d, R, name):
        stg = self.sb([128, 128], F32, "ldT")
        if getattr(self, "_ldT_ph", None) is not self.ph:
            self._ldT_ph = self.ph
            self._ldT_ps = self.ps([128, 128], F32, "ldTp")
        pt = self._ldT_ps
        k = self.nm("ldT")
        self.dma(stg[0:R, :], src2d, w=[k])
        self.tr(pt[:, 0:R], stg[0:R, :], self.ident_f[0:R, 0:R], r=[k, "identf"], w=["ldTp"])
        self.cp("dve", dst, pt[:, 0:R], r=["ldTp"], w=[name])

    def modulation(self):
        if self.skip():
            return
        nc = self.nc
        with self.phase():
            cond = self.sb([2, D], F32, "cond")
            cs = self.sb([2, D], F32, "cs")
            condT = self.sb([128, 8, 2], BF16, "condT")
            pT = self.ps([128, 8, 2], F32, "pT")
            self.dma(cond[:], self.I["cond"], w=["cond"])
            self.act(cs[:], cond[:], AF.Silu, r=["cond"], w=["cs"])
            for k in range(8):
                self.tr(pT[:, k, :], cs[0:2, k * 128:(k + 1) * 128], self.ident_f[0:2, 0:2],
                        r=["cs", "identf"], w=["pT"])
            self.cp("dve", condT[:], pT[:], r=["pT"], w=["condT"])
            mw = self.sb([128, 8, 3 * D], BF16, "mw")
            mb = self.sb([2, 3 * D], F32, "mb")
            msb = self.sb([2, 3 * D], F32, "msb")
            pm = [self.ps([2, 512], F32, "pm") for _ in range(2)]
            for li in self.layers:
                for k in range(8):
                    self.dma(mw[:, k, :], self.I["mod_w"][li, k * 128:(k + 1) * 128, :], w=[f"mw{k}"], q="pool")
                for c in range(2):
                    self.dma(mb[c:c + 1, :], self.I["mod_b"][li:li + 1, :], w=["mb"])
                for t in range(6):
                    p = pm[t % 2]
                    for k in range(8):
                        self.mm(p[:], condT[:, k, :], mw[:, k, t * 512:(t + 1) * 512], start=(k == 0), stop=(k == 7),
                                r=["condT", f"mw{k}"], w=[f"pm{t % 2}"])
                    self.tt("dve", msb[:, t * 512:(t + 1) * 512], p[:], mb[:, t * 512:(t + 1) * 512], ALU.add,
                            r=[f"pm{t % 2}", "mb"], w=["msb"])
                self.ts("dve", msb[:, D:2 * D], msb[:, D:2 * D], 1.0, None, ALU.add, r=["msb"], w=["msb"])
                self.dma(self.mod_scr[li], msb[:], r=["msb"])

    def pass_A(self, g, li, X, col):
        if self.skip():
            return
        with self.phase():
            sc = self.sb([128, D], F32, "sc")
            sh = self.sb([128, D], F32, "sh")
            self.dma(sh[:], self.mod_scr[li, g.cond, 0:D].partition_broadcast(128), w=["sh"])
            self.dma(sc[:], self.mod_scr[li, g.cond, D:2 * D].partition_broadcast(128), w=["sc"])
            xr = Rot([self.sb([128, D], F32, "xA") for _ in range(2)], "xA")
            hr = Rot([self.sb([128, D], BF16, "hA") for _ in range(2)], "hA")
            tr_ = Rot([self.sb([128, 8, 128], BF16, "hTA") for _ in range(2)], "hTA")
            pr = Rot([self.ps([128, 8, 128], BF16, "pA") for _ in range(2)], "pA")
            nchunk = g.L // 128
            for s in range(g.nseq):
                base = s * (g.L + 3)
                self.dma(g.hT[:, :, base:base + 1].rearrange("k p t -> p k t"), self.zero_b[:, :, 0:1], r=["zerob"],
                         allow_slow_non_contiguous=True)
                self.dma(g.hT[:, :, base + g.L + 1:base + g.L + 3].rearrange("k p t -> p k t"),
                         self.zero_b[:, :, 0:2], r=["zerob"], allow_slow_non_contiguous=True)
                for c in range(nchunk):
                    xt, xn = xr.next()
                    for (src, p0, n) in self.xrows(g, X, s, c, col):
                        self.dma(xt[p0:p0 + n, :], src, w=[xn])
                    ht, hn = hr.next()
                    self.tt("dve", xt[:], xt[:], sc[:], ALU.mult, r=[xn, "sc"], w=[xn])
                    self.tt("dve", ht[:], xt[:], sh[:], ALU.add, r=[xn, "sh"], w=[hn])
                    pt, pn = pr.next()
                    for k in range(8):
                        self.tr(pt[:, k, :], ht[:, k * 128:(k + 1) * 128], self.ident_b[:], r=[hn, "identb"], w=[pn])
                    tt_, tn = tr_.next()
                    self.cp("act", tt_[:], pt[:], r=[pn], w=[tn])
                    c0 = base + 1 + c * 128
                    self.dma(g.hT[:, :, c0:c0 + 128].rearrange("k p t -> p k t"), tt_[:], r=[tn])

    def pass_E(self, g, li, X, Xo, col, wout, KC):
        if self.skip():
            return
        nc = self.nc
        with self.phase():
            wo = self.sb([128, KC, D], BF16, "wo")
            for k in range(KC):
                self.dma(wo[:, k, :], wout[k * 128:(k + 1) * 128, :], w=[f"wo{k}"], q="pool")
            gt = self.sb([128, D], F32, "gate")
            lg = self.sb([128, D], F32, "lng")
            lb = self.sb([128, D], F32, "lnb")
            self.dma(gt[:], self.mod_scr[li, g.cond, 2 * D:3 * D].partition_broadcast(128), w=["gate"])
            self.dma(lg[:], self.I["ln_g"][li].partition_broadcast(128), w=["lng"])
            self.dma(lb[:], self.I["ln_b"][li].partition_broadcast(128), w=["lnb"])
            yr = Rot([self.sb([128, KC, 128], BF16, "yE") for _ in range(2)], "yE")
            xr = Rot([self.sb([128, D], F32, "xE") for _ in range(2)], "xE")
            rr = Rot([self.sb([128, D], F32, "rE") for _ in range(2)], "rE")
            sr = Rot([self.sb([128, 16], F32, "sE") for _ in range(2)], "sE")
            pr = Rot([self.ps([128, D], F32, "pE") for _ in range(2)], "pE")
            nchunk = g.L // 128

            def chunk_gen(s, c):
                t0 = s * g.L + c * 128
                yt, yn = yr.next()
                self.dma(yt[:], g.yT[0:KC, :, t0:t0 + 128].rearrange("k p t -> p k t"), w=[yn])
                xt, xn = xr.next()
                for (src, p0, n) in self.xrows(g, X, s, c, col):
                    self.dma(xt[p0:p0 + n, :], src, w=[xn])
                pt, pn = pr.next()
                rt, rn = rr.next()
                stt_, sn = sr.next()
                yield
                for hlf in range(2):
                    for k in range(KC):
                        self.mm(pt[:, hlf * 512:(hlf + 1) * 512], yt[:, k, :], wo[:, k, hlf * 512:(hlf + 1) * 512],
                                start=(k == 0), stop=(k == KC - 1), r=[yn, f"wo{k}"], w=[pn + str(hlf)])
                yield
                for hlf in range(2):
                    sl = slice(hlf * 512, (hlf + 1) * 512)
                    self.tt("dve", rt[:, sl], pt[:, sl], gt[:, sl], ALU.mult, r=[pn + str(hlf), "gate"], w=[rn])
                yield
                self.stt(rt[:], xt[:], ALPHA, rt[:], ALU.mult, ALU.add, r=[xn, rn], w=[rn])
                yield
                self.S.op("dve", lambda: nc.vector.bn_stats(out=stt_[:, 0:6], in_=rt[:, 0:512]),
                          reads=[rn], writes=[sn])
                self.S.op("dve", lambda: nc.vector.bn_stats(out=stt_[:, 6:12], in_=rt[:, 512:1024]),
                          reads=[rn], writes=[sn])
                yield
                self.S.op("dve", lambda: nc.vector.bn_aggr(out=stt_[:, 12:14], in_=stt_[:, 0:12]),
                          reads=[sn], writes=[sn])
                yield
                self.ts("dve", stt_[:, 14:15], stt_[:, 13:14], LN_EPS, None, ALU.add, r=[sn], w=[sn])
                yield
                self.act(stt_[:, 14:15], stt_[:, 14:15], AF.Sqrt, r=[sn], w=[sn])
                yield
                self.S.op("dve", lambda: nc.vector.reciprocal(out=stt_[:, 15:16], in_=stt_[:, 14:15]),
                          reads=[sn], writes=[sn])
                yield
                self.ts("dve", rt[:], rt[:], stt_[:, 12:13], stt_[:, 15:16], ALU.subtract, ALU.mult,
                        r=[rn, sn], w=[rn])
                yield
                self.tt("pool", rt[:], rt[:], lg[:], ALU.mult, r=[rn, "lng"], w=[rn])
                yield
                self.tt("pool", rt[:], rt[:], lb[:], ALU.add, r=[rn, "lnb"], w=[rn])
                yield
                for (dst, p0, n) in self.xrows(g, Xo, s, c, col):
                    self.dma(dst, rt[p0:p0 + n, :], r=[rn])

            gens = [chunk_gen(s, c) for s in range(g.nseq) for c in range(nchunk)]
            for i in range(0, len(gens), 2):
                active = gens[i:i + 2]
                while active:
                    for gch in list(active):
                        try:
                            next(gch)
                        except StopIteration:
                            active.remove(gch)

    def conv_fm(self, pp, pn, acc, an, dst, dn, cw, cb, fc, K, silu, woff):
        self.act(acc[:], pp[:, woff:woff + 256], AF.Identity, r=[pn, "cw", "cb"], w=[an],
                 scale=cw[:, 0, fc:fc + 1], bias=cb[:, fc:fc + 1])
        for k in range(1, K):
            self.stt(acc[:], pp[:, woff + k:woff + k + 256], cw[:, k, fc:fc + 1], acc[:], ALU.mult, ALU.add,
                     r=[pn, an, "cw"], w=[an])
        def fin():
            if silu:
                self.act(dst, acc[:], AF.Silu, r=[an], w=[dn])
            else:
                self.cp("act", dst, acc[:], r=[an], w=[dn])
        return fin

    def conv_steps(self, pp, pn, acc, an, dst, dn, cw, cb, fc, K, silu):
        steps = [lambda: self.act(acc[:], pp[:, 0:256], AF.Identity, r=[pn, "cw", "cb"], w=[an],
                                  scale=cw[:, 0, fc:fc + 1], bias=cb[:, fc:fc + 1])]
        for k in range(1, K):
            steps.append(lambda k=k: self.stt(acc[:], pp[:, k:k + 256], cw[:, k, fc:fc + 1], acc[:], ALU.mult, ALU.add,
                                              r=[pn, an, "cw"], w=[an]))

        def fin():
            if silu:
                self.act(dst, acc[:], AF.Silu, r=[an], w=[dn])
            else:
                self.cp("act", dst, acc[:], r=[an], w=[dn])
        return steps, fin

    def conv_block(self, fcs, ppr, accr, mm_fn, dst_fn, cw, cb, K, silu, after_fn=None):
        fcs = list(fcs)
        pend = []
        for i in range(0, len(fcs), 2):
            steps = []
            fins = []
            for fc in fcs[i:i + 2]:
                pp, pn = ppr.next()
                mm_fn(fc, pp, pn)
                acc, an = accr.next()
                dst, dn = dst_fn(fc)
                st, fin = self.conv_steps(pp, pn, acc, an, dst, dn, cw, cb, fc, K, silu)
                steps.append(st)
                fins.append((fin, fc))
            for j in range(K):
                for st in steps:
                    st[j]()
            for f, fcp in pend:
                f()
                if after_fn is not None:
                    after_fn(fcp)
            pend = fins
        for f, fcp in pend:
            f()
            if after_fn is not None:
                after_fn(fcp)

    def ssd_layer(self, g, slot, li):
        self.ssd_p1(g, slot)
        if _os.environ.get("DEBUG") == "dta":
            with self.phase():
                t = self.sb([128, 8, 128], F32, "dbg")
                self.dma(t[:], g.dta[0:1024, :].rearrange("(c p) f -> p c f", p=128), w=["dbg"])
                self.dma(self.O["yp"][:, 0:128].rearrange("(c p) f -> p c f", p=128), t[:], r=["dbg"])
        self.ssd_p2a(g, slot)
        self.ssd_pR(g, slot)
        self.ssd_p2b(g, slot)

    def ssd_p1(self, g, slot):
        if self.skip():
            return
        nc = self.nc
        with self.phase():
            w_in = self.I["ssd_in_w"][slot]
            wx = self.sb([128, 8, 4096], BF16, "wx")
            wd = self.sb([128, 8, 64], BF16, "wd")
            for k in range(8):
                self.dma(wx[:, k, :], w_in[k * 128:(k + 1) * 128, 2048:6144], w=[f"wx{k}"], q="pool")
                self.dma(wd[:, k, :], w_in[k * 128:(k + 1) * 128, 6144:6208], w=["wd"], q="pool")
            cw = self.sb([128, 4, 32], F32, "cw")
            cb = self.sb([128, 32], F32, "cb")
            self.load_T(cw[:].rearrange("p k c -> p (k c)"),
                        self.I["ssd_conv_w"][slot].rearrange("k (c p) -> (k c) p", p=128), 128, "cw")
            self.load_T(cb[:], self.I["ssd_conv_b"][slot].rearrange("(c p) -> c p", p=128), 32, "cb")
            dtb = self.sb([128, 64], F32, "dtb")
            abc = self.sb([128, 64], F32, "abc")
            self.dma(dtb[:], self.I["ssd_dt_bias"][slot].partition_broadcast(128), w=["dtb"])
            self.dma(abc[:], self.I["ssd_a_log"][slot].partition_broadcast(128), w=["abc"])
            self.act(abc[:], abc[:], AF.Exp, r=["abc"], w=["abc"])
            self.ts("dve", abc[:], abc[:], -1.0, None, ALU.mult, r=["abc"], w=["abc"])

            hwr = Rot([self.sb([128, 8, 260], BF16, "hw") for _ in range(2)], "hw")
            xbr = Rot([self.sb([128, 32, 256], BF16, "xbc") for _ in range(2)], "xbc")
            accr = Rot([self.sb([128, 256], F32, "acc") for _ in range(4)], "acc")
            tmr = Rot([self.sb([128, 1024], BF16, "tm") for _ in range(3)], "tm")
            dtr = Rot([self.sb([128, 128], F32, "dta") for _ in range(2)], "dta")
            ppr = Rot([self.ps([128, 512], F32, "pp") for _ in range(4)], "pp")
            ptr = Rot([self.ps([128, 8, 128], BF16, "ptr") for _ in range(2)], "ptr")
            pdr = Rot([self.ps([128, 64], F32, "pd") for _ in range(1)], "pd")
            for s in range(g.nseq):
                for j in range(g.L // 256):
                    c0 = s * (g.L + 3) + 256 * j
                    t0 = s * g.L + 256 * j
                    hw, hn = hwr.next()
                    self.dma(hw[:, :, 0:259], g.hT[:, :, c0:c0 + 259].rearrange("k p t -> p k t"), w=[hn])
                    xb, xn = xbr.next()
                    def mm_fn(fc, pp, pn, hw=hw, hn=hn):
                        for k in range(8):
                            self.mm(pp[:, 0:259], wx[:, k, fc * 128:(fc + 1) * 128], hw[:, k, 0:259],
                                    start=(k == 0), stop=(k == 7), r=[hn, f"wx{k}"], w=[pn])
                    self.conv_block(range(32), ppr, accr, mm_fn, lambda fc, xb=xb, xn=xn: (xb[:, fc, :], f"{xn}_{fc}"),
                                    cw, cb, 4, True)
                    self.dma(g.bcT[:, :, t0:t0 + 256].rearrange("c p t -> p c t"), xb[:, 16:32, :],
                             r=[f"{xn}_{fc}" for fc in range(16, 32)])
                    for tcn in range(2):
                        for blk in range(3):
                            pt, ptn = ptr.next()
                            for i in range(8):
                                fc = blk * 8 + i
                                self.tr(pt[:, i, :], xb[:, fc, tcn * 128:(tcn + 1) * 128], self.ident_b[:],
                                        r=[f"{xn}_{fc}", "identb"], w=[ptn])
                            tm, tn = tmr.next()
                            self.cp("act" if blk % 2 == 0 else "dve", tm[:], pt[:].rearrange("p a b -> p (a b)"),
                                    r=[ptn], w=[tn])
                            r0 = t0 + tcn * 128
                            if blk < 2:
                                self.dma(g.xs_tm[r0:r0 + 128, blk * 1024:(blk + 1) * 1024], tm[:], r=[tn])
                            else:
                                self.dma(g.b_tm[r0:r0 + 128, :], tm[:], r=[tn])
                        pd, pdn = pdr.next()
                        for k in range(8):
                            self.mm(pd[:], hw[:, k, 1 + tcn * 128:1 + (tcn + 1) * 128], wd[:, k, :],
                                    start=(k == 0), stop=(k == 7), r=[hn, "wd"], w=[pdn])
                        dt, dn = dtr.next()
                        self.tt("dve", dt[:, 0:64], pd[:], dtb[:], ALU.add, r=[pdn, "dtb"], w=[dn])
                        self.act(dt[:, 0:64], dt[:, 0:64], AF.Exp, r=[dn], w=[dn])
                        self.act(dt[:, 0:64], dt[:, 0:64], AF.Ln, r=[dn], w=[dn], bias=1.0)
                        self.tt("dve", dt[:, 64:128], dt[:, 0:64], abc[:], ALU.mult, r=[dn, "abc"], w=[dn])
                        self.dma(g.dta[r0:r0 + 128, :], dt[:], r=[dn])

    def ssd_p2a(self, g, slot):
        if self.skip():
            return
        nc = self.nc
        with self.phase():
            xsr = Rot([self.sb([128, 2048], BF16, "xs") for _ in range(2)], "xs")
            btr = Rot([self.sb([128, 1024], BF16, "bt") for _ in range(2)], "bt")
            dtr = Rot([self.sb([128, 128], F32, "dta") for _ in range(2)], "dta")
            der = Rot([self.sb([128, 64], F32, "de") for _ in range(2)], "de")
            cdr = Rot([self.sb([128, 64], F32, "cd") for _ in range(2)], "cd")
            wdr = Rot([self.sb([128, 2048], BF16, "wdd") for _ in range(2)], "wdd")
            ssr = Rot([self.sb([128, 1024], F32, "ss") for _ in range(3)], "ss")
            pcr = Rot([self.ps([128, 128], F32, "pc") for _ in range(2)], "pc")
            psr = Rot([self.ps([128, 1024], F32, "psS") for _ in range(2)], "psS")
            arr = Rot([self.sb([128, 64], F32R, "ar") for _ in range(2)], "ar")
            if _os.environ.get("DEBUG") == "alloc":
                for t in pcr.tiles + psr.tiles + der.tiles:
                    print("ALLOC", t.name, self.nc.lookup_mloc(t))
            def chunk_gen(ch):
                r0 = ch * 128
                xs, xn = xsr.next(); bt, bn = btr.next(); dt, dn = dtr.next()
                self.dma(xs[:], g.xs_tm[r0:r0 + 128, :], w=[xn])
                self.dma(bt[:], g.b_tm[r0:r0 + 128, :], w=[bn])
                self.dma(dt[:], g.dta[r0:r0 + 128, :], w=[dn])
                pc, pcn = pcr.next()
                ar, arn = arr.next()
                de, den = der.next(); cd, cdn = cdr.next()
                yield
                self.cp("dve", ar[:], dt[:, 64:128], r=[dn], w=[arn])
                yield
                self.mm(pc[:, 0:32], self.masks_r[:, 2, :], ar[:, 0:32], r=["masksr", arn], w=[pcn])
                self.mm(pc[:, 32:64], self.masks_r[:, 3, :], ar[:, 32:64], r=["masksr", arn], w=[pcn])
                self.mm(pc[:, 64:128], self.ones_r[:], ar[:, 0:64], r=["onesr", arn], w=[pcn])
                yield
                self.act(de[:], pc[:, 0:64], AF.Exp, r=[pcn], w=[den])
                self.act(cd[:], pc[:, 64:128], AF.Exp, r=[pcn], w=[cdn])
                yield
                self.dma(g.cdec[ch], cd[:], r=[cdn])
                self.tt("dve", de[:], de[:], dt[:, 0:64], ALU.mult, r=[den, dn], w=[den])
                yield
                for d in range(2):
                    wdd, wn = wdr.next()
                    self.tt("dve", wdd[:].rearrange("p (h e) -> p h e", h=32), xs[:].rearrange("p (h e) -> p h e", h=32),
                            de[:, d * 32:(d + 1) * 32].unsqueeze(2).to_broadcast([128, 32, 64]), ALU.mult,
                            r=[xn, den], w=[wn])
                    yield
                    for hlf in range(2):
                        pS, psn = psr.next()
                        for gg in range(4):
                            G8 = hlf * 4 + gg
                            self.mm(pS[:, gg * 256:(gg + 1) * 256], bt[:, G8 * 128:(G8 + 1) * 128],
                                    wdd[:, G8 * 256:(G8 + 1) * 256], r=[bn, wn], w=[psn + str(gg // 2)])
                        yield
                        ss, sn = ssr.next()
                        for q in range(2):
                            self.cp("act" if q == 0 else "dve", ss[:, q * 512:(q + 1) * 512], pS[:, q * 512:(q + 1) * 512],
                                    r=[psn + str(q)], w=[sn])
                        yield
                        self.dma(g.sloc[ch, d, :, hlf * 1024:(hlf + 1) * 1024], ss[:], r=[sn])

            gens = [chunk_gen(ch) for ch in range(g.T // 128)]
            for i in range(0, len(gens), 2):
                active = gens[i:i + 2]
                while active:
                    for gch in list(active):
                        try:
                            next(gch)
                        except StopIteration:
                            active.remove(gch)

    def ssd_pR(self, g, slot):
        if self.skip():
            return
        nc = self.nc
        nchunk = g.L // 128
        with self.phase():
            hst = [self.sb([128, 2048], F32, "hst") for _ in range(2)]
            hbr = [Rot([self.sb([128, 2048], BF16, "hb") for _ in range(2)], f"hb{d}") for d in range(2)]
            slr = [Rot([self.sb([128, 2048], F32, "sl") for _ in range(2)], f"sl{d}") for d in range(2)]
            cdr = [Rot([self.sb([128, 64], F32, "cd") for _ in range(2)], f"cd{d}") for d in range(2)]
            stg = Rot([self.sb([128, 128], F32, "stg") for _ in range(3)], "stg")
            ptr = Rot([self.ps([128, 128], F32, "ptR") for _ in range(2)], "ptR")
            eng = ["dve", "pool"]
            for s in range(g.nseq):
                for d in range(2):
                    hn = f"hst{d}"
                    if g.latent:
                        for t in range(16):
                            sg, sgn = stg.next()
                            self.dma(sg[:], self.I["st_ssd"][slot, d, t * 128:(t + 1) * 128, :], w=[sgn])
                            pt, ptn = ptr.next()
                            self.tr(pt[:], sg[:], self.ident_f[:], r=[sgn, "identf"], w=[ptn])
                            self.cp("act", hst[d][:, t * 128:(t + 1) * 128], pt[:], r=[ptn], w=[hn])
                    else:
                        self.memset(eng[d], hst[d][:], 0.0, w=[hn])
                order = [list(range(nchunk)), list(range(nchunk - 1, -1, -1))]
                for i in range(nchunk):
                    for d in range(2):
                        c = order[d][i]
                        ch = s * nchunk + c
                        hn = f"hst{d}"
                        hb, hbn = hbr[d].next()
                        self.cp("act", hb[:], hst[d][:], r=[hn], w=[hbn])
                        self.dma(g.hprev[ch, d], hb[:], r=[hbn])
                        sl, sln = slr[d].next(); cd, cdn = cdr[d].next()
                        self.dma(sl[:], g.sloc[ch, d], w=[sln])
                        self.dma(cd[:], g.cdec[ch], w=[cdn])
                        self.tt(eng[d], hst[d][:].rearrange("p (h e) -> p h e", h=32),
                                hst[d][:].rearrange("p (h e) -> p h e", h=32),
                                cd[:, d * 32:(d + 1) * 32].unsqueeze(2).to_broadcast([128, 32, 64]), ALU.mult,
                                r=[hn, cdn], w=[hn])
                        self.tt(eng[d], hst[d][:], hst[d][:], sl[:], ALU.add, r=[hn, sln], w=[hn])
                if not g.latent:
                    for d in range(2):
                        hn = f"hst{d}"
                        for t in range(16):
                            pt, ptn = ptr.next()
                            self.tr(pt[:], hst[d][:, t * 128:(t + 1) * 128], self.ident_f[:], r=[hn, "identf"], w=[ptn])
                            sg, sgn = stg.next()
                            self.cp("act", sg[:], pt[:], r=[ptn], w=[sgn])
                            self.dma(self.O["nss"][s, slot, d, t * 128:(t + 1) * 128, :], sg[:], r=[sgn])

    def ssd_p2b(self, g, slot):
        if self.skip():
            return
        nc = self.nc
        with self.phase():
            w_in = self.I["ssd_in_w"][slot]
            wz = self.sb([128, 8, 2048], BF16, "wz")
            for k in range(8):
                self.dma(wz[:, k, :], w_in[k * 128:(k + 1) * 128, 0:2048], w=[f"wz{k}"], q="pool")
            ng = self.sb([128, 2048], F32, "ng")
            self.dma(ng[:], self.I["ssd_norm_g"][slot].partition_broadcast(128), w=["ng"])
            dsk = self.sb([128, 32], F32, "dsk")
            self.dma(dsk[:], self.I["ssd_d"][slot].partition_broadcast(128), w=["dsk"])
            mgt_r = self.sb([128, 128], F32R, "mgtr")
            mlt_r = self.sb([128, 128], F32R, "mltr")
            self.cp("dve", mgt_r[:], self.masks[:, 2, :], r=["masks"], w=["mgtr"])
            self.cp("dve", mlt_r[:], self.masks[:, 3, :], r=["masks"], w=["mltr"])
            Xl = [mgt_r, mlt_r]
            Xn = ["mgtr", "mltr"]
            Ym = [0, 1]
            Sm = [0, 1]

            hTr = Rot([self.sb([128, 8, 128], BF16, "hT") for _ in range(2)], "hT")
            xsr = Rot([self.sb([128, 2048], BF16, "xs") for _ in range(2)], "xs")
            bcr = Rot([self.sb([128, 16, 128], BF16, "bc") for _ in range(2)], "bc")
            dtr = Rot([self.sb([128, 128], F32, "dta") for _ in range(2)], "dta")
            hpr = Rot([self.sb([128, 2, 2048], BF16, "hp") for _ in range(2)], "hp")
            zsr = Rot([self.sb([128, 2048], BF16, "zs") for _ in range(2)], "zs")
            xdr = Rot([self.sb([128, 2, 2048], BF16, "xd") for _ in range(2)], "xd")
            ear = Rot([self.sb([128, 64], F32, "ea") for _ in range(2)], "ea")
            scr_ = Rot([self.sb([128, 2, 128], BF16, "scm") for _ in range(2)], "scm")
            Yr = Rot([self.sb([128, 4, 128], F32R, "Y") for _ in range(3)], "Y")
            Er = Rot([self.sb([128, 4, 128], BF16, "Ee") for _ in range(3)], "Ee")
            Mr = Rot([self.sb([128, 2, 4, 128], BF16, "Mm") for _ in range(2)], "Mm")
            t1r = Rot([self.sb([128, 256], F32, "t1") for _ in range(2)], "t1")
            t2r = Rot([self.sb([128, 256], F32, "t2") for _ in range(2)], "t2")
            ypr = Rot([self.sb([128, 2048], F32, "ypre") for _ in range(2)], "ypre")
            ynr = Rot([self.sb([128, 2048], BF16, "yn") for _ in range(2)], "yn")
            ssr = Rot([self.sb([128, 4], F32, "ssq") for _ in range(2)], "ssq")
            junk = self.sb([128, 2048], BF16, "junk")
            yTr = Rot([self.sb([128, 16, 128], BF16, "yT") for _ in range(2)], "yT")

            pzr = Rot([self.ps([128, 1024], F32, "pz") for _ in range(1)], "pz")
            pcs = Rot([self.ps([128, 512], F32, "pcs") for _ in range(1)], "pcs")
            psg = Rot([self.ps([128, 512], F32, "psg") for _ in range(2)], "psg")
            pyr = Rot([self.ps([128, 1024], F32, "py") for _ in range(1)], "py")
            ptr = Rot([self.ps([128, 8, 128], BF16, "ptb") for _ in range(1)], "ptb")
            nchunk = g.L // 128
            for s in range(g.nseq):
                for c in range(nchunk):
                    ch = s * nchunk + c
                    r0 = ch * 128
                    c0 = s * (g.L + 3) + 1 + c * 128
                    hT, hTn = hTr.next(); xs, xn = xsr.next(); bc, bcn = bcr.next(); dt, dn = dtr.next()
                    hp, hpn = hpr.next()
                    self.dma(hT[:], g.hT[:, :, c0:c0 + 128].rearrange("k p t -> p k t"), w=[hTn])
                    self.dma(xs[:], g.xs_tm[r0:r0 + 128, :], w=[xn])
                    self.dma(bc[:], g.bcT[:, :, r0:r0 + 128].rearrange("c p t -> p c t"), w=[bcn])
                    self.dma(dt[:], g.dta[r0:r0 + 128, :], w=[dn])
                    self.dma(hp[:], g.hprev[ch].rearrange("d p f -> p d f"), w=[hpn])
                    zs, zn = zsr.next()
                    for hlf in range(2):
                        pz, pzn = pzr.next()
                        for q in range(2):
                            for k in range(8):
                                col0 = hlf * 1024 + q * 512
                                self.mm(pz[:, q * 512:(q + 1) * 512], hT[:, k, :], wz[:, k, col0:col0 + 512],
                                        start=(k == 0), stop=(k == 7), r=[hTn, f"wz{k}"], w=[pzn + str(q)])
                        for q in range(2):
                            self.act(zs[:, hlf * 1024 + q * 512:hlf * 1024 + (q + 1) * 512], pz[:, q * 512:(q + 1) * 512],
                                     AF.Silu, r=[pzn + str(q)], w=[zn])
                    xd, xdn = xdr.next()
                    for d in range(2):
                        self.tt("dve" if d == 0 else "pool", xd[:, d, :].rearrange("p (h e) -> p h e", h=32),
                                xs[:].rearrange("p (h e) -> p h e", h=32),
                                dt[:, d * 32:(d + 1) * 32].unsqueeze(2).to_broadcast([128, 32, 64]), ALU.mult,
                                r=[xn, dn], w=[xdn + str(d)])
                    pc, pcn = pcs.next()
                    self.mm(pc[:, 0:32], self.masks[:, 0, :], dt[:, 64:96], r=["masks", dn], w=[pcn])
                    self.mm(pc[:, 32:64], self.masks[:, 1, :], dt[:, 96:128], r=["masks", dn], w=[pcn])
                    ea, ean = ear.next()
                    self.act(ea[:], pc[:, 0:64], AF.Exp, r=[pcn], w=[ean])
                    yp, ypn = ypr.next()
                    v3 = lambda ap: ap.rearrange("p (h e) -> p h e", h=4)

                    def stA(G8):
                        self.mm(pc[:, 128:256], bc[:, G8, :], bc[:, 8 + G8, :], r=[bcn], w=[pcn])
                        sm, smn = scr_.next()
                        for d in range(2):
                            self.tt("dve", sm[:, d, :], pc[:, 128:256], self.masks[:, Sm[d], :], ALU.mult,
                                    r=[pcn, "masks"], w=[smn + str(d)])
                        Mt, Mn = Mr.next()
                        for d in range(2):
                            Y, Yn = Yr.next()
                            self.tt("pool", Y[:], self.masks[:, Ym[d], :].unsqueeze(1).to_broadcast([128, 4, 128]),
                                    dt[:, 64 + d * 32 + G8 * 4:64 + d * 32 + G8 * 4 + 4].unsqueeze(2).to_broadcast([128, 4, 128]),
                                    ALU.mult, r=["masks", dn], w=[Yn])
                            pg, pgn = psg.next()
                            self.mm(pg[:], Xl[d][:], Y[:].rearrange("p a b -> p (a b)"), r=[Xn[d], Yn], w=[pgn])
                            Et, En = Er.next()
                            self.act(Et[:].rearrange("p a b -> p (a b)"), pg[:], AF.Exp, r=[pgn], w=[En])
                            self.tt("dve", Mt[:, d, :, :], Et[:], sm[:, d, :].unsqueeze(1).to_broadcast([128, 4, 128]),
                                    ALU.mult, r=[En, smn + str(d)], w=[Mn + str(d)])
                        return Mt, Mn

                    def stB(G8, Mt, Mn):
                        py, pyn = pyr.next()
                        for h in range(4):
                            H = G8 * 4 + h
                            for d in range(2):
                                self.mm(py[:, h * 64:(h + 1) * 64], Mt[:, d, h, :], xd[:, d, H * 64:(H + 1) * 64],
                                        start=(d == 0), stop=(d == 1), r=[Mn + str(d), xdn + str(d)], w=[pyn + "d"])
                        for d in range(2):
                            self.mm(py[:, 512 + d * 256:512 + (d + 1) * 256], bc[:, 8 + G8, :],
                                    hp[:, d, G8 * 256:(G8 + 1) * 256], r=[bcn, hpn], w=[pyn + "o"])
                        return py, pyn

                    def stC(G8, py, pyn):
                        t1, t1n = t1r.next(); t2, t2n = t2r.next()
                        for d, (tt_, tn_) in enumerate(((t1, t1n), (t2, t2n))):
                            self.tt("dve", v3(tt_[:]), v3(py[:, 512 + d * 256:512 + (d + 1) * 256]),
                                    ea[:, d * 32 + G8 * 4:d * 32 + G8 * 4 + 4].unsqueeze(2).to_broadcast([128, 4, 64]),
                                    ALU.mult, r=[pyn + "o", ean], w=[tn_])
                        self.tt("pool", t1[:], t1[:], t2[:], ALU.add, r=[t1n, t2n], w=[t1n])
                        self.tt("dve", t2[:], py[:, 0:256], t1[:], ALU.add, r=[pyn + "d", t1n], w=[t2n])
                        self.tt("pool", v3(t1[:]), v3(xs[:, G8 * 256:(G8 + 1) * 256]),
                                dsk[:, G8 * 4:G8 * 4 + 4].unsqueeze(2).to_broadcast([128, 4, 64]), ALU.mult,
                                r=[xn, "dsk", t1n], w=[t1n])
                        self.tt("pool", yp[:, G8 * 256:(G8 + 1) * 256], t1[:], t2[:], ALU.add, r=[t1n, t2n],
                                w=[ypn + str(G8)])

                    MA = stA(0)
                    for G8 in range(8):
                        PB = stB(G8, *MA)
                        if G8 < 7:
                            MA = stA(G8 + 1)
                        stC(G8, *PB)
                    ypa = [ypn + str(i) for i in range(8)]
                    self.tt("dve", yp[:], yp[:], zs[:], ALU.mult, r=ypa + [zn], w=ypa)
                    sq, sqn = ssr.next()
                    self.act(junk[:], yp[:], AF.Square, r=ypa, w=["junk", sqn], accum_out=sq[:, 0:1])
                    self.ts("dve", sq[:, 1:2], sq[:, 0:1], 1.0 / 2048.0, RMS_EPS, ALU.mult, ALU.add, r=[sqn], w=[sqn])
                    self.act(sq[:, 1:2], sq[:, 1:2], AF.Sqrt, r=[sqn], w=[sqn])
                    self.S.op("dve", lambda sq=sq: nc.vector.reciprocal(out=sq[:, 2:3], in_=sq[:, 1:2]),
                              reads=[sqn], writes=[sqn])
                    yn_, ynn = ynr.next()
                    self.stt(yn_[:], yp[:], sq[:, 2:3], ng[:], ALU.mult, ALU.mult, r=ypa + [sqn, "ng"], w=[ynn])
                    yT, yTn = yTr.next()
                    for blk in range(2):
                        pt, ptn = ptr.next()
                        for i in range(8):
                            fc = blk * 8 + i
                            self.tr(pt[:, i, :], yn_[:, fc * 128:(fc + 1) * 128], self.ident_b[:], r=[ynn, "identb"], w=[ptn])
                        self.cp("act", yT[:, blk * 8:(blk + 1) * 8, :], pt[:], r=[ptn], w=[yTn])
                    self.dma(g.yT[:, :, r0:r0 + 128].rearrange("k p t -> p k t"), yT[:], r=[yTn])

    def lru_layer(self, g):
        self.lru_p1(g)
        self.lru_p2(g)

    def lru_p1(self, g):
        if self.skip():
            return
        nc = self.nc
        with self.phase():
            win = self.sb([128, 8, 2048], BF16, "win")
            for k in range(8):
                self.dma(win[:, k, :], self.I["lru_in_w"][k * 128:(k + 1) * 128, :], w=[f"win{k}"], q="pool")
            gw = self.sb([128, 32, 256], BF16, "gw")
            gsrc = self.I["lru_gate_w"].rearrange("d g n (kc p) j -> (d g n kc) p j", p=128)
            for i in range(32):
                self.dma(gw[:, i, :], gsrc[i], w=["gw"], q="pool")
            cw = self.sb([128, 4, 8], F32, "cw")
            cb = self.sb([128, 8], F32, "cb")
            gb = self.sb([128, 4, 8], F32, "gb")
            nsp = self.sb([128, 2, 8], F32, "nsp")
            h0 = self.lru_h0
            self.load_T(cw[:].rearrange("p k c -> p (k c)"),
                        self.I["lru_conv_w"].rearrange("k (c p) -> (k c) p", p=128), 32, "cw")
            self.load_T(cb[:], self.I["lru_conv_b"].rearrange("(c p) -> c p", p=128), 8, "cb")
            self.load_T(gb[:].rearrange("p k c -> p (k c)"),
                        self.I["lru_gate_b"].rearrange("k (c p) -> (k c) p", p=128), 32, "gb")
            self.load_T(nsp[:].rearrange("p k c -> p (k c)"),
                        self.I["lru_a_param"].rearrange("k (c p) -> (k c) p", p=128), 16, "nsp")
            self.act(nsp[:], nsp[:], AF.Exp, r=["nsp"], w=["nsp"], scale=-1.0)
            self.act(nsp[:], nsp[:], AF.Ln, r=["nsp"], w=["nsp"], bias=1.0)
            self.ts("dve", nsp[:], nsp[:], -8.0, None, ALU.mult, r=["nsp"], w=["nsp"])
            if g.latent:
                self.load_T(h0[:].rearrange("p k c -> p (k c)"),
                            self.I["st_lru"].rearrange("k (c p) -> (k c) p", p=128), 16, "h0")
            else:
                self.memset("dve", h0[:], 0.0, w=["h0"])
            hwr = Rot([self.sb([128, 8, 260], BF16, "hw") for _ in range(2)], "hw")
            xrr = Rot([self.sb([128, 8, 256], F32, "xr") for _ in range(1)], "xr")
            xbr = Rot([self.sb([128, 8, 256], BF16, "xrb") for _ in range(1)], "xrb")
            zsr = Rot([self.sb([128, 8, 256], F32, "zs") for _ in range(2)], "zs")
            gtr = [Rot([self.sb([128, 8, 256], F32, "gt") for _ in range(1)], f"gt{i}") for i in range(4)]
            aar = [Rot([self.sb([128, 8, 256], F32, "aa") for _ in range(1 + d)], f"aa{d}") for d in range(2)]
            bxr = [Rot([self.sb([128, 8, 256], F32, "bx") for _ in range(1 + d)], f"bx{d}") for d in range(2)]
            tmr = Rot([self.sb([128, 8, 256], F32, "tmpl") for _ in range(1)], "tmpl")
            yfr = Rot([self.sb([128, 8, 256], F32, "yf") for _ in range(2)], "yf")
            accr = Rot([self.sb([128, 256], F32, "acc") for _ in range(4)], "acc")
            fin = self.sb([128, 8], F32, "fin")
            ppr = Rot([self.ps([128, 512], F32, "pp") for _ in range(4)], "pp")
            pgr = Rot([self.ps([128, 256], F32, "pg") for _ in range(3)], "pg")
            for s in range(g.nseq):
                yf_prev = None
                ntile = g.L // 256
                for j in range(ntile):
                    c0 = s * (g.L + 3) + 256 * j
                    t0 = s * g.L + 256 * j
                    hw, hn = hwr.next()
                    self.dma(hw[:, :, 0:259], g.hT[:, :, c0:c0 + 259].rearrange("k p t -> p k t"), w=[hn])
                    xr, xn = xrr.next(); xb, xbn = xbr.next(); zs, zn = zsr.next()
                    def mm_fn(fc, pp, pn, hw=hw, hn=hn):
                        for k in range(8):
                            self.mm(pp[:, 0:259], win[:, k, fc * 128:(fc + 1) * 128], hw[:, k, 0:259],
                                    start=(k == 0), stop=(k == 7), r=[hn, f"win{k}"], w=[pn])

                    def after_fn(fc, xb=xb, xbn=xbn, xr=xr, xn=xn):
                        self.cp("pool", xb[:, fc, :], xr[:, fc, :], r=[f"{xn}_{fc}"], w=[f"{xbn}_{fc}"])
                    self.conv_block(range(8), ppr, accr, mm_fn, lambda fc, xr=xr, xn=xn: (xr[:, fc, :], f"{xn}_{fc}"),
                                    cw, cb, 4, False, after_fn)
                    for fc in range(8, 16):
                        pp, pn = ppr.next()
                        mm_fn(fc, pp, pn)
                        self.act(zs[:, fc - 8, :], pp[:, 1:257], AF.Silu, r=[pn], w=[zn])
                    gts = [gtr[i].next() for i in range(4)]
                    for dg in range(4):
                        gt, gn = gts[dg]
                        for n in range(4):
                            for jc in range(2):
                                pg, pgn = pgr.next()
                                for kc in range(2):
                                    self.mm(pg[:], gw[:, (dg * 4 + n) * 2 + kc, jc * 128:(jc + 1) * 128], xb[:, n * 2 + kc, :],
                                            start=(kc == 0), stop=(kc == 1), r=["gw", f"{xbn}_{n * 2 + kc}"], w=[pgn])
                                ch = n * 2 + jc
                                self.act(gt[:, ch, :], pg[:], AF.Sigmoid, r=[pgn, "gb"], w=[gn], bias=gb[:, dg, ch:ch + 1])
                    xall = [f"{xn}_{fc}" for fc in range(8)]
                    ab = []
                    for d in range(2):
                        aa, aan = aar[d].next(); bx, bxn = bxr[d].next(); tm, tmn = tmr.next()
                        rt, rn = gts[d * 2]; it, itn = gts[d * 2 + 1]
                        for ch in range(8):
                            self.act(aa[:, ch, :], rt[:, ch, :], AF.Exp, r=[rn, "nsp"], w=[aan], scale=nsp[:, d, ch:ch + 1])
                        self.tt("dve", tm[:], aa[:], aa[:], ALU.mult, r=[aan], w=[tmn])
                        self.ts("dve", tm[:], tm[:], -1.0, 1.0, ALU.mult, ALU.add, r=[tmn], w=[tmn])
                        self.ts("dve", tm[:], tm[:], 0.0, None, ALU.max, r=[tmn], w=[tmn])
                        self.act(tm[:], tm[:], AF.Sqrt, r=[tmn], w=[tmn])
                        self.tt("pool", tm[:], tm[:], it[:], ALU.mult, r=[tmn, itn], w=[tmn])
                        self.tt("pool", bx[:], tm[:], xr[:], ALU.mult, r=[tmn] + xall, w=[bxn])
                        ab.append((aa, aan, bx, bxn))
                    yf, yfn = yfr.next()
                    aa, aan, bx, bxn = ab[0]
                    for ch in range(8):
                        init = h0[:, 0, ch:ch + 1] if yf_prev is None else yf_prev[0][:, ch, 255:256]
                        rr = [aan, bxn, "h0"] + ([yf_prev[1]] if yf_prev is not None else [])
                        self.S.op("dve", lambda yf=yf, aa=aa, bx=bx, ch=ch, init=init: nc.vector.tensor_tensor_scan(
                            out=yf[:, ch, :], data0=aa[:, ch, :], data1=bx[:, ch, :], initial=init,
                            op0=ALU.mult, op1=ALU.add), reads=rr, writes=[yfn])
                    yf_prev = (yf, yfn)
                    fmv = lambda slot: g.fm[slot, :, :, t0:t0 + 256].rearrange("c p t -> p c t")
                    self.dma(fmv(0), yf[:], r=[yfn])
                    self.dma(fmv(1), ab[1][0][:], r=[ab[1][1]])
                    self.dma(fmv(2), ab[1][2][:], r=[ab[1][3]])
                    self.dma(fmv(3), zs[:], r=[zn])
                if not g.latent:
                    self.cp("dve", fin[:], yf_prev[0][:, :, 255], r=[yf_prev[1]], w=["fin"])
                    self.dma(self.O["nsl"][s, 0, :].rearrange("(c p) -> p c", p=128), fin[:], r=["fin"],
                             allow_slow_non_contiguous=True)

    def lru_p2(self, g):
        if self.skip():
            return
        nc = self.nc
        with self.phase():
            h0 = self.lru_h0
            ldr = [Rot([self.sb([128, 8, 256], F32, "ld") for _ in range(2)], f"ld{i}") for i in range(4)]
            ybr = Rot([self.sb([128, 8, 256], F32, "yb") for _ in range(2)], "yb")
            yTr = Rot([self.sb([128, 8, 256], BF16, "yTl") for _ in range(2)], "yTl")
            fin = self.sb([128, 8], F32, "fin")
            for s in range(g.nseq):
                yb_prev = None
                ntile = g.L // 256
                for j in range(ntile - 1, -1, -1):
                    t0 = s * g.L + 256 * j
                    lt = []
                    for i in range(4):
                        t, tn = ldr[i].next()
                        self.dma(t[:], g.fm[i, :, :, t0:t0 + 256].rearrange("c p t -> p c t"), w=[tn])
                        lt.append((t, tn))
                    (yf, yfn), (aa, aan), (bx, bxn), (zs, zn) = lt
                    yb, ybn = ybr.next()
                    for ch in range(8):
                        init = h0[:, 1, ch:ch + 1] if yb_prev is None else yb_prev[0][:, ch, 0:1]
                        rr = [aan, bxn, "h0"] + ([yb_prev[1]] if yb_prev is not None else [])
                        self.S.op("dve", lambda yb=yb, aa=aa, bx=bx, ch=ch, init=init: nc.vector.tensor_tensor_scan(
                            out=yb[:, ch, ::-1], data0=aa[:, ch, ::-1], data1=bx[:, ch, ::-1], initial=init,
                            op0=ALU.mult, op1=ALU.add), reads=rr, writes=[ybn])
                    yb_prev = (yb, ybn)
                    self.tt("pool", yf[:], yf[:], yb[:], ALU.add, r=[yfn, ybn], w=[yfn])
                    yT, yTn = yTr.next()
                    self.tt("pool", yT[:], yf[:], zs[:], ALU.mult, r=[yfn, zn], w=[yTn])
                    self.dma(g.yT[0:8, :, t0:t0 + 256].rearrange("c p t -> p c t"), yT[:], r=[yTn])
                if not g.latent:
                    self.cp("dve", fin[:], yb_prev[0][:, :, 0], r=[yb_prev[1]], w=["fin"])
                    self.dma(self.O["nsl"][s, 1, :].rearrange("(c p) -> p c", p=128), fin[:], r=["fin"],
                             allow_slow_non_contiguous=True)

    def dbg_rows(self, dst_row0, src2d, nrows):
        with self.phase():
            for r0 in range(0, nrows, 128):
                t = self.sb([128, 1024], F32, "dbg")
                self.dma(t[:], src2d[r0:r0 + 128, :], w=["dbg"])
                self.dma(self.O["yp"][dst_row0 + r0:dst_row0 + r0 + 128, :], t[:], r=["dbg"])

    def hyena_layer(self, g):
        self.hy_filters(g)
        if _os.environ.get("DEBUG") == "hyk" and g.name == "p":
            self.dbg_rows(0, g.khat2[0, 0], 256)
            self.dbg_rows(256, g.khat2[0, 1], 256)
            self.dbg_rows(512, g.kw[:, 0:1024], 256)
            self.dbg_rows(768, g.kw[:, 1024:2048], 256)
            return
        self.hy_p1(g)
        for o in range(2):
            for s in range(g.nseq):
                self.hy_fwd(g, o, s)
                self.hy_inv(g, o, s)

    def _vec(self, dst, src1d, n, name):
        self.dma(dst, src1d.rearrange("(p o) -> p o", o=1), w=[name], allow_slow_non_contiguous=True)

    def _hy_mlp(self, g, hd2, w3r):
        nc = self.nc
        L = g.L
        nm = g.name
        TWO_PI = 2.0 * math.pi
        MAGIC = 12582912.0
        with self.phase():
            zT = self.sb([33, L], F32, "zT")
            self.dma(zT[:], self.I[f"hz_{nm}"], w=["zT"])
            w1 = self.sb([33, 64], F32, "w1"); w2 = self.sb([64, 64], F32, "w2")
            self.dma(w1[:], self.I["hy_f_w1"], w=["w1"]); self.dma(w2[:], self.I["hy_f_w2"], w=["w2"])
            w3 = self.sb([64, 4096], F32, "w3")
            self.dma(w3[:], self.I["hy_f_w3"], w=["w3"])
            self.cp("pool", w3r[:], w3[:], r=["w3"], w=["w3r"])
            pv = self.sb([64, 6], F32, "pv")
            self._vec(pv[:, 0:1], self.I["hy_f_b1"], 64, "pv"); self._vec(pv[:, 1:2], self.I["hy_f_b2"], 64, "pv")
            self._vec(pv[:, 2:3], self.I["hy_f_freq"][0], 64, "pv"); self._vec(pv[:, 3:4], self.I["hy_f_freq"][1], 64, "pv")
            self.tt("dve", pv[:, 4:6], pv[:, 0:2], pv[:, 2:4], ALU.mult, r=["pv"], w=["pv"])
            hd1 = self.sb([64, L], F32, "hd1")
            argr = Rot([self.sb([64, 512], F32, "arg") for _ in range(2)], "arg")
            nr = Rot([self.sb([64, 512], F32, "nq") for _ in range(2)], "nq")
            phr = Rot([self.ps([64, 512], F32, "ph") for _ in range(2)], "ph")
            TW = min(512, L)
            for layer in range(2):
                for ti in range(L // TW):
                    sl = slice(ti * TW, (ti + 1) * TW)
                    ph, phn = phr.next()
                    if layer == 0:
                        self.mm(ph[:, 0:TW], w1[:], zT[:, sl], r=["w1", "zT"], w=[phn])
                    else:
                        self.mm(ph[:, 0:TW], w2[:], hd1[:, sl], r=["w2", "hd1"], w=[phn])
                    arg, an = argr.next(); nq, nn = nr.next()
                    self.act(arg[:, 0:TW], ph[:, 0:TW], AF.Identity, r=[phn, "pv"], w=[an],
                             scale=pv[:, 2 + layer:3 + layer], bias=pv[:, 4 + layer:5 + layer])
                    self.ts("dve", nq[:, 0:TW], arg[:, 0:TW], 1.0 / TWO_PI, MAGIC, ALU.mult, ALU.add, r=[an], w=[nn])
                    self.ts("dve", nq[:, 0:TW], nq[:, 0:TW], MAGIC, None, ALU.subtract, r=[nn], w=[nn])
                    self.stt(arg[:, 0:TW], nq[:, 0:TW], -TWO_PI, arg[:, 0:TW], ALU.mult, ALU.add, r=[nn, an], w=[an])
                    self.ts("dve", arg[:, 0:TW], arg[:, 0:TW], math.pi, -math.pi, ALU.min, ALU.max, r=[an], w=[an])
                    if layer == 0:
                        self.act(hd1[:, sl], arg[:, 0:TW], AF.Sin, r=[an], w=["hd1"])
                    else:
                        self.act(hd2[:, sl], arg[:, 0:TW], AF.Sin, r=[an], w=["hd2"])

    def hy_filters(self, g):
        if self.skip():
            return
        nc = self.nc
        L = g.L
        nL = L // 128
        nm = g.name
        TWO_PI = 2.0 * math.pi
        MAGIC = 12582912.0
        with self.phase():
            hd2 = self.sb([64, L], F32R, "hd2")
            w3r = self.sb([64, 4096], F32R, "w3r")
            self._hy_mlp(g, hd2, w3r)
            asum = self.sb([128, 4096], F32, "asum")
            self.memset("pool", asum[:], 0.0, w=["asum"])
            winr = Rot([self.sb([128, 1024], F32, "winw") for _ in range(2)], "winw")
            kwr = Rot([self.sb([128, 4096], F32, "kwt") for _ in range(2)], "kwt")
            kabs = self.sb([128, 4096], F32, "kabs")
            pkr = Rot([self.ps([128, 512], F32, "pk") for _ in range(3)], "pk")
            for tc in range(nL):
                wt, wn = winr.next()
                self.dma(wt[:], self.I[f"win_{nm}"][tc * 128:(tc + 1) * 128, :], w=[wn])
                kt, kn = kwr.next()
                for ti in range(8):
                    pk, pkn = pkr.next()
                    self.mm(pk[:], hd2[:, tc * 128:(tc + 1) * 128], w3r[:, ti * 512:(ti + 1) * 512], r=["hd2", "w3r"], w=[pkn])
                    self.tt("dve", kt[:, ti * 512:(ti + 1) * 512], pk[:], wt[:, (ti % 2) * 512:(ti % 2 + 1) * 512], ALU.mult,
                            r=[pkn, wn], w=[kn])
                self.act(kabs[:], kt[:], AF.Abs, r=[kn], w=["kabs"])
                self.tt("pool", asum[:], asum[:], kabs[:], ALU.add, r=["kabs", "asum"], w=["asum"])
                self.dma(g.kw[tc * 128:(tc + 1) * 128, :], kt[:], r=[kn])
            asr = self.sb([128, 4096], F32R, "asr")
            self.cp("dve", asr[:], asum[:], r=["asum"], w=["asr"])
            tot = self.sb([128, 4096], F32, "tot")
            for ti in range(8):
                pk, pkn = pkr.next()
                self.mm(pk[:], self.ones_r[:], asr[:, ti * 512:(ti + 1) * 512], r=["onesr", "asr"], w=[pkn])
                self.cp("act", tot[:, ti * 512:(ti + 1) * 512], pk[:], r=[pkn], w=["tot"])
            rn = self.sb([128, 2, 1024], F32, "rn")
            tv = tot[:].rearrange("p (o d c) -> p o d c", o=2, d=2)
            self.tt("dve", rn[:], tv[:, :, 0, :], tv[:, :, 1, :], ALU.add, r=["tot"], w=["rn"])
            rn0 = self.sb([128, 2, 1024], F32, "rn0")
            self.act(rn0[:], rn[:], AF.Ln, r=["rn"], w=["rn0"])
            self.act(rn[:], rn0[:], AF.Exp, r=["rn0"], w=["rn"], scale=-1.0)
            self.ts("dve", rn[:], rn[:], 2.0 / (2 * L), None, ALU.mult, r=["rn"], w=["rn"])
            self.dma(g.khat[0, 0, 0:128, :], rn[:, 0, :], r=["rn"])
            self.dma(g.khat[0, 1, 0:128, :], rn[:, 1, :], r=["rn"])
        with self.phase():
            rn = self.sb([128, 2, 1024], F32, "rn")
            self.dma(rn[:, 0, :], g.khat[0, 0, 0:128, :], w=["rn"])
            self.dma(rn[:, 1, :], g.khat[0, 1, 0:128, :], w=["rn"])
            self.S.barrier()
            ks = self.sb([128, nL, 1024], BF16, "ks")
            kldr = Rot([self.sb([128, 2, 1024], F32, "kld") for _ in range(2)], "kld")
            mbr = Rot([self.sb([128, nL, 128], BF16, "mb") for _ in range(2)], "mb")
            kor = Rot([self.sb([128, 1024], F32, "ko") for _ in range(2)], "ko")
            pur = Rot([self.ps([128, 1024], F32, "pu") for _ in range(2)], "pu")
            for o in range(2):
                for cs in range(2):
                    for tc in range(nL):
                        kl, kln = kldr.next()
                        self.dma(kl[:], g.kw[tc * 128:(tc + 1) * 128, o * 2048:(o + 1) * 2048].rearrange("p (d c) -> p d c", d=2),
                                 w=[kln])
                        if tc == 0:
                            self.memset("dve", kl[0:1, 1, :], 0.0, w=[kln])
                        self.tt("dve", kl[:, 0, :], kl[:, 0, :], kl[:, 1, :], ALU.add if cs == 0 else ALU.subtract,
                                r=[kln], w=[kln])
                        self.tt("pool", ks[:, tc, :], kl[:, 0, :], rn[:, o, :], ALU.mult, r=[kln, "rn"], w=[f"ks{tc}"])
                    for fc in range(nL):
                        mb, mbn = mbr.next()
                        self.dma(mb[:], self.I[f"dft_{nm}"][2 + cs, fc], w=[mbn])
                        pu, pun = pur.next()
                        for tc in range(nL):
                            for hlf in range(2):
                                self.mm(pu[:, hlf * 512:(hlf + 1) * 512], mb[:, tc, :], ks[:, tc, hlf * 512:(hlf + 1) * 512],
                                        start=(tc == 0), stop=(tc == nL - 1), r=[mbn, f"ks{tc}"],
                                        w=[pun + str(hlf)])
                        ko, kon = kor.next()
                        for hlf in range(2):
                            self.cp("act", ko[:, hlf * 512:(hlf + 1) * 512], pu[:, hlf * 512:(hlf + 1) * 512],
                                    r=[pun + str(hlf)], w=[kon])
                        self.dma(g.khat2[o, cs, fc * 128:(fc + 1) * 128, :], ko[:], r=[kon])

    def hy_p1(self, g):
        if self.skip():
            return
        nc = self.nc
        with self.phase():
            win = self.sb([128, 8, 4096], BF16, "win")
            for k in range(8):
                self.dma(win[:, k, :], self.I["hy_in_w"][k * 128:(k + 1) * 128, :], w=[f"win{k}"], q="pool")
            cw = self.sb([128, 3, 24], F32, "cw")
            cb = self.sb([128, 24], F32, "cb")
            self.load_T(cw[:].rearrange("p k c -> p (k c)"),
                        self.I["hy_conv_w"].rearrange("k (c p) -> (k c) p", p=128), 72, "cw")
            self.load_T(cb[:], self.I["hy_conv_b"].rearrange("(c p) -> c p", p=128), 24, "cb")
            hwr = Rot([self.sb([128, 8, 260], BF16, "hw") for _ in range(2)], "hw")
            fmr = [Rot([self.sb([128, 8, 256], F32, "fmt") for _ in range(2)], f"fmt{i}") for i in range(4)]
            vbr = Rot([self.sb([128, 8, 256], BF16, "vb") for _ in range(2)], "vb")
            utr = Rot([self.sb([128, 1024], BF16, "ut") for _ in range(2)], "ut")
            accr = Rot([self.sb([128, 256], F32, "acc") for _ in range(4)], "acc")
            ppr = Rot([self.ps([128, 512], F32, "pp") for _ in range(4)], "pp")
            ptr = Rot([self.ps([128, 8, 128], BF16, "ptr") for _ in range(2)], "ptr")
            for s in range(g.nseq):
                for j in range(g.L // 256):
                    c0 = s * (g.L + 3) + 256 * j
                    t0 = s * g.L + 256 * j
                    hw, hn = hwr.next()
                    self.dma(hw[:, :, 0:259], g.hT[:, :, c0:c0 + 259].rearrange("k p t -> p k t"), w=[hn])
                    fts = [fmr[i].next() for i in range(4)]
                    vb, vbn = vbr.next()
                    def mm_fn(fc, pp, pn, hw=hw, hn=hn):
                        for k in range(8):
                            self.mm(pp[:, 0:259], win[:, k, fc * 128:(fc + 1) * 128], hw[:, k, 0:259],
                                    start=(k == 0), stop=(k == 7), r=[hn, f"win{k}"], w=[pn])

                    def dst_fn(fc, fts=fts):
                        ft, fn = fts[fc // 8]
                        return ft[:, fc % 8, :], f"{fn}_{fc % 8}"

                    def after_fn(fc, fts=fts, vb=vb, vbn=vbn):
                        if fc < 8:
                            ft, fn = fts[0]
                            self.cp("pool", vb[:, fc, :], ft[:, fc, :], r=[f"{fn}_{fc}"], w=[f"{vbn}_{fc}"])
                    self.conv_block(range(24), ppr, accr, mm_fn, dst_fn, cw, cb, 3, False, after_fn)
                    for fc in range(24, 32):
                        pp, pn = ppr.next()
                        mm_fn(fc, pp, pn)
                        ft, fn = fts[3]
                        self.act(ft[:, fc % 8, :], pp[:, 1:257], AF.Silu, r=[pn], w=[f"{fn}_{fc % 8}"])
                    for i in range(4):
                        ft, fn = fts[i]
                        self.dma(g.fm[i, :, :, t0:t0 + 256].rearrange("c p t -> p c t"), ft[:],
                                 r=[f"{fn}_{c}" for c in range(8)])
                    for tcn in range(2):
                        pt, ptn = ptr.next()
                        for c in range(8):
                            self.tr(pt[:, c, :], vb[:, c, tcn * 128:(tcn + 1) * 128], self.ident_b[:],
                                    r=[f"{vbn}_{c}", "identb"], w=[ptn])
                        ut, utn = utr.next()
                        self.cp("dve", ut[:], pt[:].rearrange("p a b -> p (a b)"), r=[ptn], w=[utn])
                        self.dma(g.utm[t0 + tcn * 128:t0 + (tcn + 1) * 128, :], ut[:], r=[utn])

    def hy_fwd(self, g, o, s):
        if self.skip():
            return
        nc = self.nc
        L = g.L
        nL = L // 128
        nm = g.name
        with self.phase():
            u = self.sb([128, nL, 1024], BF16, "u")
            usrc = g.utm[s * L:(s + 1) * L, :].rearrange("(tc p) c -> p tc c", p=128)
            for q in range(0, nL, 8):
                qe = min(q + 8, nL)
                self.dma(u[:, q:qe, :], usrc[:, q:qe, :], w=[f"u{q}"])
            cbr = Rot([self.sb([128, nL, 128], BF16, "cbk") for _ in range(2)], "cbk")
            sbr = Rot([self.sb([128, nL, 128], BF16, "sbk") for _ in range(2)], "sbk")
            kcr = Rot([self.sb([128, 1024], F32, "kc") for _ in range(2)], "kc")
            ksr = Rot([self.sb([128, 1024], F32, "ksp") for _ in range(2)], "ksp")
            a1r = Rot([self.sb([128, 1024], F32, "a1") for _ in range(2)], "a1")
            a2r = Rot([self.sb([128, 1024], F32, "a2") for _ in range(2)], "a2")
            yor = Rot([self.sb([128, 2, 1024], BF16, "yo") for _ in range(2)], "yo")
            puc = Rot([self.ps([128, 1024], F32, "puc") for _ in range(2)], "puc")
            pus = Rot([self.ps([128, 1024], F32, "pus") for _ in range(2)], "pus")
            for fc in range(nL):
                cb_, cbn = cbr.next(); sb_, sbn = sbr.next()
                self.dma(cb_[:], self.I[f"dft_{nm}"][0, fc], w=[cbn])
                self.dma(sb_[:], self.I[f"dft_{nm}"][1, fc], w=[sbn])
                kc, kcn = kcr.next(); ksp, ksn = ksr.next()
                self.dma(kc[:], g.khat2[o, 0, fc * 128:(fc + 1) * 128, :], w=[kcn])
                self.dma(ksp[:], g.khat2[o, 1, fc * 128:(fc + 1) * 128, :], w=[ksn])
                pc_, pcn = puc.next(); ps_, psn = pus.next()
                for tc in range(nL):
                    q8 = (tc // 8) * 8
                    for hlf in range(2):
                        hs = slice(hlf * 512, (hlf + 1) * 512)
                        self.mm(pc_[:, hs], cb_[:, tc, :], u[:, tc, hs], start=(tc == 0), stop=(tc == nL - 1),
                                r=[cbn, f"u{q8}"], w=[pcn + str(hlf)])
                        self.mm(ps_[:, hs], sb_[:, tc, :], u[:, tc, hs], start=(tc == 0), stop=(tc == nL - 1),
                                r=[sbn, f"u{q8}"], w=[psn + str(hlf)])
                a1, a1n = a1r.next(); a2, a2n = a2r.next(); yo, yon = yor.next()
                for hlf in range(2):
                    hs = slice(hlf * 512, (hlf + 1) * 512)
                    self.tt("dve", a1[:, hs], pc_[:, hs], kc[:, hs], ALU.mult, r=[pcn + str(hlf), kcn], w=[a1n])
                    self.tt("dve", a2[:, hs], ps_[:, hs], ksp[:, hs], ALU.mult, r=[psn + str(hlf), ksn], w=[a2n])
                self.tt("pool", yo[:, 0, :], a1[:], a2[:], ALU.subtract, r=[a1n, a2n], w=[yon + "c"])
                a1, a1n = a1r.next(); a2, a2n = a2r.next()
                for hlf in range(2):
                    hs = slice(hlf * 512, (hlf + 1) * 512)
                    self.tt("dve", a1[:, hs], pc_[:, hs], ksp[:, hs], ALU.mult, r=[pcn + str(hlf), ksn], w=[a1n])
                    self.tt("dve", a2[:, hs], ps_[:, hs], kc[:, hs], ALU.mult, r=[psn + str(hlf), kcn], w=[a2n])
                self.tt("pool", yo[:, 1, :], a1[:], a2[:], ALU.add, r=[a1n, a2n], w=[yon + "s"])
                self.dma(g.yspec[:, :, :, fc, :].rearrange("a cc p j -> p a cc j"),
                         yo[:].rearrange("p a (cc j) -> p a cc j", j=128), r=[yon + "c", yon + "s"])

    def hy_inv(self, g, o, s):
        if self.skip():
            return
        nc = self.nc
        L = g.L
        nL = L // 128
        nm = g.name
        TT = min(512, L)
        nq = TT // 128
        with self.phase():
            fb = self.sb([128, 2, 8], F32, "fb")
            self.load_T(fb[:].rearrange("p k c -> p (k c)"),
                        self.I["hy_f_bias"].rearrange("k (c p) -> (k c) p", p=128), 16, "fb")
            cm = self.sb([128, nL, TT], BF16, "cm")
            sm = self.sb([128, nL, TT], BF16, "sm")
            ycr = Rot([self.sb([128, nL, 128], BF16, "yc") for _ in range(2)], "yc")
            ysr = Rot([self.sb([128, nL, 128], BF16, "ys") for _ in range(2)], "ys")
            utr = Rot([self.sb([128, TT], F32, "uti") for _ in range(2)], "uti")
            xgr = Rot([self.sb([128, TT], F32, "xg") for _ in range(2)], "xg")
            zgr = Rot([self.sb([128, TT], F32, "zg") for _ in range(2)], "zg")
            unr = Rot([self.sb([128, TT], F32, "un") for _ in range(2)], "un")
            ubr = Rot([self.sb([128, TT], BF16, "ub") for _ in range(2)], "ub")
            uor = Rot([self.sb([128, 8, 128], BF16, "uo") for _ in range(2)], "uo")
            par = Rot([self.ps([128, 512], F32, "pa") for _ in range(2)], "pa")
            ptq = [self.ps([128, 8, 128], BF16, "ptq") for _ in range(nq)] if o == 0 else []
            for tt in range(L // TT):
                t0 = s * L + tt * TT
                for q in range(0, nL, 8):
                    qe = min(q + 8, nL)
                    self.dma(cm[:, q:qe, :], self.I[f"dfti_{nm}"][0, tt, :, q:qe, :], w=[f"cm{q}"])
                    self.dma(sm[:, q:qe, :], self.I[f"dfti_{nm}"][1, tt, :, q:qe, :], w=[f"sm{q}"])
                for cc in range(8):
                    yc, ycn = ycr.next(); ys_, ysn = ysr.next()
                    self.dma(yc[:], g.yspec[0, cc], w=[ycn])
                    self.dma(ys_[:], g.yspec[1, cc], w=[ysn])
                    ut, utn = utr.next(); xg, xgn = xgr.next()
                    self.dma(ut[:], g.fm[0, cc, :, t0:t0 + TT], w=[utn])
                    self.dma(xg[:], g.fm[1 + o, cc, :, t0:t0 + TT], w=[xgn])
                    pa, pan = par.next()
                    for fc in range(nL):
                        q8 = (fc // 8) * 8
                        self.mm(pa[:, 0:TT], yc[:, fc, :], cm[:, fc, :], start=(fc == 0), stop=False,
                                r=[ycn, f"cm{q8}"], w=[pan])
                        self.mm(pa[:, 0:TT], ys_[:, fc, :], sm[:, fc, :], start=False, stop=(fc == nL - 1),
                                r=[ysn, f"sm{q8}"], w=[pan])
                    un, unn = unr.next()
                    self.stt(un[:], ut[:], fb[:, o, cc:cc + 1], pa[:, 0:TT], ALU.mult, ALU.add, r=[utn, "fb", pan], w=[unn])
                    self.tt("pool", un[:], un[:], xg[:], ALU.mult, r=[unn, xgn], w=[unn])
                    if o == 0:
                        self.dma(g.fm[0, cc, :, t0:t0 + TT], un[:], r=[unn])
                        ub, ubn = ubr.next()
                        self.cp("act", ub[:], un[:], r=[unn], w=[ubn])
                        for q in range(nq):
                            self.tr(ptq[q][:, cc, :], ub[:, q * 128:(q + 1) * 128], self.ident_b[:],
                                    r=[ubn, "identb"], w=[f"ptq{q}"])
                    else:
                        zg, zgn = zgr.next()
                        self.dma(zg[:], g.fm[3, cc, :, t0:t0 + TT], w=[zgn])
                        ub, ubn = ubr.next()
                        self.tt("dve", ub[:], un[:], zg[:], ALU.mult, r=[unn, zgn], w=[ubn])
                        self.dma(g.yT[cc, :, t0:t0 + TT], ub[:], r=[ubn])
                if o == 0:
                    for q in range(nq):
                        uo, uon = uor.next()
                        self.cp("act" if q % 2 == 0 else "dve", uo[:], ptq[q][:], r=[f"ptq{q}"], w=[uon])
                        self.dma(g.utm[t0 + q * 128:t0 + (q + 1) * 128, :], uo[:].rearrange("p a b -> p (a b)"), r=[uon])


def _consts():
    k = np.arange(128)[:, None]
    i = np.arange(128)[None, :]
    masks = np.stack([(k <= i), (k >= i), (k > i), (k < i)]).astype(np.float32)
    out = {"masks": masks}
    for nm, L in (("p", LP), ("s", LS)):
        n = 2 * L
        t = np.arange(L, dtype=np.float64)
        f = np.arange(L, dtype=np.float64)
        th = 2.0 * np.pi * (f + 0.5) / n
        a2 = np.outer(t + 0.5, th)
        am = np.outer(t, th)
        M = np.stack([np.cos(a2), np.sin(a2), np.cos(am), np.sin(am)]).astype(np.float32).astype(ml_dtypes.bfloat16)
        nL = L // 128
        TT = min(512, L)
        out[f"dft_{nm}"] = np.ascontiguousarray(M.reshape(4, nL, 128, nL, 128).transpose(0, 3, 2, 1, 4))
        out[f"dfti_{nm}"] = np.ascontiguousarray(M[0:2].reshape(2, nL, 128, L // TT, TT).transpose(0, 3, 2, 1, 4))
        tt = (np.arange(L, dtype=np.float32) / np.float32(L)).astype(np.float32)
        w = (np.float32(2.0 * math.pi) * np.arange(L, dtype=np.float32) / np.float32(L)).astype(np.float32)
        fr = np.linspace(1e-4, 15, 16, dtype=np.float32)
        ang = w[:, None] * fr
        z = np.concatenate([tt[:, None], np.cos(ang), np.sin(ang)], axis=-1).astype(np.float32)
        out[f"hz_{nm}"] = np.ascontiguousarray(z.T)
        deltas = np.linspace(math.log(HY_T) / 1.5, math.log(HY_T) / 0.3, D, dtype=np.float32)
        out[f"win_{nm}"] = np.exp(-tt[:, None] * np.abs(deltas)).astype(np.float32)
    return out


_CACHE = {}


def nc_inputs(nc):
    return _CACHE["in_names"]


def kernel(**inputs):
    f = lambda a: np.ascontiguousarray(np.asarray(a, dtype=np.float32))
    if "nc" not in _CACHE:
        kb = KB()
        _CACHE["nc"] = kb.build()
        _CACHE["in_names"] = set(kb.I.keys())
        _CACHE["consts"] = _consts()
    nc = _CACHE["nc"]
    consts = _CACHE["consts"]
    shared = {
        "mod_w": f(inputs["mod_w"]), "mod_b": f(inputs["mod_b"]), "ln_g": f(inputs["ln_g"]), "ln_b": f(inputs["ln_b"]),
        "ssd_in_w": f(inputs["ssd_in_w"]), "ssd_conv_w": f(inputs["ssd_conv_w"]), "ssd_conv_b": f(inputs["ssd_conv_b"]),
        "ssd_dt_bias": f(inputs["ssd_dt_bias"]).reshape(2, 64), "ssd_a_log": f(inputs["ssd_a_log"]).reshape(2, 64),
        "ssd_d": f(inputs["ssd_d"]), "ssd_norm_g": f(inputs["ssd_norm_g"]), "ssd_out_w": f(inputs["ssd_out_w"]),
        "hy_in_w": f(inputs["hy_in_w"])[0], "hy_conv_w": f(inputs["hy_conv_w"])[0], "hy_conv_b": f(inputs["hy_conv_b"])[0],
        "hy_f_w1": f(inputs["hy_f_w1"])[0], "hy_f_b1": f(inputs["hy_f_b1"])[0], "hy_f_w2": f(inputs["hy_f_w2"])[0],
        "hy_f_b2": f(inputs["hy_f_b2"])[0], "hy_f_w3": f(inputs["hy_f_w3"])[0], "hy_f_freq": f(inputs["hy_f_freq"])[0],
        "hy_f_bias": f(inputs["hy_f_bias"])[0], "hy_out_w": f(inputs["hy_out_w"])[0],
        "lru_in_w": f(inputs["lru_in_w"])[0], "lru_conv_w": f(inputs["lru_conv_w"])[0], "lru_conv_b": f(inputs["lru_conv_b"])[0],
        "lru_gate_w": f(inputs["lru_gate_w"])[0], "lru_gate_b": f(inputs["lru_gate_b"])[0].reshape(4, D),
        "lru_a_param": f(inputs["lru_a_param"])[0], "lru_out_w": f(inputs["lru_out_w"])[0],
    }
    shared.update(consts)
    shared = {k: v for k, v in shared.items() if k in nc_inputs(nc)}
    xp = f(inputs["x_prompt"]); xs = f(inputs["x_sample"])
    sts = f(inputs["state_ssd"]); stl = f(inputs["state_lru"])
    c = f(inputs["c"]); cc = f(inputs["c_ctx"])
    in_maps = []
    for core in range(8):
        b = core // 2
        m = dict(shared)
        m["xp"] = np.ascontiguousarray(xp[core * NPS:(core + 1) * NPS].reshape(NPS * LP, D))
        m["xs"] = np.ascontiguousarray(xs[b])
        m["st_ssd"] = np.ascontiguousarray(sts[b].reshape(2, 2, 2048, 128))
        m["st_lru"] = np.ascontiguousarray(stl[b].reshape(2, D))
        m["cond"] = np.ascontiguousarray(np.stack([cc, c[b]]))
        in_maps.append(m)
    res = run_bass_kernel_spmd(nc, in_maps, core_ids=list(range(8)))
    r = res.results
    y_prompt = np.concatenate([r[i]["yp"].reshape(NPS, LP, D) for i in range(8)], axis=0)
    y_sample = np.stack([r[2 * b]["ys"] for b in range(4)], axis=0)
    nss = np.concatenate([r[i]["nss"].reshape(NPS, 2, 2, 32, 64, 128) for i in range(8)], axis=0)
    nsl = np.concatenate([r[i]["nsl"].reshape(NPS, 1, 2, D) for i in range(8)], axis=0)
    return (y_prompt.astype(np.float32), y_sample.astype(np.float32), nss.astype(np.float32), nsl.astype(np.float32))
```

```python
import contextlib
import math
import numpy as np
import ml_dtypes
import concourse.bass as bass
import concourse.mybir as mybir
from concourse.bass_utils import run_bass_kernel_spmd

F32 = mybir.dt.float32
F32R = mybir.dt.float32r
BF16 = mybir.dt.bfloat16
AF = mybir.ActivationFunctionType
ALU = mybir.AluOpType

EPOCH = 8000
NDMA = 28
NDMA_HW = 20
import os as _os
MAXOPS = int(_os.environ.get("MAXOPS", "1000000000"))
NOSELF = _os.environ.get("NOSELF", "0") == "1"

D = 1024
NPS = 4
LP = 256
LS = 4096
DEPTH = 4
ALPHA = (2.0 * DEPTH) ** 0.25
LN_EPS = 1e-5
RMS_EPS = 1e-5
SSD_PROJ = 6208
HY_T = 1e-2


class Sched:
    ENG = ("pe", "act", "dve", "pool")

    def __init__(self, nc, stack):
        self.nc = nc
        self.stack = stack
        self.eng = {"pe": nc.tensor, "act": nc.scalar, "dve": nc.vector,
                    "pool": nc.gpsimd, "sp": nc.sync}
        self.ops = {e: [] for e in self.eng}
        self.cnt = {e: 0 for e in self.ENG}
        self.esems = {e: [] for e in self.ENG}
        self.dsems = [stack.enter_context(nc.semaphore(f"dma{i}")) for i in range(NDMA)]
        self.dval = [0] * NDMA
        self.dnext = 0
        self.dnext_sw = 0
        self.waited = {e: {} for e in self.eng}
        self.lastw = {}
        self.readers = {}
        self.n_inst = 0

    def _esem(self, e, count):
        k = (count - 1) // EPOCH
        while len(self.esems[e]) <= k:
            self.esems[e].append(self.stack.enter_context(
                self.nc.semaphore(f"s_{e}_{len(self.esems[e])}")))
        return self.esems[e][k], (count - 1) % EPOCH + 1, k

    def _emit_wait(self, e, ev):
        if ev[0] == "e":
            _, src, count = ev
            if src == e and (e == "pe" or NOSELF):
                return
            sem, val, k = self._esem(src, count)
            key = ("e", src, k)
        else:
            _, idx, val = ev
            sem = self.dsems[idx]
            key = ("d", idx)
        if self.waited[e].get(key, 0) >= val:
            return
        self.waited[e][key] = val
        engobj = self.eng[e]
        self.ops[e].append(lambda engobj=engobj, sem=sem, val=val: engobj.wait_ge(sem, val))

    def _deps(self, e, reads, writes):
        evs = []
        for r in reads:
            if r in self.lastw:
                evs.append(self.lastw[r])
        for w in writes:
            if w in self.lastw:
                evs.append(self.lastw[w])
            evs.extend(self.readers.get(w, ()))
        for ev in evs:
            self._emit_wait(e, ev)

    def _commit(self, ev, reads, writes):
        for r in reads:
            self.readers.setdefault(r, []).append(ev)
        for w in writes:
            self.lastw[w] = ev
            self.readers[w] = []

    def op(self, e, fn, reads=(), writes=()):
        if self.n_inst >= MAXOPS:
            return
        self._deps(e, reads, writes)
        self.cnt[e] += 1
        count = self.cnt[e]
        sem, val, k = self._esem(e, count)
        self.ops[e].append(lambda fn=fn, sem=sem: fn().then_inc(sem, 1))
        self._commit(("e", e, count), reads, writes)
        self.n_inst += 1

    def dma(self, q, fn, reads=(), writes=()):
        if self.n_inst >= MAXOPS:
            return
        if q == "sp":
            idx = self.dnext
            self.dnext = (self.dnext + 1) % NDMA_HW
        else:
            idx = NDMA_HW + self.dnext_sw
            self.dnext_sw = (self.dnext_sw + 1) % (NDMA - NDMA_HW)
        if self.dval[idx] > 0:
            self._emit_wait(q, ("d", idx, self.dval[idx]))
        self._deps(q, reads, writes)
        self.dval[idx] += 16
        val = self.dval[idx]
        sem = self.dsems[idx]
        self.ops[q].append(lambda fn=fn, sem=sem: fn().then_inc(sem, 16))
        self._commit(("d", idx, val), reads, writes)
        self.n_inst += 1

    def barrier(self):
        for e in self.eng:
            for en in self.ENG:
                if self.cnt[en] and not (en == e):
                    self._emit_wait(e, ("e", en, self.cnt[en]))
                elif self.cnt[en] and e != "pe":
                    self._emit_wait(e, ("e", en, self.cnt[en]))
            for i in range(NDMA):
                if self.dval[i]:
                    self._emit_wait(e, ("d", i, self.dval[i]))
        self.lastw = {}
        self.readers = {}

    def finish(self):
        self.barrier()
        nc = self.nc
        with nc.Block() as block:
            @block.tensor
            def _(t):
                for f in self.ops["pe"]:
                    f()

            @block.scalar
            def _(t):
                for f in self.ops["act"]:
                    f()

            @block.vector
            def _(t):
                for f in self.ops["dve"]:
                    f()

            @block.gpsimd
            def _(t):
                for f in self.ops["pool"]:
                    f()

            @block.sync
            def _(t):
                for f in self.ops["sp"]:
                    f()


class Rot:
    def __init__(self, tiles, name):
        self.tiles = tiles
        self.name = name
        self.i = -1

    def next(self):
        self.i += 1
        k = self.i % len(self.tiles)
        return self.tiles[k], f"{self.name}{k}"


class Grp:
    pass


class KB:
    def __init__(self, layers=(0, 1, 2, 3), do_prompt=True, do_sample=True):
        self.layers = layers
        self.do_prompt = do_prompt
        self.do_sample = do_sample
        self.nc = bass.Bass("TRN2", target_bir_lowering=False)
        self.I = {}
        self.O = {}
        self.uid = 0
        self.nphase = 0
        self.max_phase = 10 ** 9

    def skip(self):
        self.nphase += 1
        return self.nphase > self.max_phase

    def inp(self, name, shape, dt=F32):
        self.I[name] = self.nc.dram_tensor(name, list(shape), dt, kind="ExternalInput").ap()
        return self.I[name]

    def outp(self, name, shape, dt=F32):
        self.O[name] = self.nc.dram_tensor(name, list(shape), dt, kind="ExternalOutput").ap()
        return self.O[name]

    def scr(self, name, shape, dt):
        return self.nc.dram_tensor(name, list(shape), dt, kind="Internal").ap()

    def nm(self, p):
        self.uid += 1
        return f"{p}_{self.uid}"

    def sb(self, shape, dt, name="t"):
        return self.ph.enter_context(self.nc.sbuf_tensor(self.nm(name), list(shape), dt))

    def ps(self, shape, dt, name="p"):
        return self.ph.enter_context(self.nc.psum_tensor(self.nm(name), list(shape), dt))

    @contextlib.contextmanager
    def phase(self):
        with contextlib.ExitStack() as ph:
            old = getattr(self, "ph", None)
            self.ph = ph
            yield
            self.S.barrier()
            self.ph = old

    def dma(self, out, in_, r=(), w=(), q="sp", **kw):
        eng = self.nc.sync if q == "sp" else self.nc.gpsimd
        self.S.dma(q, lambda: eng.dma_start(out=out, in_=in_, **kw), reads=r, writes=w)

    def mm(self, out, lhsT, rhs, start=True, stop=True, r=(), w=()):
        self.S.op("pe", lambda: self.nc.tensor.matmul(out, lhsT=lhsT, rhs=rhs, start=start, stop=stop),
                  reads=r, writes=w)

    def tr(self, out, in_, ident, r=(), w=()):
        self.S.op("pe", lambda: self.nc.tensor.transpose(out=out, in_=in_, identity=ident), reads=r, writes=w)

    def act(self, out, in_, func, r=(), w=(), **kw):
        self.S.op("act", lambda: self.nc.scalar.activation(out=out, in_=in_, func=func, **kw), reads=r, writes=w)

    def E(self, e):
        return self.nc.vector if e == "dve" else self.nc.gpsimd

    def tt(self, e, out, in0, in1, op, r=(), w=()):
        self.S.op(e, lambda: self.E(e).tensor_tensor(out=out, in0=in0, in1=in1, op=op), reads=r, writes=w)

    def ts(self, e, out, in0, s1, s2, op0, op1=None, r=(), w=()):
        if op1 is None:
            self.S.op(e, lambda: self.E(e).tensor_scalar(out=out, in0=in0, scalar1=s1, scalar2=None, op0=op0),
                      reads=r, writes=w)
        else:
            self.S.op(e, lambda: self.E(e).tensor_scalar(out=out, in0=in0, scalar1=s1, scalar2=s2, op0=op0, op1=op1),
                      reads=r, writes=w)

    def stt(self, out, in0, scalar, in1, op0, op1, r=(), w=()):
        self.S.op("dve", lambda: self.nc.vector.scalar_tensor_tensor(out=out, in0=in0, scalar=scalar, in1=in1,
                                                                     op0=op0, op1=op1), reads=r, writes=w)

    def cp(self, e, out, in_, r=(), w=()):
        if e == "act":
            self.S.op("act", lambda: self.nc.scalar.copy(out=out, in_=in_), reads=r, writes=w)
        else:
            self.S.op(e, lambda: self.E(e).tensor_copy(out=out, in_=in_), reads=r, writes=w)

    def memset(self, e, ap, val, w=()):
        self.S.op(e, lambda: self.E(e).memset(ap, val), writes=w)

    def declare(self):
        inp = self.inp
        inp("xp", [NPS * LP, D]); inp("xs", [LS, D])
        inp("st_ssd", [2, 2, 2048, 128]); inp("st_lru", [2, D]); inp("cond", [2, D])
        inp("mod_w", [4, D, 3 * D]); inp("mod_b", [4, 3 * D]); inp("ln_g", [4, D]); inp("ln_b", [4, D])
        inp("ssd_in_w", [2, D, SSD_PROJ]); inp("ssd_conv_w", [2, 4, 4096]); inp("ssd_conv_b", [2, 4096])
        inp("ssd_dt_bias", [2, 64]); inp("ssd_a_log", [2, 64]); inp("ssd_d", [2, 32])
        inp("ssd_norm_g", [2, 2048]); inp("ssd_out_w", [2, 2048, D])
        inp("hy_in_w", [D, 4096]); inp("hy_conv_w", [3, 3072]); inp("hy_conv_b", [3072])
        inp("hy_f_w1", [33, 64]); inp("hy_f_b1", [64]); inp("hy_f_w2", [64, 64]); inp("hy_f_b2", [64])
        inp("hy_f_w3", [64, 4096]); inp("hy_f_freq", [2, 64]); inp("hy_f_bias", [2, D]); inp("hy_out_w", [D, D])
        inp("lru_in_w", [D, 2048]); inp("lru_conv_w", [4, D]); inp("lru_conv_b", [D])
        inp("lru_gate_w", [2, 2, 4, 256, 256]); inp("lru_gate_b", [4, D]); inp("lru_a_param", [2, D])
        inp("lru_out_w", [D, D])
        inp("masks", [4, 128, 128])
        if 1 in self.layers:
            inp("dft_p", [4, LP // 128, 128, LP // 128, 128], BF16)
            inp("dft_s", [4, LS // 128, 128, LS // 128, 128], BF16)
            inp("dfti_p", [2, 1, 128, LP // 128, 256], BF16)
            inp("dfti_s", [2, LS // 512, 128, LS // 128, 512], BF16)
            inp("hz_p", [33, LP]); inp("hz_s", [33, LS])
            inp("win_p", [LP, D]); inp("win_s", [LS, D])
        self.outp("yp", [NPS * LP, D]); self.outp("ys", [LS, D])
        self.outp("nss", [NPS, 2, 2, 2048, 128]); self.outp("nsl", [NPS, 2, D])

    def build(self):
        nc = self.nc
        self.declare()
        with contextlib.ExitStack() as st:
            self.S = Sched(nc, st)
            self.ph = st
            self.ident_f = self.sb([128, 128], F32, "identf")
            self.ident_b = self.sb([128, 128], BF16, "identb")
            self.masks = self.sb([128, 4, 128], F32, "masks")
            self.ones_f = self.sb([128, 128], F32, "ones")
            self.zero_b = self.sb([128, 8, 4], BF16, "zerob")
            self.dma(self.masks[:], self.I["masks"].rearrange("m p i -> p m i"), w=["masks"])
            self.memset("pool", self.ident_f[:], 1.0, w=["identf"])
            self.S.op("pool", lambda: nc.gpsimd.affine_select(
                out=self.ident_f[:], in_=self.ident_f[:], pattern=[[-1, 128]], compare_op=ALU.is_equal,
                fill=0.0, base=0, channel_multiplier=1), reads=["identf"], writes=["identf"])
            self.cp("dve", self.ident_b[:], self.ident_f[:], r=["identf"], w=["identb"])
            self.memset("dve", self.ones_f[:], 1.0, w=["ones"])
            self.masks_r = self.sb([128, 4, 128], F32R, "masksr")
            self.ones_r = self.sb([128, 128], F32R, "onesr")
            self.cp("dve", self.masks_r[:], self.masks[:], r=["masks"], w=["masksr"])
            self.cp("dve", self.ones_r[:], self.ones_f[:], r=["ones"], w=["onesr"])
            self.memset("dve", self.zero_b[:], 0.0, w=["zerob"])
            self.lru_h0 = self.sb([128, 2, 8], F32, "lruh0")
            self.S.barrier()

            self.mod_scr = self.scr("mod_scr", [4, 2, 3 * D], F32)
            groups = []
            if self.do_prompt:
                g = Grp(); g.name = "p"; g.nseq = NPS; g.L = LP; g.cond = 0; g.latent = False
                g.x_in = self.I["xp"]; g.x_out = self.O["yp"]
                groups.append(g)
            if self.do_sample:
                g = Grp(); g.name = "s"; g.nseq = 1; g.L = LS; g.cond = 1; g.latent = True
                g.x_in = self.I["xs"]; g.x_out = self.O["ys"]
                groups.append(g)
            for g in groups:
                g.T = g.nseq * g.L
                g.W = g.nseq * (g.L + 3)
                g.xa = self.scr(f"xa_{g.name}", [g.T, D], F32)
                g.xb = self.scr(f"xb_{g.name}", [g.T, D], F32)
                g.hT = self.scr(f"hT_{g.name}", [8, 128, g.W], BF16)
                g.yT = self.scr(f"yT_{g.name}", [16, 128, g.T], BF16)
                g.xs_tm = self.scr(f"xstm_{g.name}", [g.T, 2048], BF16)
                g.b_tm = self.scr(f"btm_{g.name}", [g.T, 1024], BF16)
                g.bcT = self.scr(f"bcT_{g.name}", [16, 128, g.T], BF16)
                g.dta = self.scr(f"dta_{g.name}", [g.T, 128], F32)
                g.sloc = self.scr(f"sloc_{g.name}", [g.T // 128, 2, 128, 2048], F32)
                g.cdec = self.scr(f"cdec_{g.name}", [g.T // 128, 128, 64], F32)
                g.hprev = self.scr(f"hprev_{g.name}", [g.T // 128, 2, 128, 2048], BF16)
                g.fm = self.scr(f"fm_{g.name}", [4, 8, 128, g.T], F32)
                g.utm = self.scr(f"utm_{g.name}", [g.T, D], BF16)
                g.kw = self.scr(f"kw_{g.name}", [g.L, 4096], F32)
                g.khat = self.scr(f"khat_{g.name}", [2, 2, g.L, D], F32)
                g.yspec = self.scr(f"ysp_{g.name}", [2, 8, 128, g.L // 128, 128], BF16)
                g.khat2 = self.scr(f"khat2_{g.name}", [2, 2, g.L, D], F32)
            self.groups = groups

            self.modulation()
            for li in self.layers:
                last = (li == self.layers[-1])
                for g in groups:
                    X_in = g.x_in if li == self.layers[0] else (g.xa if (li % 2 == 1) else g.xb)
                    X_out = g.x_out if last else (g.xa if (li % 2 == 0) else g.xb)
                    col = g.latent and li == 3
                    self.pass_A(g, li, X_in, col)
                    kind = li % 3
                    if kind == 0:
                        self.ssd_layer(g, li // 3, li)
                        KC = 16
                    elif kind == 1:
                        self.hyena_layer(g)
                        KC = 8
                    else:
                        self.lru_layer(g)
                        KC = 8
                    wname = {0: "ssd_out_w", 1: "hy_out_w", 2: "lru_out_w"}[kind]
                    wout = self.I[wname][li // 3] if kind == 0 else self.I[wname]
                    self.pass_E(g, li, X_in, X_out, col, wout, KC)
            self.S.finish()
        return nc

    def xrows(self, g, X, s, c, col):
        if not col:
            r0 = s * g.L + c * 128
            return [(X[r0:r0 + 128, :], 0, 128)]
        Xv = X.rearrange("(r w) f -> w r f", w=64)
        return [(Xv[2 * c + wo], wo * 64, 64) for wo in range(2)]

    def load_T(self, dst, src2d, R, name):
        stg = self.sb([128, 128], F32, "ldT")
        if getattr(self, "_ldT_ph", None) is not self.ph:
            self._ldT_ph = self.ph
            self._ldT_ps = self.ps([128, 128], F32, "ldTp")
        pt = self._ldT_ps
        k = self.nm("ldT")
        self.dma(stg[0:R, :], src2d, w=[k])
        self.tr(pt[:, 0:R], stg[0:R, :], self.ident_f[0:R, 0:R], r=[k, "identf"], w=["ldTp"])
        self.cp("dve", dst, pt[:, 0:R], r=["ldTp"], w=[name])

    def modulation(self):
        if self.skip():
            return
        nc = self.nc
        with self.phase():
            cond = self.sb([2, D], F32, "cond")
            cs = self.sb([2, D], F32, "cs")
            condT = self.sb([128, 8, 2], BF16, "condT")
            pT = self.ps([128, 8, 2], F32, "pT")
            self.dma(cond[:], self.I["cond"], w=["cond"])
            self.act(cs[:], cond[:], AF.Silu, r=["cond"], w=["cs"])
            for k in range(8):
                self.tr(pT[:, k, :], cs[0:2, k * 128:(k + 1) * 128], self.ident_f[0:2, 0:2],
                        r=["cs", "identf"], w=["pT"])
            self.cp("dve", condT[:], pT[:], r=["pT"], w=["condT"])
            mw = self.sb([128, 8, 3 * D], BF16, "mw")
            mb = self.sb([2, 3 * D], F32, "mb")
            msb = self.sb([2, 3 * D], F32, "msb")
            pm = [self.ps([2, 512], F32, "pm") for _ in range(2)]
            for li in self.layers:
                for k in range(8):
                    self.dma(mw[:, k, :], self.I["mod_w"][li, k * 128:(k + 1) * 128, :], w=[f"mw{k}"], q="pool")
                for c in range(2):
                    self.dma(mb[c:c + 1, :], self.I["mod_b"][li:li + 1, :], w=["mb"])
                for t in range(6):
                    p = pm[t % 2]
                    for k in range(8):
                        self.mm(p[:], condT[:, k, :], mw[:, k, t * 512:(t + 1) * 512], start=(k == 0), stop=(k == 7),
                                r=["condT", f"mw{k}"], w=[f"pm{t % 2}"])
                    self.tt("dve", msb[:, t * 512:(t + 1) * 512], p[:], mb[:, t * 512:(t + 1) * 512], ALU.add,
                            r=[f"pm{t % 2}", "mb"], w=["msb"])
                self.ts("dve", msb[:, D:2 * D], msb[:, D:2 * D], 1.0, None, ALU.add, r=["msb"], w=["msb"])
                self.dma(self.mod_scr[li], msb[:], r=["msb"])

    def pass_A(self, g, li, X, col):
        if self.skip():
            return
        with self.phase():
            sc = self.sb([128, D], F32, "sc")
            sh = self.sb([128, D], F32, "sh")
            self.dma(sh[:], self.mod_scr[li, g.cond, 0:D].partition_broadcast(128), w=["sh"])
            self.dma(sc[:], self.mod_scr[li, g.cond, D:2 * D].partition_broadcast(128), w=["sc"])
            xr = Rot([self.sb([128, D], F32, "xA") for _ in range(2)], "xA")
            hr = Rot([self.sb([128, D], BF16, "hA") for _ in range(2)], "hA")
            tr_ = Rot([self.sb([128, 8, 128], BF16, "hTA") for _ in range(2)], "hTA")
            pr = Rot([self.ps([128, 8, 128], BF16, "pA") for _ in range(2)], "pA")
            nchunk = g.L // 128
            for s in range(g.nseq):
                base = s * (g.L + 3)
                self.dma(g.hT[:, :, base:base + 1].rearrange("k p t -> p k t"), self.zero_b[:, :, 0:1], r=["zerob"],
                         allow_slow_non_contiguous=True)
                self.dma(g.hT[:, :, base + g.L + 1:base + g.L + 3].rearrange("k p t -> p k t"),
                         self.zero_b[:, :, 0:2], r=["zerob"], allow_slow_non_contiguous=True)
                def chunk_gen(c, s=s, base=base):
                    xt, xn = xr.next()
                    for (src, p0, n) in self.xrows(g, X, s, c, col):
                        self.dma(xt[p0:p0 + n, :], src, w=[xn])
                    ht, hn = hr.next()
                    pt, pn = pr.next()
                    tt_, tn = tr_.next()
                    yield
                    self.tt("dve", xt[:], xt[:], sc[:], ALU.mult, r=[xn, "sc"], w=[xn])
                    yield
                    self.tt("dve", ht[:], xt[:], sh[:], ALU.add, r=[xn, "sh"], w=[hn])
                    yield
                    for k in range(8):
                        self.tr(pt[:, k, :], ht[:, k * 128:(k + 1) * 128], self.ident_b[:], r=[hn, "identb"], w=[pn])
                    yield
                    self.cp("act", tt_[:], pt[:], r=[pn], w=[tn])
                    yield
                    c0 = base + 1 + c * 128
                    self.dma(g.hT[:, :, c0:c0 + 128].rearrange("k p t -> p k t"), tt_[:], r=[tn])

                gens = [chunk_gen(c) for c in range(nchunk)]
                for i in range(0, len(gens), 2):
                    active = gens[i:i + 2]
                    while active:
                        for gch in list(active):
                            try:
                                next(gch)
                            except StopIteration:
                                active.remove(gch)

    def pass_E(self, g, li, X, Xo, col, wout, KC):
        if self.skip():
            return
        nc = self.nc
        with self.phase():
            wo = self.sb([128, KC, D], BF16, "wo")
            for k in range(KC):
                self.dma(wo[:, k, :], wout[k * 128:(k + 1) * 128, :], w=[f"wo{k}"], q="pool")
            gt = self.sb([128, D], F32, "gate")
            lg = self.sb([128, D], F32, "lng")
            lb = self.sb([128, D], F32, "lnb")
            self.dma(gt[:], self.mod_scr[li, g.cond, 2 * D:3 * D].partition_broadcast(128), w=["gate"])
            self.dma(lg[:], self.I["ln_g"][li].partition_broadcast(128), w=["lng"])
            self.dma(lb[:], self.I["ln_b"][li].partition_broadcast(128), w=["lnb"])
            yr = Rot([self.sb([128, KC, 128], BF16, "yE") for _ in range(2)], "yE")
            xr = Rot([self.sb([128, D], F32, "xE") for _ in range(2)], "xE")
            rr = Rot([self.sb([128, D], F32, "rE") for _ in range(2)], "rE")
            sr = Rot([self.sb([128, 16], F32, "sE") for _ in range(2)], "sE")
            pr = Rot([self.ps([128, D], F32, "pE") for _ in range(2)], "pE")
            nchunk = g.L // 128

            def chunk_gen(s, c):
                t0 = s * g.L + c * 128
                yt, yn = yr.next()
                self.dma(yt[:], g.yT[0:KC, :, t0:t0 + 128].rearrange("k p t -> p k t"), w=[yn])
                xt, xn = xr.next()
                for (src, p0, n) in self.xrows(g, X, s, c, col):
                    self.dma(xt[p0:p0 + n, :], src, w=[xn])
                pt, pn = pr.next()
                rt, rn = rr.next()
                stt_, sn = sr.next()
                yield
                for hlf in range(2):
                    for k in range(KC):
                        self.mm(pt[:, hlf * 512:(hlf + 1) * 512], yt[:, k, :], wo[:, k, hlf * 512:(hlf + 1) * 512],
                                start=(k == 0), stop=(k == KC - 1), r=[yn, f"wo{k}"], w=[pn + str(hlf)])
                yield
                for hlf in range(2):
                    sl = slice(hlf * 512, (hlf + 1) * 512)
                    self.tt("dve", rt[:, sl], pt[:, sl], gt[:, sl], ALU.mult, r=[pn + str(hlf), "gate"], w=[rn])
                yield
                self.stt(rt[:], xt[:], ALPHA, rt[:], ALU.mult, ALU.add, r=[xn, rn], w=[rn])
                yield
                self.S.op("dve", lambda: nc.vector.bn_stats(out=stt_[:, 0:6], in_=rt[:, 0:512]),
                          reads=[rn], writes=[sn])
                self.S.op("dve", lambda: nc.vector.bn_stats(out=stt_[:, 6:12], in_=rt[:, 512:1024]),
                          reads=[rn], writes=[sn])
                yield
                self.S.op("dve", lambda: nc.vector.bn_aggr(out=stt_[:, 12:14], in_=stt_[:, 0:12]),
                          reads=[sn], writes=[sn])
                yield
                self.ts("dve", stt_[:, 14:15], stt_[:, 13:14], LN_EPS, None, ALU.add, r=[sn], w=[sn])
                yield
                self.act(stt_[:, 14:15], stt_[:, 14:15], AF.Sqrt, r=[sn], w=[sn])
                yield
                self.S.op("dve", lambda: nc.vector.reciprocal(out=stt_[:, 15:16], in_=stt_[:, 14:15]),
                          reads=[sn], writes=[sn])
                yield
                self.ts("dve", rt[:], rt[:], stt_[:, 12:13], stt_[:, 15:16], ALU.subtract, ALU.mult,
                        r=[rn, sn], w=[rn])
                yield
                self.tt("pool", rt[:], rt[:], lg[:], ALU.mult, r=[rn, "lng"], w=[rn])
                yield
                self.tt("pool", rt[:], rt[:], lb[:], ALU.add, r=[rn, "lnb"], w=[rn])
                yield
                for (dst, p0, n) in self.xrows(g, Xo, s, c, col):
                    self.dma(dst, rt[p0:p0 + n, :], r=[rn])

            gens = [chunk_gen(s, c) for s in range(g.nseq) for c in range(nchunk)]
            for i in range(0, len(gens), 2):
                active = gens[i:i + 2]
                while active:
                    for gch in list(active):
                        try:
                            next(gch)
                        except StopIteration:
                            active.remove(gch)

    def conv_fm(self, pp, pn, acc, an, dst, dn, cw, cb, fc, K, silu, woff):
        self.act(acc[:], pp[:, woff:woff + 256], AF.Identity, r=[pn, "cw", "cb"], w=[an],
                 scale=cw[:, 0, fc:fc + 1], bias=cb[:, fc:fc + 1])
        for k in range(1, K):
            self.stt(acc[:], pp[:, woff + k:woff + k + 256], cw[:, k, fc:fc + 1], acc[:], ALU.mult, ALU.add,
                     r=[pn, an, "cw"], w=[an])
        def fin():
            if silu:
                self.act(dst, acc[:], AF.Silu, r=[an], w=[dn])
            else:
                self.cp("act", dst, acc[:], r=[an], w=[dn])
        return fin

    def conv_steps(self, pp, pn, acc, an, dst, dn, cw, cb, fc, K, silu):
        steps = [lambda: self.act(acc[:], pp[:, 0:256], AF.Identity, r=[pn, "cw", "cb"], w=[an],
                                  scale=cw[:, 0, fc:fc + 1], bias=cb[:, fc:fc + 1])]
        for k in range(1, K):
            steps.append(lambda k=k: self.stt(acc[:], pp[:, k:k + 256], cw[:, k, fc:fc + 1], acc[:], ALU.mult, ALU.add,
                                              r=[pn, an, "cw"], w=[an]))

        def fin():
            if silu:
                self.act(dst, acc[:], AF.Silu, r=[an], w=[dn])
            else:
                self.cp("act", dst, acc[:], r=[an], w=[dn])
        return steps, fin

    def conv_block(self, fcs, ppr, accr, mm_fn, dst_fn, cw, cb, K, silu, after_fn=None):
        fcs = list(fcs)
        pend = []
        for i in range(0, len(fcs), 2):
            steps = []
            fins = []
            for fc in fcs[i:i + 2]:
                pp, pn = ppr.next()
                mm_fn(fc, pp, pn)
                acc, an = accr.next()
                dst, dn = dst_fn(fc)
                st, fin = self.conv_steps(pp, pn, acc, an, dst, dn, cw, cb, fc, K, silu)
                steps.append(st)
                fins.append((fin, fc))
            for j in range(K):
                for st in steps:
                    st[j]()
            for f, fcp in pend:
                f()
                if after_fn is not None:
                    after_fn(fcp)
            pend = fins
        for f, fcp in pend:
            f()
            if after_fn is not None:
                after_fn(fcp)

    def ssd_layer(self, g, slot, li):
        self.ssd_p1(g, slot)
        if _os.environ.get("DEBUG") == "dta":
            with self.phase():
                t = self.sb([128, 8, 128], F32, "dbg")
                self.dma(t[:], g.dta[0:1024, :].rearrange("(c p) f -> p c f", p=128), w=["dbg"])
                self.dma(self.O["yp"][:, 0:128].rearrange("(c p) f -> p c f", p=128), t[:], r=["dbg"])
        self.ssd_p2a(g, slot)
        self.ssd_pR(g, slot)
        self.ssd_p2b(g, slot)

    def ssd_p1(self, g, slot):
        if self.skip():
            return
        nc = self.nc
        with self.phase():
            w_in = self.I["ssd_in_w"][slot]
            wx = self.sb([128, 8, 4096], BF16, "wx")
            wd = self.sb([128, 8, 64], BF16, "wd")
            for k in range(8):
                self.dma(wx[:, k, :], w_in[k * 128:(k + 1) * 128, 2048:6144], w=[f"wx{k}"], q="pool")
                self.dma(wd[:, k, :], w_in[k * 128:(k + 1) * 128, 6144:6208], w=["wd"], q="pool")
            cw = self.sb([128, 4, 32], F32, "cw")
            cb = self.sb([128, 32], F32, "cb")
            self.load_T(cw[:].rearrange("p k c -> p (k c)"),
                        self.I["ssd_conv_w"][slot].rearrange("k (c p) -> (k c) p", p=128), 128, "cw")
            self.load_T(cb[:], self.I["ssd_conv_b"][slot].rearrange("(c p) -> c p", p=128), 32, "cb")
            dtb = self.sb([128, 64], F32, "dtb")
            abc = self.sb([128, 64], F32, "abc")
            self.dma(dtb[:], self.I["ssd_dt_bias"][slot].partition_broadcast(128), w=["dtb"])
            self.dma(abc[:], self.I["ssd_a_log"][slot].partition_broadcast(128), w=["abc"])
            self.act(abc[:], abc[:], AF.Exp, r=["abc"], w=["abc"])
            self.ts("dve", abc[:], abc[:], -1.0, None, ALU.mult, r=["abc"], w=["abc"])

            hwr = Rot([self.sb([128, 8, 260], BF16, "hw") for _ in range(2)], "hw")
            xbr = Rot([self.sb([128, 32, 256], BF16, "xbc") for _ in range(2)], "xbc")
            accr = Rot([self.sb([128, 256], F32, "acc") for _ in range(4)], "acc")
            tmr = Rot([self.sb([128, 1024], BF16, "tm") for _ in range(3)], "tm")
            dtr = Rot([self.sb([128, 128], F32, "dta") for _ in range(2)], "dta")
            ppr = Rot([self.ps([128, 512], F32, "pp") for _ in range(4)], "pp")
            ptr = Rot([self.ps([128, 8, 128], BF16, "ptr") for _ in range(2)], "ptr")
            pdr = Rot([self.ps([128, 64], F32, "pd") for _ in range(1)], "pd")
            for s in range(g.nseq):
                for j in range(g.L // 256):
                    c0 = s * (g.L + 3) + 256 * j
                    t0 = s * g.L + 256 * j
                    hw, hn = hwr.next()
                    self.dma(hw[:, :, 0:259], g.hT[:, :, c0:c0 + 259].rearrange("k p t -> p k t"), w=[hn])
                    xb, xn = xbr.next()
                    def mm_fn(fc, pp, pn, hw=hw, hn=hn):
                        for k in range(8):
                            self.mm(pp[:, 0:259], wx[:, k, fc * 128:(fc + 1) * 128], hw[:, k, 0:259],
                                    start=(k == 0), stop=(k == 7), r=[hn, f"wx{k}"], w=[pn])
                    self.conv_block(range(32), ppr, accr, mm_fn, lambda fc, xb=xb, xn=xn: (xb[:, fc, :], f"{xn}_{fc}"),
                                    cw, cb, 4, True)
                    self.dma(g.bcT[:, :, t0:t0 + 256].rearrange("c p t -> p c t"), xb[:, 16:32, :],
                             r=[f"{xn}_{fc}" for fc in range(16, 32)])
                    for tcn in range(2):
                        for blk in range(3):
                            pt, ptn = ptr.next()
                            for i in range(8):
                                fc = blk * 8 + i
                                self.tr(pt[:, i, :], xb[:, fc, tcn * 128:(tcn + 1) * 128], self.ident_b[:],
                                        r=[f"{xn}_{fc}", "identb"], w=[ptn])
                            tm, tn = tmr.next()
                            self.cp("act" if blk % 2 == 0 else "dve", tm[:], pt[:].rearrange("p a b -> p (a b)"),
                                    r=[ptn], w=[tn])
                            r0 = t0 + tcn * 128
                            if blk < 2:
                                self.dma(g.xs_tm[r0:r0 + 128, blk * 1024:(blk + 1) * 1024], tm[:], r=[tn])
                            else:
                                self.dma(g.b_tm[r0:r0 + 128, :], tm[:], r=[tn])
                        pd, pdn = pdr.next()
                        for k in range(8):
                            self.mm(pd[:], hw[:, k, 1 + tcn * 128:1 + (tcn + 1) * 128], wd[:, k, :],
                                    start=(k == 0), stop=(k == 7), r=[hn, "wd"], w=[pdn])
                        dt, dn = dtr.next()
                        self.tt("dve", dt[:, 0:64], pd[:], dtb[:], ALU.add, r=[pdn, "dtb"], w=[dn])
                        self.act(dt[:, 0:64], dt[:, 0:64], AF.Exp, r=[dn], w=[dn])
                        self.act(dt[:, 0:64], dt[:, 0:64], AF.Ln, r=[dn], w=[dn], bias=1.0)
                        self.tt("dve", dt[:, 64:128], dt[:, 0:64], abc[:], ALU.mult, r=[dn, "abc"], w=[dn])
                        self.dma(g.dta[r0:r0 + 128, :], dt[:], r=[dn])

    def ssd_p2a(self, g, slot):
        if self.skip():
            return
        nc = self.nc
        with self.phase():
            xsr = Rot([self.sb([128, 2048], BF16, "xs") for _ in range(2)], "xs")
            btr = Rot([self.sb([128, 1024], BF16, "bt") for _ in range(2)], "bt")
            dtr = Rot([self.sb([128, 128], F32, "dta") for _ in range(2)], "dta")
            der = Rot([self.sb([128, 64], F32, "de") for _ in range(2)], "de")
            cdr = Rot([self.sb([128, 64], F32, "cd") for _ in range(2)], "cd")
            wdr = Rot([self.sb([128, 2048], BF16, "wdd") for _ in range(2)], "wdd")
            ssr = Rot([self.sb([128, 1024], F32, "ss") for _ in range(3)], "ss")
            pcr = Rot([self.ps([128, 128], F32, "pc") for _ in range(2)], "pc")
            psr = Rot([self.ps([128, 1024], F32, "psS") for _ in range(2)], "psS")
            arr = Rot([self.sb([128, 64], F32R, "ar") for _ in range(2)], "ar")
            if _os.environ.get("DEBUG") == "alloc":
                for t in pcr.tiles + psr.tiles + der.tiles:
                    print("ALLOC", t.name, self.nc.lookup_mloc(t))
            def chunk_gen(ch):
                r0 = ch * 128
                xs, xn = xsr.next(); bt, bn = btr.next(); dt, dn = dtr.next()
                self.dma(xs[:], g.xs_tm[r0:r0 + 128, :], w=[xn])
                self.dma(bt[:], g.b_tm[r0:r0 + 128, :], w=[bn])
                self.dma(dt[:], g.dta[r0:r0 + 128, :], w=[dn])
                pc, pcn = pcr.next()
                ar, arn = arr.next()
                de, den = der.next(); cd, cdn = cdr.next()
                yield
                self.cp("dve", ar[:], dt[:, 64:128], r=[dn], w=[arn])
                yield
                self.mm(pc[:, 0:32], self.masks_r[:, 2, :], ar[:, 0:32], r=["masksr", arn], w=[pcn])
                self.mm(pc[:, 32:64], self.masks_r[:, 3, :], ar[:, 32:64], r=["masksr", arn], w=[pcn])
                self.mm(pc[:, 64:128], self.ones_r[:], ar[:, 0:64], r=["onesr", arn], w=[pcn])
                yield
                self.act(de[:], pc[:, 0:64], AF.Exp, r=[pcn], w=[den])
                self.act(cd[:], pc[:, 64:128], AF.Exp, r=[pcn], w=[cdn])
                yield
                self.dma(g.cdec[ch], cd[:], r=[cdn])
                self.tt("dve", de[:], de[:], dt[:, 0:64], ALU.mult, r=[den, dn], w=[den])
                yield
                for d in range(2):
                    wdd, wn = wdr.next()
                    self.tt("dve", wdd[:].rearrange("p (h e) -> p h e", h=32), xs[:].rearrange("p (h e) -> p h e", h=32),
                            de[:, d * 32:(d + 1) * 32].unsqueeze(2).to_broadcast([128, 32, 64]), ALU.mult,
                            r=[xn, den], w=[wn])
                    yield
                    for hlf in range(2):
                        pS, psn = psr.next()
                        for gg in range(4):
                            G8 = hlf * 4 + gg
                            self.mm(pS[:, gg * 256:(gg + 1) * 256], bt[:, G8 * 128:(G8 + 1) * 128],
                                    wdd[:, G8 * 256:(G8 + 1) * 256], r=[bn, wn], w=[psn + str(gg // 2)])
                        yield
                        ss, sn = ssr.next()
                        for q in range(2):
                            self.cp("act" if q == 0 else "dve", ss[:, q * 512:(q + 1) * 512], pS[:, q * 512:(q + 1) * 512],
                                    r=[psn + str(q)], w=[sn])
                        yield
                        self.dma(g.sloc[ch, d, :, hlf * 1024:(hlf + 1) * 1024], ss[:], r=[sn])

            gens = [chunk_gen(ch) for ch in range(g.T // 128)]
            for i in range(0, len(gens), 2):
                active = gens[i:i + 2]
                while active:
                    for gch in list(active):
                        try:
                            next(gch)
                        except StopIteration:
                            active.remove(gch)

    def ssd_pR(self, g, slot):
        if self.skip():
            return
        nc = self.nc
        nchunk = g.L // 128
        with self.phase():
            hst = [self.sb([128, 2048], F32, "hst") for _ in range(2)]
            hbr = [Rot([self.sb([128, 2048], BF16, "hb") for _ in range(2)], f"hb{d}") for d in range(2)]
            slr = [Rot([self.sb([128, 2048], F32, "sl") for _ in range(2)], f"sl{d}") for d in range(2)]
            cdr = [Rot([self.sb([128, 64], F32, "cd") for _ in range(2)], f"cd{d}") for d in range(2)]
            stg = Rot([self.sb([128, 128], F32, "stg") for _ in range(3)], "stg")
            ptr = Rot([self.ps([128, 128], F32, "ptR") for _ in range(2)], "ptR")
            eng = ["dve", "pool"]
            for s in range(g.nseq):
                for d in range(2):
                    hn = f"hst{d}"
                    if g.latent:
                        for t in range(16):
                            sg, sgn = stg.next()
                            self.dma(sg[:], self.I["st_ssd"][slot, d, t * 128:(t + 1) * 128, :], w=[sgn])
                            pt, ptn = ptr.next()
                            self.tr(pt[:], sg[:], self.ident_f[:], r=[sgn, "identf"], w=[ptn])
                            self.cp("act", hst[d][:, t * 128:(t + 1) * 128], pt[:], r=[ptn], w=[hn])
                    else:
                        self.memset(eng[d], hst[d][:], 0.0, w=[hn])
                order = [list(range(nchunk)), list(range(nchunk - 1, -1, -1))]
                for i in range(nchunk):
                    for d in range(2):
                        c = order[d][i]
                        ch = s * nchunk + c
                        hn = f"hst{d}"
                        hb, hbn = hbr[d].next()
                        self.cp("act", hb[:], hst[d][:], r=[hn], w=[hbn])
                        self.dma(g.hprev[ch, d], hb[:], r=[hbn])
                        sl, sln = slr[d].next(); cd, cdn = cdr[d].next()
                        self.dma(sl[:], g.sloc[ch, d], w=[sln])
                        self.dma(cd[:], g.cdec[ch], w=[cdn])
                        self.tt(eng[d], hst[d][:].rearrange("p (h e) -> p h e", h=32),
                                hst[d][:].rearrange("p (h e) -> p h e", h=32),
                                cd[:, d * 32:(d + 1) * 32].unsqueeze(2).to_broadcast([128, 32, 64]), ALU.mult,
                                r=[hn, cdn], w=[hn])
                        self.tt(eng[d], hst[d][:], hst[d][:], sl[:], ALU.add, r=[hn, sln], w=[hn])
                if not g.latent:
                    for d in range(2):
                        hn = f"hst{d}"
                        for t in range(16):
                            pt, ptn = ptr.next()
                            self.tr(pt[:], hst[d][:, t * 128:(t + 1) * 128], self.ident_f[:], r=[hn, "identf"], w=[ptn])
                            sg, sgn = stg.next()
                            self.cp("act", sg[:], pt[:], r=[ptn], w=[sgn])
                            self.dma(self.O["nss"][s, slot, d, t * 128:(t + 1) * 128, :], sg[:], r=[sgn])

    def ssd_p2b(self, g, slot):
        if self.skip():
            return
        nc = self.nc
        with self.phase():
            w_in = self.I["ssd_in_w"][slot]
            wz = self.sb([128, 8, 2048], BF16, "wz")
            for k in range(8):
                self.dma(wz[:, k, :], w_in[k * 128:(k + 1) * 128, 0:2048], w=[f"wz{k}"], q="pool")
            ng = self.sb([128, 2048], F32, "ng")
            self.dma(ng[:], self.I["ssd_norm_g"][slot].partition_broadcast(128), w=["ng"])
            dsk = self.sb([128, 32], F32, "dsk")
            self.dma(dsk[:], self.I["ssd_d"][slot].partition_broadcast(128), w=["dsk"])
            mgt_r = self.sb([128, 128], F32R, "mgtr")
            mlt_r = self.sb([128, 128], F32R, "mltr")
            self.cp("dve", mgt_r[:], self.masks[:, 2, :], r=["masks"], w=["mgtr"])
            self.cp("dve", mlt_r[:], self.masks[:, 3, :], r=["masks"], w=["mltr"])
            Xl = [mgt_r, mlt_r]
            Xn = ["mgtr", "mltr"]
            Ym = [0, 1]
            Sm = [0, 1]

            hTr = Rot([self.sb([128, 8, 128], BF16, "hT") for _ in range(2)], "hT")
            xsr = Rot([self.sb([128, 2048], BF16, "xs") for _ in range(2)], "xs")
            bcr = Rot([self.sb([128, 16, 128], BF16, "bc") for _ in range(2)], "bc")
            dtr = Rot([self.sb([128, 128], F32, "dta") for _ in range(2)], "dta")
            hpr = Rot([self.sb([128, 2, 2048], BF16, "hp") for _ in range(2)], "hp")
            zsr = Rot([self.sb([128, 2048], BF16, "zs") for _ in range(2)], "zs")
            xdr = Rot([self.sb([128, 2, 2048], BF16, "xd") for _ in range(2)], "xd")
            ear = Rot([self.sb([128, 64], F32, "ea") for _ in range(2)], "ea")
            scr_ = Rot([self.sb([128, 2, 128], BF16, "scm") for _ in range(2)], "scm")
            Yr = Rot([self.sb([128, 4, 128], F32R, "Y") for _ in range(3)], "Y")
            Er = Rot([self.sb([128, 4, 128], BF16, "Ee") for _ in range(3)], "Ee")
            Mr = Rot([self.sb([128, 2, 4, 128], BF16, "Mm") for _ in range(2)], "Mm")
            t1r = Rot([self.sb([128, 256], F32, "t1") for _ in range(2)], "t1")
            t2r = Rot([self.sb([128, 256], F32, "t2") for _ in range(2)], "t2")
            ypr = Rot([self.sb([128, 2048], F32, "ypre") for _ in range(2)], "ypre")
            ynr = Rot([self.sb([128, 2048], BF16, "yn") for _ in range(2)], "yn")
            ssr = Rot([self.sb([128, 4], F32, "ssq") for _ in range(2)], "ssq")
            junk = self.sb([128, 2048], BF16, "junk")
            yTr = Rot([self.sb([128, 16, 128], BF16, "yT") for _ in range(2)], "yT")

            pzr = Rot([self.ps([128, 1024], F32, "pz") for _ in range(1)], "pz")
            pcs = Rot([self.ps([128, 512], F32, "pcs") for _ in range(1)], "pcs")
            psg = Rot([self.ps([128, 512], F32, "psg") for _ in range(2)], "psg")
            pyr = Rot([self.ps([128, 1024], F32, "py") for _ in range(1)], "py")
            ptr = Rot([self.ps([128, 8, 128], BF16, "ptb") for _ in range(1)], "ptb")
            nchunk = g.L // 128
            for s in range(g.nseq):
                for c in range(nchunk):
                    ch = s * nchunk + c
                    r0 = ch * 128
                    c0 = s * (g.L + 3) + 1 + c * 128
                    hT, hTn = hTr.next(); xs, xn = xsr.next(); bc, bcn = bcr.next(); dt, dn = dtr.next()
                    hp, hpn = hpr.next()
                    self.dma(hT[:], g.hT[:, :, c0:c0 + 128].rearrange("k p t -> p k t"), w=[hTn])
                    self.dma(xs[:], g.xs_tm[r0:r0 + 128, :], w=[xn])
                    self.dma(bc[:], g.bcT[:, :, r0:r0 + 128].rearrange("c p t -> p c t"), w=[bcn])
                    self.dma(dt[:], g.dta[r0:r0 + 128, :], w=[dn])
                    self.dma(hp[:], g.hprev[ch].rearrange("d p f -> p d f"), w=[hpn])
                    zs, zn = zsr.next()
                    for hlf in range(2):
                        pz, pzn = pzr.next()
                        for q in range(2):
                            for k in range(8):
                                col0 = hlf * 1024 + q * 512
                                self.mm(pz[:, q * 512:(q + 1) * 512], hT[:, k, :], wz[:, k, col0:col0 + 512],
                                        start=(k == 0), stop=(k == 7), r=[hTn, f"wz{k}"], w=[pzn + str(q)])
                        for q in range(2):
                            self.act(zs[:, hlf * 1024 + q * 512:hlf * 1024 + (q + 1) * 512], pz[:, q * 512:(q + 1) * 512],
                                     AF.Silu, r=[pzn + str(q)], w=[zn])
                    xd, xdn = xdr.next()
                    for d in range(2):
                        self.tt("dve" if d == 0 else "pool", xd[:, d, :].rearrange("p (h e) -> p h e", h=32),
                                xs[:].rearrange("p (h e) -> p h e", h=32),
                                dt[:, d * 32:(d + 1) * 32].unsqueeze(2).to_broadcast([128, 32, 64]), ALU.mult,
                                r=[xn, dn], w=[xdn + str(d)])
                    pc, pcn = pcs.next()
                    self.mm(pc[:, 0:32], self.masks[:, 0, :], dt[:, 64:96], r=["masks", dn], w=[pcn])
                    self.mm(pc[:, 32:64], self.masks[:, 1, :], dt[:, 96:128], r=["masks", dn], w=[pcn])
                    ea, ean = ear.next()
                    self.act(ea[:], pc[:, 0:64], AF.Exp, r=[pcn], w=[ean])
                    yp, ypn = ypr.next()
                    v3 = lambda ap: ap.rearrange("p (h e) -> p h e", h=4)

                    def stA(G8):
                        self.mm(pc[:, 128:256], bc[:, G8, :], bc[:, 8 + G8, :], r=[bcn], w=[pcn])
                        sm, smn = scr_.next()
                        for d in range(2):
                            self.tt("dve", sm[:, d, :], pc[:, 128:256], self.masks[:, Sm[d], :], ALU.mult,
                                    r=[pcn, "masks"], w=[smn + str(d)])
                        Mt, Mn = Mr.next()
                        for d in range(2):
                            Y, Yn = Yr.next()
                            self.tt("pool", Y[:], self.masks[:, Ym[d], :].unsqueeze(1).to_broadcast([128, 4, 128]),
                                    dt[:, 64 + d * 32 + G8 * 4:64 + d * 32 + G8 * 4 + 4].unsqueeze(2).to_broadcast([128, 4, 128]),
                                    ALU.mult, r=["masks", dn], w=[Yn])
                            pg, pgn = psg.next()
                            self.mm(pg[:], Xl[d][:], Y[:].rearrange("p a b -> p (a b)"), r=[Xn[d], Yn], w=[pgn])
                            Et, En = Er.next()
                            self.act(Et[:].rearrange("p a b -> p (a b)"), pg[:], AF.Exp, r=[pgn], w=[En])
                            self.tt("dve", Mt[:, d, :, :], Et[:], sm[:, d, :].unsqueeze(1).to_broadcast([128, 4, 128]),
                                    ALU.mult, r=[En, smn + str(d)], w=[Mn + str(d)])
                        return Mt, Mn

                    def stB(G8, Mt, Mn):
                        py, pyn = pyr.next()
                        for h in range(4):
                            H = G8 * 4 + h
                            for d in range(2):
                                self.mm(py[:, h * 64:(h + 1) * 64], Mt[:, d, h, :], xd[:, d, H * 64:(H + 1) * 64],
                                        start=(d == 0), stop=(d == 1), r=[Mn + str(d), xdn + str(d)], w=[pyn + "d"])
                        for d in range(2):
                            self.mm(py[:, 512 + d * 256:512 + (d + 1) * 256], bc[:, 8 + G8, :],
                                    hp[:, d, G8 * 256:(G8 + 1) * 256], r=[bcn, hpn], w=[pyn + "o"])
                        return py, pyn

                    def stC(G8, py, pyn):
                        t1, t1n = t1r.next(); t2, t2n = t2r.next()
                        for d, (tt_, tn_) in enumerate(((t1, t1n), (t2, t2n))):
                            self.tt("dve", v3(tt_[:]), v3(py[:, 512 + d * 256:512 + (d + 1) * 256]),
                                    ea[:, d * 32 + G8 * 4:d * 32 + G8 * 4 + 4].unsqueeze(2).to_broadcast([128, 4, 64]),
                                    ALU.mult, r=[pyn + "o", ean], w=[tn_])
                        self.tt("pool", t1[:], t1[:], t2[:], ALU.add, r=[t1n, t2n], w=[t1n])
                        self.tt("dve", t2[:], py[:, 0:256], t1[:], ALU.add, r=[pyn + "d", t1n], w=[t2n])
                        self.tt("pool", v3(t1[:]), v3(xs[:, G8 * 256:(G8 + 1) * 256]),
                                dsk[:, G8 * 4:G8 * 4 + 4].unsqueeze(2).to_broadcast([128, 4, 64]), ALU.mult,
                                r=[xn, "dsk", t1n], w=[t1n])
                        self.tt("pool", yp[:, G8 * 256:(G8 + 1) * 256], t1[:], t2[:], ALU.add, r=[t1n, t2n],
                                w=[ypn + str(G8)])

                    MA = stA(0)
                    for G8 in range(8):
                        PB = stB(G8, *MA)
                        if G8 < 7:
                            MA = stA(G8 + 1)
                        stC(G8, *PB)
                    ypa = [ypn + str(i) for i in range(8)]
                    self.tt("dve", yp[:], yp[:], zs[:], ALU.mult, r=ypa + [zn], w=ypa)
                    sq, sqn = ssr.next()
                    self.act(junk[:], yp[:], AF.Square, r=ypa, w=["junk", sqn], accum_out=sq[:, 0:1])
                    self.ts("dve", sq[:, 1:2], sq[:, 0:1], 1.0 / 2048.0, RMS_EPS, ALU.mult, ALU.add, r=[sqn], w=[sqn])
                    self.act(sq[:, 1:2], sq[:, 1:2], AF.Sqrt, r=[sqn], w=[sqn])
                    self.S.op("dve", lambda sq=sq: nc.vector.reciprocal(out=sq[:, 2:3], in_=sq[:, 1:2]),
                              reads=[sqn], writes=[sqn])
                    yn_, ynn = ynr.next()
                    self.stt(yn_[:], yp[:], sq[:, 2:3], ng[:], ALU.mult, ALU.mult, r=ypa + [sqn, "ng"], w=[ynn])
                    yT, yTn = yTr.next()
                    for blk in range(2):
                        pt, ptn = ptr.next()
                        for i in range(8):
                            fc = blk * 8 + i
                            self.tr(pt[:, i, :], yn_[:, fc * 128:(fc + 1) * 128], self.ident_b[:], r=[ynn, "identb"], w=[ptn])
                        self.cp("act", yT[:, blk * 8:(blk + 1) * 8, :], pt[:], r=[ptn], w=[yTn])
                    self.dma(g.yT[:, :, r0:r0 + 128].rearrange("k p t -> p k t"), yT[:], r=[yTn])

    def lru_layer(self, g):
        self.lru_p1(g)
        self.lru_p2(g)

    def lru_p1(self, g):
        if self.skip():
            return
        nc = self.nc
        with self.phase():
            win = self.sb([128, 8, 2048], BF16, "win")
            for k in range(8):
                self.dma(win[:, k, :], self.I["lru_in_w"][k * 128:(k + 1) * 128, :], w=[f"win{k}"], q="pool")
            gw = self.sb([128, 32, 256], BF16, "gw")
            gsrc = self.I["lru_gate_w"].rearrange("d g n (kc p) j -> (d g n kc) p j", p=128)
            for i in range(32):
                self.dma(gw[:, i, :], gsrc[i], w=["gw"], q="pool")
            cw = self.sb([128, 4, 8], F32, "cw")
            cb = self.sb([128, 8], F32, "cb")
            gb = self.sb([128, 4, 8], F32, "gb")
            nsp = self.sb([128, 2, 8], F32, "nsp")
            h0 = self.lru_h0
            self.load_T(cw[:].rearrange("p k c -> p (k c)"),
                        self.I["lru_conv_w"].rearrange("k (c p) -> (k c) p", p=128), 32, "cw")
            self.load_T(cb[:], self.I["lru_conv_b"].rearrange("(c p) -> c p", p=128), 8, "cb")
            self.load_T(gb[:].rearrange("p k c -> p (k c)"),
                        self.I["lru_gate_b"].rearrange("k (c p) -> (k c) p", p=128), 32, "gb")
            self.load_T(nsp[:].rearrange("p k c -> p (k c)"),
                        self.I["lru_a_param"].rearrange("k (c p) -> (k c) p", p=128), 16, "nsp")
            self.act(nsp[:], nsp[:], AF.Exp, r=["nsp"], w=["nsp"], scale=-1.0)
            self.act(nsp[:], nsp[:], AF.Ln, r=["nsp"], w=["nsp"], bias=1.0)
            self.ts("dve", nsp[:], nsp[:], -8.0, None, ALU.mult, r=["nsp"], w=["nsp"])
            if g.latent:
                self.load_T(h0[:].rearrange("p k c -> p (k c)"),
                            self.I["st_lru"].rearrange("k (c p) -> (k c) p", p=128), 16, "h0")
            else:
                self.memset("dve", h0[:], 0.0, w=["h0"])
            hwr = Rot([self.sb([128, 8, 260], BF16, "hw") for _ in range(2)], "hw")
            xrr = Rot([self.sb([128, 8, 256], F32, "xr") for _ in range(1)], "xr")
            xbr = Rot([self.sb([128, 8, 256], BF16, "xrb") for _ in range(1)], "xrb")
            zsr = Rot([self.sb([128, 8, 256], F32, "zs") for _ in range(2)], "zs")
            gtr = [Rot([self.sb([128, 8, 256], F32, "gt") for _ in range(1)], f"gt{i}") for i in range(4)]
            aar = [Rot([self.sb([128, 8, 256], F32, "aa") for _ in range(1 + d)], f"aa{d}") for d in range(2)]
            bxr = [Rot([self.sb([128, 8, 256], F32, "bx") for _ in range(1 + d)], f"bx{d}") for d in range(2)]
            tmr = Rot([self.sb([128, 8, 256], F32, "tmpl") for _ in range(1)], "tmpl")
            yfr = Rot([self.sb([128, 8, 256], F32, "yf") for _ in range(2)], "yf")
            accr = Rot([self.sb([128, 256], F32, "acc") for _ in range(4)], "acc")
            fin = self.sb([128, 8], F32, "fin")
            ppr = Rot([self.ps([128, 512], F32, "pp") for _ in range(4)], "pp")
            pgr = Rot([self.ps([128, 256], F32, "pg") for _ in range(3)], "pg")
            for s in range(g.nseq):
                yf_prev = None
                ntile = g.L // 256
                for j in range(ntile):
                    c0 = s * (g.L + 3) + 256 * j
                    t0 = s * g.L + 256 * j
                    hw, hn = hwr.next()
                    self.dma(hw[:, :, 0:259], g.hT[:, :, c0:c0 + 259].rearrange("k p t -> p k t"), w=[hn])
                    xr, xn = xrr.next(); xb, xbn = xbr.next(); zs, zn = zsr.next()
                    def mm_fn(fc, pp, pn, hw=hw, hn=hn):
                        for k in range(8):
                            self.mm(pp[:, 0:259], win[:, k, fc * 128:(fc + 1) * 128], hw[:, k, 0:259],
                                    start=(k == 0), stop=(k == 7), r=[hn, f"win{k}"], w=[pn])

                    def after_fn(fc, xb=xb, xbn=xbn, xr=xr, xn=xn):
                        self.cp("pool", xb[:, fc, :], xr[:, fc, :], r=[f"{xn}_{fc}"], w=[f"{xbn}_{fc}"])
                    self.conv_block(range(8), ppr, accr, mm_fn, lambda fc, xr=xr, xn=xn: (xr[:, fc, :], f"{xn}_{fc}"),
                                    cw, cb, 4, False, after_fn)
                    for fc in range(8, 16):
                        pp, pn = ppr.next()
                        mm_fn(fc, pp, pn)
                        self.act(zs[:, fc - 8, :], pp[:, 1:257], AF.Silu, r=[pn], w=[zn])
                    gts = [gtr[i].next() for i in range(4)]
                    for dg in range(4):
                        gt, gn = gts[dg]
                        for n in range(4):
                            for jc in range(2):
                                pg, pgn = pgr.next()
                                for kc in range(2):
                                    self.mm(pg[:], gw[:, (dg * 4 + n) * 2 + kc, jc * 128:(jc + 1) * 128], xb[:, n * 2 + kc, :],
                                            start=(kc == 0), stop=(kc == 1), r=["gw", f"{xbn}_{n * 2 + kc}"], w=[pgn])
                                ch = n * 2 + jc
                                self.act(gt[:, ch, :], pg[:], AF.Sigmoid, r=[pgn, "gb"], w=[gn], bias=gb[:, dg, ch:ch + 1])
                    xall = [f"{xn}_{fc}" for fc in range(8)]
                    ab = []
                    for d in range(2):
                        aa, aan = aar[d].next(); bx, bxn = bxr[d].next(); tm, tmn = tmr.next()
                        rt, rn = gts[d * 2]; it, itn = gts[d * 2 + 1]
                        for ch in range(8):
                            self.act(aa[:, ch, :], rt[:, ch, :], AF.Exp, r=[rn, "nsp"], w=[aan], scale=nsp[:, d, ch:ch + 1])
                        self.tt("dve", tm[:], aa[:], aa[:], ALU.mult, r=[aan], w=[tmn])
                        self.ts("dve", tm[:], tm[:], -1.0, 1.0, ALU.mult, ALU.add, r=[tmn], w=[tmn])
                        self.ts("dve", tm[:], tm[:], 0.0, None, ALU.max, r=[tmn], w=[tmn])
                        self.act(tm[:], tm[:], AF.Sqrt, r=[tmn], w=[tmn])
                        self.tt("pool", tm[:], tm[:], it[:], ALU.mult, r=[tmn, itn], w=[tmn])
                        self.tt("pool", bx[:], tm[:], xr[:], ALU.mult, r=[tmn] + xall, w=[bxn])
                        ab.append((aa, aan, bx, bxn))
                    yf, yfn = yfr.next()
                    aa, aan, bx, bxn = ab[0]
                    for ch in range(8):
                        init = h0[:, 0, ch:ch + 1] if yf_prev is None else yf_prev[0][:, ch, 255:256]
                        rr = [aan, bxn, "h0"] + ([yf_prev[1]] if yf_prev is not None else [])
                        self.S.op("dve", lambda yf=yf, aa=aa, bx=bx, ch=ch, init=init: nc.vector.tensor_tensor_scan(
                            out=yf[:, ch, :], data0=aa[:, ch, :], data1=bx[:, ch, :], initial=init,
                            op0=ALU.mult, op1=ALU.add), reads=rr, writes=[yfn])
                    yf_prev = (yf, yfn)
                    fmv = lambda slot: g.fm[slot, :, :, t0:t0 + 256].rearrange("c p t -> p c t")
                    self.dma(fmv(0), yf[:], r=[yfn])
                    self.dma(fmv(1), ab[1][0][:], r=[ab[1][1]])
                    self.dma(fmv(2), ab[1][2][:], r=[ab[1][3]])
                    self.dma(fmv(3), zs[:], r=[zn])
                if not g.latent:
                    self.cp("dve", fin[:], yf_prev[0][:, :, 255], r=[yf_prev[1]], w=["fin"])
                    self.dma(self.O["nsl"][s, 0, :].rearrange("(c p) -> p c", p=128), fin[:], r=["fin"],
                             allow_slow_non_contiguous=True)

    def lru_p2(self, g):
        if self.skip():
            return
        nc = self.nc
        with self.phase():
            h0 = self.lru_h0
            ldr = [Rot([self.sb([128, 8, 256], F32, "ld") for _ in range(2)], f"ld{i}") for i in range(4)]
            ybr = Rot([self.sb([128, 8, 256], F32, "yb") for _ in range(2)], "yb")
            yTr = Rot([self.sb([128, 8, 256], BF16, "yTl") for _ in range(2)], "yTl")
            fin = self.sb([128, 8], F32, "fin")
            for s in range(g.nseq):
                yb_prev = None
                ntile = g.L // 256
                for j in range(ntile - 1, -1, -1):
                    t0 = s * g.L + 256 * j
                    lt = []
                    for i in range(4):
                        t, tn = ldr[i].next()
                        self.dma(t[:], g.fm[i, :, :, t0:t0 + 256].rearrange("c p t -> p c t"), w=[tn])
                        lt.append((t, tn))
                    (yf, yfn), (aa, aan), (bx, bxn), (zs, zn) = lt
                    yb, ybn = ybr.next()
                    for ch in range(8):
                        init = h0[:, 1, ch:ch + 1] if yb_prev is None else yb_prev[0][:, ch, 0:1]
                        rr = [aan, bxn, "h0"] + ([yb_prev[1]] if yb_prev is not None else [])
                        self.S.op("dve", lambda yb=yb, aa=aa, bx=bx, ch=ch, init=init: nc.vector.tensor_tensor_scan(
                            out=yb[:, ch, ::-1], data0=aa[:, ch, ::-1], data1=bx[:, ch, ::-1], initial=init,
                            op0=ALU.mult, op1=ALU.add), reads=rr, writes=[ybn])
                    yb_prev = (yb, ybn)
                    self.tt("pool", yf[:], yf[:], yb[:], ALU.add, r=[yfn, ybn], w=[yfn])
                    yT, yTn = yTr.next()
                    self.tt("pool", yT[:], yf[:], zs[:], ALU.mult, r=[yfn, zn], w=[yTn])
                    self.dma(g.yT[0:8, :, t0:t0 + 256].rearrange("c p t -> p c t"), yT[:], r=[yTn])
                if not g.latent:
                    self.cp("dve", fin[:], yb_prev[0][:, :, 0], r=[yb_prev[1]], w=["fin"])
                    self.dma(self.O["nsl"][s, 1, :].rearrange("(c p) -> p c", p=128), fin[:], r=["fin"],
                             allow_slow_non_contiguous=True)

    def dbg_rows(self, dst_row0, src2d, nrows):
        with self.phase():
            for r0 in range(0, nrows, 128):
                t = self.sb([128, 1024], F32, "dbg")
                self.dma(t[:], src2d[r0:r0 + 128, :], w=["dbg"])
                self.dma(self.O["yp"][dst_row0 + r0:dst_row0 + r0 + 128, :], t[:], r=["dbg"])

    def hyena_layer(self, g):
        self.hy_filters(g)
        if _os.environ.get("DEBUG") == "hyk" and g.name == "p":
            self.dbg_rows(0, g.khat2[0, 0], 256)
            self.dbg_rows(256, g.khat2[0, 1], 256)
            self.dbg_rows(512, g.kw[:, 0:1024], 256)
            self.dbg_rows(768, g.kw[:, 1024:2048], 256)
            return
        self.hy_p1(g)
        for o in range(2):
            for s in range(g.nseq):
                self.hy_fwd(g, o, s)
                self.hy_inv(g, o, s)

    def _vec(self, dst, src1d, n, name):
        self.dma(dst, src1d.rearrange("(p o) -> p o", o=1), w=[name], allow_slow_non_contiguous=True)

    def _hy_mlp(self, g, hd2, w3r):
        nc = self.nc
        L = g.L
        nm = g.name
        TWO_PI = 2.0 * math.pi
        MAGIC = 12582912.0
        with self.phase():
            zT = self.sb([33, L], F32, "zT")
            self.dma(zT[:], self.I[f"hz_{nm}"], w=["zT"])
            w1 = self.sb([33, 64], F32, "w1"); w2 = self.sb([64, 64], F32, "w2")
            self.dma(w1[:], self.I["hy_f_w1"], w=["w1"]); self.dma(w2[:], self.I["hy_f_w2"], w=["w2"])
            w3 = self.sb([64, 4096], F32, "w3")
            self.dma(w3[:], self.I["hy_f_w3"], w=["w3"])
            self.cp("pool", w3r[:], w3[:], r=["w3"], w=["w3r"])
            pv = self.sb([64, 6], F32, "pv")
            self._vec(pv[:, 0:1], self.I["hy_f_b1"], 64, "pv"); self._vec(pv[:, 1:2], self.I["hy_f_b2"], 64, "pv")
            self._vec(pv[:, 2:3], self.I["hy_f_freq"][0], 64, "pv"); self._vec(pv[:, 3:4], self.I["hy_f_freq"][1], 64, "pv")
            self.tt("dve", pv[:, 4:6], pv[:, 0:2], pv[:, 2:4], ALU.mult, r=["pv"], w=["pv"])
            hd1 = self.sb([64, L], F32, "hd1")
            argr = Rot([self.sb([64, 512], F32, "arg") for _ in range(2)], "arg")
            nr = Rot([self.sb([64, 512], F32, "nq") for _ in range(2)], "nq")
            phr = Rot([self.ps([64, 512], F32, "ph") for _ in range(2)], "ph")
            TW = min(512, L)
            for layer in range(2):
                for ti in range(L // TW):
                    sl = slice(ti * TW, (ti + 1) * TW)
                    ph, phn = phr.next()
                    if layer == 0:
                        self.mm(ph[:, 0:TW], w1[:], zT[:, sl], r=["w1", "zT"], w=[phn])
                    else:
                        self.mm(ph[:, 0:TW], w2[:], hd1[:, sl], r=["w2", "hd1"], w=[phn])
                    arg, an = argr.next(); nq, nn = nr.next()
                    self.act(arg[:, 0:TW], ph[:, 0:TW], AF.Identity, r=[phn, "pv"], w=[an],
                             scale=pv[:, 2 + layer:3 + layer], bias=pv[:, 4 + layer:5 + layer])
                    self.ts("dve", nq[:, 0:TW], arg[:, 0:TW], 1.0 / TWO_PI, MAGIC, ALU.mult, ALU.add, r=[an], w=[nn])
                    self.ts("dve", nq[:, 0:TW], nq[:, 0:TW], MAGIC, None, ALU.subtract, r=[nn], w=[nn])
                    self.stt(arg[:, 0:TW], nq[:, 0:TW], -TWO_PI, arg[:, 0:TW], ALU.mult, ALU.add, r=[nn, an], w=[an])
                    self.ts("dve", arg[:, 0:TW], arg[:, 0:TW], math.pi, -math.pi, ALU.min, ALU.max, r=[an], w=[an])
                    if layer == 0:
                        self.act(hd1[:, sl], arg[:, 0:TW], AF.Sin, r=[an], w=["hd1"])
                    else:
                        self.act(hd2[:, sl], arg[:, 0:TW], AF.Sin, r=[an], w=["hd2"])

    def hy_filters(self, g):
        if self.skip():
            return
        nc = self.nc
        L = g.L
        nL = L // 128
        nm = g.name
        TWO_PI = 2.0 * math.pi
        MAGIC = 12582912.0
        with self.phase():
            hd2 = self.sb([64, L], F32R, "hd2")
            w3r = self.sb([64, 4096], F32R, "w3r")
            self._hy_mlp(g, hd2, w3r)
            asum = self.sb([128, 4096], F32, "asum")
            self.memset("pool", asum[:], 0.0, w=["asum"])
            winr = Rot([self.sb([128, 1024], F32, "winw") for _ in range(2)], "winw")
            kwr = Rot([self.sb([128, 4096], F32, "kwt") for _ in range(2)], "kwt")
            kabs = self.sb([128, 4096], F32, "kabs")
            pkr = Rot([self.ps([128, 512], F32, "pk") for _ in range(3)], "pk")
            for tc in range(nL):
                wt, wn = winr.next()
                self.dma(wt[:], self.I[f"win_{nm}"][tc * 128:(tc + 1) * 128, :], w=[wn])
                kt, kn = kwr.next()
                for ti in range(8):
                    pk, pkn = pkr.next()
                    self.mm(pk[:], hd2[:, tc * 128:(tc + 1) * 128], w3r[:, ti * 512:(ti + 1) * 512], r=["hd2", "w3r"], w=[pkn])
                    self.tt("dve", kt[:, ti * 512:(ti + 1) * 512], pk[:], wt[:, (ti % 2) * 512:(ti % 2 + 1) * 512], ALU.mult,
                            r=[pkn, wn], w=[kn])
                self.act(kabs[:], kt[:], AF.Abs, r=[kn], w=["kabs"])
                self.tt("pool", asum[:], asum[:], kabs[:], ALU.add, r=["kabs", "asum"], w=["asum"])
                self.dma(g.kw[tc * 128:(tc + 1) * 128, :], kt[:], r=[kn])
            asr = self.sb([128, 4096], F32R, "asr")
            self.cp("dve", asr[:], asum[:], r=["asum"], w=["asr"])
            tot = self.sb([128, 4096], F32, "tot")
            for ti in range(8):
                pk, pkn = pkr.next()
                self.mm(pk[:], self.ones_r[:], asr[:, ti * 512:(ti + 1) * 512], r=["onesr", "asr"], w=[pkn])
                self.cp("act", tot[:, ti * 512:(ti + 1) * 512], pk[:], r=[pkn], w=["tot"])
            rn = self.sb([128, 2, 1024], F32, "rn")
            tv = tot[:].rearrange("p (o d c) -> p o d c", o=2, d=2)
            self.tt("dve", rn[:], tv[:, :, 0, :], tv[:, :, 1, :], ALU.add, r=["tot"], w=["rn"])
            rn0 = self.sb([128, 2, 1024], F32, "rn0")
            self.act(rn0[:], rn[:], AF.Ln, r=["rn"], w=["rn0"])
            self.act(rn[:], rn0[:], AF.Exp, r=["rn0"], w=["rn"], scale=-1.0)
            self.ts("dve", rn[:], rn[:], 2.0 / (2 * L), None, ALU.mult, r=["rn"], w=["rn"])
            self.dma(g.khat[0, 0, 0:128, :], rn[:, 0, :], r=["rn"])
            self.dma(g.khat[0, 1, 0:128, :], rn[:, 1, :], r=["rn"])
        with self.phase():
            rn = self.sb([128, 2, 1024], F32, "rn")
            self.dma(rn[:, 0, :], g.khat[0, 0, 0:128, :], w=["rn"])
            self.dma(rn[:, 1, :], g.khat[0, 1, 0:128, :], w=["rn"])
            self.S.barrier()
            ks = self.sb([128, nL, 1024], BF16, "ks")
            kldr = Rot([self.sb([128, 2, 1024], F32, "kld") for _ in range(2)], "kld")
            mbr = Rot([self.sb([128, nL, 128], BF16, "mb") for _ in range(2)], "mb")
            kor = Rot([self.sb([128, 1024], F32, "ko") for _ in range(2)], "ko")
            pur = Rot([self.ps([128, 1024], F32, "pu") for _ in range(2)], "pu")
            for o in range(2):
                for cs in range(2):
                    for tc in range(nL):
                        kl, kln = kldr.next()
                        self.dma(kl[:], g.kw[tc * 128:(tc + 1) * 128, o * 2048:(o + 1) * 2048].rearrange("p (d c) -> p d c", d=2),
                                 w=[kln])
                        if tc == 0:
                            self.memset("dve", kl[0:1, 1, :], 0.0, w=[kln])
                        self.tt("dve", kl[:, 0, :], kl[:, 0, :], kl[:, 1, :], ALU.add if cs == 0 else ALU.subtract,
                                r=[kln], w=[kln])
                        self.tt("pool", ks[:, tc, :], kl[:, 0, :], rn[:, o, :], ALU.mult, r=[kln, "rn"], w=[f"ks{tc}"])
                    for fc in range(nL):
                        mb, mbn = mbr.next()
                        self.dma(mb[:], self.I[f"dft_{nm}"][2 + cs, fc], w=[mbn])
                        pu, pun = pur.next()
                        for tc in range(nL):
                            for hlf in range(2):
                                self.mm(pu[:, hlf * 512:(hlf + 1) * 512], mb[:, tc, :], ks[:, tc, hlf * 512:(hlf + 1) * 512],
                                        start=(tc == 0), stop=(tc == nL - 1), r=[mbn, f"ks{tc}"],
                                        w=[pun + str(hlf)])
                        ko, kon = kor.next()
                        for hlf in range(2):
                            self.cp("act", ko[:, hlf * 512:(hlf + 1) * 512], pu[:, hlf * 512:(hlf + 1) * 512],
                                    r=[pun + str(hlf)], w=[kon])
                        self.dma(g.khat2[o, cs, fc * 128:(fc + 1) * 128, :], ko[:], r=[kon])

    def hy_p1(self, g):
        if self.skip():
            return
        nc = self.nc
        with self.phase():
            win = self.sb([128, 8, 4096], BF16, "win")
            for k in range(8):
                self.dma(win[:, k, :], self.I["hy_in_w"][k * 128:(k + 1) * 128, :], w=[f"win{k}"], q="pool")
            cw = self.sb([128, 3, 24], F32, "cw")
            cb = self.sb([128, 24], F32, "cb")
            self.load_T(cw[:].rearrange("p k c -> p (k c)"),
                        self.I["hy_conv_w"].rearrange("k (c p) -> (k c) p", p=128), 72, "cw")
            self.load_T(cb[:], self.I["hy_conv_b"].rearrange("(c p) -> c p", p=128), 24, "cb")
            hwr = Rot([self.sb([128, 8, 260], BF16, "hw") for _ in range(2)], "hw")
            fmr = [Rot([self.sb([128, 8, 256], F32, "fmt") for _ in range(2)], f"fmt{i}") for i in range(4)]
            vbr = Rot([self.sb([128, 8, 256], BF16, "vb") for _ in range(2)], "vb")
            utr = Rot([self.sb([128, 1024], BF16, "ut") for _ in range(2)], "ut")
            accr = Rot([self.sb([128, 256], F32, "acc") for _ in range(4)], "acc")
            ppr = Rot([self.ps([128, 512], F32, "pp") for _ in range(4)], "pp")
            ptr = Rot([self.ps([128, 8, 128], BF16, "ptr") for _ in range(2)], "ptr")
            for s in range(g.nseq):
                for j in range(g.L // 256):
                    c0 = s * (g.L + 3) + 256 * j
                    t0 = s * g.L + 256 * j
                    hw, hn = hwr.next()
                    self.dma(hw[:, :, 0:259], g.hT[:, :, c0:c0 + 259].rearrange("k p t -> p k t"), w=[hn])
                    fts = [fmr[i].next() for i in range(4)]
                    vb, vbn = vbr.next()
                    def mm_fn(fc, pp, pn, hw=hw, hn=hn):
                        for k in range(8):
                            self.mm(pp[:, 0:259], win[:, k, fc * 128:(fc + 1) * 128], hw[:, k, 0:259],
                                    start=(k == 0), stop=(k == 7), r=[hn, f"win{k}"], w=[pn])

                    def dst_fn(fc, fts=fts):
                        ft, fn = fts[fc // 8]
                        return ft[:, fc % 8, :], f"{fn}_{fc % 8}"

                    def after_fn(fc, fts=fts, vb=vb, vbn=vbn):
                        if fc < 8:
                            ft, fn = fts[0]
                            self.cp("pool", vb[:, fc, :], ft[:, fc, :], r=[f"{fn}_{fc}"], w=[f"{vbn}_{fc}"])
                    self.conv_block(range(24), ppr, accr, mm_fn, dst_fn, cw, cb, 3, False, after_fn)
                    for fc in range(24, 32):
                        pp, pn = ppr.next()
                        mm_fn(fc, pp, pn)
                        ft, fn = fts[3]
                        self.act(ft[:, fc % 8, :], pp[:, 1:257], AF.Silu, r=[pn], w=[f"{fn}_{fc % 8}"])
                    for i in range(4):
                        ft, fn = fts[i]
                        self.dma(g.fm[i, :, :, t0:t0 + 256].rearrange("c p t -> p c t"), ft[:],
                                 r=[f"{fn}_{c}" for c in range(8)])
                    for tcn in range(2):
                        pt, ptn = ptr.next()
                        for c in range(8):
                            self.tr(pt[:, c, :], vb[:, c, tcn * 128:(tcn + 1) * 128], self.ident_b[:],
                                    r=[f"{vbn}_{c}", "identb"], w=[ptn])
                        ut, utn = utr.next()
                        self.cp("dve", ut[:], pt[:].rearrange("p a b -> p (a b)"), r=[ptn], w=[utn])
                        self.dma(g.utm[t0 + tcn * 128:t0 + (tcn + 1) * 128, :], ut[:], r=[utn])

    def hy_fwd(self, g, o, s):
        if self.skip():
            return
        nc = self.nc
        L = g.L
        nL = L // 128
        nm = g.name
        with self.phase():
            u = self.sb([128, nL, 1024], BF16, "u")
            usrc = g.utm[s * L:(s + 1) * L, :].rearrange("(tc p) c -> p tc c", p=128)
            for q in range(0, nL, 8):
                qe = min(q + 8, nL)
                self.dma(u[:, q:qe, :], usrc[:, q:qe, :], w=[f"u{q}"])
            cbr = Rot([self.sb([128, nL, 128], BF16, "cbk") for _ in range(2)], "cbk")
            sbr = Rot([self.sb([128, nL, 128], BF16, "sbk") for _ in range(2)], "sbk")
            kcr = Rot([self.sb([128, 1024], F32, "kc") for _ in range(2)], "kc")
            ksr = Rot([self.sb([128, 1024], F32, "ksp") for _ in range(2)], "ksp")
            a1r = Rot([self.sb([128, 1024], F32, "a1") for _ in range(2)], "a1")
            a2r = Rot([self.sb([128, 1024], F32, "a2") for _ in range(2)], "a2")
            yor = Rot([self.sb([128, 2, 1024], BF16, "yo") for _ in range(2)], "yo")
            puc = Rot([self.ps([128, 1024], F32, "puc") for _ in range(2)], "puc")
            pus = Rot([self.ps([128, 1024], F32, "pus") for _ in range(2)], "pus")
            for fc in range(nL):
                cb_, cbn = cbr.next(); sb_, sbn = sbr.next()
                self.dma(cb_[:], self.I[f"dft_{nm}"][0, fc], w=[cbn])
                self.dma(sb_[:], self.I[f"dft_{nm}"][1, fc], w=[sbn])
                kc, kcn = kcr.next(); ksp, ksn = ksr.next()
                self.dma(kc[:], g.khat2[o, 0, fc * 128:(fc + 1) * 128, :], w=[kcn])
                self.dma(ksp[:], g.khat2[o, 1, fc * 128:(fc + 1) * 128, :], w=[ksn])
                pc_, pcn = puc.next(); ps_, psn = pus.next()
                for tc in range(nL):
                    q8 = (tc // 8) * 8
                    for hlf in range(2):
                        hs = slice(hlf * 512, (hlf + 1) * 512)
                        self.mm(pc_[:, hs], cb_[:, tc, :], u[:, tc, hs], start=(tc == 0), stop=(tc == nL - 1),
                                r=[cbn, f"u{q8}"], w=[pcn + str(hlf)])
                        self.mm(ps_[:, hs], sb_[:, tc, :], u[:, tc, hs], start=(tc == 0), stop=(tc == nL - 1),
                                r=[sbn, f"u{q8}"], w=[psn + str(hlf)])
                a1, a1n = a1r.next(); a2, a2n = a2r.next(); yo, yon = yor.next()
                for hlf in range(2):
                    hs = slice(hlf * 512, (hlf + 1) * 512)
                    self.tt("dve", a1[:, hs], pc_[:, hs], kc[:, hs], ALU.mult, r=[pcn + str(hlf), kcn], w=[a1n])
                    self.tt("dve", a2[:, hs], ps_[:, hs], ksp[:, hs], ALU.mult, r=[psn + str(hlf), ksn], w=[a2n])
                self.tt("pool", yo[:, 0, :], a1[:], a2[:], ALU.subtract, r=[a1n, a2n], w=[yon + "c"])
                a1, a1n = a1r.next(); a2, a2n = a2r.next()
                for hlf in range(2):
                    hs = slice(hlf * 512, (hlf + 1) * 512)
                    self.tt("dve", a1[:, hs], pc_[:, hs], ksp[:, hs], ALU.mult, r=[pcn + str(hlf), ksn], w=[a1n])
                    self.tt("dve", a2[:, hs], ps_[:, hs], kc[:, hs], ALU.mult, r=[psn + str(hlf), kcn], w=[a2n])
                self.tt("pool", yo[:, 1, :], a1[:], a2[:], ALU.add, r=[a1n, a2n], w=[yon + "s"])
                self.dma(g.yspec[:, :, :, fc, :].rearrange("a cc p j -> p a cc j"),
                         yo[:].rearrange("p a (cc j) -> p a cc j", j=128), r=[yon + "c", yon + "s"])

    def hy_inv(self, g, o, s):
        if self.skip():
            return
        nc = self.nc
        L = g.L
        nL = L // 128
        nm = g.name
        TT = min(512, L)
        nq = TT // 128
        with self.phase():
            fb = self.sb([128, 2, 8], F32, "fb")
            self.load_T(fb[:].rearrange("p k c -> p (k c)"),
                        self.I["hy_f_bias"].rearrange("k (c p) -> (k c) p", p=128), 16, "fb")
            cm = self.sb([128, nL, TT], BF16, "cm")
            sm = self.sb([128, nL, TT], BF16, "sm")
            ycr = Rot([self.sb([128, nL, 128], BF16, "yc") for _ in range(2)], "yc")
            ysr = Rot([self.sb([128, nL, 128], BF16, "ys") for _ in range(2)], "ys")
            utr = Rot([self.sb([128, TT], F32, "uti") for _ in range(2)], "uti")
            xgr = Rot([self.sb([128, TT], F32, "xg") for _ in range(2)], "xg")
            zgr = Rot([self.sb([128, TT], F32, "zg") for _ in range(2)], "zg")
            unr = Rot([self.sb([128, TT], F32, "un") for _ in range(2)], "un")
            ubr = Rot([self.sb([128, TT], BF16, "ub") for _ in range(2)], "ub")
            uor = Rot([self.sb([128, 8, 128], BF16, "uo") for _ in range(2)], "uo")
            par = Rot([self.ps([128, 512], F32, "pa") for _ in range(2)], "pa")
            ptq = [self.ps([128, 8, 128], BF16, "ptq") for _ in range(nq)] if o == 0 else []
            for tt in range(L // TT):
                t0 = s * L + tt * TT
                for q in range(0, nL, 8):
                    qe = min(q + 8, nL)
                    self.dma(cm[:, q:qe, :], self.I[f"dfti_{nm}"][0, tt, :, q:qe, :], w=[f"cm{q}"])
                    self.dma(sm[:, q:qe, :], self.I[f"dfti_{nm}"][1, tt, :, q:qe, :], w=[f"sm{q}"])
                for cc in range(8):
                    yc, ycn = ycr.next(); ys_, ysn = ysr.next()
                    self.dma(yc[:], g.yspec[0, cc], w=[ycn])
                    self.dma(ys_[:], g.yspec[1, cc], w=[ysn])
                    ut, utn = utr.next(); xg, xgn = xgr.next()
                    self.dma(ut[:], g.fm[0, cc, :, t0:t0 + TT], w=[utn])
                    self.dma(xg[:], g.fm[1 + o, cc, :, t0:t0 + TT], w=[xgn])
                    pa, pan = par.next()
                    for fc in range(nL):
                        q8 = (fc // 8) * 8
                        self.mm(pa[:, 0:TT], yc[:, fc, :], cm[:, fc, :], start=(fc == 0), stop=False,
                                r=[ycn, f"cm{q8}"], w=[pan])
                        self.mm(pa[:, 0:TT], ys_[:, fc, :], sm[:, fc, :], start=False, stop=(fc == nL - 1),
                                r=[ysn, f"sm{q8}"], w=[pan])
                    un, unn = unr.next()
                    self.stt(un[:], ut[:], fb[:, o, cc:cc + 1], pa[:, 0:TT], ALU.mult, ALU.add, r=[utn, "fb", pan], w=[unn])
                    self.tt("pool", un[:], un[:], xg[:], ALU.mult, r=[unn, xgn], w=[unn])
                    if o == 0:
                        self.dma(g.fm[0, cc, :, t0:t0 + TT], un[:], r=[unn])
                        ub, ubn = ubr.next()
                        self.cp("act", ub[:], un[:], r=[unn], w=[ubn])
                        for q in range(nq):
                            self.tr(ptq[q][:, cc, :], ub[:, q * 128:(q + 1) * 128], self.ident_b[:],
                                    r=[ubn, "identb"], w=[f"ptq{q}"])
                    else:
                        zg, zgn = zgr.next()
                        self.dma(zg[:], g.fm[3, cc, :, t0:t0 + TT], w=[zgn])
                        ub, ubn = ubr.next()
                        self.tt("dve", ub[:], un[:], zg[:], ALU.mult, r=[unn, zgn], w=[ubn])
                        self.dma(g.yT[cc, :, t0:t0 + TT], ub[:], r=[ubn])
                if o == 0:
                    for q in range(nq):
                        uo, uon = uor.next()
                        self.cp("act" if q % 2 == 0 else "dve", uo[:], ptq[q][:], r=[f"ptq{q}"], w=[uon])
                        self.dma(g.utm[t0 + q * 128:t0 + (q + 1) * 128, :], uo[:].rearrange("p a b -> p (a b)"), r=[uon])


def _consts():
    k = np.arange(128)[:, None]
    i = np.arange(128)[None, :]
    masks = np.stack([(k <= i), (k >= i), (k > i), (k < i)]).astype(np.float32)
    out = {"masks": masks}
    for nm, L in (("p", LP), ("s", LS)):
        n = 2 * L
        t = np.arange(L, dtype=np.float64)
        f = np.arange(L, dtype=np.float64)
        th = 2.0 * np.pi * (f + 0.5) / n
        a2 = np.outer(t + 0.5, th)
        am = np.outer(t, th)
        M = np.stack([np.cos(a2), np.sin(a2), np.cos(am), np.sin(am)]).astype(np.float32).astype(ml_dtypes.bfloat16)
        nL = L // 128
        TT = min(512, L)
        out[f"dft_{nm}"] = np.ascontiguousarray(M.reshape(4, nL, 128, nL, 128).transpose(0, 3, 2, 1, 4))
        out[f"dfti_{nm}"] = np.ascontiguousarray(M[0:2].reshape(2, nL, 128, L // TT, TT).transpose(0, 3, 2, 1, 4))
        tt = (np.arange(L, dtype=np.float32) / np.float32(L)).astype(np.float32)
        w = (np.float32(2.0 * math.pi) * np.arange(L, dtype=np.float32) / np.float32(L)).astype(np.float32)
        fr = np.linspace(1e-4, 15, 16, dtype=np.float32)
        ang = w[:, None] * fr
        z = np.concatenate([tt[:, None], np.cos(ang), np.sin(ang)], axis=-1).astype(np.float32)
        out[f"hz_{nm}"] = np.ascontiguousarray(z.T)
        deltas = np.linspace(math.log(HY_T) / 1.5, math.log(HY_T) / 0.3, D, dtype=np.float32)
        out[f"win_{nm}"] = np.exp(-tt[:, None] * np.abs(deltas)).astype(np.float32)
    return out


_CACHE = {}


def nc_inputs(nc):
    return _CACHE["in_names"]


def kernel(**inputs):
    f = lambda a: np.ascontiguousarray(np.asarray(a, dtype=np.float32))
    if "nc" not in _CACHE:
        kb = KB()
        _CACHE["nc"] = kb.build()
        _CACHE["in_names"] = set(kb.I.keys())
        _CACHE["consts"] = _consts()
    nc = _CACHE["nc"]
    consts = _CACHE["consts"]
    shared = {
        "mod_w": f(inputs["mod_w"]), "mod_b": f(inputs["mod_b"]), "ln_g": f(inputs["ln_g"]), "ln_b": f(inputs["ln_b"]),
        "ssd_in_w": f(inputs["ssd_in_w"]), "ssd_conv_w": f(inputs["ssd_conv_w"]), "ssd_conv_b": f(inputs["ssd_conv_b"]),
        "ssd_dt_bias": f(inputs["ssd_dt_bias"]).reshape(2, 64), "ssd_a_log": f(inputs["ssd_a_log"]).reshape(2, 64),
        "ssd_d": f(inputs["ssd_d"]), "ssd_norm_g": f(inputs["ssd_norm_g"]), "ssd_out_w": f(inputs["ssd_out_w"]),
        "hy_in_w": f(inputs["hy_in_w"])[0], "hy_conv_w": f(inputs["hy_conv_w"])[0], "hy_conv_b": f(inputs["hy_conv_b"])[0],
        "hy_f_w1": f(inputs["hy_f_w1"])[0], "hy_f_b1": f(inputs["hy_f_b1"])[0], "hy_f_w2": f(inputs["hy_f_w2"])[0],
        "hy_f_b2": f(inputs["hy_f_b2"])[0], "hy_f_w3": f(inputs["hy_f_w3"])[0], "hy_f_freq": f(inputs["hy_f_freq"])[0],
        "hy_f_bias": f(inputs["hy_f_bias"])[0], "hy_out_w": f(inputs["hy_out_w"])[0],
        "lru_in_w": f(inputs["lru_in_w"])[0], "lru_conv_w": f(inputs["lru_conv_w"])[0], "lru_conv_b": f(inputs["lru_conv_b"])[0],
        "lru_gate_w": f(inputs["lru_gate_w"])[0], "lru_gate_b": f(inputs["lru_gate_b"])[0].reshape(4, D),
        "lru_a_param": f(inputs["lru_a_param"])[0], "lru_out_w": f(inputs["lru_out_w"])[0],
    }
    shared.update(consts)
    shared = {k: v for k, v in shared.items() if k in nc_inputs(nc)}
    xp = f(inputs["x_prompt"]); xs = f(inputs["x_sample"])
    sts = f(inputs["state_ssd"]); stl = f(inputs["state_lru"])
    c = f(inputs["c"]); cc = f(inputs["c_ctx"])
    in_maps = []
    for core in range(8):
        b = core // 2
        m = dict(shared)
        m["xp"] = np.ascontiguousarray(xp[core * NPS:(core + 1) * NPS].reshape(NPS * LP, D))
        m["xs"] = np.ascontiguousarray(xs[b])
        m["st_ssd"] = np.ascontiguousarray(sts[b].reshape(2, 2, 2048, 128))
        m["st_lru"] = np.ascontiguousarray(stl[b].reshape(2, D))
        m["cond"] = np.ascontiguousarray(np.stack([cc, c[b]]))
        in_maps.append(m)
    res = run_bass_kernel_spmd(nc, in_maps, core_ids=list(range(8)))
    r = res.results
    y_prompt = np.concatenate([r[i]["yp"].reshape(NPS, LP, D) for i in range(8)], axis=0)
    y_sample = np.stack([r[2 * b]["ys"] for b in range(4)], axis=0)
    nss = np.concatenate([r[i]["nss"].reshape(NPS, 2, 2, 32, 64, 128) for i in range(8)], axis=0)
    nsl = np.concatenate([r[i]["nsl"].reshape(NPS, 1, 2, D) for i in range(8)], axis=0)
    return (y_prompt.astype(np.float32), y_sample.astype(np.float32), nss.astype(np.float32), nsl.astype(np.float32))
```
